# Optimizing a Trainium2 kernel written in Bass

```python
import math
import numpy as np
import jax
import jax.numpy as jnp
from jax import lax

D_MODEL = 1024
BATCH = 4
SEQ = 8192
DEPTH = 2

SSD_D_INNER = D_MODEL
SSD_HEADDIM = 64
SSD_HEADS = SSD_D_INNER // SSD_HEADDIM
SSD_GROUPS = 2
SSD_STATE = 128
SSD_CONV = 5
SSD_CHUNK = 128
MLSTM_D = D_MODEL
MLSTM_HEADS = 8
MLSTM_HEADDIM = MLSTM_D // MLSTM_HEADS
MLSTM_CHUNK = 128
GLA_HEADS = 4
GLA_DK = D_MODEL // 2
GLA_DV = D_MODEL
GLA_HEAD_DK = GLA_DK // GLA_HEADS
GLA_HEAD_DV = GLA_DV // GLA_HEADS
GLA_RANK = 16
GLA_TAU = 16.0
GLA_CHUNK = 64
FFN_DIM = 11 * D_MODEL // 4
FFN_CONV = 3
NORM_EPS = 1e-6

SSD_XBC = SSD_D_INNER + 2 * SSD_GROUPS * SSD_STATE
L0_SPLITS = (SSD_D_INNER, SSD_XBC, 2 * SSD_HEADS, MLSTM_D, MLSTM_D, MLSTM_D, MLSTM_D, 4 * MLSTM_HEADS)
L0_IN = sum(L0_SPLITS)
L0_MIX = SSD_D_INNER + MLSTM_D
L1_SPLITS = (GLA_DK, GLA_DK, GLA_DV, GLA_DV, 2 * GLA_RANK)
L1_IN = sum(L1_SPLITS)
N_EVEN_LAYERS = (DEPTH + 1) // 2
N_ODD_LAYERS = DEPTH // 2

kernel_name = "bidir_hybrid_ssd_mlstm_gla_convffn"


def _split(u, sizes):
    return jnp.split(u, np.cumsum(sizes)[:-1].tolist(), axis=-1)


def _rev(t):
    return jnp.flip(t, axis=1)


def _lower_tri(n):
    return jnp.tril(jnp.ones((n, n), dtype=bool))


def _rmsnorm(x, g):
    xf = x.astype(jnp.float32)
    y = xf * lax.rsqrt(jnp.mean(xf * xf, axis=-1, keepdims=True) + NORM_EPS)
    return (y * g.astype(jnp.float32)).astype(x.dtype)


def _head_rmsnorm(x, g, n_heads):
    shp = x.shape
    xh = x.reshape(shp[:-1] + (n_heads, shp[-1] // n_heads))
    return _rmsnorm(xh, g.reshape(n_heads, -1)).reshape(shp)


def _dwconv(x, w, b):
    k = w.shape[0]
    y = lax.conv_general_dilated(x, w[:, None, :], window_strides=(1,), padding=[(k // 2, k // 2)],
                                 dimension_numbers=('NWC', 'WIO', 'NWC'), feature_group_count=x.shape[-1])
    return y + b


def _bidirectional(scan_fn, seq_fwd, seq_bwd, par_fwd=(), par_bwd=()):
    y_f = scan_fn(*seq_fwd, *par_fwd)
    y_b = scan_fn(*[_rev(t) for t in seq_bwd], *par_bwd)
    return y_f + _rev(y_b)


def _ssd_scan(x, dt, bm, cm, a):
    bsz, s, nh, p = x.shape
    g, n = bm.shape[2], bm.shape[3]
    e = nh // g
    l = SSD_CHUNK
    c = s // l
    dt = dt.astype(jnp.float32)
    xdt = (x * dt[..., None]).reshape(bsz, c, l, g, e, p)
    cs = jnp.cumsum((dt * a.astype(jnp.float32)).reshape(bsz, c, l, g, e), axis=2)
    bm = bm.reshape(bsz, c, l, g, n)
    cm = cm.reshape(bsz, c, l, g, n)
    cs_t = jnp.transpose(cs, (0, 3, 4, 1, 2))
    decay = jnp.exp(jnp.where(_lower_tri(l), cs_t[..., :, None] - cs_t[..., None, :], -jnp.inf))
    cb = jnp.einsum('bclgn,bcsgn->bgcls', cm, bm)
    y = jnp.einsum('bgecls,bcsgep->bclgep', cb[:, :, None] * decay, xdt)
    x_end = xdt * jnp.exp(cs[:, :, -1:] - cs)[..., None]
    states = jnp.einsum('bclgn,bclgep->cbgepn', bm, x_end)
    chunk_decay = jnp.exp(jnp.transpose(cs[:, :, -1], (1, 0, 2, 3)))

    def step(hs, inp):
        st, dec = inp
        return dec[..., None, None] * hs + st, hs

    _, h_prev = lax.scan(step, jnp.zeros(states.shape[1:], states.dtype), (states, chunk_decay))
    y = y + jnp.einsum('bclgn,cbgepn->bclgep', cm, h_prev) * jnp.exp(cs)[..., None]
    return y.reshape(bsz, s, nh, p)


def _mlstm_scan(q, k, v, ig, lf):
    bsz, s, nh, dk = q.shape
    dv = v.shape[-1]
    l = MLSTM_CHUNK
    c = s // l
    q = q.reshape(bsz, c, l, nh, dk)
    k = k.reshape(bsz, c, l, nh, dk)
    v = v.reshape(bsz, c, l, nh, dv)
    ig = ig.astype(jnp.float32).reshape(bsz, c, l, nh)
    fc = jnp.cumsum(lf.astype(jnp.float32).reshape(bsz, c, l, nh), axis=2)
    f_end = fc[:, :, -1]
    a = f_end[:, :, None] - fc + ig
    m_loc = jnp.max(a, axis=2)
    kw = k * jnp.exp(a - m_loc[:, :, None])[..., None]
    c_loc = jnp.einsum('bclhk,bclhv->cbhkv', kw, v)
    n_loc = jnp.transpose(jnp.sum(kw, axis=2), (1, 0, 2, 3))

    def step(carry, inp):
        c_st, n_st, m_st = carry
        c_l, n_l, m_l, f_l = inp
        m_new = jnp.maximum(f_l + m_st, m_l)
        s_old = jnp.exp(f_l + m_st - m_new)
        s_loc = jnp.exp(m_l - m_new)
        c_new = s_old[..., None, None] * c_st + s_loc[..., None, None] * c_l
        n_new = s_old[..., None] * n_st + s_loc[..., None] * n_l
        return (c_new, n_new, m_new), (c_st, n_st, m_st)

    init = (jnp.zeros(c_loc.shape[1:], c_loc.dtype), jnp.zeros(n_loc.shape[1:], n_loc.dtype),
            jnp.zeros((bsz, nh), jnp.float32))
    _, (c_st, n_st, m_st) = lax.scan(step, init, (c_loc, n_loc, jnp.transpose(m_loc, (1, 0, 2)),
                                                  jnp.transpose(f_end, (1, 0, 2))))
    fc_t = jnp.transpose(fc, (0, 3, 1, 2))
    ig_t = jnp.transpose(ig, (0, 3, 1, 2))
    dmat = jnp.where(_lower_tri(l), fc_t[..., :, None] - fc_t[..., None, :] + ig_t[..., None, :], -jnp.inf)
    m_inter = fc_t + jnp.transpose(m_st, (1, 2, 0))[..., None]
    m_t = jnp.maximum(m_inter, jnp.max(dmat, axis=-1))
    wmat = jnp.einsum('bclhk,bcshk->bhcls', q, k) * jnp.exp(dmat - m_t[..., None])
    s_inter = jnp.transpose(jnp.exp(m_inter - m_t), (0, 2, 3, 1))
    num = (jnp.einsum('bhcls,bcshv->bclhv', wmat, v)
           + s_inter[..., None] * jnp.einsum('bclhk,cbhkv->bclhv', q, c_st))
    den = (jnp.transpose(jnp.sum(wmat, axis=-1), (0, 2, 3, 1))
           + s_inter * jnp.einsum('bclhk,cbhk->bclh', q, n_st))
    floor = jnp.exp(-jnp.transpose(m_t, (0, 2, 3, 1)))
    h = num / jnp.maximum(jnp.abs(den), floor)[..., None]
    return h.reshape(bsz, s, nh, dv)


def _gla_scan(q, k, v, la):
    bsz, s, nh, dk = q.shape
    dv = v.shape[-1]
    l = GLA_CHUNK
    c = s // l
    q = q.reshape(bsz, c, l, nh, dk)
    k = k.reshape(bsz, c, l, nh, dk)
    v = v.reshape(bsz, c, l, nh, dv)
    bc = jnp.cumsum(la.astype(jnp.float32).reshape(bsz, c, l, nh, dk), axis=2)
    b_end = bc[:, :, -1:]
    q_in = q * jnp.exp(bc)
    att = jnp.einsum('bclhk,bcshk->bhcls', q_in, k * jnp.exp(-bc))
    att = jnp.where(_lower_tri(l), att, 0.0)
    o = jnp.einsum('bhcls,bcshv->bclhv', att, v)
    u = jnp.einsum('bclhk,bclhv->cbhkv', k * jnp.exp(b_end - bc), v)
    dec = jnp.exp(jnp.transpose(b_end[:, :, 0], (1, 0, 2, 3)))

    def step(st, inp):
        u_c, d_c = inp
        return d_c[..., None] * st + u_c, st

    _, s_prev = lax.scan(step, jnp.zeros(u.shape[1:], u.dtype), (u, dec))
    o = o + jnp.einsum('bclhk,cbhkv->bclhv', q_in, s_prev)
    return o.reshape(bsz, s, nh, dv)


def _ssd_mlstm_mixer(h, w_in, conv_w, conv_b, dt_bias, a_log, d_skip, ssd_norm, ig_bias, fg_bias, mlstm_norm, w_out):
    bsz, s, _ = h.shape
    z, xbc, dt_raw, q, k, v, o_pre, gates = _split(h @ w_in, L0_SPLITS)
    xbc = jax.nn.silu(_dwconv(xbc, conv_w, conv_b))
    xs, bm, cm = _split(xbc, (SSD_D_INNER, SSD_GROUPS * SSD_STATE, SSD_GROUPS * SSD_STATE))
    xs = xs.reshape(bsz, s, SSD_HEADS, SSD_HEADDIM)
    bm = bm.reshape(bsz, s, SSD_GROUPS, SSD_STATE)
    cm = cm.reshape(bsz, s, SSD_GROUPS, SSD_STATE)
    dt = jax.nn.softplus(dt_raw.reshape(bsz, s, 2, SSD_HEADS).astype(jnp.float32) + dt_bias)
    a = -jnp.exp(a_log.astype(jnp.float32))
    y = _bidirectional(_ssd_scan, (xs, dt[:, :, 0], bm, cm), (xs, dt[:, :, 1], bm, cm), (a[0],), (a[1],))
    y = y + d_skip[:, None] * xs
    y = _head_rmsnorm(y.reshape(bsz, s, SSD_D_INNER) * jax.nn.silu(z), ssd_norm, SSD_GROUPS)
    q = q.reshape(bsz, s, MLSTM_HEADS, MLSTM_HEADDIM)
    k = k.reshape(bsz, s, MLSTM_HEADS, MLSTM_HEADDIM) * MLSTM_HEADDIM ** -0.5
    v = v.reshape(bsz, s, MLSTM_HEADS, MLSTM_HEADDIM)
    gates = gates.reshape(bsz, s, 2, 2, MLSTM_HEADS).astype(jnp.float32)
    ig = gates[:, :, 0] + ig_bias
    lf = jax.nn.log_sigmoid(gates[:, :, 1] + fg_bias)
    hm = _bidirectional(_mlstm_scan, (q, k, v, ig[:, :, 0], lf[:, :, 0]), (q, k, v, ig[:, :, 1], lf[:, :, 1]))
    hm = jax.nn.sigmoid(o_pre) * hm.reshape(bsz, s, MLSTM_D)
    hm = _head_rmsnorm(hm, mlstm_norm, MLSTM_HEADS)
    return jnp.concatenate([y, hm], axis=-1) @ w_out


def _gla_mixer(h, w_in, gate_w2, gate_b, gla_norm, w_out):
    bsz, s, _ = h.shape
    q, k, v, r, g_lr = _split(h @ w_in, L1_SPLITS)
    q = q.reshape(bsz, s, GLA_HEADS, GLA_HEAD_DK) * GLA_HEAD_DK ** -0.5
    k = k.reshape(bsz, s, GLA_HEADS, GLA_HEAD_DK)
    v = v.reshape(bsz, s, GLA_HEADS, GLA_HEAD_DV)
    g_lr = g_lr.reshape(bsz, s, 2, GLA_RANK)
    logits = jnp.einsum('bsdr,drk->bsdk', g_lr, gate_w2) + gate_b
    la = (jax.nn.log_sigmoid(logits.astype(jnp.float32)) / GLA_TAU).reshape(bsz, s, 2, GLA_HEADS, GLA_HEAD_DK)
    o = _bidirectional(_gla_scan, (q, k, v, la[:, :, 0]), (q, k, v, la[:, :, 1]))
    o = _head_rmsnorm(o.reshape(bsz, s, GLA_DV), gla_norm, GLA_HEADS) * jax.nn.silu(r)
    return o @ w_out


def _conv_ffn(h, w_up, conv_w, conv_b, w_down):
    u = _dwconv(h @ w_up, conv_w, conv_b)
    gate, val = jnp.split(u, 2, axis=-1)
    return (jax.nn.gelu(gate, approximate=True) * val) @ w_down


def setup_inputs(seed: int = 0) -> dict:
    key = jax.random.key(seed)
    keys = jax.random.split(key, 24)

    def nrm(i, shape, scale):
        return scale * jax.random.normal(keys[i], shape, jnp.float32)

    ne, no = N_EVEN_LAYERS, N_ODD_LAYERS
    dt0 = jnp.exp(jax.random.uniform(keys[5], (ne, 2, SSD_HEADS), jnp.float32, math.log(1e-3), math.log(1e-1)))
    return {
        'x': nrm(0, (BATCH, SEQ, D_MODEL), 1.0),
        'norm_g': 1.0 + nrm(1, (DEPTH, 4, D_MODEL), 0.05),
        'ab_w_in': nrm(2, (ne, D_MODEL, L0_IN), D_MODEL ** -0.5),
        'ab_conv_w': nrm(3, (ne, SSD_CONV, SSD_XBC), SSD_CONV ** -0.5),
        'ab_conv_b': nrm(4, (ne, SSD_XBC), 0.02),
        'ab_dt_bias': dt0 + jnp.log(-jnp.expm1(-dt0)),
        'ab_a_log': jnp.log(jax.random.uniform(keys[6], (ne, 2, SSD_HEADS), jnp.float32, 1.0, 16.0)),
        'ab_d_skip': 1.0 + nrm(7, (ne, SSD_HEADS), 0.1),
        'ab_ssd_norm': 1.0 + nrm(8, (ne, SSD_D_INNER), 0.05),
        'ab_ig_bias': nrm(9, (ne, 2, MLSTM_HEADS), 0.1),
        'ab_fg_bias': jnp.linspace(3.0, 6.0, MLSTM_HEADS, dtype=jnp.float32) + nrm(10, (ne, 2, MLSTM_HEADS), 0.1),
        'ab_mlstm_norm': 1.0 + nrm(11, (ne, MLSTM_D), 0.05),
        'ab_w_out': nrm(12, (ne, L0_MIX, D_MODEL), L0_MIX ** -0.5),
        'c_w_in': nrm(13, (no, D_MODEL, L1_IN), D_MODEL ** -0.5),
        'c_gate_w2': nrm(14, (no, 2, GLA_RANK, GLA_DK), GLA_RANK ** -0.5),
        'c_gate_b': nrm(15, (no, 2, GLA_DK), 0.1),
        'c_norm': 1.0 + nrm(16, (no, GLA_DV), 0.05),
        'c_w_out': nrm(17, (no, GLA_DV, D_MODEL), GLA_DV ** -0.5),
        'ffn_w_up': nrm(18, (DEPTH, D_MODEL, 2 * FFN_DIM), D_MODEL ** -0.5),
        'ffn_conv_w': nrm(19, (DEPTH, FFN_CONV, 2 * FFN_DIM), FFN_CONV ** -0.5),
        'ffn_conv_b': nrm(20, (DEPTH, 2 * FFN_DIM), 0.02),
        'ffn_w_down': nrm(21, (DEPTH, FFN_DIM, D_MODEL), FFN_DIM ** -0.5),
    }


def reference(x, norm_g, ab_w_in, ab_conv_w, ab_conv_b, ab_dt_bias, ab_a_log, ab_d_skip, ab_ssd_norm,
              ab_ig_bias, ab_fg_bias, ab_mlstm_norm, ab_w_out, c_w_in, c_gate_w2, c_gate_b, c_norm, c_w_out,
              ffn_w_up, ffn_conv_w, ffn_conv_b, ffn_w_down):
    for i in range(DEPTH):
        j = i // 2
        h = _rmsnorm(x, norm_g[i, 0])
        if i % 2 == 0:
            h = _ssd_mlstm_mixer(h, ab_w_in[j], ab_conv_w[j], ab_conv_b[j], ab_dt_bias[j], ab_a_log[j],
                                 ab_d_skip[j], ab_ssd_norm[j], ab_ig_bias[j], ab_fg_bias[j], ab_mlstm_norm[j],
                                 ab_w_out[j])
        else:
            h = _gla_mixer(h, c_w_in[j], c_gate_w2[j], c_gate_b[j], c_norm[j], c_w_out[j])
        x = x + _rmsnorm(h, norm_g[i, 1])
        f = _conv_ffn(_rmsnorm(x, norm_g[i, 2]), ffn_w_up[i], ffn_conv_w[i], ffn_conv_b[i], ffn_w_down[i])
        x = x + _rmsnorm(f, norm_g[i, 3])
    return x
```

```python
import heapq
import numpy as np
from contextlib import ExitStack
import concourse.bass as bass
import concourse.mybir as mybir
from concourse.bass_utils import run_bass_kernel_spmd

F32 = mybir.dt.float32
BF16 = mybir.dt.bfloat16
AF = mybir.ActivationFunctionType
ALU = mybir.AluOpType
AX = mybir.AxisListType

D = 1024
EPS = 1e-6
NEGBIG = -30000.0


class Trk:
    __slots__ = ("name", "writers", "readers", "manual")

    def __init__(self, name, manual=False):
        self.name = name
        self.writers = []
        self.readers = []
        self.manual = manual


class Tile:
    def __init__(self, t, name, manual=False):
        self.t = t
        self.k = Trk(name, manual)

    def __getitem__(self, idx):
        return self.t[idx]


class DB:
    def __init__(self, tiles):
        self.tiles = tiles
        self.i = 0

    def __getitem__(self, idx):
        return self.tiles[self.i].t[idx]

    @property
    def k(self):
        return self.tiles[self.i].k


def _trk(x):
    return x.k if isinstance(x, (Tile, DB)) else x


class _Rec:
    def __getattr__(self, name):
        return lambda *a, **k: (name, a, k)


_REC = _Rec()


def _is_ap(v):
    return hasattr(v, "ap") and hasattr(v, "tensor") and hasattr(v, "offset")


def _esize(dt):
    return 4 if dt == F32 else 2


def _free_elems(ap):
    n = 1
    for v in ap.shape[1:]:
        n *= v
    return n


class Op:
    __slots__ = ("idx", "eng", "name", "args", "kw", "cost", "is_dma", "dma_t", "preds", "succs", "npred", "finish", "ev")


class Sched:
    ENGS = ("pe", "act", "dve", "pool", "sp")
    NDMA = 24

    def __init__(self, nc, es, self_wait=True, reorder=True):
        self.nc = nc
        self.self_wait = self_wait
        self.reorder = reorder
        self.semobj = {e: es.enter_context(nc.semaphore("s_" + e)) for e in self.ENGS}
        self.cnt = {e: 0 for e in self.ENGS}
        self.seen = {e: {} for e in self.ENGS}
        for i in range(self.NDMA):
            self.semobj["d%d" % i] = es.enter_context(nc.semaphore("s_d%d" % i))
        self.dma_cnt = [0] * self.NDMA
        self.dma_next = 0
        self.h = {"pe": nc.tensor, "act": nc.scalar, "dve": nc.vector, "pool": nc.gpsimd, "sp": nc.sync}
        self.nins = {e: 0 for e in self.ENGS}
        self.ops = []
        self.touched = set()
        self.reg = {}
        self.est_ns = 0.0
        self.verbose = False
        self.seg_name = ""

    def register(self, tname, lo, hi, tile):
        self.reg.setdefault(tname, []).append((lo, hi, tile))

    def clear_reg(self, tname):
        self.reg[tname] = []

    def unregister(self, tname, tile):
        self.reg[tname] = [x for x in self.reg.get(tname, []) if x[2] is not tile]

    def _auto(self, ap):
        lst = self.reg.get(ap.name)
        if not lst:
            return ()
        es = _esize(ap.dtype)
        lo = int(ap.offset) * es
        ext = 0
        for st, n in ap.ap[1:]:
            ext += (n - 1) * abs(st)
        hi = lo + (ext + 1) * es
        return [t for (a, b, t) in lst if a < hi and lo < b]

    def _mk(self, eng, fn, reads, writes, is_dma):
        name, a, k = fn(_REC)
        rs = [_trk(t) for t in reads]
        ws = [_trk(t) for t in writes]
        out_ap = None
        nbytes = 0
        for i, v in list(enumerate(a)) + list(k.items()):
            if not _is_ap(v):
                continue
            is_out = (i == 0 and isinstance(i, int)) or i in ("out", "accum_out")
            if is_out and out_ap is None and i != "accum_out":
                out_ap = v
            sp = str(v.space)
            if sp == "DRAM":
                if is_dma:
                    nbytes = max(nbytes, _free_elems(v) * v.shape[0] * _esize(v.dtype))
                continue
            for t in self._auto(v):
                tk = _trk(t)
                if tk.manual:
                    continue
                if is_out or sp == "PSUM":
                    if tk not in ws:
                        ws.append(tk)
                elif tk not in rs:
                    rs.append(tk)
            if is_dma:
                nbytes = max(nbytes, _free_elems(v) * v.shape[0] * _esize(v.dtype))
        op = Op()
        op.idx = len(self.ops)
        op.eng, op.name, op.args, op.kw, op.is_dma = eng, name, a, k, is_dma
        n = _free_elems(out_ap) if out_ap is not None else 64
        if is_dma:
            op.cost = 60.0
            op.dma_t = nbytes / 120.0
        elif eng == "pe":
            passes = 1
            if name == "matmul" and k["lhsT"].dtype == F32:
                passes = 4
            op.cost = 30.0 + 0.42 * n * passes + (25.0 if name == "matmul" else 40.0)
            op.dma_t = 0.0
        elif eng == "act":
            op.cost = 210.0 + 0.72 * n
            op.dma_t = 0.0
        elif eng == "dve":
            f = 2.0 if name in ("tensor_tensor_scan",) else 1.05
            op.cost = 70.0 + f * n
            op.dma_t = 0.0
        else:
            op.cost = 160.0 + 1.5 * n
            op.dma_t = 0.0
        preds = set()
        for t in rs:
            preds.update(t.writers)
        for t in ws:
            preds.update(t.writers)
            preds.update(t.readers)
        preds.discard(op.idx)
        op.preds = preds
        op.succs = []
        for t in rs:
            t.readers.append(op.idx)
            self.touched.add(t)
        for t in ws:
            t.writers = [op.idx]
            t.readers = []
            self.touched.add(t)
        self.ops.append(op)

    def op(self, eng, fn, reads=(), writes=()):
        self._mk(eng, fn, reads, writes, False)

    def dma(self, fn, reads=(), writes=(), eng="sp"):
        self._mk(eng, fn, reads, writes, True)

    def flush(self):
        ops = self.ops
        if not ops:
            return
        n = len(ops)
        order = {e: [] for e in self.ENGS}
        if not self.reorder:
            for op in ops:
                order[op.eng].append(op)
        else:
            for op in ops:
                op.npred = len(op.preds)
                for p in op.preds:
                    ops[p].succs.append(op.idx)
            ready = {e: [] for e in self.ENGS}
            rtime = [0.0] * n
            for op in ops:
                if op.npred == 0:
                    heapq.heappush(ready[op.eng], op.idx)
            busy = {e: 0.0 for e in self.ENGS}
            crit = [-1] * n
            stall = {}
            ev = []
            now = 0.0
            dma_free = 0.0
            done = 0
            while done < n:
                for e in self.ENGS:
                    if busy[e] <= now and ready[e]:
                        i = heapq.heappop(ready[e])
                        op = ops[i]
                        order[e].append(op)
                        if self.verbose and e == "pe":
                            st_ = now - max(busy[e], 0.0)
                            if st_ > 0 and crit[i] >= 0:
                                cp = ops[crit[i]]
                                kk = (cp.eng, cp.name, getattr(cp, "tag", ""))
                                stall[kk] = stall.get(kk, 0.0) + st_
                        busy[e] = now + op.cost
                        if op.is_dma:
                            st = max(now + op.cost, dma_free)
                            dma_free = st + op.dma_t
                            fin = dma_free + 2000.0
                        else:
                            fin = busy[e]
                        heapq.heappush(ev, (fin, 1, i))
                        heapq.heappush(ev, (busy[e], 0, -1))
                if not ev:
                    raise RuntimeError("scheduler deadlock")
                t, kind, i = heapq.heappop(ev)
                now = max(now, t)
                if kind == 1:
                    done += 1
                    for s in ops[i].succs:
                        so = ops[s]
                        so.npred -= 1
                        crit[s] = i
                        if so.npred == 0:
                            heapq.heappush(ready[so.eng], s)
            self.est_ns += now
            if self.verbose:
                bs = {e: sum(o.cost for o in order[e]) / 1e3 for e in self.ENGS}
                for kk, vv in sorted(stall.items(), key=lambda x: -x[1])[:8]:
                    print("      pe stall %-40s %.0f us" % (str(kk), vv / 1e3))
                print("  segment %-8s n=%6d est_us=%8.1f busy_us: %s" % (self.seg_name, n, now / 1e3, " ".join("%s=%.0f" % (e, bs[e]) for e in self.ENGS)), flush=True)
        for e in self.ENGS:
            for op in order[e]:
                if op.is_dma:
                    i = self.dma_next
                    self.dma_next = (i + 1) % self.NDMA
                    key = "d%d" % i
                    prev = self.dma_cnt[i]
                    self.dma_cnt[i] += 16
                    op.ev = (key, self.dma_cnt[i], prev)
                else:
                    self.cnt[e] += 1
                    op.ev = (e, self.cnt[e], 0)
        for e in self.ENGS:
            h = self.h[e]
            seen = self.seen[e]
            for op in order[e]:
                need = {}
                for p in op.preds:
                    k, v, _ = ops[p].ev
                    if k == e and (e == "pe" or not self.self_wait):
                        continue
                    if need.get(k, 0) < v:
                        need[k] = v
                if op.is_dma and op.ev[2]:
                    k, _, prev = op.ev
                    if need.get(k, 0) < prev:
                        need[k] = prev
                for k, v in need.items():
                    if seen.get(k, 0) >= v:
                        continue
                    seen[k] = v
                    h.wait_ge(self.semobj[k], v)
                ins = getattr(h, op.name)(*op.args, **op.kw)
                ins.then_inc(self.semobj[op.ev[0]], 16 if op.is_dma else 1)
                self.nins[e] += 1
        self.ops = []
        for t in self.touched:
            t.writers = []
            t.readers = []
        self.touched = set()
        self._full_wait()

    def _full_wait(self):
        cur = dict(self.cnt)
        for i in range(self.NDMA):
            cur["d%d" % i] = self.dma_cnt[i]
        for e in self.ENGS:
            for k, v in cur.items():
                if k == e or v == 0 or self.seen[e].get(k, 0) >= v:
                    continue
                self.seen[e][k] = v
                self.h[e].wait_ge(self.semobj[k], v)

    def barrier(self):
        self.flush()


ARENA_N = 85000


def build(T, dbg=None, self_wait=True, reorder=True, verbose=False):
    NC = T // 128
    nc = bass.Bass("TRN2", target_bir_lowering=False)
    es = ExitStack()

    def dram(name, shape, dt, kind="Internal"):
        return nc.dram_tensor(name, shape, dt, kind=kind).ap()

    x_in = dram("x", [T, D], F32, "ExternalInput")
    norm_g = dram("norm_g", [2, 4, D], F32, "ExternalInput")
    ab_w_in = dram("ab_w_in", [D, 6720], F32, "ExternalInput")
    ab_conv_w = dram("ab_conv_w", [5, 1536], F32, "ExternalInput")
    ab_conv_b = dram("ab_conv_b", [1536], F32, "ExternalInput")
    ab_dt_bias = dram("ab_dt_bias", [32], F32, "ExternalInput")
    ab_a_log = dram("ab_a_log", [32], F32, "ExternalInput")
    ab_d_skip = dram("ab_d_skip", [16], F32, "ExternalInput")
    ab_ssd_norm = dram("ab_ssd_norm", [1024], F32, "ExternalInput")
    ab_ig_bias = dram("ab_ig_bias", [16], F32, "ExternalInput")
    ab_fg_bias = dram("ab_fg_bias", [16], F32, "ExternalInput")
    ab_mlstm_norm = dram("ab_mlstm_norm", [1024], F32, "ExternalInput")
    ab_w_out = dram("ab_w_out", [2048, D], F32, "ExternalInput")
    c_w_in = dram("c_w_in", [D, 3104], F32, "ExternalInput")
    c_gate_w2 = dram("c_gate_w2", [2, 16, 512], F32, "ExternalInput")
    c_gate_b = dram("c_gate_b", [2, 512], F32, "ExternalInput")
    c_norm = dram("c_norm", [1024], F32, "ExternalInput")
    c_w_out = dram("c_w_out", [D, D], F32, "ExternalInput")
    ffn_w_up = dram("ffn_w_up", [2, D, 5632], F32, "ExternalInput")
    ffn_conv_w = dram("ffn_conv_w", [2, 3, 5632], F32, "ExternalInput")
    ffn_conv_b = dram("ffn_conv_b", [2, 5632], F32, "ExternalInput")
    ffn_w_down = dram("ffn_w_down", [2, 2816, D], F32, "ExternalInput")
    x_halo = dram("x_halo", [2, D], F32, "ExternalInput")
    sel_in = dram("sel", [2], F32, "ExternalInput")
    out = dram("out", [T, D], F32, "ExternalOutput")
    exS = dram("exS", [128, 2056], F32)
    exG = dram("exG", [256, 2056], F32)
    exS1 = dram("exS1", [128, 1024], F32)
    exG1 = dram("exG1", [256, 1024], F32)
    exR = [dram("exR%d" % i, [1, D], F32) for i in range(2)]
    exRG = [dram("exRG%d" % i, [2, D], F32) for i in range(2)]
    PAIRS = [[0, 1], [2, 3], [4, 5], [6, 7]]
    WB_L0P2 = dram("WB_L0P2", [128, 32 * 1024], BF16)
    WB_up = [dram("WB_up%d" % i, [128, 8 * 5632], BF16) for i in range(2)]
    WB_dn = [dram("WB_dn%d" % i, [128, 22 * 1024], BF16) for i in range(2)]
    WB_g = dram("WB_g", [128, 8 * 3104], BF16)
    WB_o1 = dram("WB_o1", [128, 8 * 1024], BF16)
    WK = Trk("weights")

    PT0 = dram("PT0", [T, 3072], BF16)
    XSB = dram("XSB", [T, 1280], BF16)
    BCT = dram("BCT", [NC, 128, 512], BF16)
    GT = dram("GT", [T, 64], F32)
    HS = dram("HS", [NC, 128, 1024], BF16)
    HM = dram("HM", [NC, 128, 1032], BF16)
    X1 = dram("X1", [T, D], F32)
    X2 = dram("X2", [T, D], F32)
    QKT = dram("QKT", [NC, 128, 1024], BF16)
    LTd = dram("LTd", [NC, 128, 1024], F32)
    VR = dram("VR", [T, 2048], BF16)
    SG = dram("SG", [NC, 128, 1024], BF16)
    X3 = dram("X3", [T, D], F32)
    dk = {}

    def DK(name, c):
        key = (name, c)
        if key not in dk:
            dk[key] = Trk("%s%d" % key)
        return dk[key]

    with es:
        S = Sched(nc, es, self_wait=self_wait, reorder=reorder)
        S.verbose = verbose
        PS = [Tile(es.enter_context(nc.psum_tensor("ps%d" % i, [128, 512], F32)), "ps%d" % i) for i in range(8)]
        for i in range(8):
            S.register("ps%d" % i, 0, 1 << 30, PS[i])

        def sb(name, shape, dt):
            t = Tile(es.enter_context(nc.sbuf_tensor(name, shape, dt)), name)
            S.register(name, 0, 1 << 30, t)
            return t

        arena_t = es.enter_context(nc.sbuf_tensor("arena", [128, ARENA_N], BF16))
        aoff = [0]

        def reset_arena():
            if verbose:
                print("  arena used", aoff[0] * 2, "of", ARENA_N * 2)
            S.barrier()
            S.clear_reg("arena")
            aoff[0] = 0

        dbs = []

        def ar2(name, shape, dt, nbuf=2):
            d = DB([ar("%s_%d" % (name, i), shape, dt) for i in range(nbuf)])
            dbs.append(d)
            return d

        def setpar(c):
            for d in dbs:
                d.i = c % len(d.tiles)

        def ar(name, shape, dt, manual=False):
            n = 1
            for v in shape[1:]:
                n *= v
            nb = n * 2 if dt == F32 else n
            if aoff[0] % 2:
                aoff[0] += 1
            o = aoff[0]
            aoff[0] += nb
            assert aoff[0] <= ARENA_N, (name, aoff[0])
            ap = arena_t[0:shape[0], o:o + nb]
            if dt == F32:
                ap = ap.bitcast(F32)
            if len(shape) == 3:
                ap = ap.rearrange("p (a b) -> p a b", a=shape[1])
            elif len(shape) == 4:
                ap = ap.rearrange("p (a b c) -> p a b c", a=shape[1], b=shape[2])
            t = Tile(ap, name, manual)
            S.register("arena", o * 2, (o + nb) * 2, t)
            return t

        def V(fn, r=(), w=()):
            S.op("dve", fn, r, w)

        def A(fn, r=(), w=()):
            S.op("act", fn, r, w)

        def G(fn, r=(), w=()):
            S.op("pool", fn, r, w)

        def P(fn, r=(), w=()):
            S.op("pe", fn, r, w)

        def MM(bank, o, lhsT, rhs, r, start=True, stop=True):
            P(lambda h: h.matmul(o, lhsT=lhsT, rhs=rhs, start=start, stop=stop), r, [bank])

        def TR(bank, o, in_, ident, r):
            P(lambda h: h.transpose(out=o, in_=in_, identity=ident), r, [bank])

        def LD(o, i, r, w):
            S.dma(lambda h: h.dma_start(out=o, in_=i), r, w)

        def LDS(o, i, r, w):
            S.dma(lambda h: h.dma_start(out=o, in_=i, allow_slow_non_contiguous=True), r, w)

        def v3(ap, a):
            return ap.rearrange("p (a b) -> p a b", a=a)

        def bc_in(ap, a, b):
            return ap.unsqueeze(2).to_broadcast([128, a, b])

        def bc_mid(ap, a, b):
            return ap.unsqueeze(1).to_broadcast([128, a, b])

        block = es.enter_context(nc.Block())
        cur_scope = [None]

        def SC(name):
            S.seg_name = name or ""
            return
            if cur_scope[0] is not None:
                cur_scope[0].__exit__(None, None, None)
                cur_scope[0] = None
            if name is not None:
                cm = nc.named_scope(name)
                cm.__enter__()
                cur_scope[0] = cm

        onesf = sb("onesf", [128, 128], F32)
        zerof = sb("zerof", [128, 128], F32)
        identf = sb("identf", [128, 128], F32)
        identb = sb("identb", [128, 128], BF16)
        Uf = sb("Uf", [128, 128], F32)
        Ub = sb("Ub", [128, 128], F32)
        maskb16 = sb("maskb16", [128, 2, 128], BF16)
        NEG = sb("NEG", [128, 2, 128], BF16)
        negf = sb("negf", [128, 2, 128], F32)
        onesb = sb("onesb", [128, 8], BF16)
        epsc = sb("epsc", [128, 1], F32)
        G(lambda h: h.memset(onesf[:], 1.0), w=[onesf])
        G(lambda h: h.memset(zerof[:], 0.0), w=[zerof])
        G(lambda h: h.memset(onesb[:], 1.0), w=[onesb])
        G(lambda h: h.memset(epsc[:], EPS), w=[epsc])
        G(lambda h: h.affine_select(out=identf[:], in_=onesf[:], pattern=[[1, 128]], compare_op=ALU.is_equal, fill=0.0, base=0, channel_multiplier=-1), [onesf], [identf])
        G(lambda h: h.tensor_copy(out=identb[:], in_=identf[:]), [identf], [identb])
        G(lambda h: h.affine_select(out=Uf[:], in_=onesf[:], pattern=[[1, 128]], compare_op=ALU.is_ge, fill=0.0, base=0, channel_multiplier=-1), [onesf], [Uf])
        G(lambda h: h.affine_select(out=Ub[:], in_=onesf[:], pattern=[[-1, 128]], compare_op=ALU.is_ge, fill=0.0, base=0, channel_multiplier=1), [onesf], [Ub])
        G(lambda h: h.tensor_copy(out=maskb16[:, 0, :], in_=Uf[:]), [Uf], [maskb16])
        G(lambda h: h.tensor_copy(out=maskb16[:, 1, :], in_=Ub[:]), [Ub], [maskb16])
        G(lambda h: h.affine_select(out=negf[:, 0, :], in_=zerof[:], pattern=[[1, 128]], compare_op=ALU.is_ge, fill=NEGBIG, base=0, channel_multiplier=-1), [zerof], [negf])
        G(lambda h: h.affine_select(out=negf[:, 1, :], in_=zerof[:], pattern=[[-1, 128]], compare_op=ALU.is_ge, fill=NEGBIG, base=0, channel_multiplier=1), [zerof], [negf])
        G(lambda h: h.tensor_copy(out=NEG[:], in_=negf[:]), [negf], [NEG])

        stg = [sb("stg%d" % i, [128, 1024], F32) for i in range(2)]
        stg_ctr = [0]
        gcol = sb("gcol", [128, 8, 8], F32)
        gres = sb("gres", [128, 1024], F32)
        xin = [sb("xin%d" % i, [128, 1024], F32) for i in range(2)]
        sqj = sb("sqj", [128, 1024], BF16)
        ss = sb("ss", [128, 16], F32)
        hb = sb("hb", [128, 1024], BF16)
        dumm = sb("dumm", [128, 8], F32)
        selt = sb("selt", [128, 2], F32)
        LD(selt[:], sel_in.partition_broadcast(128), [WK], [selt])

        def allgather(src_d, dst_d, rk, wk):
            for rep_ in range(2):
                G(lambda h: h.collective_compute("AllGather", ALU.bypass, replica_groups=PAIRS, ins=[src_d.opt()], outs=[dst_d.opt()]), [rk], [wk])

        def select2(out_ap, a0, a1, r, w):
            V(lambda h: h.tensor_scalar(out=out_ap, in0=a0, scalar1=selt[:, 0:1], scalar2=None, op0=ALU.mult), r, w)
            V(lambda h: h.scalar_tensor_tensor(out=out_ap, in0=a1, scalar=selt[:, 1:2], in1=out_ap, op0=ALU.mult, op1=ALU.add), r, w)

        def halo_cands(src_rows_ap, src_trk, nrows, dst_tile):
            xi = xin[0]
            V(lambda h: h.memset(xi[:], 0.0), [], [xi])
            LD(xi[0:nrows, :], src_rows_ap, [src_trk], [xi])
            norm_chunk(None, 0, None, xi, load=False)
            pb = PS[3][:].bitcast(BF16)
            for k in range(8):
                TR(PS[3], pb[:, k * 128:(k + 1) * 128], hb[:, k * 128:(k + 1) * 128], identb[:], [hb, identb])
            V(lambda h: h.tensor_copy(out=dst_tile[:], in_=v3(pb, 8)[:, :, 0:nrows]), [], [PS[3], dst_tile])
        WG = Trk("wguard")

        gvecs = [norm_g[0, 0], norm_g[0, 2], norm_g[1, 0], norm_g[1, 2], ab_ssd_norm, ab_mlstm_norm, c_norm]
        for i, gv in enumerate(gvecs):
            LDS(gcol[:, i, :], gv.rearrange("(k p) -> p k", p=128), [WK], [gcol])

        class WLoad:
            def __init__(self, wt):
                self.wt = wt
                self.tmps = []
                G(lambda h: h.memset(dumm[:, 0:1], 0.0), [], [wt, WG, dumm])

            def load(self, dst_fn, src, KC, ncols, gc, factor=1.0):
                for kc in range(KC):
                    for c0 in range(0, ncols, 1024):
                        w = min(1024, ncols - c0)
                        st = stg[stg_ctr[0] % 2]
                        stg_ctr[0] += 1
                        LD(st[:, 0:w], src[kc * 128:(kc + 1) * 128, c0:c0 + w], [WK], [st])
                        o = dst_fn(kc, c0, w)
                        sc1 = gc(kc) if gc is not None else 1.0
                        tk = Trk("wtmp")
                        self.tmps.append(tk)
                        if factor == 1.0 and stg_ctr[0] % 2 == 0:
                            A(lambda h, o=o, st=st, w=w, sc1=sc1: h.activation(out=o, in_=st[:, 0:w], func=AF.Copy, scale=sc1), [st, gcol, WG], [tk])
                        else:
                            V(lambda h, o=o, st=st, w=w, sc1=sc1: h.tensor_scalar(out=o, in0=st[:, 0:w], scalar1=sc1, scalar2=float(factor), op0=ALU.mult, op1=ALU.mult), [st, gcol, WG], [tk])

            def done(self):
                G(lambda h: h.memset(dumm[:, 1:2], 0.0), self.tmps, [self.wt, dumm])

        cvt = [sb("cvt%d" % i, [128, 1024], BF16) for i in range(2)]
        prep_ctr = [0]
        gcolq = sb("gcolq", [128, 8], F32)
        V(lambda h: h.tensor_scalar(out=gcolq[:], in0=gcol[:, 2, :], scalar1=128 ** -0.5, scalar2=None, op0=ALU.mult), [gcol], [gcolq])

        def prep(dst_d, dst_off_fn, src, KC, ncols, gc_fn):
            for kc in range(KC):
                for c0 in range(0, ncols, 1024):
                    w = min(1024, ncols - c0)
                    i = prep_ctr[0]
                    prep_ctr[0] += 1
                    st, cv = stg[i % 2], cvt[i % 2]
                    LD(st[:, 0:w], src[kc * 128:(kc + 1) * 128, c0:c0 + w], [WK], [st])
                    if gc_fn is not None:
                        gc = gc_fn(kc)
                        G(lambda h, st=st, cv=cv, w=w, gc=gc: h.tensor_tensor(out=cv[:, 0:w], in0=st[:, 0:w], in1=gc.to_broadcast([128, w]), op=ALU.mult), [st, gcol, gcolq], [cv])
                    else:
                        G(lambda h, st=st, cv=cv, w=w: h.tensor_copy(out=cv[:, 0:w], in_=st[:, 0:w]), [st], [cv])
                    o = dst_off_fn(kc, c0)
                    LD(dst_d[:, o:o + w], cv[:, 0:w], [cv], [Trk("wbst")])

        def prep_ffn_up(li):
            prep(WB_up[li], lambda kc, c0: kc * 5632 + c0, ffn_w_up[li], 8, 5632, lambda kc: gcol[:, 1 + 2 * li, kc:kc + 1])

        def prep_ffn_dn(li):
            prep(WB_dn[li], lambda kc, c0: kc * 1024 + c0, ffn_w_down[li], 22, 1024, None)

        def sigmoid_act(out_ap, in_ap, r, w, scale=1.0):
            A(lambda h: h.activation(out=out_ap, in_=in_ap, func=AF.Exp, scale=-float(scale)), r, w)
            A(lambda h: h.activation(out=out_ap, in_=out_ap, func=AF.Ln, bias=1.0), [], w)
            A(lambda h: h.activation(out=out_ap, in_=out_ap, func=AF.Exp, scale=-1.0), [], w)

        def rstd_from(src_ap, n, col, junk_ap, r, w_extra=()):
            V(lambda h: h.memset(ss[:, col:col + 1], 0.0), [], [ss])
            A(lambda h: h.activation(out=junk_ap, in_=src_ap, func=AF.Square, accum_out=ss[:, col:col + 1]), r, [sqj, ss] + list(w_extra))
            A(lambda h: h.activation(out=ss[:, col:col + 1], in_=ss[:, col:col + 1], func=AF.Ln, scale=1.0 / n, bias=epsc[:, 0:1]), [epsc], [ss])
            A(lambda h: h.activation(out=ss[:, col:col + 1], in_=ss[:, col:col + 1], func=AF.Exp, scale=-0.5), [], [ss])

        def norm_chunk(src_dram, c, src_trk, xi, load=True):
            if load:
                LD(xi[:], src_dram[c * 128:(c + 1) * 128, :], [src_trk], [xi])
            rstd_from(xi[:], 1024, 0, sqj[:], [xi])
            V(lambda h: h.tensor_scalar(out=hb[:], in0=xi[:], scalar1=ss[:, 0:1], scalar2=None, op0=ALU.mult), [xi, ss], [hb])

        def transpose8(src_tile, src_c0, dst_ap, dst_trk, eng="dve"):
            pb = PS[3][:].bitcast(BF16)
            for k in range(8):
                TR(PS[3], pb[:, k * 128:(k + 1) * 128], src_tile[:, src_c0 + k * 128:src_c0 + (k + 1) * 128], identb[:], [src_tile, identb])
            if eng == "dve":
                V(lambda h: h.tensor_copy(out=dst_ap, in_=v3(pb, 8)), [], [PS[3]] + dst_trk)
            else:
                A(lambda h: h.activation(out=dst_ap, in_=v3(pb, 8), func=AF.Copy), [], [PS[3]] + dst_trk)

        def resid_out(banks, xres, dst_dram, c, dst_name, xo):
            V(lambda h: h.memset(ss[:, 3:5], 0.0), [], [ss])
            for n2 in range(2):
                A(lambda h, n2=n2: h.activation(out=sqj[:, n2 * 512:(n2 + 1) * 512], in_=banks[n2][:, :], func=AF.Square, accum_out=ss[:, 3 + n2:4 + n2]), [], [banks[n2], sqj, ss])
            V(lambda h: h.tensor_tensor(out=ss[:, 3:4], in0=ss[:, 3:4], in1=ss[:, 4:5], op=ALU.add), [], [ss])
            A(lambda h: h.activation(out=ss[:, 3:4], in_=ss[:, 3:4], func=AF.Ln, scale=1.0 / 1024, bias=epsc[:, 0:1]), [epsc], [ss])
            A(lambda h: h.activation(out=ss[:, 3:4], in_=ss[:, 3:4], func=AF.Exp, scale=-0.5), [], [ss])
            for n2 in range(2):
                V(lambda h, n2=n2: h.scalar_tensor_tensor(out=xo[:, n2 * 512:(n2 + 1) * 512], in0=banks[n2][:, :], scalar=ss[:, 3:4], in1=gres[:, n2 * 512:(n2 + 1) * 512], op0=ALU.mult, op1=ALU.mult), [ss, gres], [banks[n2], xo])
            G(lambda h: h.tensor_tensor(out=xo[:], in0=xo[:], in1=xres[:], op=ALU.add), [xres], [xo])
            LD(dst_dram[c * 128:(c + 1) * 128, :], xo[:], [xo], [DK(dst_name, c)])

        def layer0():
            reset_arena()
            SC("L0P1_w")
            WT1 = ar("W1", [128, 8, 4672], BF16, manual=True)
            hTr = ar("hTr", [128, 8, 516], BF16, manual=True)
            hslot = [Trk("hslot%d" % i) for i in range(4)]
            hmirL, hmirR = Trk("hmirL"), Trk("hmirR")
            dtb = ar("dtb", [128, 32], F32)
            negA = ar("negA", [128, 32], F32)
            igfg = ar("igfg", [128, 32], F32)
            dsk = ar("dsk", [128, 16], F32)
            cw = ar("cw", [128, 12, 5], F32)
            cbias = ar("cbias", [128, 12], F32)
            Hs = ar("Hs", [128, 1024], F32)
            Cm = ar("Cm", [128, 1032], F32)
            hsb16 = ar("hsb16", [128, 1024], BF16)
            cmb16 = ar("cmb16", [128, 1032], BF16)
            pt = [ar("pt%d" % i, [128, 3072], BF16) for i in range(2)]
            gt = [ar("gt%d" % i, [128, 64], F32) for i in range(2)]
            acc = [ar("acc%d" % i, [128, 12, 128], F32) for i in range(2)]
            sg12s = [ar("sg12_%d" % i, [128, 12, 128], F32) for i in range(2)]
            xbcTs = [ar("xbcT%d" % i, [128, 12, 128], BF16) for i in range(2)]
            xsb = [ar("xsb%d" % i, [128, 1280], BF16) for i in range(2)]
            gw = ar("gw", [128, 64], F32)
            G48 = ar("G48", [128, 48], F32)
            cst = ar("cst", [128, 96], F32)
            scw = ar("scw", [128, 48], F32)
            xe = ar("xe", [128, 1024], BF16)
            kw = ar("kw", [128, 1024], BF16)
            dts = ar("dts", [128, 32], F32)
            igs = ar("igs", [128, 16], F32)
            halo0 = ar("halo0", [128, 8, 2], BF16)
            p1_end = aoff[0]

            def load_params():
                LD(dtb[:], ab_dt_bias.partition_broadcast(128), [WK], [dtb])
                LD(negA[:], ab_a_log.partition_broadcast(128), [WK], [negA])
                LD(igfg[:, 0:16], ab_ig_bias.partition_broadcast(128), [WK], [igfg])
                LD(igfg[:, 16:32], ab_fg_bias.partition_broadcast(128), [WK], [igfg])
                LD(dsk[:], ab_d_skip.partition_broadcast(128), [WK], [dsk])
                for j in range(5):
                    LDS(cw[:, :, j], ab_conv_w[j].rearrange("(i p) -> p i", p=128), [WK], [cw])
                LDS(cbias[:], ab_conv_b.rearrange("(i p) -> p i", p=128), [WK], [cbias])
                A(lambda h: h.activation(out=negA[:], in_=negA[:], func=AF.Exp), [], [negA])
                V(lambda h: h.tensor_scalar(out=negA[:], in0=negA[:], scalar1=-1.0, scalar2=None, op0=ALU.mult), [], [negA])
            load_params()
            wl = WLoad(WT1)
            segs = [(2592, 1024, 0, 1.0), (3616, 1024, 1024, 128 ** -0.5), (4640, 1024, 2048, 1.0),
                    (2560, 32, 3072, 1.0), (6688, 32, 3104, 1.0), (1024, 1536, 3136, 1.0)]
            for (s0, n, d0, fac) in segs:
                wl.load(lambda kc, c0, w, d0=d0: WT1[:, kc, d0 + c0:d0 + c0 + w], ab_w_in[:, s0:s0 + n], 8, n, lambda kc: gcol[:, 0, kc:kc + 1], fac)
            wl.done()
            V(lambda h: h.memset(Hs[:], 0.0), [], [Hs])
            V(lambda h: h.memset(Cm[:], 0.0), [], [Cm])
            V(lambda h: h.memset(hTr[:], 0.0), [], hslot + [hmirL, hmirR])
            halo_cands(x_halo, WK, 2, halo0)

            def gates(gtile, dirs):
                for d in dirs:
                    c0 = d * 16
                    V(lambda h, c0=c0: h.tensor_tensor(out=gw[:, c0:c0 + 16], in0=gtile[:, c0:c0 + 16], in1=dtb[:, c0:c0 + 16], op=ALU.add), [gtile, dtb], [gw])
                    A(lambda h, c0=c0: h.activation(out=gw[:, c0:c0 + 16], in_=gw[:, c0:c0 + 16], func=AF.Exp), [], [gw])
                    A(lambda h, c0=c0: h.activation(out=dts[:, c0:c0 + 16], in_=gw[:, c0:c0 + 16], func=AF.Ln, bias=1.0), [gw], [dts])
                    V(lambda h, c0=c0: h.tensor_tensor(out=G48[:, c0:c0 + 16], in0=dts[:, c0:c0 + 16], in1=negA[:, c0:c0 + 16], op=ALU.mult), [dts, negA], [G48])
                    i0 = 32 + d * 8
                    f0 = 48 + d * 8
                    V(lambda h, i0=i0, d=d: h.tensor_tensor(out=igs[:, d * 8:d * 8 + 8], in0=gtile[:, i0:i0 + 8], in1=igfg[:, d * 8:d * 8 + 8], op=ALU.add), [gtile, igfg], [igs])
                    V(lambda h, f0=f0, d=d: h.tensor_tensor(out=gw[:, f0:f0 + 8], in0=gtile[:, f0:f0 + 8], in1=igfg[:, 16 + d * 8:24 + d * 8], op=ALU.add), [gtile, igfg], [gw])
                    A(lambda h, f0=f0: h.activation(out=gw[:, f0:f0 + 8], in_=gw[:, f0:f0 + 8], func=AF.Exp, scale=-1.0), [], [gw])
                    A(lambda h, f0=f0: h.activation(out=gw[:, f0:f0 + 8], in_=gw[:, f0:f0 + 8], func=AF.Ln, bias=1.0), [], [gw])
                    V(lambda h, f0=f0, d=d: h.tensor_scalar(out=G48[:, 32 + d * 8:40 + d * 8], in0=gw[:, f0:f0 + 8], scalar1=-1.0, scalar2=None, op0=ALU.mult), [gw], [G48])

            def cumsums(dirs, bank):
                for d in dirs:
                    U = Uf if d == 0 else Ub
                    MM(bank, bank[:, d * 16:d * 16 + 16], U[:], G48[:, d * 16:d * 16 + 16], [U, G48])
                    MM(bank, bank[:, 32 + d * 8:40 + d * 8], U[:], G48[:, 32 + d * 8:40 + d * 8], [U, G48])
                MM(bank, bank[:, 64:112], onesf[:], G48[:], [onesf, G48])
                V(lambda h: h.tensor_copy(out=cst[:, 0:48], in_=bank[:, 0:48]), [], [bank, cst])
                V(lambda h: h.tensor_copy(out=cst[:, 48:96], in_=bank[:, 64:112]), [], [bank, cst])

            def state_step(d, xs_tile, pt_tile, Hst, Cst, store_fn):
                o16, o8 = d * 16, 32 + d * 8
                k_ap = pt_tile[:, 1024:2048]
                V(lambda h: h.tensor_tensor(out=scw[:, 0:16], in0=cst[:, 48 + o16:64 + o16], in1=cst[:, o16:o16 + 16], op=ALU.subtract), [cst], [scw])
                A(lambda h: h.activation(out=scw[:, 0:16], in_=scw[:, 0:16], func=AF.Exp), [], [scw])
                V(lambda h: h.tensor_tensor(out=scw[:, 0:16], in0=scw[:, 0:16], in1=dts[:, o16:o16 + 16], op=ALU.mult), [dts], [scw])
                V(lambda h: h.tensor_tensor(out=v3(xe[:], 16), in0=v3(xs_tile[:, 0:1024], 16), in1=bc_in(scw[:, 0:16], 16, 64), op=ALU.mult), [xs_tile, scw], [xe])
                for g in range(2):
                    MM(PS[4 + g], PS[4 + g][:, :], xs_tile[:, 1024 + g * 128:1152 + g * 128], xe[:, g * 512:(g + 1) * 512], [xs_tile, xe])
                V(lambda h: h.tensor_tensor(out=scw[:, 16:24], in0=cst[:, 48 + o8:56 + o8], in1=cst[:, o8:o8 + 8], op=ALU.subtract), [cst], [scw])
                V(lambda h: h.tensor_tensor(out=scw[:, 16:24], in0=scw[:, 16:24], in1=igs[:, d * 8:d * 8 + 8], op=ALU.add), [igs], [scw])
                A(lambda h: h.activation(out=scw[:, 16:24], in_=scw[:, 16:24], func=AF.Exp), [], [scw])
                V(lambda h: h.tensor_tensor(out=v3(kw[:], 8), in0=v3(k_ap, 8), in1=bc_in(scw[:, 16:24], 8, 128), op=ALU.mult), [pt_tile, scw], [kw])
                for hh in range(8):
                    bk = PS[6 + hh // 4]
                    MM(bk, bk[:, (hh % 4) * 128:(hh % 4 + 1) * 128], kw[:, hh * 128:(hh + 1) * 128], pt_tile[:, 2048 + hh * 128:2048 + (hh + 1) * 128], [kw, pt_tile])
                for hh in range(8):
                    MM(PS[2], PS[2][:, 128 + hh:129 + hh], kw[:, hh * 128:(hh + 1) * 128], onesb[:, 0:1], [kw, onesb])
                store_fn()
                A(lambda h: h.activation(out=scw[:, 24:40], in_=cst[:, 48 + o16:64 + o16], func=AF.Exp), [cst], [scw])
                A(lambda h: h.activation(out=scw[:, 40:48], in_=cst[:, 48 + o8:56 + o8], func=AF.Exp), [cst], [scw])
                V(lambda h: h.tensor_tensor(out=v3(Hst[:], 16), in0=v3(Hst[:], 16), in1=bc_in(scw[:, 24:40], 16, 64), op=ALU.mult), [scw], [Hst])
                for g in range(2):
                    V(lambda h, g=g: h.tensor_tensor(out=Hst[:, g * 512:(g + 1) * 512], in0=Hst[:, g * 512:(g + 1) * 512], in1=PS[4 + g][:, :], op=ALU.add), [], [Hst, PS[4 + g]])
                V(lambda h: h.tensor_tensor(out=v3(Cst[:, 0:1024], 8), in0=v3(Cst[:, 0:1024], 8), in1=bc_in(scw[:, 40:48], 8, 128), op=ALU.mult), [scw], [Cst])
                for q in range(2):
                    V(lambda h, q=q: h.tensor_tensor(out=Cst[:, q * 512:(q + 1) * 512], in0=Cst[:, q * 512:(q + 1) * 512], in1=PS[6 + q][:, :], op=ALU.add), [], [Cst, PS[6 + q]])
                V(lambda h: h.tensor_tensor(out=Cst[:, 1024:1032], in0=Cst[:, 1024:1032], in1=scw[:, 40:48], op=ALU.mult), [scw], [Cst])
                V(lambda h: h.tensor_tensor(out=Cst[:, 1024:1032], in0=Cst[:, 1024:1032], in1=PS[2][:, 128:136], op=ALU.add), [], [Cst, PS[2]])

            xk = Trk("x_in")

            def norm_to_ring(c):
                s = c % 4
                norm_chunk(x_in, c, xk, xin[c % 2])
                transpose8(hb, 0, hTr[:, :, 2 + s * 128:2 + (s + 1) * 128], [hslot[s]])
                if s == 3:
                    V(lambda h: h.tensor_copy(out=hTr[:, :, 0:2], in_=hTr[:, :, 2 + 3 * 128 + 126:2 + 4 * 128]), [hslot[3]], [hmirL])
                if s == 0:
                    V(lambda h: h.tensor_copy(out=hTr[:, :, 514:516], in_=hTr[:, :, 2:4]), [hslot[0]], [hmirR])
                if c == NC - 1:
                    if s == 3:
                        V(lambda h: h.tensor_copy(out=hTr[:, :, 514:516], in_=halo0[:]), [halo0], [hmirR])
                    else:
                        V(lambda h: h.tensor_copy(out=hTr[:, :, 2 + (s + 1) * 128:4 + (s + 1) * 128], in_=halo0[:]), [halo0], [hslot[(s + 1) % 4]])

            SC("L0P1")
            norm_to_ring(0)
            for c in range(NC):
                if c + 1 < NC:
                    norm_to_ring(c + 1)
                s = c % 4
                ptc, gtc, xsc = pt[c % 2], gt[c % 2], xsb[c % 2]
                sg12, xbcT = sg12s[c % 2], xbcTs[c % 2]
                rr = [hslot[(c - 1) % 4], hslot[s], hslot[(c + 1) % 4], hmirL, hmirR, WT1]
                for j in range(6):
                    bk = PS[j % 2]
                    for k in range(8):
                        MM(bk, bk[:, :], hTr[:, k, 2 + s * 128:2 + (s + 1) * 128], WT1[:, k, j * 512:(j + 1) * 512], rr, start=(k == 0), stop=(k == 7))
                    if j % 2 == 0:
                        A(lambda h, bk=bk, j=j: h.activation(out=ptc[:, j * 512:(j + 1) * 512], in_=bk[:, :], func=AF.Copy), [], [bk, ptc])
                    else:
                        V(lambda h, bk=bk, j=j: h.tensor_copy(out=ptc[:, j * 512:(j + 1) * 512], in_=bk[:, :]), [], [bk, ptc])
                for k in range(8):
                    MM(PS[0], PS[0][:, 0:64], hTr[:, k, 2 + s * 128:2 + (s + 1) * 128], WT1[:, k, 3072:3136], rr, start=(k == 0), stop=(k == 7))
                V(lambda h: h.tensor_copy(out=gtc[:], in_=PS[0][:, 0:64]), [], [PS[0], gtc])
                LD(PT0[c * 128:(c + 1) * 128, :], ptc[:], [ptc], [DK("PT0", c)])
                LD(GT[c * 128:(c + 1) * 128, :], gtc[:], [gtc], [DK("GT", c)])
                for i in range(12):
                    bk = PS[1 + i % 2]
                    ac = acc[c % 2][:, i, :]
                    ack = acc[c % 2]
                    for k in range(8):
                        MM(bk, bk[:, 0:132], WT1[:, k, 3136 + i * 128:3136 + (i + 1) * 128], hTr[:, k, s * 128:s * 128 + 132], rr, start=(k == 0), stop=(k == 7))
                    A(lambda h, bk=bk, i=i, ac=ac: h.activation(out=ac, in_=bk[:, 2:130], func=AF.Identity, scale=cw[:, i, 2:3], bias=cbias[:, i:i + 1]), [cw, cbias], [bk, ack])
                    for j in (0, 1, 3, 4):
                        V(lambda h, bk=bk, i=i, j=j, ac=ac: h.scalar_tensor_tensor(out=ac, in0=bk[:, j:j + 128], scalar=cw[:, i, j:j + 1], in1=ac, op0=ALU.mult, op1=ALU.add), [cw], [bk, ack])
                ack = acc[c % 2]
                sigmoid_act(sg12[:], ack[:], [ack], [sg12])
                G(lambda h, ack=ack: h.tensor_tensor(out=xbcT[:], in0=ack[:], in1=sg12[:], op=ALU.mult), [ack, sg12], [xbcT])
                LD(BCT[c], xbcT[:, 8:12, :].rearrange("p a b -> p (a b)"), [xbcT], [DK("BCT", c)])
                pb = PS[3][:].bitcast(BF16)
                for i in range(8):
                    TR(PS[3], pb[:, i * 128:(i + 1) * 128], xbcT[:, i, :], identb[:], [xbcT, identb])
                V(lambda h: h.tensor_copy(out=xsc[:, 0:1024], in_=pb), [], [PS[3], xsc])
                for i in range(2):
                    TR(PS[3], pb[:, i * 128:(i + 1) * 128], xbcT[:, 8 + i, :], identb[:], [xbcT, identb])
                V(lambda h: h.tensor_copy(out=xsc[:, 1024:1280], in_=pb[:, 0:256]), [], [PS[3], xsc])
                LD(XSB[c * 128:(c + 1) * 128, :], xsc[:], [xsc], [DK("XSB", c)])
                gates(gtc, [0])
                cumsums([0], PS[2])

                def store1(c=c):
                    A(lambda h: h.activation(out=hsb16[:], in_=Hs[:], func=AF.Copy), [Hs], [hsb16])
                    G(lambda h: h.tensor_copy(out=cmb16[:], in_=Cm[:]), [Cm], [cmb16])
                    LD(HS[c], hsb16[:], [hsb16], [DK("HS", c)])
                    LD(HM[c], cmb16[:], [cmb16], [DK("HM", c)])
                state_step(0, xsc, ptc, Hs, Cm, store1)
            prep(WB_L0P2, lambda kc, c0: kc * 2048 + c0, ab_w_in[:, 0:1024], 8, 1024, lambda kc: gcol[:, 0, kc:kc + 1])
            prep(WB_L0P2, lambda kc, c0: kc * 2048 + 1024 + c0, ab_w_in[:, 5664:6688], 8, 1024, lambda kc: gcol[:, 0, kc:kc + 1])
            prep(WB_L0P2, lambda kc, c0: 16 * 1024 + kc * 1024 + c0, ab_w_out, 16, 1024, lambda kc: gcol[:, 4 + kc // 8, (kc % 8):(kc % 8) + 1])
            LD(exS[:, 0:1024], Hs[:], [Hs], [DK("exS", 0)])
            LD(exS[:, 1024:2056], Cm[:], [Cm], [DK("exS", 0)])
            allgather(exS, exG, DK("exS", 0), DK("exG", 0))

            S.barrier()
            S.clear_reg("arena")
            SC("L0P2_w")
            aoff[0] = 0
            Sel = ar("Sel", [112, 48, 128], BF16)
            Hsb = ar("Hsb", [128, 1024], F32)
            Cmb = ar("Cmb", [128, 1032], F32)
            sv = aoff[0]
            gath = ar("gath", [128, 2, 2056], F32)
            LD(gath[:], exG.rearrange("(r p) n -> p r n", p=128), [DK("exG", 0)], [gath])
            select2(Hsb[:], gath[:, 0, 0:1024], gath[:, 1, 0:1024], [gath, selt], [Hsb])
            select2(Cmb[:], gath[:, 0, 1024:2056], gath[:, 1, 1024:2056], [gath, selt], [Cmb])
            ones3 = ar("ones3", [112, 48, 128], BF16)
            selA = ar("selA", [112, 48, 128], BF16)
            selB = ar("selB", [112, 48, 128], BF16)
            G(lambda h: h.memset(ones3[:], 1.0), [], [ones3])
            G(lambda h: h.affine_select(out=selA[:], in_=ones3[:], pattern=[[1, 48], [0, 128]], compare_op=ALU.is_equal, fill=0.0, base=0, channel_multiplier=-1), [ones3], [selA])
            G(lambda h: h.affine_select(out=selB[:], in_=ones3[:], pattern=[[1, 48], [0, 128]], compare_op=ALU.is_equal, fill=0.0, base=64, channel_multiplier=-1), [ones3], [selB])
            G(lambda h: h.tensor_tensor(out=Sel[:], in0=selA[:], in1=selB[:], op=ALU.add), [selA, selB], [Sel])
            S.barrier()
            for tt in (ones3, selA, selB, gath):
                S.unregister("arena", tt)
            aoff[0] = sv
            WT2 = ar("W2", [128, 32, 1024], BF16, manual=True)
            Wzo = WT2[:, 0:16, :].rearrange("p a b -> p (a b)").rearrange("p (k n) -> p k n", k=8)
            Wo = WT2[:, 16:32, :]
            dtb = ar("dtb", [128, 32], F32)
            negA = ar("negA", [128, 32], F32)
            igfg = ar("igfg", [128, 32], F32)
            dsk = ar("dsk", [128, 16], F32)
            cw = None
            LD(dtb[:], ab_dt_bias.partition_broadcast(128), [WK], [dtb])
            LD(negA[:], ab_a_log.partition_broadcast(128), [WK], [negA])
            LD(igfg[:, 0:16], ab_ig_bias.partition_broadcast(128), [WK], [igfg])
            LD(igfg[:, 16:32], ab_fg_bias.partition_broadcast(128), [WK], [igfg])
            LD(dsk[:], ab_d_skip.partition_broadcast(128), [WK], [dsk])
            A(lambda h: h.activation(out=negA[:], in_=negA[:], func=AF.Exp), [], [negA])
            V(lambda h: h.tensor_scalar(out=negA[:], in0=negA[:], scalar1=-1.0, scalar2=None, op0=ALU.mult), [], [negA])
            hsb16 = ar("hsb16", [128, 1024], BF16)
            cmb16 = ar("cmb16", [128, 1032], BF16)
            gw = ar("gw", [128, 64], F32)
            G48 = ar("G48", [128, 48], F32)
            cst = ar("cst", [128, 96], F32)
            scw = ar("scw", [128, 48], F32)
            xe = ar("xe", [128, 1024], BF16)
            kw = ar("kw", [128, 1024], BF16)
            dts = ar("dts", [128, 32], F32)
            igs = ar("igs", [128, 16], F32)
            pt = [ar("pt%d" % i, [128, 3072], BF16) for i in range(2)]
            xsb = [ar("xsb%d" % i, [128, 1280], BF16) for i in range(2)]
            bct = [ar("bct%d" % i, [128, 4, 128], BF16) for i in range(2)]
            gt = [ar("gt%d" % i, [128, 64], F32) for i in range(2)]
            hsl = [ar("hsl%d" % i, [128, 1024], BF16) for i in range(2)]
            hml = [ar("hml%d" % i, [128, 1032], BF16) for i in range(2)]
            hTc = ar("hTc", [128, 8, 128], BF16)
            zo = ar("zo", [128, 2048], BF16)
            LS = ar("LS", [128, 2, 112], F32)
            X2t = ar("X2t", [112, 2, 128], BF16)
            tmpb = ar("tmpb", [112, 2, 128], BF16)
            DT = [ar("DT%d" % i, [128, 512], BF16) for i in range(2)]
            WT = [ar("WT%d" % i, [128, 512], BF16) for i in range(2)]
            cbt = ar("cbt", [128, 2, 128], BF16)
            xdt = ar("xdt", [128, 2, 1024], BF16)
            sfac = ar("sfac", [128, 48], F32)
            T1 = ar("T1", [128, 1024], F32)
            T2 = ar("T2", [128, 1024], F32)
            yv = ar("yv", [128, 1024], F32)
            ycat = ar("ycat", [128, 2048], BF16)
            kqT = ar("kqT", [128, 16, 128], BF16)
            hdir = ar("hdir", [128, 512], F32)
            dn = ar("dn", [128, 16], F32)
            YT = ar("YT", [128, 16, 128], BF16)
            sz, hmv, xo = T2, yv, T1
            wzoT, woT = Trk("wzoT"), Trk("woT")
            WB2v = WB_L0P2.rearrange("p (a n) -> p a n", a=32)
            for q in range(2):
                LD(WT2[:, q * 8:(q + 1) * 8, :], WB2v[:, q * 8:(q + 1) * 8, :], [], [wzoT])
            for q in range(2, 4):
                LD(WT2[:, q * 8:(q + 1) * 8, :], WB2v[:, q * 8:(q + 1) * 8, :], [], [woT])
            LD(gres[:], norm_g[0, 1].partition_broadcast(128), [WK], [gres])
            V(lambda h: h.memset(LS[:], 0.0), [], [LS])
            V(lambda h: h.memset(X2t[:], 0.0), [], [X2t])
            V(lambda h: h.memset(tmpb[:], 0.0), [], [tmpb])

            def p2_loads(c):
                i = c % 2
                LD(pt[i][:], PT0[c * 128:(c + 1) * 128, :], [DK("PT0", c)], [pt[i]])
                LD(xsb[i][:], XSB[c * 128:(c + 1) * 128, :], [DK("XSB", c)], [xsb[i]])
                LD(bct[i][:].rearrange("p a b -> p (a b)"), BCT[c], [DK("BCT", c)], [bct[i]])
                LD(gt[i][:], GT[c * 128:(c + 1) * 128, :], [DK("GT", c)], [gt[i]])
                LD(hsl[i][:], HS[c], [DK("HS", c)], [hsl[i]])
                LD(hml[i][:], HM[c], [DK("HM", c)], [hml[i]])
                LD(xin[i][:], x_in[c * 128:(c + 1) * 128, :], [xk], [xin[i]])

            def decay_block(b, hds):
                bk = PS[b % 2]
                for j, (hd, d) in enumerate(hds):
                    o = bk[:, j * 128:(j + 1) * 128]
                    MM(bk, o, Sel[:, hd, :], X2t[:, 0, :], [Sel, X2t], start=True, stop=False)
                    MM(bk, o, X2t[:, 1, :], Sel[:, hd, :], [Sel, X2t], start=False, stop=False)
                    MM(bk, o, identb[:], NEG[:, d, :], [identb, NEG], start=False, stop=True)
                A(lambda h: h.activation(out=DT[b % 2][:], in_=bk[:, :], func=AF.Exp), [], [bk, DT[b % 2]])

            SC("L0P2")
            p2_loads(NC - 1)
            for c in range(NC - 1, -1, -1):
                if c - 1 >= 0:
                    p2_loads(c - 1)
                i2 = c % 2
                ptc, xsc, bcc, gtc, hsc, hmc, xic = pt[i2], xsb[i2], bct[i2], gt[i2], hsl[i2], hml[i2], xin[i2]
                norm_chunk(x_in, c, xk, xic, load=False)
                transpose8(hb, 0, hTc[:], [hTc], eng="act")
                for j in range(4):
                    bk = PS[4 + j % 2]
                    for k in range(8):
                        MM(bk, bk[:, :], hTc[:, k, :], Wzo[:, k, j * 512:(j + 1) * 512], [hTc, wzoT], start=(k == 0), stop=(k == 7))
                    if j % 2 == 0:
                        A(lambda h, bk=bk, j=j: h.activation(out=zo[:, j * 512:(j + 1) * 512], in_=bk[:, :], func=AF.Copy), [], [bk, zo])
                    else:
                        V(lambda h, bk=bk, j=j: h.tensor_copy(out=zo[:, j * 512:(j + 1) * 512], in_=bk[:, :]), [], [bk, zo])
                gates(gtc, [0, 1])
                cumsums([0, 1], PS[2])
                V(lambda h: h.tensor_copy(out=LS[:, 0, 0:48], in_=cst[:, 0:48]), [cst], [LS])
                V(lambda h: h.tensor_scalar(out=LS[:, 1, 0:32], in0=cst[:, 0:32], scalar1=-1.0, scalar2=None, op0=ALU.mult), [cst], [LS])
                V(lambda h: h.tensor_tensor(out=LS[:, 1, 32:48], in0=igs[:, 0:16], in1=cst[:, 32:48], op=ALU.subtract), [cst, igs], [LS])
                V(lambda h: h.tensor_copy(out=LS[:, :, 64:112], in_=LS[:, :, 0:48]), [], [LS])
                TR(PS[2], PS[2][0:112, 0:128], LS[:, 0, :], identf[:], [LS, identf])
                TR(PS[2], PS[2][0:112, 128:256], LS[:, 1, :], identf[:], [LS, identf])
                V(lambda h: h.tensor_copy(out=X2t[0:48, :, :], in_=v3(PS[2][0:48, 0:256], 2)), [], [PS[2], X2t])
                V(lambda h: h.tensor_copy(out=tmpb[64:112, :, :], in_=v3(PS[2][64:112, 0:256], 2)), [], [PS[2], tmpb])
                V(lambda h: h.tensor_tensor(out=X2t[64:112, :, :], in0=v3(PS[2][64:112, 0:256], 2), in1=tmpb[64:112, :, :], op=ALU.subtract), [tmpb], [PS[2], X2t])
                A(lambda h: h.activation(out=sfac[:], in_=cst[:, 0:48], func=AF.Exp), [cst], [sfac])
                for g in range(2):
                    MM(PS[2], PS[2][:, 256 + g * 128:384 + g * 128], bcc[:, g, :], bcc[:, 2 + g, :], [bcc])
                V(lambda h: h.tensor_copy(out=cbt[:], in_=v3(PS[2][:, 256:512], 2)), [], [PS[2], cbt])
                for d in range(2):
                    V(lambda h, d=d: h.tensor_tensor(out=v3(xdt[:, d, :], 16), in0=v3(xsc[:, 0:1024], 16), in1=bc_in(dts[:, d * 16:d * 16 + 16], 16, 64), op=ALU.mult), [xsc, dts], [xdt])
                for b in range(8):
                    g, e0 = b // 4, (b % 4) * 2
                    hds = [(g * 8 + e0, 0), (16 + g * 8 + e0, 1), (g * 8 + e0 + 1, 0), (16 + g * 8 + e0 + 1, 1)]
                    decay_block(b, hds)
                    V(lambda h, b=b, g=g: h.tensor_tensor(out=v3(WT[b % 2][:], 4), in0=v3(DT[b % 2][:], 4), in1=bc_mid(cbt[:, g, :], 4, 128), op=ALU.mult), [DT[b % 2], cbt], [WT[b % 2]])
                    for j, (hd, d) in enumerate(hds):
                        hh = hd % 16
                        e = hh % 8
                        MM(PS[4 + g], PS[4 + g][:, e * 64:(e + 1) * 64], WT[b % 2][:, j * 128:(j + 1) * 128], xdt[:, d, hh * 64:(hh + 1) * 64], [WT[b % 2], xdt], start=(d == 0), stop=(d == 1))
                A(lambda h: h.activation(out=hsb16[:], in_=Hsb[:], func=AF.Copy), [Hsb], [hsb16])
                for g in range(2):
                    MM(PS[6 + g], PS[6 + g][:, :], bcc[:, 2 + g, :], hsc[:, g * 512:(g + 1) * 512], [bcc, hsc])
                for g in range(2):
                    V(lambda h, g=g: h.tensor_tensor(out=v3(T1[:, g * 512:(g + 1) * 512], 8), in0=v3(PS[6 + g][:, :], 8), in1=bc_in(sfac[:, g * 8:g * 8 + 8], 8, 64), op=ALU.mult), [sfac], [PS[6 + g], T1])
                for g in range(2):
                    MM(PS[6 + g], PS[6 + g][:, :], bcc[:, 2 + g, :], hsb16[:, g * 512:(g + 1) * 512], [bcc, hsb16])
                for g in range(2):
                    V(lambda h, g=g: h.tensor_tensor(out=v3(T2[:, g * 512:(g + 1) * 512], 8), in0=v3(PS[6 + g][:, :], 8), in1=bc_in(sfac[:, 16 + g * 8:24 + g * 8], 8, 64), op=ALU.mult), [sfac], [PS[6 + g], T2])
                G(lambda h: h.tensor_tensor(out=T1[:], in0=T1[:], in1=T2[:], op=ALU.add), [T2], [T1])
                G(lambda h: h.tensor_tensor(out=v3(T2[:], 16), in0=v3(xsc[:, 0:1024], 16), in1=bc_in(dsk[:], 16, 64), op=ALU.mult), [xsc, dsk], [T2])
                G(lambda h: h.tensor_tensor(out=T1[:], in0=T1[:], in1=T2[:], op=ALU.add), [T2], [T1])
                for g in range(2):
                    V(lambda h, g=g: h.tensor_tensor(out=yv[:, g * 512:(g + 1) * 512], in0=PS[4 + g][:, :], in1=T1[:, g * 512:(g + 1) * 512], op=ALU.add), [T1], [PS[4 + g], yv])
                sigmoid_act(sz[:], zo[:, 0:1024], [zo], [sz])
                G(lambda h: h.tensor_tensor(out=sz[:], in0=sz[:], in1=zo[:, 0:1024], op=ALU.mult), [zo], [sz])
                V(lambda h: h.tensor_tensor(out=yv[:], in0=yv[:], in1=sz[:], op=ALU.mult), [sz], [yv])
                for g in range(2):
                    rstd_from(yv[:, g * 512:(g + 1) * 512], 512, 1 + g, sqj[:, g * 512:(g + 1) * 512], [yv])
                for g in range(2):
                    V(lambda h, g=g: h.tensor_scalar(out=ycat[:, g * 512:(g + 1) * 512], in0=yv[:, g * 512:(g + 1) * 512], scalar1=ss[:, 1 + g:2 + g], scalar2=None, op0=ALU.mult), [yv, ss], [ycat])
                transpose8(ptc, 1024, kqT[:, 0:8, :], [kqT])
                transpose8(ptc, 0, kqT[:, 8:16, :], [kqT], eng="act")
                A(lambda h: h.activation(out=cmb16[:], in_=Cmb[:], func=AF.Copy), [Cmb], [cmb16])
                for hf in range(2):
                    h0 = hf * 4
                    for j in range(4):
                        MM(PS[4], PS[4][:, j * 128:(j + 1) * 128], kqT[:, h0 + j, :], kqT[:, 8 + h0 + j, :], [kqT])
                    for bb in range(2):
                        b = 8 + hf * 2 + bb
                        ha = h0 + bb * 2
                        hds = [(32 + ha, 0), (40 + ha, 1), (32 + ha + 1, 0), (40 + ha + 1, 1)]
                        decay_block(b, hds)
                        V(lambda h, b=b, bb=bb: h.tensor_tensor(out=WT[b % 2][:].rearrange("p (a d b) -> p a d b", a=2, d=2),
                                                               in0=v3(PS[4][:, bb * 256:(bb + 1) * 256], 2).unsqueeze(2).to_broadcast([128, 2, 2, 128]),
                                                               in1=DT[b % 2][:].rearrange("p (a d b) -> p a d b", a=2, d=2), op=ALU.mult), [DT[b % 2]], [PS[4], WT[b % 2]])
                        for j, (hd, d) in enumerate(hds):
                            hh = (hd - 32) % 8
                            bkn = PS[5 + d]
                            MM(bkn, bkn[:, (hh % 4) * 128:(hh % 4 + 1) * 128], WT[b % 2][:, j * 128:(j + 1) * 128], ptc[:, 2048 + hh * 128:2048 + (hh + 1) * 128], [WT[b % 2], ptc])
                            MM(PS[2], PS[2][:, d * 8 + hh:d * 8 + hh + 1], WT[b % 2][:, j * 128:(j + 1) * 128], onesb[:, 0:1], [WT[b % 2], onesb])
                    for d in range(2):
                        st16 = hmc if d == 0 else cmb16
                        for j in range(4):
                            hh = h0 + j
                            MM(PS[7], PS[7][:, j * 128:(j + 1) * 128], kqT[:, 8 + hh, :], st16[:, hh * 128:(hh + 1) * 128], [kqT, st16])
                            MM(PS[2], PS[2][:, 16 + d * 8 + hh:17 + d * 8 + hh], kqT[:, 8 + hh, :], st16[:, 1024 + hh:1025 + hh], [kqT, st16])
                        sc = sfac[:, 32 + d * 8 + h0:32 + d * 8 + h0 + 4]
                        V(lambda h, sc=sc: h.tensor_tensor(out=v3(hdir[:], 4), in0=v3(PS[7][:, :], 4), in1=bc_in(sc, 4, 128), op=ALU.mult), [sfac], [PS[7], hdir])
                        V(lambda h, d=d: h.tensor_tensor(out=hdir[:], in0=hdir[:], in1=PS[5 + d][:, :], op=ALU.add), [], [hdir, PS[5 + d]])
                        V(lambda h, d=d, sc=sc, h0=h0: h.tensor_tensor(out=dn[:, 0:4], in0=PS[2][:, 16 + d * 8 + h0:16 + d * 8 + h0 + 4], in1=sc, op=ALU.mult), [sfac], [PS[2], dn])
                        V(lambda h, d=d, h0=h0: h.tensor_tensor(out=dn[:, 0:4], in0=dn[:, 0:4], in1=PS[2][:, d * 8 + h0:d * 8 + h0 + 4], op=ALU.add), [], [PS[2], dn])
                        V(lambda h: h.scalar_tensor_tensor(out=dn[:, 4:8], in0=dn[:, 0:4], scalar=-1.0, in1=dn[:, 0:4], op0=ALU.mult, op1=ALU.max), [], [dn])
                        V(lambda h: h.tensor_scalar(out=dn[:, 4:8], in0=dn[:, 4:8], scalar1=1.0, scalar2=None, op0=ALU.max), [], [dn])
                        V(lambda h: h.reciprocal(out=dn[:, 8:12], in_=dn[:, 4:8]), [], [dn])
                        if d == 0:
                            V(lambda h, h0=h0: h.tensor_tensor(out=v3(hmv[:, h0 * 128:(h0 + 4) * 128], 4), in0=v3(hdir[:], 4), in1=bc_in(dn[:, 8:12], 4, 128), op=ALU.mult), [hdir, dn], [hmv])
                        else:
                            V(lambda h: h.tensor_tensor(out=v3(hdir[:], 4), in0=v3(hdir[:], 4), in1=bc_in(dn[:, 8:12], 4, 128), op=ALU.mult), [dn], [hdir])
                            V(lambda h, h0=h0: h.tensor_tensor(out=hmv[:, h0 * 128:(h0 + 4) * 128], in0=hmv[:, h0 * 128:(h0 + 4) * 128], in1=hdir[:], op=ALU.add), [hdir], [hmv])
                sigmoid_act(sz[:], zo[:, 1024:2048], [zo], [sz])
                V(lambda h: h.tensor_tensor(out=hmv[:], in0=hmv[:], in1=sz[:], op=ALU.mult), [sz], [hmv])
                G(lambda h: h.tensor_tensor(out=sz[:], in0=hmv[:], in1=hmv[:], op=ALU.mult), [hmv], [sz])
                V(lambda h: h.tensor_reduce(out=ss[:, 8:16], in_=v3(sz[:], 8), axis=AX.X, op=ALU.add), [sz], [ss])
                A(lambda h: h.activation(out=ss[:, 8:16], in_=ss[:, 8:16], func=AF.Ln, scale=1.0 / 128, bias=epsc[:, 0:1]), [epsc], [ss])
                A(lambda h: h.activation(out=ss[:, 8:16], in_=ss[:, 8:16], func=AF.Exp, scale=-0.5), [], [ss])
                V(lambda h: h.tensor_tensor(out=v3(ycat[:, 1024:2048], 8), in0=v3(hmv[:], 8), in1=bc_in(ss[:, 8:16], 8, 128), op=ALU.mult), [hmv, ss], [ycat])
                transpose8(ycat, 0, YT[:, 0:8, :], [YT])
                transpose8(ycat, 1024, YT[:, 8:16, :], [YT], eng="act")
                for n2 in range(2):
                    for kk in range(16):
                        MM(PS[6 + n2], PS[6 + n2][:, :], YT[:, kk, :], Wo[:, kk, n2 * 512:(n2 + 1) * 512], [YT, woT], start=(kk == 0), stop=(kk == 15))
                resid_out([PS[6], PS[7]], xic, X1, c, "X1", xo)
                if c == NC - 1:
                    LD(exR[0], X1[T - 1:T, :], [DK("X1", c)], [DK("exR0", 0)])
                    allgather(exR[0], exRG[0], DK("exR0", 0), DK("exRG0", 0))
                state_step(1, xsc, ptc, Hsb, Cmb, lambda: None)
            prep_ffn_up(0)
            prep_ffn_dn(0)

        NBT = 256

        def ffn(li, src, src_name, dst, dst_name):
            reset_arena()
            SC("FFN%d_w" % li)
            NB = T // NBT
            CPB = NBT // 128
            Wup = ar("Wup", [128, 8, 5632], BF16, manual=True)
            Wdn = ar("Wdn", [128, 22, 1024], BF16, manual=True)
            fw = ar("fw", [128, 44, 3], F32)
            fb = ar("fb", [128, 44], F32)
            hT2 = [ar("hT2_%d" % i, [128, 8, NBT + 2], BF16, manual=True) for i in range(3)]
            hmain = [Trk("hmain%d" % i) for i in range(3)]
            hleft = [Trk("hleft%d" % i) for i in range(3)]
            hright = [Trk("hright%d" % i) for i in range(3)]
            usb = [ar("usb%d" % i, [128, NBT + 2], F32) for i in range(3)]
            cgs = [ar("cg%d" % i, [128, NBT], F32) for i in range(2)]
            cvs = [ar("cvv%d" % i, [128, NBT], F32) for i in range(2)]
            ggs = [ar("gg%d" % i, [128, NBT], F32) for i in range(2)]
            gTs = [ar("gT%d" % i, [128, NBT], BF16) for i in range(3)]
            xos = [ar("xo0", [128, 1024], F32)] * 2
            xr = [ar("xr0", [128, 1024], F32)] * 2
            for j in range(3):
                LDS(fw[:, :, j], ffn_conv_w[li, j].rearrange("(i p) -> p i", p=128), [WK], [fw])
            LDS(fb[:], ffn_conv_b[li].rearrange("(i p) -> p i", p=128), [WK], [fb])
            LD(gres[:], norm_g[li, 3].partition_broadcast(128), [WK], [gres])
            wgT = [Trk("wffn0"), Trk("wffn1")]
            WBu = WB_up[li].rearrange("p (k n) -> p k n", k=8)
            WBd = WB_dn[li].rearrange("p (k n) -> p k n", k=22)
            for g in range(2):
                for a in (g * 1408, 2816 + g * 1408):
                    LD(Wup[:, :, a:a + 1408], WBu[:, :, a:a + 1408], [], [wgT[g]])
                LD(Wdn[:, g * 11:(g + 1) * 11, :], WBd[:, g * 11:(g + 1) * 11, :], [], [wgT[g]])
            for i in range(3):
                V(lambda h, i=i: h.memset(hT2[i][:], 0.0), [], [hmain[i], hleft[i], hright[i]])
            hcand = ar("hcand", [128, 8, 2], BF16)
            halo1 = ar("halo1", [128, 8, 1], BF16)
            halo_cands(exRG[li], DK("exRG%d" % li, 0), 2, hcand)
            select2(halo1[:], hcand[:, :, 0:1], hcand[:, :, 1:2], [hcand, selt], [halo1])

            def norm_block(b):
                p, pp = b % 3, (b - 1) % 3
                for cc in range(CPB):
                    c = b * CPB + cc
                    norm_chunk(src, c, DK(src_name, c), xin[c % 2])
                    transpose8(hb, 0, hT2[p][:, :, 1 + cc * 128:1 + (cc + 1) * 128], [hmain[p]])
                if b > 0:
                    V(lambda h: h.tensor_copy(out=hT2[pp][:, :, NBT + 1:NBT + 2], in_=hT2[p][:, :, 1:2]), [hmain[p]], [hright[pp]])
                    V(lambda h: h.tensor_copy(out=hT2[p][:, :, 0:1], in_=hT2[pp][:, :, NBT:NBT + 1]), [hmain[pp]], [hleft[p]])
                else:
                    V(lambda h: h.memset(hT2[p][:, :, 0:1], 0.0), [], [hleft[p]])
                if b + 1 == NB:
                    V(lambda h: h.tensor_copy(out=hT2[p][:, :, NBT + 1:NBT + 2], in_=halo1[:]), [halo1], [hright[p]])

            SC("FFN%d" % li)
            norm_block(0)
            if NB > 1:
                norm_block(1)
            tctr = [0]
            accb = [[PS[0], PS[1]], [PS[2], PS[7]]]
            for b in range(NB):
                p = b % 3
                if b + 2 < NB:
                    norm_block(b + 2)
                for i in range(22):
                    hr = [hmain[p], hleft[p], hright[p], wgT[i // 11]]
                    cg, cvv, gg, gTi = cgs[i % 2], cvs[i % 2], ggs[i % 2], gTs[i % 3]
                    for t in (i, 22 + i):
                        q = tctr[0] % 3
                        tctr[0] += 1
                        bk = PS[4 + q]
                        for k in range(8):
                            MM(bk, bk[:, 0:NBT + 2], Wup[:, k, t * 128:(t + 1) * 128], hT2[p][:, k, :], hr, start=(k == 0), stop=(k == 7))
                        us = usb[q]
                        A(lambda h, bk=bk, us=us: h.activation(out=us[:], in_=bk[:, 0:NBT + 2], func=AF.Copy), [], [bk, us])
                        cv = cg if t < 22 else cvv
                        A(lambda h, us=us, cv=cv, t=t: h.activation(out=cv[:], in_=us[:, 1:NBT + 1], func=AF.Identity, scale=fw[:, t, 1:2], bias=fb[:, t:t + 1]), [us, fw, fb], [cv])
                        V(lambda h, us=us, cv=cv, t=t: h.scalar_tensor_tensor(out=cv[:], in0=us[:, 0:NBT], scalar=fw[:, t, 0:1], in1=cv[:], op0=ALU.mult, op1=ALU.add), [us, fw], [cv])
                        V(lambda h, us=us, cv=cv, t=t: h.scalar_tensor_tensor(out=cv[:], in0=us[:, 2:NBT + 2], scalar=fw[:, t, 2:3], in1=cv[:], op0=ALU.mult, op1=ALU.add), [us, fw], [cv])
                    A(lambda h, cg=cg, gg=gg: h.activation(out=gg[:], in_=cg[:], func=AF.Square), [cg], [gg])
                    V(lambda h, gg=gg: h.tensor_scalar(out=gg[:], in0=gg[:], scalar1=0.044715, scalar2=1.0, op0=ALU.mult, op1=ALU.add), [], [gg])
                    G(lambda h, gg=gg, cg=cg: h.tensor_tensor(out=gg[:], in0=gg[:], in1=cg[:], op=ALU.mult), [cg], [gg])
                    A(lambda h, gg=gg: h.activation(out=gg[:], in_=gg[:], func=AF.Exp, scale=-1.5957691216), [], [gg])
                    V(lambda h, gg=gg: h.tensor_scalar(out=gg[:], in0=gg[:], scalar1=1.0, scalar2=None, op0=ALU.add), [], [gg])
                    V(lambda h, gg=gg: h.reciprocal(out=gg[:], in_=gg[:]), [], [gg])
                    G(lambda h, gg=gg, cg=cg: h.tensor_tensor(out=gg[:], in0=gg[:], in1=cg[:], op=ALU.mult), [cg], [gg])
                    G(lambda h, gg=gg, cvv=cvv, gTi=gTi: h.tensor_tensor(out=gTi[:], in0=gg[:], in1=cvv[:], op=ALU.mult), [gg, cvv], [gTi])
                    for m in range(CPB):
                        for n2 in range(2):
                            bk = accb[m][n2]
                            MM(bk, bk[:, :], gTi[:, m * 128:(m + 1) * 128], Wdn[:, i, n2 * 512:(n2 + 1) * 512], [gTi, wgT[i // 11]], start=(i == 0), stop=(i == 21))
                for m in range(CPB):
                    c = b * CPB + m
                    xrc = xr[c % 2]
                    LD(xrc[:], src[c * 128:(c + 1) * 128, :], [DK(src_name, c)], [xrc])
                    resid_out(accb[m], xrc, dst, c, dst_name, xos[c % 2])
            if li == 0:
                for (s0, n, d0, gq) in [(1024, 2048, 0, False), (0, 512, 2048, True), (512, 512, 2560, False), (3072, 32, 3072, False)]:
                    prep(WB_g, lambda kc, c0, d0=d0: kc * 3104 + d0 + c0, c_w_in[:, s0:s0 + n], 8, n,
                         (lambda kc: gcolq[:, kc:kc + 1]) if gq else (lambda kc: gcol[:, 2, kc:kc + 1]))
                prep(WB_o1, lambda kc, c0: kc * 1024 + c0, c_w_out, 8, 1024, lambda kc: gcol[:, 6, kc:kc + 1])

        def layer1():
            reset_arena()
            SC("L1P1_w")
            W1 = ar("Wg", [128, 8, 3104], BF16, manual=True)
            gw2 = ar("gw2", [16, 2, 512], BF16)
            gw2f = ar("gw2f", [16, 2, 512], F32)
            gbias = ar("gbias", [128, 2, 4], F32)
            ones1 = ar("ones1", [128, 128], F32)
            Sst = ar("Sst", [128, 1024], F32)
            n0 = len(dbs)
            hTc = ar2("hTc", [128, 8, 128], BF16)
            vr = ar2("vr", [128, 2048], BF16, 1)
            qkT = ar2("qkT", [128, 8, 128], BF16)
            lt = ar2("lt", [128, 8, 128], F32)
            glr = ar2("glr", [16, 2, 128], BF16)
            Lc = ar2("Lc", [128, 8, 128], F32)
            ex = ar2("ex", [128, 8, 128], F32)
            kend = ar2("kend", [128, 4, 128], BF16)
            kendT = ar2("kendT", [128, 512], BF16)
            S16 = ar2("S16", [128, 1024], BF16)
            tots = ar2("tots", [128, 16], F32)
            for d in range(2):
                LD(gw2f[:, d, :], c_gate_w2[d], [WK], [gw2f])
            V(lambda h: h.tensor_copy(out=gw2[:], in_=gw2f[:]), [gw2f], [gw2])
            for d in range(2):
                LDS(gbias[:, d, :], c_gate_b[d].rearrange("(j p) -> p j", p=128), [WK], [gbias])
            V(lambda h: h.tensor_scalar(out=gbias[:], in0=gbias[:], scalar1=-1.0, scalar2=None, op0=ALU.mult), [], [gbias])
            V(lambda h: h.memset(ones1[:], 1.0), [], [ones1])
            WBgv = WB_g.rearrange("p (k n) -> p k n", k=8)
            for q in range(2):
                LD(W1[:, q * 4:(q + 1) * 4, :], WBgv[:, q * 4:(q + 1) * 4, :], [], [W1])
            V(lambda h: h.memset(Sst[:], 0.0), [], [Sst])

            def scans(ltile, dirs, Lc, tots):
                for d in dirs:
                    for hh in range(4):
                        t = d * 4 + hh
                        V(lambda h, t=t: h.tensor_tensor_scan(out=Lc[:, t, :], data0=ones1[:], data1=ltile[:, t, :], initial=0.0, op0=ALU.mult, op1=ALU.add), [ones1, ltile], [Lc])
                    V(lambda h, d=d: h.tensor_copy(out=tots[:, d * 4:d * 4 + 4], in_=Lc[:, d * 4:d * 4 + 4, 127:128].rearrange("p a b -> p (a b)")), [Lc], [tots])
                    if d == 1:
                        V(lambda h: h.tensor_tensor(out=Lc[:, 4:8, :], in0=ltile[:, 4:8, :], in1=Lc[:, 4:8, :], op=ALU.subtract), [ltile], [Lc])
                        V(lambda h: h.tensor_tensor(out=Lc[:, 4:8, :], in0=Lc[:, 4:8, :], in1=bc_in(tots[:, 4:8], 4, 128), op=ALU.add), [tots], [Lc])
                V(lambda h: h.tensor_scalar(out=tots[:, 8:16], in0=tots[:, 0:8], scalar1=-1.0 / 16, scalar2=None, op0=ALU.mult), [], [tots])

            def gla_state(d, qk_tile, vr_tile, Stt, store_fn, Lc, tots, ex, kend, kendT):
                for hh in range(4):
                    t = d * 4 + hh
                    A(lambda h, t=t: h.activation(out=ex[:, t, :], in_=Lc[:, t, :], func=AF.Exp, scale=1.0 / 16, bias=tots[:, 8 + t:9 + t]), [Lc, tots], [ex])
                V(lambda h: h.tensor_tensor(out=kend[:], in0=qk_tile[:, 4:8, :], in1=ex[:, d * 4:d * 4 + 4, :], op=ALU.mult), [qk_tile, ex], [kend])
                pb = PS[3][:].bitcast(BF16)
                for hh in range(4):
                    TR(PS[3], pb[:, hh * 128:(hh + 1) * 128], kend[:, hh, :], identb[:], [kend, identb])
                V(lambda h: h.tensor_copy(out=kendT[:], in_=pb[:, 0:512]), [], [PS[3], kendT])
                for hh in range(4):
                    bk = PS[4 + hh // 2]
                    MM(bk, bk[:, (hh % 2) * 256:(hh % 2 + 1) * 256], kendT[:, hh * 128:(hh + 1) * 128], vr_tile[:, hh * 256:(hh + 1) * 256], [kendT, vr_tile])
                store_fn()
                A(lambda h: h.activation(out=tots[:, 0:4], in_=tots[:, 8 + d * 4:12 + d * 4], func=AF.Exp), [], [tots])
                for hh in range(4):
                    bk = PS[4 + hh // 2]
                    V(lambda h, hh=hh, bk=bk: h.scalar_tensor_tensor(out=Stt[:, hh * 256:(hh + 1) * 256], in0=Stt[:, hh * 256:(hh + 1) * 256], scalar=tots[:, hh:hh + 1], in1=bk[:, (hh % 2) * 256:(hh % 2 + 1) * 256], op0=ALU.mult, op1=ALU.add), [tots], [Stt, bk])

            x2k = lambda c: DK("X2", c)
            SC("L1P1")
            for c in range(NC):
                setpar(c)
                i2 = c % 2
                norm_chunk(X2, c, x2k(c), xin[i2])
                transpose8(hb, 0, hTc[:], [hTc])
                for j in range(4):
                    bk = PS[j % 2]
                    for k in range(8):
                        MM(bk, bk[:, :], hTc[:, k, :], W1[:, k, j * 512:(j + 1) * 512], [hTc, W1], start=(k == 0), stop=(k == 7))
                    if j % 2 == 0:
                        A(lambda h, bk=bk, j=j: h.activation(out=vr[:, j * 512:(j + 1) * 512], in_=bk[:, :], func=AF.Copy), [], [bk, vr])
                    else:
                        V(lambda h, bk=bk, j=j: h.tensor_copy(out=vr[:, j * 512:(j + 1) * 512], in_=bk[:, :]), [], [bk, vr])
                LD(VR[c * 128:(c + 1) * 128, :], vr[:], [vr], [DK("VR", c)])
                for i in range(8):
                    bk = PS[6 + i // 4]
                    for k in range(8):
                        MM(bk, bk[:, (i % 4) * 128:(i % 4 + 1) * 128], W1[:, k, 2048 + i * 128:2048 + (i + 1) * 128], hTc[:, k, :], [hTc, W1], start=(k == 0), stop=(k == 7))
                    if i % 4 == 3:
                        q4 = i // 4
                        A(lambda h, bk=bk, q4=q4: h.activation(out=qkT[:, q4 * 4:q4 * 4 + 4, :], in_=v3(bk[:, :], 4), func=AF.Copy), [], [bk, qkT])
                LD(QKT[c], qkT[:].rearrange("p a b -> p (a b)"), [qkT], [DK("QKT", c)])
                for d in range(2):
                    for k in range(8):
                        MM(PS[2], PS[2][0:16, d * 128:(d + 1) * 128], W1[:, k, 3072 + d * 16:3088 + d * 16], hTc[:, k, :], [hTc, W1], start=(k == 0), stop=(k == 7))
                V(lambda h: h.tensor_copy(out=glr[:], in_=v3(PS[2][0:16, 0:256], 2)), [], [PS[2], glr])
                for d in range(2):
                    bk = PS[d]
                    for j in range(4):
                        MM(bk, bk[:, j * 128:(j + 1) * 128], gw2[:, d, j * 128:(j + 1) * 128], glr[:, d, :], [gw2, glr])
                    for j in range(4):
                        t = d * 4 + j
                        A(lambda h, bk=bk, j=j, t=t, d=d: h.activation(out=lt[:, t, :], in_=bk[:, j * 128:(j + 1) * 128], func=AF.Exp, scale=-1.0, bias=gbias[:, d, j:j + 1]), [gbias], [bk, lt])
                A(lambda h: h.activation(out=lt[:], in_=lt[:], func=AF.Ln, bias=1.0), [], [lt])
                LD(LTd[c], lt[:].rearrange("p a b -> p (a b)"), [lt], [DK("LTd", c)])
                scans(lt, [0], Lc, tots)

                def store1(c=c):
                    A(lambda h: h.activation(out=S16[:], in_=Sst[:], func=AF.Copy), [Sst], [S16])
                    LD(SG[c], S16[:], [S16], [DK("SG", c)])
                gla_state(0, qkT, vr, Sst, store1, Lc, tots, ex, kend, kendT)
            prep_ffn_up(1)
            LD(exS1, Sst[:], [Sst], [DK("exS1", 0)])
            allgather(exS1, exG1, DK("exS1", 0), DK("exG1", 0))
            del dbs[n0:]

            S.barrier()
            S.clear_reg("arena")
            aoff[0] = 0
            SC("L1P2_w")
            Wo = ar("Wo1", [128, 8, 1024], BF16, manual=True)
            ones1 = ar("ones1", [128, 128], F32)
            Sb = ar("Sb", [128, 1024], F32)
            gath1 = ar("gath1", [128, 2, 1024], F32)
            n0 = len(dbs)
            vr = ar2("vr", [128, 2048], BF16, 2)
            qkT = ar2("qkT", [128, 8, 128], BF16, 2)
            lt = ar2("lt", [128, 8, 128], F32, 2)
            sfl = ar2("sfl", [128, 1024], BF16, 2)
            xres = ar2("xres", [128, 1024], F32, 2)
            Lc = ar2("Lc", [128, 8, 128], F32)
            ex = ar2("ex", [128, 8, 128], F32)
            kend = ar2("kend", [128, 4, 128], BF16)
            kendT = ar2("kendT", [128, 512], BF16)
            tots = ar2("tots", [128, 16], F32)
            Sb16 = ar2("Sb16", [128, 1024], BF16)
            qin = ar2("qin", [128, 8, 128], BF16)
            kout = ar2("kout", [128, 8, 128], BF16)
            ex2 = ar2("ex2", [128, 8, 128], F32)
            ex3 = ar2("ex3", [128, 8, 128], F32)
            attm = ar2("attm", [128, 8, 128], BF16)
            ov = ar2("ov", [128, 1024], F32)
            t4 = ar2("t4", [128, 1024], F32)
            t5 = ar2("t5", [128, 1024], F32)
            ycat = ar2("ycat1", [128, 1024], BF16)
            YT = ar2("YT1", [128, 8, 128], BF16)
            xo = ar2("xo1", [128, 1024], F32)
            V(lambda h: h.memset(ones1[:], 1.0), [], [ones1])
            LD(Wo[:], WB_o1.rearrange("p (k n) -> p k n", k=8), [], [Wo])
            LD(gres[:], norm_g[1, 1].partition_broadcast(128), [WK], [gres])
            LD(gath1[:], exG1.rearrange("(r p) n -> p r n", p=128), [DK("exG1", 0)], [gath1])
            select2(Sb[:], gath1[:, 0, :], gath1[:, 1, :], [gath1, selt], [Sb])

            def p2_loads(c):
                setpar(c)
                LD(vr[:], VR[c * 128:(c + 1) * 128, :], [DK("VR", c)], [vr])
                LD(qkT[:].rearrange("p a b -> p (a b)"), QKT[c], [DK("QKT", c)], [qkT])
                LD(lt[:].rearrange("p a b -> p (a b)"), LTd[c], [DK("LTd", c)], [lt])
                LD(sfl[:], SG[c], [DK("SG", c)], [sfl])
                LD(xres[:], X2[c * 128:(c + 1) * 128, :], [x2k(c)], [xres])

            SC("L1P2")
            p2_loads(NC - 1)
            for c in range(NC - 1, -1, -1):
                if c - 1 >= 0:
                    p2_loads(c - 1)
                setpar(c)
                scans(lt, [0, 1], Lc, tots)
                A(lambda h: h.activation(out=ex2[:], in_=Lc[:], func=AF.Exp, scale=-1.0 / 16), [Lc], [ex2])
                for d in range(2):
                    V(lambda h, d=d: h.tensor_tensor(out=qin[:, d * 4:d * 4 + 4, :], in0=qkT[:, 0:4, :], in1=ex2[:, d * 4:d * 4 + 4, :], op=ALU.mult), [qkT, ex2], [qin])
                A(lambda h: h.activation(out=ex3[:], in_=Lc[:], func=AF.Exp, scale=1.0 / 16), [Lc], [ex3])
                for d in range(2):
                    G(lambda h, d=d: h.tensor_tensor(out=kout[:, d * 4:d * 4 + 4, :], in0=qkT[:, 4:8, :], in1=ex3[:, d * 4:d * 4 + 4, :], op=ALU.mult), [qkT, ex3], [kout])
                for d in range(2):
                    bk = PS[d]
                    for hh in range(4):
                        MM(bk, bk[:, hh * 128:(hh + 1) * 128], kout[:, d * 4 + hh, :], qin[:, d * 4 + hh, :], [kout, qin])
                    V(lambda h, d=d, bk=bk: h.tensor_tensor(out=attm[:, d * 4:d * 4 + 4, :], in0=v3(bk[:, :], 4), in1=bc_mid(maskb16[:, d, :], 4, 128), op=ALU.mult), [maskb16], [bk, attm])
                A(lambda h: h.activation(out=Sb16[:], in_=Sb[:], func=AF.Copy), [Sb], [Sb16])
                for hh in range(4):
                    bk = PS[6 + hh // 2]
                    o = bk[:, (hh % 2) * 256:(hh % 2 + 1) * 256]
                    vh = vr[:, hh * 256:(hh + 1) * 256]
                    MM(bk, o, attm[:, hh, :], vh, [attm, vr], start=True, stop=False)
                    MM(bk, o, qin[:, hh, :], sfl[:, hh * 256:(hh + 1) * 256], [qin, sfl], start=False, stop=False)
                    MM(bk, o, attm[:, 4 + hh, :], vh, [attm, vr], start=False, stop=False)
                    MM(bk, o, qin[:, 4 + hh, :], Sb16[:, hh * 256:(hh + 1) * 256], [qin, Sb16], start=False, stop=True)
                for q in range(2):
                    A(lambda h, q=q: h.activation(out=ov[:, q * 512:(q + 1) * 512], in_=PS[6 + q][:, :], func=AF.Copy), [], [PS[6 + q], ov])
                G(lambda h: h.tensor_tensor(out=t4[:], in0=ov[:], in1=ov[:], op=ALU.mult), [ov], [t4])
                V(lambda h: h.tensor_reduce(out=ss[:, 8:12], in_=v3(t4[:], 4), axis=AX.X, op=ALU.add), [t4], [ss])
                A(lambda h: h.activation(out=ss[:, 8:12], in_=ss[:, 8:12], func=AF.Ln, scale=1.0 / 256, bias=epsc[:, 0:1]), [epsc], [ss])
                A(lambda h: h.activation(out=ss[:, 8:12], in_=ss[:, 8:12], func=AF.Exp, scale=-0.5), [], [ss])
                sigmoid_act(t5[:], vr[:, 1024:2048], [vr], [t5])
                G(lambda h: h.tensor_tensor(out=t5[:], in0=t5[:], in1=vr[:, 1024:2048], op=ALU.mult), [vr], [t5])
                G(lambda h: h.tensor_tensor(out=v3(ov[:], 4), in0=v3(ov[:], 4), in1=bc_in(ss[:, 8:12], 4, 256), op=ALU.mult), [ss], [ov])
                V(lambda h: h.tensor_tensor(out=ycat[:], in0=ov[:], in1=t5[:], op=ALU.mult), [ov, t5], [ycat])
                transpose8(ycat, 0, YT[:], [YT])
                for n2 in range(2):
                    for kk in range(8):
                        MM(PS[4 + n2], PS[4 + n2][:, :], YT[:, kk, :], Wo[:, kk, n2 * 512:(n2 + 1) * 512], [YT, Wo], start=(kk == 0), stop=(kk == 7))
                resid_out([PS[4], PS[5]], xres, X3, c, "X3", xo)
                if c == NC - 1:
                    LD(exR[1], X3[T - 1:T, :], [DK("X3", c)], [DK("exR1", 0)])
                    allgather(exR[1], exRG[1], DK("exR1", 0), DK("exRG1", 0))
                gla_state(1, qkT, vr, Sb, lambda: None, Lc, tots, ex, kend, kendT)
            prep_ffn_dn(1)
            del dbs[n0:]

        final = None
        layer0()
        if dbg == "x1":
            final = (X1, "X1")
        else:
            ffn(0, X1, "X1", X2, "X2")
            if dbg == "x2":
                final = (X2, "X2")
            else:
                layer1()
                if dbg == "x3":
                    final = (X3, "X3")
                else:
                    ffn(1, X3, "X3", out, "out")
        SC(None)
        if final is not None:
            for c in range(NC):
                t = xin[c % 2]
                LD(t[:], final[0][c * 128:(c + 1) * 128, :], [DK(final[1], c)], [t])
                LD(out[c * 128:(c + 1) * 128, :], t[:], [t], [DK("out", c)])
        S.flush()
        print("instructions:", S.nins, "est_us=%.0f" % (S.est_ns / 1e3), flush=True)
    return nc


_CACHE = {}


def _core_inputs(inputs, b, half):
    f = lambda k: np.asarray(inputs[k], dtype=np.float32)
    x = f("x")[b]
    S_ = x.shape[0]
    T = S_ // 2
    rv = half == 1
    if not rv:
        xl = x[:T]
        xh = x[T:T + 2]
    else:
        xl = x[T:][::-1]
        xh = x[T - 2:T][::-1]
    w_in = f("ab_w_in")[0]
    cw = f("ab_conv_w")[0]
    dtb = f("ab_dt_bias")[0]
    alog = f("ab_a_log")[0]
    igb = f("ab_ig_bias")[0]
    fgb = f("ab_fg_bias")[0]
    c_w_in = f("c_w_in")[0]
    gw2 = f("c_gate_w2")[0]
    gb = f("c_gate_b")[0]
    fcw = f("ffn_conv_w")
    if rv:
        perm = np.arange(w_in.shape[1])
        perm[2560:2576], perm[2576:2592] = np.arange(2576, 2592), np.arange(2560, 2576)
        perm[6688:6696], perm[6696:6704] = np.arange(6696, 6704), np.arange(6688, 6696)
        perm[6704:6712], perm[6712:6720] = np.arange(6712, 6720), np.arange(6704, 6712)
        w_in = w_in[:, perm]
        cw = cw[::-1]
        dtb, alog, igb, fgb = dtb[::-1], alog[::-1], igb[::-1], fgb[::-1]
        p2 = np.arange(c_w_in.shape[1])
        p2[3072:3088], p2[3088:3104] = np.arange(3088, 3104), np.arange(3072, 3088)
        c_w_in = c_w_in[:, p2]
        gw2 = gw2[::-1]
        gb = gb[::-1]
        fcw = fcw[:, ::-1]
    c = np.ascontiguousarray
    return {
        "x": c(xl), "x_halo": c(xh), "sel": np.array([0.0, 1.0] if half == 0 else [1.0, 0.0], np.float32),
        "norm_g": c(f("norm_g")),
        "ab_w_in": c(w_in), "ab_conv_w": c(cw), "ab_conv_b": c(f("ab_conv_b")[0]),
        "ab_dt_bias": c(dtb).reshape(32), "ab_a_log": c(alog).reshape(32),
        "ab_d_skip": c(f("ab_d_skip")[0]), "ab_ssd_norm": c(f("ab_ssd_norm")[0]),
        "ab_ig_bias": c(igb).reshape(16), "ab_fg_bias": c(fgb).reshape(16),
        "ab_mlstm_norm": c(f("ab_mlstm_norm")[0]), "ab_w_out": c(f("ab_w_out")[0]),
        "c_w_in": c(c_w_in), "c_gate_w2": c(gw2), "c_gate_b": c(gb),
        "c_norm": c(f("c_norm")[0]), "c_w_out": c(f("c_w_out")[0]),
        "ffn_w_up": c(f("ffn_w_up")), "ffn_conv_w": c(fcw), "ffn_conv_b": c(f("ffn_conv_b")), "ffn_w_down": c(f("ffn_w_down")),
    }


def kernel(**inputs):
    x = np.asarray(inputs["x"])
    B, S_, _ = x.shape
    T = S_ // 2
    if T not in _CACHE:
        _CACHE[T] = build(T)
    nc = _CACHE[T]
    in_maps = [_core_inputs(inputs, b, half) for b in range(B) for half in range(2)]
    res = run_bass_kernel_spmd(nc, in_maps, core_ids=list(range(2 * B)))
    outp = np.empty((B, S_, D), np.float32)
    for b in range(B):
        outp[b, :T] = np.asarray(res.results[2 * b]["out"], dtype=np.float32)
        outp[b, T:] = np.asarray(res.results[2 * b + 1]["out"], dtype=np.float32)[::-1]
    return outp
```

```python
import heapq
import numpy as np
from contextlib import ExitStack
import concourse.bass as bass
import concourse.mybir as mybir
from concourse.bass_utils import run_bass_kernel_spmd

F32 = mybir.dt.float32
BF16 = mybir.dt.bfloat16
AF = mybir.ActivationFunctionType
ALU = mybir.AluOpType
AX = mybir.AxisListType

D = 1024
EPS = 1e-6
NEGBIG = -30000.0


class Trk:
    __slots__ = ("name", "writers", "readers", "manual")

    def __init__(self, name, manual=False):
        self.name = name
        self.writers = []
        self.readers = []
        self.manual = manual


class Tile:
    def __init__(self, t, name, manual=False):
        self.t = t
        self.k = Trk(name, manual)

    def __getitem__(self, idx):
        return self.t[idx]


class DB:
    def __init__(self, tiles):
        self.tiles = tiles
        self.i = 0

    def __getitem__(self, idx):
        return self.tiles[self.i].t[idx]

    @property
    def k(self):
        return self.tiles[self.i].k


def _trk(x):
    return x.k if isinstance(x, (Tile, DB)) else x


class _Rec:
    def __getattr__(self, name):
        return lambda *a, **k: (name, a, k)


_REC = _Rec()


def _is_ap(v):
    return hasattr(v, "ap") and hasattr(v, "tensor") and hasattr(v, "offset")


def _esize(dt):
    return 4 if dt == F32 else 2


def _free_elems(ap):
    n = 1
    for v in ap.shape[1:]:
        n *= v
    return n


class Op:
    __slots__ = ("idx", "eng", "name", "args", "kw", "cost", "is_dma", "dma_t", "preds", "succs", "npred", "finish", "ev")


class Sched:
    ENGS = ("pe", "act", "dve", "pool", "sp")
    NDMA = 24

    def __init__(self, nc, es, self_wait=True, reorder=True):
        self.nc = nc
        self.self_wait = self_wait
        self.reorder = reorder
        self.semobj = {e: es.enter_context(nc.semaphore("s_" + e)) for e in self.ENGS}
        self.cnt = {e: 0 for e in self.ENGS}
        self.seen = {e: {} for e in self.ENGS}
        for i in range(self.NDMA):
            self.semobj["d%d" % i] = es.enter_context(nc.semaphore("s_d%d" % i))
        self.dma_cnt = [0] * self.NDMA
        self.dma_next = 0
        self.h = {"pe": nc.tensor, "act": nc.scalar, "dve": nc.vector, "pool": nc.gpsimd, "sp": nc.sync}
        self.nins = {e: 0 for e in self.ENGS}
        self.ops = []
        self.touched = set()
        self.reg = {}
        self.est_ns = 0.0
        self.verbose = False
        self.prio_mode = 1
        self.seg_name = ""

    def register(self, tname, lo, hi, tile):
        self.reg.setdefault(tname, []).append((lo, hi, tile))

    def clear_reg(self, tname):
        self.reg[tname] = []

    def unregister(self, tname, tile):
        self.reg[tname] = [x for x in self.reg.get(tname, []) if x[2] is not tile]

    def _auto(self, ap):
        lst = self.reg.get(ap.name)
        if not lst:
            return ()
        es = _esize(ap.dtype)
        lo = int(ap.offset) * es
        ext = 0
        for st, n in ap.ap[1:]:
            ext += (n - 1) * abs(st)
        hi = lo + (ext + 1) * es
        return [t for (a, b, t) in lst if a < hi and lo < b]

    def _mk(self, eng, fn, reads, writes, is_dma):
        name, a, k = fn(_REC)
        rs = [_trk(t) for t in reads]
        ws = [_trk(t) for t in writes]
        out_ap = None
        nbytes = 0
        for i, v in list(enumerate(a)) + list(k.items()):
            if not _is_ap(v):
                continue
            is_out = (i == 0 and isinstance(i, int)) or i in ("out", "accum_out")
            if is_out and out_ap is None and i != "accum_out":
                out_ap = v
            sp = str(v.space)
            if sp == "DRAM":
                if is_dma:
                    nbytes = max(nbytes, _free_elems(v) * v.shape[0] * _esize(v.dtype))
                continue
            for t in self._auto(v):
                tk = _trk(t)
                if tk.manual:
                    continue
                if is_out or sp == "PSUM":
                    if tk not in ws:
                        ws.append(tk)
                elif tk not in rs:
                    rs.append(tk)
            if is_dma:
                nbytes = max(nbytes, _free_elems(v) * v.shape[0] * _esize(v.dtype))
        op = Op()
        op.idx = len(self.ops)
        op.eng, op.name, op.args, op.kw, op.is_dma = eng, name, a, k, is_dma
        n = _free_elems(out_ap) if out_ap is not None else 64
        if is_dma:
            op.cost = 60.0
            op.dma_t = nbytes / 120.0
        elif eng == "pe":
            passes = 1
            if name == "matmul" and k["lhsT"].dtype == F32:
                passes = 4
            op.cost = 70.0 + 0.5 * n * passes
            op.dma_t = 0.0
        elif eng == "act":
            op.cost = 250.0 + 0.85 * n
            op.dma_t = 0.0
        elif eng == "dve":
            if name == "scalar_tensor_tensor":
                op.cost = 150.0 + 2.0 * n
            elif name == "reciprocal":
                op.cost = 100.0 + 6.0 * n
            elif name == "tensor_tensor_scan":
                op.cost = 100.0 + 2.0 * n
            else:
                op.cost = 90.0 + 1.1 * n
            op.dma_t = 0.0
        else:
            op.cost = 200.0 + 1.9 * n
            op.dma_t = 0.0
        preds = set()
        for t in rs:
            preds.update(t.writers)
        for t in ws:
            preds.update(t.writers)
            preds.update(t.readers)
        preds.discard(op.idx)
        op.preds = preds
        op.succs = []
        for t in rs:
            t.readers.append(op.idx)
            self.touched.add(t)
        for t in ws:
            t.writers = [op.idx]
            t.readers = []
            self.touched.add(t)
        self.ops.append(op)

    def op(self, eng, fn, reads=(), writes=()):
        self._mk(eng, fn, reads, writes, False)

    def dma(self, fn, reads=(), writes=(), eng="sp"):
        self._mk(eng, fn, reads, writes, True)

    def flush(self):
        ops = self.ops
        if not ops:
            return
        n = len(ops)
        order = {e: [] for e in self.ENGS}
        if not self.reorder:
            for op in ops:
                order[op.eng].append(op)
        else:
            for op in ops:
                op.npred = len(op.preds)
                for p in op.preds:
                    ops[p].succs.append(op.idx)
            ready = {e: [] for e in self.ENGS}
            bl = [0.0] * n
            for op in reversed(ops):
                m = 0.0
                for s in op.succs:
                    if bl[s] > m:
                        m = bl[s]
                bl[op.idx] = m + (op.cost if not op.is_dma else 2000.0 + op.dma_t)
            if self.prio_mode == 0:
                key = list(range(n))
            else:
                order_ix = sorted(range(n), key=lambda i: (-bl[i], i))
                key = [0] * n
                for r_, i in enumerate(order_ix):
                    key[i] = r_
            inv = {}
            for i in range(n):
                inv[key[i]] = i
            for op in ops:
                if op.npred == 0:
                    heapq.heappush(ready[op.eng], key[op.idx])
            busy = {e: 0.0 for e in self.ENGS}
            crit = [-1] * n
            blk = [-1] * n
            stt = [0.0] * n
            rdy = {}
            stall = {}
            ev = []
            now = 0.0
            dma_free = 0.0
            done = 0
            while done < n:
                for e in self.ENGS:
                    if busy[e] <= now and ready[e]:
                        i = inv[heapq.heappop(ready[e])]
                        op = ops[i]
                        if self.verbose:
                            rt = rdy.get(i, 0.0)
                            if order[e] and rt < now - 1e-9:
                                blk[i] = order[e][-1].idx
                            else:
                                blk[i] = crit[i]
                            stt[i] = now
                        order[e].append(op)
                        if self.verbose and e == "pe":
                            st_ = now - max(busy[e], 0.0)
                            if st_ > 0 and crit[i] >= 0:
                                cp = ops[crit[i]]
                                kk = (cp.eng, cp.name, getattr(cp, "tag", ""))
                                stall[kk] = stall.get(kk, 0.0) + st_
                        busy[e] = now + op.cost
                        if op.is_dma:
                            st = max(now + op.cost, dma_free)
                            dma_free = st + op.dma_t
                            fin = dma_free + 2000.0
                        else:
                            fin = busy[e]
                        heapq.heappush(ev, (fin, 1, i))
                        heapq.heappush(ev, (busy[e], 0, -1))
                if not ev:
                    raise RuntimeError("scheduler deadlock")
                t, kind, i = heapq.heappop(ev)
                now = max(now, t)
                if kind == 1:
                    done += 1
                    for s in ops[i].succs:
                        so = ops[s]
                        so.npred -= 1
                        crit[s] = i
                        if so.npred == 0:
                            rdy[s] = now
                            heapq.heappush(ready[so.eng], key[s])
            self.est_ns += now
            if self.verbose:
                bs = {e: sum(o.cost for o in order[e]) / 1e3 for e in self.ENGS}
                last = max(range(n), key=lambda i: stt[i] + ops[i].cost)
                cp = {}
                i = last
                guard = 0
                while i >= 0 and guard < 10 * n:
                    guard += 1
                    o_ = ops[i]
                    kk = (o_.eng, o_.name, "dma" if o_.is_dma else "")
                    dur = (2000.0 + o_.dma_t) if o_.is_dma else o_.cost
                    cp[kk] = cp.get(kk, 0.0) + dur
                    i = blk[i]
                for kk, vv in sorted(cp.items(), key=lambda x: -x[1])[:10]:
                    print("      critpath %-40s %.0f us" % (str(kk), vv / 1e3))
                for kk, vv in sorted(stall.items(), key=lambda x: -x[1])[:0]:
                    print("      pe stall %-40s %.0f us" % (str(kk), vv / 1e3))
                print("  segment %-8s n=%6d est_us=%8.1f busy_us: %s" % (self.seg_name, n, now / 1e3, " ".join("%s=%.0f" % (e, bs[e]) for e in self.ENGS)), flush=True)
        for e in self.ENGS:
            for op in order[e]:
                if op.is_dma:
                    i = self.dma_next
                    self.dma_next = (i + 1) % self.NDMA
                    key = "d%d" % i
                    prev = self.dma_cnt[i]
                    self.dma_cnt[i] += 16
                    op.ev = (key, self.dma_cnt[i], prev)
                else:
                    self.cnt[e] += 1
                    op.ev = (e, self.cnt[e], 0)
        for e in self.ENGS:
            h = self.h[e]
            seen = self.seen[e]
            for op in order[e]:
                need = {}
                for p in op.preds:
                    k, v, _ = ops[p].ev
                    if k == e and (e == "pe" or not self.self_wait):
                        continue
                    if need.get(k, 0) < v:
                        need[k] = v
                if op.is_dma and op.ev[2]:
                    k, _, prev = op.ev
                    if need.get(k, 0) < prev:
                        need[k] = prev
                for k, v in need.items():
                    if seen.get(k, 0) >= v:
                        continue
                    seen[k] = v
                    h.wait_ge(self.semobj[k], v)
                ins = getattr(h, op.name)(*op.args, **op.kw)
                ins.then_inc(self.semobj[op.ev[0]], 16 if op.is_dma else 1)
                self.nins[e] += 1
        self.ops = []
        for t in self.touched:
            t.writers = []
            t.readers = []
        self.touched = set()
        self._full_wait()

    def _full_wait(self):
        cur = dict(self.cnt)
        for i in range(self.NDMA):
            cur["d%d" % i] = self.dma_cnt[i]
        for e in self.ENGS:
            for k, v in cur.items():
                if k == e or v == 0 or self.seen[e].get(k, 0) >= v:
                    continue
                self.seen[e][k] = v
                self.h[e].wait_ge(self.semobj[k], v)

    def barrier(self):
        self.flush()


ARENA_N = 85000


def build(T, dbg=None, self_wait=True, reorder=True, verbose=False):
    NC = T // 128
    nc = bass.Bass("TRN2", target_bir_lowering=False)
    es = ExitStack()

    def dram(name, shape, dt, kind="Internal"):
        return nc.dram_tensor(name, shape, dt, kind=kind).ap()

    x_in = dram("x", [T, D], F32, "ExternalInput")
    norm_g = dram("norm_g", [2, 4, D], F32, "ExternalInput")
    ab_w_in = dram("ab_w_in", [D, 6720], F32, "ExternalInput")
    ab_conv_w = dram("ab_conv_w", [5, 1536], F32, "ExternalInput")
    ab_conv_b = dram("ab_conv_b", [1536], F32, "ExternalInput")
    ab_dt_bias = dram("ab_dt_bias", [32], F32, "ExternalInput")
    ab_a_log = dram("ab_a_log", [32], F32, "ExternalInput")
    ab_d_skip = dram("ab_d_skip", [16], F32, "ExternalInput")
    ab_ssd_norm = dram("ab_ssd_norm", [1024], F32, "ExternalInput")
    ab_ig_bias = dram("ab_ig_bias", [16], F32, "ExternalInput")
    ab_fg_bias = dram("ab_fg_bias", [16], F32, "ExternalInput")
    ab_mlstm_norm = dram("ab_mlstm_norm", [1024], F32, "ExternalInput")
    ab_w_out = dram("ab_w_out", [2048, D], F32, "ExternalInput")
    c_w_in = dram("c_w_in", [D, 3104], F32, "ExternalInput")
    c_gate_w2 = dram("c_gate_w2", [2, 16, 512], F32, "ExternalInput")
    c_gate_b = dram("c_gate_b", [2, 512], F32, "ExternalInput")
    c_norm = dram("c_norm", [1024], F32, "ExternalInput")
    c_w_out = dram("c_w_out", [D, D], F32, "ExternalInput")
    ffn_w_up = dram("ffn_w_up", [2, D, 5632], F32, "ExternalInput")
    ffn_conv_w = dram("ffn_conv_w", [2, 3, 5632], F32, "ExternalInput")
    ffn_conv_b = dram("ffn_conv_b", [2, 5632], F32, "ExternalInput")
    ffn_w_down = dram("ffn_w_down", [2, 2816, D], F32, "ExternalInput")
    x_halo = dram("x_halo", [2, D], F32, "ExternalInput")
    sel_in = dram("sel", [2], F32, "ExternalInput")
    out = dram("out", [T, D], F32, "ExternalOutput")
    exS = dram("exS", [128, 2056], F32)
    exG = dram("exG", [256, 2056], F32)
    exS1 = dram("exS1", [128, 1024], F32)
    exG1 = dram("exG1", [256, 1024], F32)
    exR = [dram("exR%d" % i, [1, D], F32) for i in range(2)]
    exRG = [dram("exRG%d" % i, [2, D], F32) for i in range(2)]
    PAIRS = [[0, 1], [2, 3], [4, 5], [6, 7]]
    WB_L0P2 = dram("WB_L0P2", [128, 32 * 1024], BF16)
    WB_up = [dram("WB_up%d" % i, [128, 8 * 5632], BF16) for i in range(2)]
    WB_dn = [dram("WB_dn%d" % i, [128, 22 * 1024], BF16) for i in range(2)]
    WB_g = dram("WB_g", [128, 8 * 3104], BF16)
    WB_o1 = dram("WB_o1", [128, 8 * 1024], BF16)
    WK = Trk("weights")

    PT0 = dram("PT0", [T, 3072], BF16)
    XSB = dram("XSB", [T, 1280], BF16)
    BCT = dram("BCT", [NC, 128, 512], BF16)
    GT = dram("GT", [T, 64], F32)
    HS = dram("HS", [NC, 128, 1024], BF16)
    HM = dram("HM", [NC, 128, 1032], BF16)
    X1 = dram("X1", [T, D], F32)
    X2 = dram("X2", [T, D], F32)
    QKT = dram("QKT", [NC, 128, 1024], BF16)
    LTd = dram("LTd", [NC, 128, 1024], F32)
    VR = dram("VR", [T, 2048], BF16)
    SG = dram("SG", [NC, 128, 1024], BF16)
    X3 = dram("X3", [T, D], F32)
    dk = {}

    def DK(name, c):
        key = (name, c)
        if key not in dk:
            dk[key] = Trk("%s%d" % key)
        return dk[key]

    with es:
        S = Sched(nc, es, self_wait=self_wait, reorder=reorder)
        S.verbose = verbose
        PS = [Tile(es.enter_context(nc.psum_tensor("ps%d" % i, [128, 512], F32)), "ps%d" % i) for i in range(8)]
        for i in range(8):
            S.register("ps%d" % i, 0, 1 << 30, PS[i])

        def sb(name, shape, dt):
            t = Tile(es.enter_context(nc.sbuf_tensor(name, shape, dt)), name)
            S.register(name, 0, 1 << 30, t)
            return t

        arena_t = es.enter_context(nc.sbuf_tensor("arena", [128, ARENA_N], BF16))
        aoff = [0]

        def reset_arena():
            if verbose:
                print("  arena used", aoff[0] * 2, "of", ARENA_N * 2)
            S.barrier()
            S.clear_reg("arena")
            aoff[0] = 0

        dbs = []

        def ar2(name, shape, dt, nbuf=2):
            d = DB([ar("%s_%d" % (name, i), shape, dt) for i in range(nbuf)])
            dbs.append(d)
            return d

        def setpar(c):
            for d in dbs:
                d.i = c % len(d.tiles)

        def ar(name, shape, dt, manual=False):
            n = 1
            for v in shape[1:]:
                n *= v
            nb = n * 2 if dt == F32 else n
            if aoff[0] % 2:
                aoff[0] += 1
            o = aoff[0]
            aoff[0] += nb
            assert aoff[0] <= ARENA_N, (name, aoff[0])
            ap = arena_t[0:shape[0], o:o + nb]
            if dt == F32:
                ap = ap.bitcast(F32)
            if len(shape) == 3:
                ap = ap.rearrange("p (a b) -> p a b", a=shape[1])
            elif len(shape) == 4:
                ap = ap.rearrange("p (a b c) -> p a b c", a=shape[1], b=shape[2])
            t = Tile(ap, name, manual)
            S.register("arena", o * 2, (o + nb) * 2, t)
            return t

        def V(fn, r=(), w=()):
            S.op("dve", fn, r, w)

        def A(fn, r=(), w=()):
            S.op("act", fn, r, w)

        def G(fn, r=(), w=()):
            S.op("pool", fn, r, w)

        def P(fn, r=(), w=()):
            S.op("pe", fn, r, w)

        def MM(bank, o, lhsT, rhs, r, start=True, stop=True):
            P(lambda h: h.matmul(o, lhsT=lhsT, rhs=rhs, start=start, stop=stop), r, [bank])

        def TR(bank, o, in_, ident, r):
            P(lambda h: h.transpose(out=o, in_=in_, identity=ident), r, [bank])

        def LD(o, i, r, w):
            S.dma(lambda h: h.dma_start(out=o, in_=i), r, w)

        def LDS(o, i, r, w):
            S.dma(lambda h: h.dma_start(out=o, in_=i, allow_slow_non_contiguous=True), r, w)

        def v3(ap, a):
            return ap.rearrange("p (a b) -> p a b", a=a)

        def bc_in(ap, a, b):
            return ap.unsqueeze(2).to_broadcast([128, a, b])

        def bc_mid(ap, a, b):
            return ap.unsqueeze(1).to_broadcast([128, a, b])

        block = es.enter_context(nc.Block())
        cur_scope = [None]

        def SC(name):
            S.seg_name = name or ""
            return
            if cur_scope[0] is not None:
                cur_scope[0].__exit__(None, None, None)
                cur_scope[0] = None
            if name is not None:
                cm = nc.named_scope(name)
                cm.__enter__()
                cur_scope[0] = cm

        onesf = sb("onesf", [128, 128], F32)
        zerof = sb("zerof", [128, 128], F32)
        identf = sb("identf", [128, 128], F32)
        identb = sb("identb", [128, 128], BF16)
        Uf = sb("Uf", [128, 128], F32)
        Ub = sb("Ub", [128, 128], F32)
        maskb16 = sb("maskb16", [128, 2, 128], BF16)
        NEG = sb("NEG", [128, 2, 128], BF16)
        negf = sb("negf", [128, 2, 128], F32)
        onesb = sb("onesb", [128, 8], BF16)
        epsc = sb("epsc", [128, 1], F32)
        G(lambda h: h.memset(onesf[:], 1.0), w=[onesf])
        G(lambda h: h.memset(zerof[:], 0.0), w=[zerof])
        G(lambda h: h.memset(onesb[:], 1.0), w=[onesb])
        G(lambda h: h.memset(epsc[:], EPS), w=[epsc])
        G(lambda h: h.affine_select(out=identf[:], in_=onesf[:], pattern=[[1, 128]], compare_op=ALU.is_equal, fill=0.0, base=0, channel_multiplier=-1), [onesf], [identf])
        G(lambda h: h.tensor_copy(out=identb[:], in_=identf[:]), [identf], [identb])
        G(lambda h: h.affine_select(out=Uf[:], in_=onesf[:], pattern=[[1, 128]], compare_op=ALU.is_ge, fill=0.0, base=0, channel_multiplier=-1), [onesf], [Uf])
        G(lambda h: h.affine_select(out=Ub[:], in_=onesf[:], pattern=[[-1, 128]], compare_op=ALU.is_ge, fill=0.0, base=0, channel_multiplier=1), [onesf], [Ub])
        G(lambda h: h.tensor_copy(out=maskb16[:, 0, :], in_=Uf[:]), [Uf], [maskb16])
        G(lambda h: h.tensor_copy(out=maskb16[:, 1, :], in_=Ub[:]), [Ub], [maskb16])
        G(lambda h: h.affine_select(out=negf[:, 0, :], in_=zerof[:], pattern=[[1, 128]], compare_op=ALU.is_ge, fill=NEGBIG, base=0, channel_multiplier=-1), [zerof], [negf])
        G(lambda h: h.affine_select(out=negf[:, 1, :], in_=zerof[:], pattern=[[-1, 128]], compare_op=ALU.is_ge, fill=NEGBIG, base=0, channel_multiplier=1), [zerof], [negf])
        G(lambda h: h.tensor_copy(out=NEG[:], in_=negf[:]), [negf], [NEG])

        stg = [sb("stg%d" % i, [128, 1024], F32) for i in range(2)]
        stg_ctr = [0]
        gcol = sb("gcol", [128, 8, 8], F32)
        gres = sb("gres", [128, 1024], F32)
        xin = [sb("xin%d" % i, [128, 1024], F32) for i in range(2)]
        sqj = sb("sqj", [128, 1024], BF16)
        ss = sb("ss", [128, 16], F32)
        hb = sb("hb", [128, 1024], BF16)
        dumm = sb("dumm", [128, 8], F32)
        selt = sb("selt", [128, 2], F32)
        LD(selt[:], sel_in.partition_broadcast(128), [WK], [selt])

        def allgather(src_d, dst_d, rk, wk):
            for rep_ in range(2):
                G(lambda h: h.collective_compute("AllGather", ALU.bypass, replica_groups=PAIRS, ins=[src_d.opt()], outs=[dst_d.opt()]), [rk], [wk])

        def select2(out_ap, a0, a1, r, w):
            V(lambda h: h.tensor_scalar(out=out_ap, in0=a0, scalar1=selt[:, 0:1], scalar2=None, op0=ALU.mult), r, w)
            V(lambda h: h.scalar_tensor_tensor(out=out_ap, in0=a1, scalar=selt[:, 1:2], in1=out_ap, op0=ALU.mult, op1=ALU.add), r, w)

        def halo_cands(src_rows_ap, src_trk, nrows, dst_tile):
            xi = xin[0]
            V(lambda h: h.memset(xi[:], 0.0), [], [xi])
            LD(xi[0:nrows, :], src_rows_ap, [src_trk], [xi])
            norm_chunk(None, 0, None, xi, load=False)
            pb = PS[3][:].bitcast(BF16)
            for k in range(8):
                TR(PS[3], pb[:, k * 128:(k + 1) * 128], hb[:, k * 128:(k + 1) * 128], identb[:], [hb, identb])
            V(lambda h: h.tensor_copy(out=dst_tile[:], in_=v3(pb, 8)[:, :, 0:nrows]), [], [PS[3], dst_tile])
        WG = Trk("wguard")

        gvecs = [norm_g[0, 0], norm_g[0, 2], norm_g[1, 0], norm_g[1, 2], ab_ssd_norm, ab_mlstm_norm, c_norm]
        for i, gv in enumerate(gvecs):
            LDS(gcol[:, i, :], gv.rearrange("(k p) -> p k", p=128), [WK], [gcol])

        class WLoad:
            def __init__(self, wt):
                self.wt = wt
                self.tmps = []
                G(lambda h: h.memset(dumm[:, 0:1], 0.0), [], [wt, WG, dumm])

            def load(self, dst_fn, src, KC, ncols, gc, factor=1.0):
                for kc in range(KC):
                    for c0 in range(0, ncols, 1024):
                        w = min(1024, ncols - c0)
                        st = stg[stg_ctr[0] % 2]
                        stg_ctr[0] += 1
                        LD(st[:, 0:w], src[kc * 128:(kc + 1) * 128, c0:c0 + w], [WK], [st])
                        o = dst_fn(kc, c0, w)
                        sc1 = gc(kc) if gc is not None else 1.0
                        tk = Trk("wtmp")
                        self.tmps.append(tk)
                        if factor == 1.0 and stg_ctr[0] % 2 == 0:
                            A(lambda h, o=o, st=st, w=w, sc1=sc1: h.activation(out=o, in_=st[:, 0:w], func=AF.Copy, scale=sc1), [st, gcol, WG], [tk])
                        else:
                            V(lambda h, o=o, st=st, w=w, sc1=sc1: h.tensor_scalar(out=o, in0=st[:, 0:w], scalar1=sc1, scalar2=float(factor), op0=ALU.mult, op1=ALU.mult), [st, gcol, WG], [tk])

            def done(self):
                G(lambda h: h.memset(dumm[:, 1:2], 0.0), self.tmps, [self.wt, dumm])

        cvt = [sb("cvt%d" % i, [128, 1024], BF16) for i in range(2)]
        prep_ctr = [0]
        gcolq = sb("gcolq", [128, 8], F32)
        V(lambda h: h.tensor_scalar(out=gcolq[:], in0=gcol[:, 2, :], scalar1=128 ** -0.5, scalar2=None, op0=ALU.mult), [gcol], [gcolq])

        def prep(dst_d, dst_off_fn, src, KC, ncols, gc_fn):
            for kc in range(KC):
                for c0 in range(0, ncols, 1024):
                    w = min(1024, ncols - c0)
                    i = prep_ctr[0]
                    prep_ctr[0] += 1
                    st, cv = stg[i % 2], cvt[i % 2]
                    LD(st[:, 0:w], src[kc * 128:(kc + 1) * 128, c0:c0 + w], [WK], [st])
                    if gc_fn is not None:
                        gc = gc_fn(kc)
                        G(lambda h, st=st, cv=cv, w=w, gc=gc: h.tensor_tensor(out=cv[:, 0:w], in0=st[:, 0:w], in1=gc.to_broadcast([128, w]), op=ALU.mult), [st, gcol, gcolq], [cv])
                    else:
                        G(lambda h, st=st, cv=cv, w=w: h.tensor_copy(out=cv[:, 0:w], in_=st[:, 0:w]), [st], [cv])
                    o = dst_off_fn(kc, c0)
                    LD(dst_d[:, o:o + w], cv[:, 0:w], [cv], [Trk("wbst")])

        def prep_ffn_up(li):
            prep(WB_up[li], lambda kc, c0: kc * 5632 + c0, ffn_w_up[li], 8, 5632, lambda kc: gcol[:, 1 + 2 * li, kc:kc + 1])

        def prep_ffn_dn(li):
            prep(WB_dn[li], lambda kc, c0: kc * 1024 + c0, ffn_w_down[li], 22, 1024, None)

        def sigmoid_act(out_ap, in_ap, r, w, scale=1.0):
            A(lambda h: h.activation(out=out_ap, in_=in_ap, func=AF.Exp, scale=-float(scale)), r, w)
            A(lambda h: h.activation(out=out_ap, in_=out_ap, func=AF.Ln, bias=1.0), [], w)
            A(lambda h: h.activation(out=out_ap, in_=out_ap, func=AF.Exp, scale=-1.0), [], w)

        def rstd_from(src_ap, n, col, junk_ap, r, w_extra=()):
            V(lambda h: h.memset(ss[:, col:col + 1], 0.0), [], [ss])
            A(lambda h: h.activation(out=junk_ap, in_=src_ap, func=AF.Square, accum_out=ss[:, col:col + 1]), r, [sqj, ss] + list(w_extra))
            A(lambda h: h.activation(out=ss[:, col:col + 1], in_=ss[:, col:col + 1], func=AF.Ln, scale=1.0 / n, bias=epsc[:, 0:1]), [epsc], [ss])
            A(lambda h: h.activation(out=ss[:, col:col + 1], in_=ss[:, col:col + 1], func=AF.Exp, scale=-0.5), [], [ss])

        def norm_chunk(src_dram, c, src_trk, xi, load=True):
            if load:
                LD(xi[:], src_dram[c * 128:(c + 1) * 128, :], [src_trk], [xi])
            rstd_from(xi[:], 1024, 0, sqj[:], [xi])
            V(lambda h: h.tensor_scalar(out=hb[:], in0=xi[:], scalar1=ss[:, 0:1], scalar2=None, op0=ALU.mult), [xi, ss], [hb])

        def transpose8(src_tile, src_c0, dst_ap, dst_trk, eng="dve"):
            pb = PS[3][:].bitcast(BF16)
            for k in range(8):
                TR(PS[3], pb[:, k * 128:(k + 1) * 128], src_tile[:, src_c0 + k * 128:src_c0 + (k + 1) * 128], identb[:], [src_tile, identb])
            if eng == "dve":
                V(lambda h: h.tensor_copy(out=dst_ap, in_=v3(pb, 8)), [], [PS[3]] + dst_trk)
            else:
                A(lambda h: h.activation(out=dst_ap, in_=v3(pb, 8), func=AF.Copy), [], [PS[3]] + dst_trk)

        def resid_out(banks, xres, dst_dram, c, dst_name, xo):
            V(lambda h: h.memset(ss[:, 3:5], 0.0), [], [ss])
            for n2 in range(2):
                A(lambda h, n2=n2: h.activation(out=sqj[:, n2 * 512:(n2 + 1) * 512], in_=banks[n2][:, :], func=AF.Square, accum_out=ss[:, 3 + n2:4 + n2]), [], [banks[n2], sqj, ss])
            V(lambda h: h.tensor_tensor(out=ss[:, 3:4], in0=ss[:, 3:4], in1=ss[:, 4:5], op=ALU.add), [], [ss])
            A(lambda h: h.activation(out=ss[:, 3:4], in_=ss[:, 3:4], func=AF.Ln, scale=1.0 / 1024, bias=epsc[:, 0:1]), [epsc], [ss])
            A(lambda h: h.activation(out=ss[:, 3:4], in_=ss[:, 3:4], func=AF.Exp, scale=-0.5), [], [ss])
            for n2 in range(2):
                V(lambda h, n2=n2: h.scalar_tensor_tensor(out=xo[:, n2 * 512:(n2 + 1) * 512], in0=banks[n2][:, :], scalar=ss[:, 3:4], in1=gres[:, n2 * 512:(n2 + 1) * 512], op0=ALU.mult, op1=ALU.mult), [ss, gres], [banks[n2], xo])
            G(lambda h: h.tensor_tensor(out=xo[:], in0=xo[:], in1=xres[:], op=ALU.add), [xres], [xo])
            LD(dst_dram[c * 128:(c + 1) * 128, :], xo[:], [xo], [DK(dst_name, c)])

        def layer0():
            reset_arena()
            SC("L0P1_w")
            WT1 = ar("W1", [128, 8, 4672], BF16, manual=True)
            hTr = ar("hTr", [128, 8, 516], BF16, manual=True)
            hslot = [Trk("hslot%d" % i) for i in range(4)]
            hmirL, hmirR = Trk("hmirL"), Trk("hmirR")
            dtb = ar("dtb", [128, 32], F32)
            negA = ar("negA", [128, 32], F32)
            igfg = ar("igfg", [128, 32], F32)
            dsk = ar("dsk", [128, 16], F32)
            cw = ar("cw", [128, 12, 5], F32)
            cbias = ar("cbias", [128, 12], F32)
            Hs = ar("Hs", [128, 1024], F32)
            Cm = ar("Cm", [128, 1032], F32)
            n0p1 = len(dbs)
            hsb16 = ar2("hsb16", [128, 1024], BF16)
            cmb16 = ar2("cmb16", [128, 1032], BF16)
            pt = [ar("pt%d" % i, [128, 3072], BF16) for i in range(2)]
            gt = [ar("gt%d" % i, [128, 64], F32) for i in range(2)]
            acc = [ar("acc%d" % i, [128, 12, 128], F32) for i in range(2)]
            sg12s = [ar("sg12_0", [128, 12, 128], F32)] * 2
            xbcTs = [ar("xbcT%d" % i, [128, 12, 128], BF16) for i in range(2)]
            xsb = [ar("xsb%d" % i, [128, 1280], BF16) for i in range(2)]
            gw = ar2("gw", [128, 64], F32)
            G48 = ar2("G48", [128, 48], F32)
            cst = ar2("cst", [128, 96], F32)
            scw = ar2("scw", [128, 48], F32)
            xe = ar2("xe", [128, 1024], BF16)
            kw = ar2("kw", [128, 1024], BF16)
            dts = ar2("dts", [128, 32], F32)
            igs = ar2("igs", [128, 16], F32)
            halo0 = ar("halo0", [128, 8, 2], BF16)
            xring = [ar("xring%d" % i, [128, 1024], F32) for i in range(3)]
            if verbose:
                print("  L0P1 arena", aoff[0] * 2)
            p1_end = aoff[0]

            def load_params():
                LD(dtb[:], ab_dt_bias.partition_broadcast(128), [WK], [dtb])
                LD(negA[:], ab_a_log.partition_broadcast(128), [WK], [negA])
                LD(igfg[:, 0:16], ab_ig_bias.partition_broadcast(128), [WK], [igfg])
                LD(igfg[:, 16:32], ab_fg_bias.partition_broadcast(128), [WK], [igfg])
                LD(dsk[:], ab_d_skip.partition_broadcast(128), [WK], [dsk])
                for j in range(5):
                    LDS(cw[:, :, j], ab_conv_w[j].rearrange("(i p) -> p i", p=128), [WK], [cw])
                LDS(cbias[:], ab_conv_b.rearrange("(i p) -> p i", p=128), [WK], [cbias])
                A(lambda h: h.activation(out=negA[:], in_=negA[:], func=AF.Exp), [], [negA])
                V(lambda h: h.tensor_scalar(out=negA[:], in0=negA[:], scalar1=-1.0, scalar2=None, op0=ALU.mult), [], [negA])
            load_params()
            wl = WLoad(WT1)
            segs = [(2592, 1024, 0, 1.0), (3616, 1024, 1024, 128 ** -0.5), (4640, 1024, 2048, 1.0),
                    (2560, 32, 3072, 1.0), (6688, 32, 3104, 1.0), (1024, 1536, 3136, 1.0)]
            for (s0, n, d0, fac) in segs:
                wl.load(lambda kc, c0, w, d0=d0: WT1[:, kc, d0 + c0:d0 + c0 + w], ab_w_in[:, s0:s0 + n], 8, n, lambda kc: gcol[:, 0, kc:kc + 1], fac)
            wl.done()
            V(lambda h: h.memset(Hs[:], 0.0), [], [Hs])
            V(lambda h: h.memset(Cm[:], 0.0), [], [Cm])
            V(lambda h: h.memset(hTr[:], 0.0), [], hslot + [hmirL, hmirR])
            halo_cands(x_halo, WK, 2, halo0)

            def gates(gtile, dirs):
                for d in dirs:
                    c0 = d * 16
                    V(lambda h, c0=c0: h.tensor_tensor(out=gw[:, c0:c0 + 16], in0=gtile[:, c0:c0 + 16], in1=dtb[:, c0:c0 + 16], op=ALU.add), [gtile, dtb], [gw])
                    A(lambda h, c0=c0: h.activation(out=gw[:, c0:c0 + 16], in_=gw[:, c0:c0 + 16], func=AF.Exp), [], [gw])
                    A(lambda h, c0=c0: h.activation(out=dts[:, c0:c0 + 16], in_=gw[:, c0:c0 + 16], func=AF.Ln, bias=1.0), [gw], [dts])
                    V(lambda h, c0=c0: h.tensor_tensor(out=G48[:, c0:c0 + 16], in0=dts[:, c0:c0 + 16], in1=negA[:, c0:c0 + 16], op=ALU.mult), [dts, negA], [G48])
                    i0 = 32 + d * 8
                    f0 = 48 + d * 8
                    V(lambda h, i0=i0, d=d: h.tensor_tensor(out=igs[:, d * 8:d * 8 + 8], in0=gtile[:, i0:i0 + 8], in1=igfg[:, d * 8:d * 8 + 8], op=ALU.add), [gtile, igfg], [igs])
                    V(lambda h, f0=f0, d=d: h.tensor_tensor(out=gw[:, f0:f0 + 8], in0=gtile[:, f0:f0 + 8], in1=igfg[:, 16 + d * 8:24 + d * 8], op=ALU.add), [gtile, igfg], [gw])
                    A(lambda h, f0=f0: h.activation(out=gw[:, f0:f0 + 8], in_=gw[:, f0:f0 + 8], func=AF.Exp, scale=-1.0), [], [gw])
                    A(lambda h, f0=f0: h.activation(out=gw[:, f0:f0 + 8], in_=gw[:, f0:f0 + 8], func=AF.Ln, bias=1.0), [], [gw])
                    V(lambda h, f0=f0, d=d: h.tensor_scalar(out=G48[:, 32 + d * 8:40 + d * 8], in0=gw[:, f0:f0 + 8], scalar1=-1.0, scalar2=None, op0=ALU.mult), [gw], [G48])

            def cumsums(dirs, bank):
                for d in dirs:
                    U = Uf if d == 0 else Ub
                    MM(bank, bank[:, d * 16:d * 16 + 16], U[:], G48[:, d * 16:d * 16 + 16], [U, G48])
                    MM(bank, bank[:, 32 + d * 8:40 + d * 8], U[:], G48[:, 32 + d * 8:40 + d * 8], [U, G48])
                MM(bank, bank[:, 64:112], onesf[:], G48[:], [onesf, G48])
                V(lambda h: h.tensor_copy(out=cst[:, 0:48], in_=bank[:, 0:48]), [], [bank, cst])
                V(lambda h: h.tensor_copy(out=cst[:, 48:96], in_=bank[:, 64:112]), [], [bank, cst])

            def state_step(d, xs_tile, pt_tile, Hst, Cst, store_fn):
                o16, o8 = d * 16, 32 + d * 8
                k_ap = pt_tile[:, 1024:2048]
                V(lambda h: h.tensor_tensor(out=scw[:, 0:16], in0=cst[:, 48 + o16:64 + o16], in1=cst[:, o16:o16 + 16], op=ALU.subtract), [cst], [scw])
                A(lambda h: h.activation(out=scw[:, 0:16], in_=scw[:, 0:16], func=AF.Exp), [], [scw])
                V(lambda h: h.tensor_tensor(out=scw[:, 0:16], in0=scw[:, 0:16], in1=dts[:, o16:o16 + 16], op=ALU.mult), [dts], [scw])
                V(lambda h: h.tensor_tensor(out=v3(xe[:], 16), in0=v3(xs_tile[:, 0:1024], 16), in1=bc_in(scw[:, 0:16], 16, 64), op=ALU.mult), [xs_tile, scw], [xe])
                for g in range(2):
                    MM(PS[4 + g], PS[4 + g][:, :], xs_tile[:, 1024 + g * 128:1152 + g * 128], xe[:, g * 512:(g + 1) * 512], [xs_tile, xe])
                V(lambda h: h.tensor_tensor(out=scw[:, 16:24], in0=cst[:, 48 + o8:56 + o8], in1=cst[:, o8:o8 + 8], op=ALU.subtract), [cst], [scw])
                V(lambda h: h.tensor_tensor(out=scw[:, 16:24], in0=scw[:, 16:24], in1=igs[:, d * 8:d * 8 + 8], op=ALU.add), [igs], [scw])
                A(lambda h: h.activation(out=scw[:, 16:24], in_=scw[:, 16:24], func=AF.Exp), [], [scw])
                V(lambda h: h.tensor_tensor(out=v3(kw[:], 8), in0=v3(k_ap, 8), in1=bc_in(scw[:, 16:24], 8, 128), op=ALU.mult), [pt_tile, scw], [kw])
                for hh in range(8):
                    bk = PS[6 + hh // 4]
                    MM(bk, bk[:, (hh % 4) * 128:(hh % 4 + 1) * 128], kw[:, hh * 128:(hh + 1) * 128], pt_tile[:, 2048 + hh * 128:2048 + (hh + 1) * 128], [kw, pt_tile])
                for hh in range(8):
                    MM(PS[2], PS[2][:, 128 + hh:129 + hh], kw[:, hh * 128:(hh + 1) * 128], onesb[:, 0:1], [kw, onesb])
                store_fn()
                A(lambda h: h.activation(out=scw[:, 24:40], in_=cst[:, 48 + o16:64 + o16], func=AF.Exp), [cst], [scw])
                A(lambda h: h.activation(out=scw[:, 40:48], in_=cst[:, 48 + o8:56 + o8], func=AF.Exp), [cst], [scw])
                V(lambda h: h.tensor_tensor(out=v3(Hst[:], 16), in0=v3(Hst[:], 16), in1=bc_in(scw[:, 24:40], 16, 64), op=ALU.mult), [scw], [Hst])
                for g in range(2):
                    V(lambda h, g=g: h.tensor_tensor(out=Hst[:, g * 512:(g + 1) * 512], in0=Hst[:, g * 512:(g + 1) * 512], in1=PS[4 + g][:, :], op=ALU.add), [], [Hst, PS[4 + g]])
                V(lambda h: h.tensor_tensor(out=v3(Cst[:, 0:1024], 8), in0=v3(Cst[:, 0:1024], 8), in1=bc_in(scw[:, 40:48], 8, 128), op=ALU.mult), [scw], [Cst])
                for q in range(2):
                    V(lambda h, q=q: h.tensor_tensor(out=Cst[:, q * 512:(q + 1) * 512], in0=Cst[:, q * 512:(q + 1) * 512], in1=PS[6 + q][:, :], op=ALU.add), [], [Cst, PS[6 + q]])
                V(lambda h: h.tensor_tensor(out=Cst[:, 1024:1032], in0=Cst[:, 1024:1032], in1=scw[:, 40:48], op=ALU.mult), [scw], [Cst])
                V(lambda h: h.tensor_tensor(out=Cst[:, 1024:1032], in0=Cst[:, 1024:1032], in1=PS[2][:, 128:136], op=ALU.add), [], [Cst, PS[2]])

            xk = Trk("x_in")

            def norm_to_ring(c):
                s = c % 4
                norm_chunk(x_in, c, xk, xring[c % 3])
                transpose8(hb, 0, hTr[:, :, 2 + s * 128:2 + (s + 1) * 128], [hslot[s]], eng="act")
                if s == 3:
                    V(lambda h: h.tensor_copy(out=hTr[:, :, 0:2], in_=hTr[:, :, 2 + 3 * 128 + 126:2 + 4 * 128]), [hslot[3]], [hmirL])
                if s == 0:
                    V(lambda h: h.tensor_copy(out=hTr[:, :, 514:516], in_=hTr[:, :, 2:4]), [hslot[0]], [hmirR])
                if c == NC - 1:
                    if s == 3:
                        V(lambda h: h.tensor_copy(out=hTr[:, :, 514:516], in_=halo0[:]), [halo0], [hmirR])
                    else:
                        V(lambda h: h.tensor_copy(out=hTr[:, :, 2 + (s + 1) * 128:4 + (s + 1) * 128], in_=halo0[:]), [halo0], [hslot[(s + 1) % 4]])

            SC("L0P1")
            norm_to_ring(0)
            for c in range(NC):
                if c + 1 < NC:
                    norm_to_ring(c + 1)
                s = c % 4
                setpar(c)
                ptc, gtc, xsc = pt[c % 2], gt[c % 2], xsb[c % 2]
                sg12, xbcT = sg12s[c % 2], xbcTs[c % 2]
                rr = [hslot[(c - 1) % 4], hslot[s], hslot[(c + 1) % 4], hmirL, hmirR, WT1]
                for j in range(6):
                    bk = PS[j % 2]
                    for k in range(8):
                        MM(bk, bk[:, :], hTr[:, k, 2 + s * 128:2 + (s + 1) * 128], WT1[:, k, j * 512:(j + 1) * 512], rr, start=(k == 0), stop=(k == 7))
                    A(lambda h, bk=bk, j=j: h.activation(out=ptc[:, j * 512:(j + 1) * 512], in_=bk[:, :], func=AF.Copy), [], [bk, ptc])
                for k in range(8):
                    MM(PS[0], PS[0][:, 0:64], hTr[:, k, 2 + s * 128:2 + (s + 1) * 128], WT1[:, k, 3072:3136], rr, start=(k == 0), stop=(k == 7))
                V(lambda h: h.tensor_copy(out=gtc[:], in_=PS[0][:, 0:64]), [], [PS[0], gtc])
                LD(PT0[c * 128:(c + 1) * 128, :], ptc[:], [ptc], [DK("PT0", c)])
                LD(GT[c * 128:(c + 1) * 128, :], gtc[:], [gtc], [DK("GT", c)])
                for i in range(12):
                    bk = PS[(1, 2, 6, 7)[i % 4]]
                    ac = acc[c % 2][:, i, :]
                    ack = acc[c % 2]
                    for k in range(8):
                        MM(bk, bk[:, 0:132], WT1[:, k, 3136 + i * 128:3136 + (i + 1) * 128], hTr[:, k, s * 128:s * 128 + 132], rr, start=(k == 0), stop=(k == 7))
                    A(lambda h, bk=bk, i=i, ac=ac: h.activation(out=ac, in_=bk[:, 2:130], func=AF.Identity, scale=cw[:, i, 2:3], bias=cbias[:, i:i + 1]), [cw, cbias], [bk, ack])
                    for j in (0, 1, 3, 4):
                        V(lambda h, bk=bk, i=i, j=j, ac=ac: h.scalar_tensor_tensor(out=ac, in0=bk[:, j:j + 128], scalar=cw[:, i, j:j + 1], in1=ac, op0=ALU.mult, op1=ALU.add), [cw], [bk, ack])
                ack = acc[c % 2]
                sigmoid_act(sg12[:], ack[:], [ack], [sg12])
                G(lambda h, ack=ack: h.tensor_tensor(out=xbcT[:], in0=ack[:], in1=sg12[:], op=ALU.mult), [ack, sg12], [xbcT])
                LD(BCT[c], xbcT[:, 8:12, :].rearrange("p a b -> p (a b)"), [xbcT], [DK("BCT", c)])
                pb = PS[3][:].bitcast(BF16)
                for i in range(8):
                    TR(PS[3], pb[:, i * 128:(i + 1) * 128], xbcT[:, i, :], identb[:], [xbcT, identb])
                A(lambda h: h.activation(out=xsc[:, 0:1024], in_=pb, func=AF.Copy), [], [PS[3], xsc])
                for i in range(2):
                    TR(PS[3], pb[:, i * 128:(i + 1) * 128], xbcT[:, 8 + i, :], identb[:], [xbcT, identb])
                A(lambda h: h.activation(out=xsc[:, 1024:1280], in_=pb[:, 0:256], func=AF.Copy), [], [PS[3], xsc])
                LD(XSB[c * 128:(c + 1) * 128, :], xsc[:], [xsc], [DK("XSB", c)])
                gates(gtc, [0])
                cumsums([0], PS[2])

                def store1(c=c):
                    A(lambda h: h.activation(out=hsb16[:], in_=Hs[:], func=AF.Copy), [Hs], [hsb16])
                    G(lambda h: h.tensor_copy(out=cmb16[:], in_=Cm[:]), [Cm], [cmb16])
                    LD(HS[c], hsb16[:], [hsb16], [DK("HS", c)])
                    LD(HM[c], cmb16[:], [cmb16], [DK("HM", c)])
                state_step(0, xsc, ptc, Hs, Cm, store1)
            del dbs[n0p1:]
            prep(WB_L0P2, lambda kc, c0: kc * 2048 + c0, ab_w_in[:, 0:1024], 8, 1024, lambda kc: gcol[:, 0, kc:kc + 1])
            prep(WB_L0P2, lambda kc, c0: kc * 2048 + 1024 + c0, ab_w_in[:, 5664:6688], 8, 1024, lambda kc: gcol[:, 0, kc:kc + 1])
            prep(WB_L0P2, lambda kc, c0: 16 * 1024 + kc * 1024 + c0, ab_w_out, 16, 1024, lambda kc: gcol[:, 4 + kc // 8, (kc % 8):(kc % 8) + 1])
            LD(exS[:, 0:1024], Hs[:], [Hs], [DK("exS", 0)])
            LD(exS[:, 1024:2056], Cm[:], [Cm], [DK("exS", 0)])
            allgather(exS, exG, DK("exS", 0), DK("exG", 0))

            S.barrier()
            S.clear_reg("arena")
            SC("L0P2_w")
            aoff[0] = 0
            Sel = ar("Sel", [112, 48, 128], BF16)
            Hsb = ar("Hsb", [128, 1024], F32)
            Cmb = ar("Cmb", [128, 1032], F32)
            sv = aoff[0]
            gath = ar("gath", [128, 2, 2056], F32)
            LD(gath[:], exG.rearrange("(r p) n -> p r n", p=128), [DK("exG", 0)], [gath])
            select2(Hsb[:], gath[:, 0, 0:1024], gath[:, 1, 0:1024], [gath, selt], [Hsb])
            select2(Cmb[:], gath[:, 0, 1024:2056], gath[:, 1, 1024:2056], [gath, selt], [Cmb])
            ones3 = ar("ones3", [112, 48, 128], BF16)
            selA = ar("selA", [112, 48, 128], BF16)
            selB = ar("selB", [112, 48, 128], BF16)
            G(lambda h: h.memset(ones3[:], 1.0), [], [ones3])
            G(lambda h: h.affine_select(out=selA[:], in_=ones3[:], pattern=[[1, 48], [0, 128]], compare_op=ALU.is_equal, fill=0.0, base=0, channel_multiplier=-1), [ones3], [selA])
            G(lambda h: h.affine_select(out=selB[:], in_=ones3[:], pattern=[[1, 48], [0, 128]], compare_op=ALU.is_equal, fill=0.0, base=64, channel_multiplier=-1), [ones3], [selB])
            G(lambda h: h.tensor_tensor(out=Sel[:], in0=selA[:], in1=selB[:], op=ALU.add), [selA, selB], [Sel])
            S.barrier()
            for tt in (ones3, selA, selB, gath):
                S.unregister("arena", tt)
            aoff[0] = sv
            WT2 = ar("W2", [128, 32, 1024], BF16, manual=True)
            Wzo = WT2[:, 0:16, :].rearrange("p a b -> p (a b)").rearrange("p (k n) -> p k n", k=8)
            Wo = WT2[:, 16:32, :]
            dtb = ar("dtb", [128, 32], F32)
            negA = ar("negA", [128, 32], F32)
            igfg = ar("igfg", [128, 32], F32)
            dsk = ar("dsk", [128, 16], F32)
            cw = None
            LD(dtb[:], ab_dt_bias.partition_broadcast(128), [WK], [dtb])
            LD(negA[:], ab_a_log.partition_broadcast(128), [WK], [negA])
            LD(igfg[:, 0:16], ab_ig_bias.partition_broadcast(128), [WK], [igfg])
            LD(igfg[:, 16:32], ab_fg_bias.partition_broadcast(128), [WK], [igfg])
            LD(dsk[:], ab_d_skip.partition_broadcast(128), [WK], [dsk])
            A(lambda h: h.activation(out=negA[:], in_=negA[:], func=AF.Exp), [], [negA])
            V(lambda h: h.tensor_scalar(out=negA[:], in0=negA[:], scalar1=-1.0, scalar2=None, op0=ALU.mult), [], [negA])
            hsb16 = ar("hsb16", [128, 1024], BF16)
            cmb16 = ar("cmb16", [128, 1032], BF16)
            gw = ar("gw", [128, 64], F32)
            G48 = ar("G48", [128, 48], F32)
            cst = ar("cst", [128, 96], F32)
            scw = ar("scw", [128, 48], F32)
            xe = ar("xe", [128, 1024], BF16)
            kw = ar("kw", [128, 1024], BF16)
            dts = ar("dts", [128, 32], F32)
            igs = ar("igs", [128, 16], F32)
            pt = [ar("pt%d" % i, [128, 3072], BF16) for i in range(2)]
            xsb = [ar("xsb%d" % i, [128, 1280], BF16) for i in range(2)]
            bct = [ar("bct%d" % i, [128, 4, 128], BF16) for i in range(2)]
            gt = [ar("gt%d" % i, [128, 64], F32) for i in range(2)]
            hsl = [ar("hsl%d" % i, [128, 1024], BF16) for i in range(2)]
            hml = [ar("hml%d" % i, [128, 1032], BF16) for i in range(2)]
            hTc = ar("hTc", [128, 8, 128], BF16)
            zo = ar("zo", [128, 2048], BF16)
            LS = ar("LS", [128, 2, 112], F32)
            X2t = ar("X2t", [112, 2, 128], BF16)
            tmpb = ar("tmpb", [112, 2, 128], BF16)
            DT = [ar("DT%d" % i, [128, 512], BF16) for i in range(2)]
            WT = [ar("WT%d" % i, [128, 512], BF16) for i in range(2)]
            cbt = ar("cbt", [128, 2, 128], BF16)
            xdt = ar("xdt", [128, 2, 1024], BF16)
            sfac = ar("sfac", [128, 48], F32)
            T1 = ar("T1", [128, 1024], F32)
            T2 = ar("T2", [128, 1024], F32)
            yv = ar("yv", [128, 1024], F32)
            ycat = ar("ycat", [128, 2048], BF16)
            kqT = ar("kqT", [128, 16, 128], BF16)
            hdir = ar("hdir", [128, 512], F32)
            dn = ar("dn", [128, 16], F32)
            YT = ar("YT", [128, 16, 128], BF16)
            sz, hmv, xo = T2, yv, T1
            wzoT, woT = Trk("wzoT"), Trk("woT")
            WB2v = WB_L0P2.rearrange("p (a n) -> p a n", a=32)
            for q in range(2):
                LD(WT2[:, q * 8:(q + 1) * 8, :], WB2v[:, q * 8:(q + 1) * 8, :], [], [wzoT])
            for q in range(2, 4):
                LD(WT2[:, q * 8:(q + 1) * 8, :], WB2v[:, q * 8:(q + 1) * 8, :], [], [woT])
            LD(gres[:], norm_g[0, 1].partition_broadcast(128), [WK], [gres])
            V(lambda h: h.memset(LS[:], 0.0), [], [LS])
            V(lambda h: h.memset(X2t[:], 0.0), [], [X2t])
            V(lambda h: h.memset(tmpb[:], 0.0), [], [tmpb])

            def p2_loads(c):
                i = c % 2
                LD(pt[i][:], PT0[c * 128:(c + 1) * 128, :], [DK("PT0", c)], [pt[i]])
                LD(xsb[i][:], XSB[c * 128:(c + 1) * 128, :], [DK("XSB", c)], [xsb[i]])
                LD(bct[i][:].rearrange("p a b -> p (a b)"), BCT[c], [DK("BCT", c)], [bct[i]])
                LD(gt[i][:], GT[c * 128:(c + 1) * 128, :], [DK("GT", c)], [gt[i]])
                LD(hsl[i][:], HS[c], [DK("HS", c)], [hsl[i]])
                LD(hml[i][:], HM[c], [DK("HM", c)], [hml[i]])
                LD(xin[i][:], x_in[c * 128:(c + 1) * 128, :], [xk], [xin[i]])

            def decay_block(b, hds):
                bk = PS[b % 2]
                for j, (hd, d) in enumerate(hds):
                    o = bk[:, j * 128:(j + 1) * 128]
                    MM(bk, o, Sel[:, hd, :], X2t[:, 0, :], [Sel, X2t], start=True, stop=False)
                    MM(bk, o, X2t[:, 1, :], Sel[:, hd, :], [Sel, X2t], start=False, stop=False)
                    MM(bk, o, identb[:], NEG[:, d, :], [identb, NEG], start=False, stop=True)
                A(lambda h: h.activation(out=DT[b % 2][:], in_=bk[:, :], func=AF.Exp), [], [bk, DT[b % 2]])

            SC("L0P2")
            p2_loads(NC - 1)
            for c in range(NC - 1, -1, -1):
                if c - 1 >= 0:
                    p2_loads(c - 1)
                i2 = c % 2
                ptc, xsc, bcc, gtc, hsc, hmc, xic = pt[i2], xsb[i2], bct[i2], gt[i2], hsl[i2], hml[i2], xin[i2]
                norm_chunk(x_in, c, xk, xic, load=False)
                transpose8(hb, 0, hTc[:], [hTc], eng="act")
                for j in range(4):
                    bk = PS[4 + j % 2]
                    for k in range(8):
                        MM(bk, bk[:, :], hTc[:, k, :], Wzo[:, k, j * 512:(j + 1) * 512], [hTc, wzoT], start=(k == 0), stop=(k == 7))
                    A(lambda h, bk=bk, j=j: h.activation(out=zo[:, j * 512:(j + 1) * 512], in_=bk[:, :], func=AF.Copy), [], [bk, zo])
                gates(gtc, [0, 1])
                cumsums([0, 1], PS[2])
                V(lambda h: h.tensor_copy(out=LS[:, 0, 0:48], in_=cst[:, 0:48]), [cst], [LS])
                V(lambda h: h.tensor_scalar(out=LS[:, 1, 0:32], in0=cst[:, 0:32], scalar1=-1.0, scalar2=None, op0=ALU.mult), [cst], [LS])
                V(lambda h: h.tensor_tensor(out=LS[:, 1, 32:48], in0=igs[:, 0:16], in1=cst[:, 32:48], op=ALU.subtract), [cst, igs], [LS])
                V(lambda h: h.tensor_copy(out=LS[:, :, 64:112], in_=LS[:, :, 0:48]), [], [LS])
                TR(PS[2], PS[2][0:112, 0:128], LS[:, 0, :], identf[:], [LS, identf])
                TR(PS[2], PS[2][0:112, 128:256], LS[:, 1, :], identf[:], [LS, identf])
                V(lambda h: h.tensor_copy(out=X2t[0:48, :, :], in_=v3(PS[2][0:48, 0:256], 2)), [], [PS[2], X2t])
                V(lambda h: h.tensor_copy(out=tmpb[64:112, :, :], in_=v3(PS[2][64:112, 0:256], 2)), [], [PS[2], tmpb])
                V(lambda h: h.tensor_tensor(out=X2t[64:112, :, :], in0=v3(PS[2][64:112, 0:256], 2), in1=tmpb[64:112, :, :], op=ALU.subtract), [tmpb], [PS[2], X2t])
                A(lambda h: h.activation(out=sfac[:], in_=cst[:, 0:48], func=AF.Exp), [cst], [sfac])
                for g in range(2):
                    MM(PS[2], PS[2][:, 256 + g * 128:384 + g * 128], bcc[:, g, :], bcc[:, 2 + g, :], [bcc])
                V(lambda h: h.tensor_copy(out=cbt[:], in_=v3(PS[2][:, 256:512], 2)), [], [PS[2], cbt])
                for d in range(2):
                    V(lambda h, d=d: h.tensor_tensor(out=v3(xdt[:, d, :], 16), in0=v3(xsc[:, 0:1024], 16), in1=bc_in(dts[:, d * 16:d * 16 + 16], 16, 64), op=ALU.mult), [xsc, dts], [xdt])
                for b in range(8):
                    g, e0 = b // 4, (b % 4) * 2
                    hds = [(g * 8 + e0, 0), (16 + g * 8 + e0, 1), (g * 8 + e0 + 1, 0), (16 + g * 8 + e0 + 1, 1)]
                    decay_block(b, hds)
                    V(lambda h, b=b, g=g: h.tensor_tensor(out=v3(WT[b % 2][:], 4), in0=v3(DT[b % 2][:], 4), in1=bc_mid(cbt[:, g, :], 4, 128), op=ALU.mult), [DT[b % 2], cbt], [WT[b % 2]])
                    for j, (hd, d) in enumerate(hds):
                        hh = hd % 16
                        e = hh % 8
                        MM(PS[4 + g], PS[4 + g][:, e * 64:(e + 1) * 64], WT[b % 2][:, j * 128:(j + 1) * 128], xdt[:, d, hh * 64:(hh + 1) * 64], [WT[b % 2], xdt], start=(d == 0), stop=(d == 1))
                A(lambda h: h.activation(out=hsb16[:], in_=Hsb[:], func=AF.Copy), [Hsb], [hsb16])
                for g in range(2):
                    MM(PS[6 + g], PS[6 + g][:, :], bcc[:, 2 + g, :], hsc[:, g * 512:(g + 1) * 512], [bcc, hsc])
                for g in range(2):
                    V(lambda h, g=g: h.tensor_tensor(out=v3(T1[:, g * 512:(g + 1) * 512], 8), in0=v3(PS[6 + g][:, :], 8), in1=bc_in(sfac[:, g * 8:g * 8 + 8], 8, 64), op=ALU.mult), [sfac], [PS[6 + g], T1])
                for g in range(2):
                    MM(PS[6 + g], PS[6 + g][:, :], bcc[:, 2 + g, :], hsb16[:, g * 512:(g + 1) * 512], [bcc, hsb16])
                for g in range(2):
                    V(lambda h, g=g: h.tensor_tensor(out=v3(T2[:, g * 512:(g + 1) * 512], 8), in0=v3(PS[6 + g][:, :], 8), in1=bc_in(sfac[:, 16 + g * 8:24 + g * 8], 8, 64), op=ALU.mult), [sfac], [PS[6 + g], T2])
                G(lambda h: h.tensor_tensor(out=T1[:], in0=T1[:], in1=T2[:], op=ALU.add), [T2], [T1])
                G(lambda h: h.tensor_tensor(out=v3(T2[:], 16), in0=v3(xsc[:, 0:1024], 16), in1=bc_in(dsk[:], 16, 64), op=ALU.mult), [xsc, dsk], [T2])
                G(lambda h: h.tensor_tensor(out=T1[:], in0=T1[:], in1=T2[:], op=ALU.add), [T2], [T1])
                for g in range(2):
                    V(lambda h, g=g: h.tensor_tensor(out=yv[:, g * 512:(g + 1) * 512], in0=PS[4 + g][:, :], in1=T1[:, g * 512:(g + 1) * 512], op=ALU.add), [T1], [PS[4 + g], yv])
                sigmoid_act(sz[:], zo[:, 0:1024], [zo], [sz])
                G(lambda h: h.tensor_tensor(out=sz[:], in0=sz[:], in1=zo[:, 0:1024], op=ALU.mult), [zo], [sz])
                V(lambda h: h.tensor_tensor(out=yv[:], in0=yv[:], in1=sz[:], op=ALU.mult), [sz], [yv])
                for g in range(2):
                    rstd_from(yv[:, g * 512:(g + 1) * 512], 512, 1 + g, sqj[:, g * 512:(g + 1) * 512], [yv])
                for g in range(2):
                    V(lambda h, g=g: h.tensor_scalar(out=ycat[:, g * 512:(g + 1) * 512], in0=yv[:, g * 512:(g + 1) * 512], scalar1=ss[:, 1 + g:2 + g], scalar2=None, op0=ALU.mult), [yv, ss], [ycat])
                transpose8(ptc, 1024, kqT[:, 0:8, :], [kqT], eng="act")
                transpose8(ptc, 0, kqT[:, 8:16, :], [kqT], eng="act")
                A(lambda h: h.activation(out=cmb16[:], in_=Cmb[:], func=AF.Copy), [Cmb], [cmb16])
                for hf in range(2):
                    h0 = hf * 4
                    for j in range(4):
                        MM(PS[4], PS[4][:, j * 128:(j + 1) * 128], kqT[:, h0 + j, :], kqT[:, 8 + h0 + j, :], [kqT])
                    for bb in range(2):
                        b = 8 + hf * 2 + bb
                        ha = h0 + bb * 2
                        hds = [(32 + ha, 0), (40 + ha, 1), (32 + ha + 1, 0), (40 + ha + 1, 1)]
                        decay_block(b, hds)
                        V(lambda h, b=b, bb=bb: h.tensor_tensor(out=WT[b % 2][:].rearrange("p (a d b) -> p a d b", a=2, d=2),
                                                               in0=v3(PS[4][:, bb * 256:(bb + 1) * 256], 2).unsqueeze(2).to_broadcast([128, 2, 2, 128]),
                                                               in1=DT[b % 2][:].rearrange("p (a d b) -> p a d b", a=2, d=2), op=ALU.mult), [DT[b % 2]], [PS[4], WT[b % 2]])
                        for j, (hd, d) in enumerate(hds):
                            hh = (hd - 32) % 8
                            bkn = PS[5 + d]
                            MM(bkn, bkn[:, (hh % 4) * 128:(hh % 4 + 1) * 128], WT[b % 2][:, j * 128:(j + 1) * 128], ptc[:, 2048 + hh * 128:2048 + (hh + 1) * 128], [WT[b % 2], ptc])
                            MM(PS[2], PS[2][:, d * 8 + hh:d * 8 + hh + 1], WT[b % 2][:, j * 128:(j + 1) * 128], onesb[:, 0:1], [WT[b % 2], onesb])
                    for d in range(2):
                        st16 = hmc if d == 0 else cmb16
                        for j in range(4):
                            hh = h0 + j
                            MM(PS[7], PS[7][:, j * 128:(j + 1) * 128], kqT[:, 8 + hh, :], st16[:, hh * 128:(hh + 1) * 128], [kqT, st16])
                            MM(PS[2], PS[2][:, 16 + d * 8 + hh:17 + d * 8 + hh], kqT[:, 8 + hh, :], st16[:, 1024 + hh:1025 + hh], [kqT, st16])
                        sc = sfac[:, 32 + d * 8 + h0:32 + d * 8 + h0 + 4]
                        V(lambda h, sc=sc: h.tensor_tensor(out=v3(hdir[:], 4), in0=v3(PS[7][:, :], 4), in1=bc_in(sc, 4, 128), op=ALU.mult), [sfac], [PS[7], hdir])
                        V(lambda h, d=d: h.tensor_tensor(out=hdir[:], in0=hdir[:], in1=PS[5 + d][:, :], op=ALU.add), [], [hdir, PS[5 + d]])
                        V(lambda h, d=d, sc=sc, h0=h0: h.tensor_tensor(out=dn[:, 0:4], in0=PS[2][:, 16 + d * 8 + h0:16 + d * 8 + h0 + 4], in1=sc, op=ALU.mult), [sfac], [PS[2], dn])
                        V(lambda h, d=d, h0=h0: h.tensor_tensor(out=dn[:, 0:4], in0=dn[:, 0:4], in1=PS[2][:, d * 8 + h0:d * 8 + h0 + 4], op=ALU.add), [], [PS[2], dn])
                        V(lambda h: h.scalar_tensor_tensor(out=dn[:, 4:8], in0=dn[:, 0:4], scalar=-1.0, in1=dn[:, 0:4], op0=ALU.mult, op1=ALU.max), [], [dn])
                        V(lambda h: h.tensor_scalar(out=dn[:, 4:8], in0=dn[:, 4:8], scalar1=1.0, scalar2=None, op0=ALU.max), [], [dn])
                        V(lambda h: h.reciprocal(out=dn[:, 8:12], in_=dn[:, 4:8]), [], [dn])
                        if d == 0:
                            V(lambda h, h0=h0: h.tensor_tensor(out=v3(hmv[:, h0 * 128:(h0 + 4) * 128], 4), in0=v3(hdir[:], 4), in1=bc_in(dn[:, 8:12], 4, 128), op=ALU.mult), [hdir, dn], [hmv])
                        else:
                            V(lambda h: h.tensor_tensor(out=v3(hdir[:], 4), in0=v3(hdir[:], 4), in1=bc_in(dn[:, 8:12], 4, 128), op=ALU.mult), [dn], [hdir])
                            V(lambda h, h0=h0: h.tensor_tensor(out=hmv[:, h0 * 128:(h0 + 4) * 128], in0=hmv[:, h0 * 128:(h0 + 4) * 128], in1=hdir[:], op=ALU.add), [hdir], [hmv])
                sigmoid_act(sz[:], zo[:, 1024:2048], [zo], [sz])
                V(lambda h: h.tensor_tensor(out=hmv[:], in0=hmv[:], in1=sz[:], op=ALU.mult), [sz], [hmv])
                G(lambda h: h.tensor_tensor(out=sz[:], in0=hmv[:], in1=hmv[:], op=ALU.mult), [hmv], [sz])
                V(lambda h: h.tensor_reduce(out=ss[:, 8:16], in_=v3(sz[:], 8), axis=AX.X, op=ALU.add), [sz], [ss])
                A(lambda h: h.activation(out=ss[:, 8:16], in_=ss[:, 8:16], func=AF.Ln, scale=1.0 / 128, bias=epsc[:, 0:1]), [epsc], [ss])
                A(lambda h: h.activation(out=ss[:, 8:16], in_=ss[:, 8:16], func=AF.Exp, scale=-0.5), [], [ss])
                V(lambda h: h.tensor_tensor(out=v3(ycat[:, 1024:2048], 8), in0=v3(hmv[:], 8), in1=bc_in(ss[:, 8:16], 8, 128), op=ALU.mult), [hmv, ss], [ycat])
                transpose8(ycat, 0, YT[:, 0:8, :], [YT], eng="act")
                transpose8(ycat, 1024, YT[:, 8:16, :], [YT], eng="act")
                for n2 in range(2):
                    for kk in range(16):
                        MM(PS[6 + n2], PS[6 + n2][:, :], YT[:, kk, :], Wo[:, kk, n2 * 512:(n2 + 1) * 512], [YT, woT], start=(kk == 0), stop=(kk == 15))
                resid_out([PS[6], PS[7]], xic, X1, c, "X1", xo)
                if c == NC - 1:
                    LD(exR[0], X1[T - 1:T, :], [DK("X1", c)], [DK("exR0", 0)])
                    allgather(exR[0], exRG[0], DK("exR0", 0), DK("exRG0", 0))
                state_step(1, xsc, ptc, Hsb, Cmb, lambda: None)
            prep_ffn_up(0)
            prep_ffn_dn(0)

        NBT = 256

        def ffn(li, src, src_name, dst, dst_name):
            reset_arena()
            SC("FFN%d_w" % li)
            NB = T // NBT
            CPB = NBT // 128
            Wup = ar("Wup", [128, 8, 5632], BF16, manual=True)
            Wdn = ar("Wdn", [128, 22, 1024], BF16, manual=True)
            fw = ar("fw", [128, 44, 3], F32)
            fb = ar("fb", [128, 44], F32)
            hT2 = [ar("hT2_%d" % i, [128, 8, NBT + 2], BF16, manual=True) for i in range(3)]
            hmain = [Trk("hmain%d" % i) for i in range(3)]
            hleft = [Trk("hleft%d" % i) for i in range(3)]
            hright = [Trk("hright%d" % i) for i in range(3)]
            usb = [ar("usb%d" % i, [128, NBT + 2], F32) for i in range(3)]
            cgs = [ar("cg%d" % i, [128, NBT], F32) for i in range(2)]
            cvs = [ar("cvv%d" % i, [128, NBT], F32) for i in range(2)]
            ggs = [ar("gg%d" % i, [128, NBT], F32) for i in range(2)]
            gTs = [ar("gT%d" % i, [128, NBT], BF16) for i in range(3)]
            xos = [ar("xo0", [128, 1024], F32)] * 2
            neg20 = ar("neg20", [128, NBT], F32)
            G(lambda h: h.memset(neg20[:], -20.0), [], [neg20])
            xr = [ar("xr0", [128, 1024], F32)] * 2
            for j in range(3):
                LDS(fw[:, :, j], ffn_conv_w[li, j].rearrange("(i p) -> p i", p=128), [WK], [fw])
            LDS(fb[:], ffn_conv_b[li].rearrange("(i p) -> p i", p=128), [WK], [fb])
            LD(gres[:], norm_g[li, 3].partition_broadcast(128), [WK], [gres])
            wgT = [Trk("wffn0"), Trk("wffn1")]
            WBu = WB_up[li].rearrange("p (k n) -> p k n", k=8)
            WBd = WB_dn[li].rearrange("p (k n) -> p k n", k=22)
            for g in range(2):
                for a in (g * 1408, 2816 + g * 1408):
                    LD(Wup[:, :, a:a + 1408], WBu[:, :, a:a + 1408], [], [wgT[g]])
                LD(Wdn[:, g * 11:(g + 1) * 11, :], WBd[:, g * 11:(g + 1) * 11, :], [], [wgT[g]])
            for i in range(3):
                V(lambda h, i=i: h.memset(hT2[i][:], 0.0), [], [hmain[i], hleft[i], hright[i]])
            hcand = ar("hcand", [128, 8, 2], BF16)
            halo1 = ar("halo1", [128, 8, 1], BF16)
            halo_cands(exRG[li], DK("exRG%d" % li, 0), 2, hcand)
            select2(halo1[:], hcand[:, :, 0:1], hcand[:, :, 1:2], [hcand, selt], [halo1])

            def norm_block(b):
                p, pp = b % 3, (b - 1) % 3
                for cc in range(CPB):
                    c = b * CPB + cc
                    norm_chunk(src, c, DK(src_name, c), xin[c % 2])
                    transpose8(hb, 0, hT2[p][:, :, 1 + cc * 128:1 + (cc + 1) * 128], [hmain[p]])
                if b > 0:
                    V(lambda h: h.tensor_copy(out=hT2[pp][:, :, NBT + 1:NBT + 2], in_=hT2[p][:, :, 1:2]), [hmain[p]], [hright[pp]])
                    V(lambda h: h.tensor_copy(out=hT2[p][:, :, 0:1], in_=hT2[pp][:, :, NBT:NBT + 1]), [hmain[pp]], [hleft[p]])
                else:
                    V(lambda h: h.memset(hT2[p][:, :, 0:1], 0.0), [], [hleft[p]])
                if b + 1 == NB:
                    V(lambda h: h.tensor_copy(out=hT2[p][:, :, NBT + 1:NBT + 2], in_=halo1[:]), [halo1], [hright[p]])

            SC("FFN%d" % li)
            norm_block(0)
            if NB > 1:
                norm_block(1)
            tctr = [0]
            accb = [[PS[0], PS[1]], [PS[2], PS[7]]]
            for b in range(NB):
                p = b % 3
                if b + 2 < NB:
                    norm_block(b + 2)
                for i in range(22):
                    hr = [hmain[p], hleft[p], hright[p], wgT[i // 11]]
                    cg, cvv, gg, gTi = cgs[i % 2], cvs[i % 2], ggs[i % 2], gTs[i % 3]
                    for t in (i, 22 + i):
                        q = tctr[0] % 3
                        tctr[0] += 1
                        bk = PS[4 + q]
                        for k in range(8):
                            MM(bk, bk[:, 0:NBT + 2], Wup[:, k, t * 128:(t + 1) * 128], hT2[p][:, k, :], hr, start=(k == 0), stop=(k == 7))
                        us = usb[q]
                        A(lambda h, bk=bk, us=us: h.activation(out=us[:], in_=bk[:, 0:NBT + 2], func=AF.Copy), [], [bk, us])
                        cv = cg if t < 22 else cvv
                        A(lambda h, us=us, cv=cv, t=t: h.activation(out=cv[:], in_=us[:, 1:NBT + 1], func=AF.Identity, scale=fw[:, t, 1:2], bias=fb[:, t:t + 1]), [us, fw, fb], [cv])
                        V(lambda h, us=us, cv=cv, t=t: h.scalar_tensor_tensor(out=cv[:], in0=us[:, 0:NBT], scalar=fw[:, t, 0:1], in1=cv[:], op0=ALU.mult, op1=ALU.add), [us, fw], [cv])
                        V(lambda h, us=us, cv=cv, t=t: h.scalar_tensor_tensor(out=cv[:], in0=us[:, 2:NBT + 2], scalar=fw[:, t, 2:3], in1=cv[:], op0=ALU.mult, op1=ALU.add), [us, fw], [cv])
                    A(lambda h, cg=cg, gg=gg: h.activation(out=gg[:], in_=cg[:], func=AF.Square), [cg], [gg])
                    V(lambda h, gg=gg: h.tensor_scalar(out=gg[:], in0=gg[:], scalar1=0.044715, scalar2=1.0, op0=ALU.mult, op1=ALU.add), [], [gg])
                    G(lambda h, gg=gg, cg=cg: h.tensor_tensor(out=gg[:], in0=gg[:], in1=cg[:], op=ALU.mult), [cg], [gg])
                    V(lambda h, gg=gg: h.tensor_scalar(out=gg[:], in0=gg[:], scalar1=-20.0, scalar2=None, op0=ALU.max), [], [gg])
                    A(lambda h, gg=gg: h.activation(out=gg[:], in_=gg[:], func=AF.Exp, scale=-1.5957691216), [], [gg])
                    A(lambda h, gg=gg: h.activation(out=gg[:], in_=gg[:], func=AF.Ln, bias=1.0), [], [gg])
                    A(lambda h, gg=gg: h.activation(out=gg[:], in_=gg[:], func=AF.Exp, scale=-1.0), [], [gg])
                    G(lambda h, gg=gg, cg=cg: h.tensor_tensor(out=gg[:], in0=gg[:], in1=cg[:], op=ALU.mult), [cg], [gg])
                    G(lambda h, gg=gg, cvv=cvv, gTi=gTi: h.tensor_tensor(out=gTi[:], in0=gg[:], in1=cvv[:], op=ALU.mult), [gg, cvv], [gTi])
                    for m in range(CPB):
                        for n2 in range(2):
                            bk = accb[m][n2]
                            MM(bk, bk[:, :], gTi[:, m * 128:(m + 1) * 128], Wdn[:, i, n2 * 512:(n2 + 1) * 512], [gTi, wgT[i // 11]], start=(i == 0), stop=(i == 21))
                for m in range(CPB):
                    c = b * CPB + m
                    xrc = xr[c % 2]
                    LD(xrc[:], src[c * 128:(c + 1) * 128, :], [DK(src_name, c)], [xrc])
                    resid_out(accb[m], xrc, dst, c, dst_name, xos[c % 2])
            if li == 0:
                for (s0, n, d0, gq) in [(1024, 2048, 0, False), (0, 512, 2048, True), (512, 512, 2560, False), (3072, 32, 3072, False)]:
                    prep(WB_g, lambda kc, c0, d0=d0: kc * 3104 + d0 + c0, c_w_in[:, s0:s0 + n], 8, n,
                         (lambda kc: gcolq[:, kc:kc + 1]) if gq else (lambda kc: gcol[:, 2, kc:kc + 1]))
                prep(WB_o1, lambda kc, c0: kc * 1024 + c0, c_w_out, 8, 1024, lambda kc: gcol[:, 6, kc:kc + 1])

        def layer1():
            reset_arena()
            SC("L1P1_w")
            W1 = ar("Wg", [128, 8, 3104], BF16, manual=True)
            gw2 = ar("gw2", [16, 2, 512], BF16)
            gw2f = ar("gw2f", [16, 2, 512], F32)
            gbias = ar("gbias", [128, 2, 4], F32)
            ones1 = ar("ones1", [128, 128], F32)
            Sst = ar("Sst", [128, 1024], F32)
            n0 = len(dbs)
            hTc = ar2("hTc", [128, 8, 128], BF16)
            vr = ar2("vr", [128, 2048], BF16, 1)
            qkT = ar2("qkT", [128, 8, 128], BF16)
            lt = ar2("lt", [128, 8, 128], F32)
            glr = ar2("glr", [16, 2, 128], BF16)
            Lc = ar2("Lc", [128, 8, 128], F32)
            ex = ar2("ex", [128, 8, 128], F32)
            kend = ar2("kend", [128, 4, 128], BF16)
            kendT = ar2("kendT", [128, 512], BF16)
            S16 = ar2("S16", [128, 1024], BF16)
            tots = ar2("tots", [128, 16], F32)
            for d in range(2):
                LD(gw2f[:, d, :], c_gate_w2[d], [WK], [gw2f])
            V(lambda h: h.tensor_copy(out=gw2[:], in_=gw2f[:]), [gw2f], [gw2])
            for d in range(2):
                LDS(gbias[:, d, :], c_gate_b[d].rearrange("(j p) -> p j", p=128), [WK], [gbias])
            V(lambda h: h.tensor_scalar(out=gbias[:], in0=gbias[:], scalar1=-1.0, scalar2=None, op0=ALU.mult), [], [gbias])
            V(lambda h: h.memset(ones1[:], 1.0), [], [ones1])
            WBgv = WB_g.rearrange("p (k n) -> p k n", k=8)
            for q in range(2):
                LD(W1[:, q * 4:(q + 1) * 4, :], WBgv[:, q * 4:(q + 1) * 4, :], [], [W1])
            V(lambda h: h.memset(Sst[:], 0.0), [], [Sst])

            def scans(ltile, dirs, Lc, tots):
                for d in dirs:
                    for hh in range(4):
                        t = d * 4 + hh
                        V(lambda h, t=t: h.tensor_tensor_scan(out=Lc[:, t, :], data0=ones1[:], data1=ltile[:, t, :], initial=0.0, op0=ALU.mult, op1=ALU.add), [ones1, ltile], [Lc])
                    V(lambda h, d=d: h.tensor_copy(out=tots[:, d * 4:d * 4 + 4], in_=Lc[:, d * 4:d * 4 + 4, 127:128].rearrange("p a b -> p (a b)")), [Lc], [tots])
                    if d == 1:
                        V(lambda h: h.tensor_tensor(out=Lc[:, 4:8, :], in0=ltile[:, 4:8, :], in1=Lc[:, 4:8, :], op=ALU.subtract), [ltile], [Lc])
                        V(lambda h: h.tensor_tensor(out=Lc[:, 4:8, :], in0=Lc[:, 4:8, :], in1=bc_in(tots[:, 4:8], 4, 128), op=ALU.add), [tots], [Lc])
                V(lambda h: h.tensor_scalar(out=tots[:, 8:16], in0=tots[:, 0:8], scalar1=-1.0 / 16, scalar2=None, op0=ALU.mult), [], [tots])

            def gla_state(d, qk_tile, vr_tile, Stt, store_fn, Lc, tots, ex, kend, kendT):
                for hh in range(4):
                    t = d * 4 + hh
                    A(lambda h, t=t: h.activation(out=ex[:, t, :], in_=Lc[:, t, :], func=AF.Exp, scale=1.0 / 16, bias=tots[:, 8 + t:9 + t]), [Lc, tots], [ex])
                V(lambda h: h.tensor_tensor(out=kend[:], in0=qk_tile[:, 4:8, :], in1=ex[:, d * 4:d * 4 + 4, :], op=ALU.mult), [qk_tile, ex], [kend])
                pb = PS[3][:].bitcast(BF16)
                for hh in range(4):
                    TR(PS[3], pb[:, hh * 128:(hh + 1) * 128], kend[:, hh, :], identb[:], [kend, identb])
                V(lambda h: h.tensor_copy(out=kendT[:], in_=pb[:, 0:512]), [], [PS[3], kendT])
                for hh in range(4):
                    bk = PS[4 + hh // 2]
                    MM(bk, bk[:, (hh % 2) * 256:(hh % 2 + 1) * 256], kendT[:, hh * 128:(hh + 1) * 128], vr_tile[:, hh * 256:(hh + 1) * 256], [kendT, vr_tile])
                store_fn()
                A(lambda h: h.activation(out=tots[:, 0:4], in_=tots[:, 8 + d * 4:12 + d * 4], func=AF.Exp), [], [tots])
                for hh in range(4):
                    bk = PS[4 + hh // 2]
                    V(lambda h, hh=hh, bk=bk: h.scalar_tensor_tensor(out=Stt[:, hh * 256:(hh + 1) * 256], in0=Stt[:, hh * 256:(hh + 1) * 256], scalar=tots[:, hh:hh + 1], in1=bk[:, (hh % 2) * 256:(hh % 2 + 1) * 256], op0=ALU.mult, op1=ALU.add), [tots], [Stt, bk])

            x2k = lambda c: DK("X2", c)
            SC("L1P1")
            for c in range(NC):
                setpar(c)
                i2 = c % 2
                norm_chunk(X2, c, x2k(c), xin[i2])
                transpose8(hb, 0, hTc[:], [hTc])
                for j in range(4):
                    bk = PS[j % 2]
                    for k in range(8):
                        MM(bk, bk[:, :], hTc[:, k, :], W1[:, k, j * 512:(j + 1) * 512], [hTc, W1], start=(k == 0), stop=(k == 7))
                    if j % 2 == 0:
                        A(lambda h, bk=bk, j=j: h.activation(out=vr[:, j * 512:(j + 1) * 512], in_=bk[:, :], func=AF.Copy), [], [bk, vr])
                    else:
                        V(lambda h, bk=bk, j=j: h.tensor_copy(out=vr[:, j * 512:(j + 1) * 512], in_=bk[:, :]), [], [bk, vr])
                LD(VR[c * 128:(c + 1) * 128, :], vr[:], [vr], [DK("VR", c)])
                for i in range(8):
                    bk = PS[6 + i // 4]
                    for k in range(8):
                        MM(bk, bk[:, (i % 4) * 128:(i % 4 + 1) * 128], W1[:, k, 2048 + i * 128:2048 + (i + 1) * 128], hTc[:, k, :], [hTc, W1], start=(k == 0), stop=(k == 7))
                    if i % 4 == 3:
                        q4 = i // 4
                        A(lambda h, bk=bk, q4=q4: h.activation(out=qkT[:, q4 * 4:q4 * 4 + 4, :], in_=v3(bk[:, :], 4), func=AF.Copy), [], [bk, qkT])
                LD(QKT[c], qkT[:].rearrange("p a b -> p (a b)"), [qkT], [DK("QKT", c)])
                for d in range(2):
                    for k in range(8):
                        MM(PS[2], PS[2][0:16, d * 128:(d + 1) * 128], W1[:, k, 3072 + d * 16:3088 + d * 16], hTc[:, k, :], [hTc, W1], start=(k == 0), stop=(k == 7))
                V(lambda h: h.tensor_copy(out=glr[:], in_=v3(PS[2][0:16, 0:256], 2)), [], [PS[2], glr])
                for d in range(2):
                    bk = PS[d]
                    for j in range(4):
                        MM(bk, bk[:, j * 128:(j + 1) * 128], gw2[:, d, j * 128:(j + 1) * 128], glr[:, d, :], [gw2, glr])
                    for j in range(4):
                        t = d * 4 + j
                        A(lambda h, bk=bk, j=j, t=t, d=d: h.activation(out=lt[:, t, :], in_=bk[:, j * 128:(j + 1) * 128], func=AF.Exp, scale=-1.0, bias=gbias[:, d, j:j + 1]), [gbias], [bk, lt])
                A(lambda h: h.activation(out=lt[:], in_=lt[:], func=AF.Ln, bias=1.0), [], [lt])
                LD(LTd[c], lt[:].rearrange("p a b -> p (a b)"), [lt], [DK("LTd", c)])
                scans(lt, [0], Lc, tots)

                def store1(c=c):
                    A(lambda h: h.activation(out=S16[:], in_=Sst[:], func=AF.Copy), [Sst], [S16])
                    LD(SG[c], S16[:], [S16], [DK("SG", c)])
                gla_state(0, qkT, vr, Sst, store1, Lc, tots, ex, kend, kendT)
            prep_ffn_up(1)
            LD(exS1, Sst[:], [Sst], [DK("exS1", 0)])
            allgather(exS1, exG1, DK("exS1", 0), DK("exG1", 0))
            del dbs[n0:]

            S.barrier()
            S.clear_reg("arena")
            aoff[0] = 0
            SC("L1P2_w")
            Wo = ar("Wo1", [128, 8, 1024], BF16, manual=True)
            ones1 = ar("ones1", [128, 128], F32)
            Sb = ar("Sb", [128, 1024], F32)
            gath1 = ar("gath1", [128, 2, 1024], F32)
            n0 = len(dbs)
            vr = ar2("vr", [128, 2048], BF16, 2)
            qkT = ar2("qkT", [128, 8, 128], BF16, 2)
            lt = ar2("lt", [128, 8, 128], F32, 2)
            sfl = ar2("sfl", [128, 1024], BF16, 2)
            xres = ar2("xres", [128, 1024], F32, 2)
            Lc = ar2("Lc", [128, 8, 128], F32)
            ex = ar2("ex", [128, 8, 128], F32)
            kend = ar2("kend", [128, 4, 128], BF16)
            kendT = ar2("kendT", [128, 512], BF16)
            tots = ar2("tots", [128, 16], F32)
            Sb16 = ar2("Sb16", [128, 1024], BF16)
            qin = ar2("qin", [128, 8, 128], BF16)
            kout = ar2("kout", [128, 8, 128], BF16)
            ex2 = ar2("ex2", [128, 8, 128], F32)
            ex3 = ar2("ex3", [128, 8, 128], F32)
            attm = ar2("attm", [128, 8, 128], BF16)
            ov = ar2("ov", [128, 1024], F32)
            t4 = ar2("t4", [128, 1024], F32)
            t5 = ar2("t5", [128, 1024], F32)
            ycat = ar2("ycat1", [128, 1024], BF16)
            YT = ar2("YT1", [128, 8, 128], BF16)
            xo = ar2("xo1", [128, 1024], F32)
            V(lambda h: h.memset(ones1[:], 1.0), [], [ones1])
            LD(Wo[:], WB_o1.rearrange("p (k n) -> p k n", k=8), [], [Wo])
            LD(gres[:], norm_g[1, 1].partition_broadcast(128), [WK], [gres])
            LD(gath1[:], exG1.rearrange("(r p) n -> p r n", p=128), [DK("exG1", 0)], [gath1])
            select2(Sb[:], gath1[:, 0, :], gath1[:, 1, :], [gath1, selt], [Sb])

            def p2_loads(c):
                setpar(c)
                LD(vr[:], VR[c * 128:(c + 1) * 128, :], [DK("VR", c)], [vr])
                LD(qkT[:].rearrange("p a b -> p (a b)"), QKT[c], [DK("QKT", c)], [qkT])
                LD(lt[:].rearrange("p a b -> p (a b)"), LTd[c], [DK("LTd", c)], [lt])
                LD(sfl[:], SG[c], [DK("SG", c)], [sfl])
                LD(xres[:], X2[c * 128:(c + 1) * 128, :], [x2k(c)], [xres])

            SC("L1P2")
            p2_loads(NC - 1)
            for c in range(NC - 1, -1, -1):
                if c - 1 >= 0:
                    p2_loads(c - 1)
                setpar(c)
                scans(lt, [0, 1], Lc, tots)
                A(lambda h: h.activation(out=ex2[:], in_=Lc[:], func=AF.Exp, scale=-1.0 / 16), [Lc], [ex2])
                for d in range(2):
                    V(lambda h, d=d: h.tensor_tensor(out=qin[:, d * 4:d * 4 + 4, :], in0=qkT[:, 0:4, :], in1=ex2[:, d * 4:d * 4 + 4, :], op=ALU.mult), [qkT, ex2], [qin])
                A(lambda h: h.activation(out=ex3[:], in_=Lc[:], func=AF.Exp, scale=1.0 / 16), [Lc], [ex3])
                for d in range(2):
                    G(lambda h, d=d: h.tensor_tensor(out=kout[:, d * 4:d * 4 + 4, :], in0=qkT[:, 4:8, :], in1=ex3[:, d * 4:d * 4 + 4, :], op=ALU.mult), [qkT, ex3], [kout])
                for d in range(2):
                    bk = PS[d]
                    for hh in range(4):
                        MM(bk, bk[:, hh * 128:(hh + 1) * 128], kout[:, d * 4 + hh, :], qin[:, d * 4 + hh, :], [kout, qin])
                    V(lambda h, d=d, bk=bk: h.tensor_tensor(out=attm[:, d * 4:d * 4 + 4, :], in0=v3(bk[:, :], 4), in1=bc_mid(maskb16[:, d, :], 4, 128), op=ALU.mult), [maskb16], [bk, attm])
                A(lambda h: h.activation(out=Sb16[:], in_=Sb[:], func=AF.Copy), [Sb], [Sb16])
                for hh in range(4):
                    bk = PS[6 + hh // 2]
                    o = bk[:, (hh % 2) * 256:(hh % 2 + 1) * 256]
                    vh = vr[:, hh * 256:(hh + 1) * 256]
                    MM(bk, o, attm[:, hh, :], vh, [attm, vr], start=True, stop=False)
                    MM(bk, o, qin[:, hh, :], sfl[:, hh * 256:(hh + 1) * 256], [qin, sfl], start=False, stop=False)
                    MM(bk, o, attm[:, 4 + hh, :], vh, [attm, vr], start=False, stop=False)
                    MM(bk, o, qin[:, 4 + hh, :], Sb16[:, hh * 256:(hh + 1) * 256], [qin, Sb16], start=False, stop=True)
                for q in range(2):
                    A(lambda h, q=q: h.activation(out=ov[:, q * 512:(q + 1) * 512], in_=PS[6 + q][:, :], func=AF.Copy), [], [PS[6 + q], ov])
                G(lambda h: h.tensor_tensor(out=t4[:], in0=ov[:], in1=ov[:], op=ALU.mult), [ov], [t4])
                V(lambda h: h.tensor_reduce(out=ss[:, 8:12], in_=v3(t4[:], 4), axis=AX.X, op=ALU.add), [t4], [ss])
                A(lambda h: h.activation(out=ss[:, 8:12], in_=ss[:, 8:12], func=AF.Ln, scale=1.0 / 256, bias=epsc[:, 0:1]), [epsc], [ss])
                A(lambda h: h.activation(out=ss[:, 8:12], in_=ss[:, 8:12], func=AF.Exp, scale=-0.5), [], [ss])
                sigmoid_act(t5[:], vr[:, 1024:2048], [vr], [t5])
                G(lambda h: h.tensor_tensor(out=t5[:], in0=t5[:], in1=vr[:, 1024:2048], op=ALU.mult), [vr], [t5])
                G(lambda h: h.tensor_tensor(out=v3(ov[:], 4), in0=v3(ov[:], 4), in1=bc_in(ss[:, 8:12], 4, 256), op=ALU.mult), [ss], [ov])
                V(lambda h: h.tensor_tensor(out=ycat[:], in0=ov[:], in1=t5[:], op=ALU.mult), [ov, t5], [ycat])
                transpose8(ycat, 0, YT[:], [YT])
                for n2 in range(2):
                    for kk in range(8):
                        MM(PS[4 + n2], PS[4 + n2][:, :], YT[:, kk, :], Wo[:, kk, n2 * 512:(n2 + 1) * 512], [YT, Wo], start=(kk == 0), stop=(kk == 7))
                resid_out([PS[4], PS[5]], xres, X3, c, "X3", xo)
                if c == NC - 1:
                    LD(exR[1], X3[T - 1:T, :], [DK("X3", c)], [DK("exR1", 0)])
                    allgather(exR[1], exRG[1], DK("exR1", 0), DK("exRG1", 0))
                gla_state(1, qkT, vr, Sb, lambda: None, Lc, tots, ex, kend, kendT)
            prep_ffn_dn(1)
            del dbs[n0:]

        final = None
        layer0()
        if dbg == "x1":
            final = (X1, "X1")
        else:
            ffn(0, X1, "X1", X2, "X2")
            if dbg == "x2":
                final = (X2, "X2")
            else:
                layer1()
                if dbg == "x3":
                    final = (X3, "X3")
                else:
                    ffn(1, X3, "X3", out, "out")
        SC(None)
        if final is not None:
            for c in range(NC):
                t = xin[c % 2]
                LD(t[:], final[0][c * 128:(c + 1) * 128, :], [DK(final[1], c)], [t])
                LD(out[c * 128:(c + 1) * 128, :], t[:], [t], [DK("out", c)])
        S.flush()
        print("instructions:", S.nins, "est_us=%.0f" % (S.est_ns / 1e3), flush=True)
    return nc


_CACHE = {}


def _core_inputs(inputs, b, half):
    f = lambda k: np.asarray(inputs[k], dtype=np.float32)
    x = f("x")[b]
    S_ = x.shape[0]
    T = S_ // 2
    rv = half == 1
    if not rv:
        xl = x[:T]
        xh = x[T:T + 2]
    else:
        xl = x[T:][::-1]
        xh = x[T - 2:T][::-1]
    w_in = f("ab_w_in")[0]
    cw = f("ab_conv_w")[0]
    dtb = f("ab_dt_bias")[0]
    alog = f("ab_a_log")[0]
    igb = f("ab_ig_bias")[0]
    fgb = f("ab_fg_bias")[0]
    c_w_in = f("c_w_in")[0]
    gw2 = f("c_gate_w2")[0]
    gb = f("c_gate_b")[0]
    fcw = f("ffn_conv_w")
    if rv:
        perm = np.arange(w_in.shape[1])
        perm[2560:2576], perm[2576:2592] = np.arange(2576, 2592), np.arange(2560, 2576)
        perm[6688:6696], perm[6696:6704] = np.arange(6696, 6704), np.arange(6688, 6696)
        perm[6704:6712], perm[6712:6720] = np.arange(6712, 6720), np.arange(6704, 6712)
        w_in = w_in[:, perm]
        cw = cw[::-1]
        dtb, alog, igb, fgb = dtb[::-1], alog[::-1], igb[::-1], fgb[::-1]
        p2 = np.arange(c_w_in.shape[1])
        p2[3072:3088], p2[3088:3104] = np.arange(3088, 3104), np.arange(3072, 3088)
        c_w_in = c_w_in[:, p2]
        gw2 = gw2[::-1]
        gb = gb[::-1]
        fcw = fcw[:, ::-1]
    c = np.ascontiguousarray
    return {
        "x": c(xl), "x_halo": c(xh), "sel": np.array([0.0, 1.0] if half == 0 else [1.0, 0.0], np.float32),
        "norm_g": c(f("norm_g")),
        "ab_w_in": c(w_in), "ab_conv_w": c(cw), "ab_conv_b": c(f("ab_conv_b")[0]),
        "ab_dt_bias": c(dtb).reshape(32), "ab_a_log": c(alog).reshape(32),
        "ab_d_skip": c(f("ab_d_skip")[0]), "ab_ssd_norm": c(f("ab_ssd_norm")[0]),
        "ab_ig_bias": c(igb).reshape(16), "ab_fg_bias": c(fgb).reshape(16),
        "ab_mlstm_norm": c(f("ab_mlstm_norm")[0]), "ab_w_out": c(f("ab_w_out")[0]),
        "c_w_in": c(c_w_in), "c_gate_w2": c(gw2), "c_gate_b": c(gb),
        "c_norm": c(f("c_norm")[0]), "c_w_out": c(f("c_w_out")[0]),
        "ffn_w_up": c(f("ffn_w_up")), "ffn_conv_w": c(fcw), "ffn_conv_b": c(f("ffn_conv_b")), "ffn_w_down": c(f("ffn_w_down")),
    }


def kernel(**inputs):
    x = np.asarray(inputs["x"])
    B, S_, _ = x.shape
    T = S_ // 2
    if T not in _CACHE:
        _CACHE[T] = build(T)
    nc = _CACHE[T]
    in_maps = [_core_inputs(inputs, b, half) for b in range(B) for half in range(2)]
    res = run_bass_kernel_spmd(nc, in_maps, core_ids=list(range(2 * B)))
    outp = np.empty((B, S_, D), np.float32)
    for b in range(B):
        outp[b, :T] = np.asarray(res.results[2 * b]["out"], dtype=np.float32)
        outp[b, T:] = np.asarray(res.results[2 * b + 1]["out"], dtype=np.float32)[::-1]
    return outp
```

```python
import heapq
import numpy as np
from contextlib import ExitStack
import concourse.bass as bass
import concourse.mybir as mybir
from concourse.bass_utils import run_bass_kernel_spmd

F32 = mybir.dt.float32
BF16 = mybir.dt.bfloat16
AF = mybir.ActivationFunctionType
ALU = mybir.AluOpType
AX = mybir.AxisListType

D = 1024
EPS = 1e-6
NEGBIG = -30000.0


class Trk:
    __slots__ = ("name", "writers", "readers", "manual")

    def __init__(self, name, manual=False):
        self.name = name
        self.writers = []
        self.readers = []
        self.manual = manual


class Tile:
    def __init__(self, t, name, manual=False):
        self.t = t
        self.k = Trk(name, manual)

    def __getitem__(self, idx):
        return self.t[idx]


class DB:
    def __init__(self, tiles):
        self.tiles = tiles
        self.i = 0

    def __getitem__(self, idx):
        return self.tiles[self.i].t[idx]

    @property
    def k(self):
        return self.tiles[self.i].k


def _trk(x):
    return x.k if isinstance(x, (Tile, DB)) else x


class _Rec:
    def __getattr__(self, name):
        return lambda *a, **k: (name, a, k)


_REC = _Rec()


def _is_ap(v):
    return hasattr(v, "ap") and hasattr(v, "tensor") and hasattr(v, "offset")


def _esize(dt):
    return 4 if dt == F32 else 2


def _free_elems(ap):
    n = 1
    for v in ap.shape[1:]:
        n *= v
    return n


class Op:
    __slots__ = ("idx", "eng", "name", "args", "kw", "cost", "is_dma", "dma_t", "preds", "succs", "npred", "finish", "ev")


class Sched:
    ENGS = ("pe", "act", "dve", "pool", "sp")
    NDMA = 24

    def __init__(self, nc, es, self_wait=True, reorder=True):
        self.nc = nc
        self.self_wait = self_wait
        self.reorder = reorder
        self.semobj = {e: es.enter_context(nc.semaphore("s_" + e)) for e in self.ENGS}
        self.cnt = {e: 0 for e in self.ENGS}
        self.seen = {e: {} for e in self.ENGS}
        for i in range(self.NDMA):
            self.semobj["d%d" % i] = es.enter_context(nc.semaphore("s_d%d" % i))
        self.dma_cnt = [0] * self.NDMA
        self.dma_next = 0
        self.h = {"pe": nc.tensor, "act": nc.scalar, "dve": nc.vector, "pool": nc.gpsimd, "sp": nc.sync}
        self.nins = {e: 0 for e in self.ENGS}
        self.ops = []
        self.touched = set()
        self.reg = {}
        self.est_ns = 0.0
        self.verbose = False
        self.prio_mode = 1
        self.seg_name = ""

    def register(self, tname, lo, hi, tile):
        self.reg.setdefault(tname, []).append((lo, hi, tile))

    def clear_reg(self, tname):
        self.reg[tname] = []

    def unregister(self, tname, tile):
        self.reg[tname] = [x for x in self.reg.get(tname, []) if x[2] is not tile]

    def _auto(self, ap):
        lst = self.reg.get(ap.name)
        if not lst:
            return ()
        es = _esize(ap.dtype)
        lo = int(ap.offset) * es
        ext = 0
        for st, n in ap.ap[1:]:
            ext += (n - 1) * abs(st)
        hi = lo + (ext + 1) * es
        return [t for (a, b, t) in lst if a < hi and lo < b]

    def _mk(self, eng, fn, reads, writes, is_dma):
        name, a, k = fn(_REC)
        rs = [_trk(t) for t in reads]
        ws = [_trk(t) for t in writes]
        out_ap = None
        nbytes = 0
        for i, v in list(enumerate(a)) + list(k.items()):
            if not _is_ap(v):
                continue
            is_out = (i == 0 and isinstance(i, int)) or i in ("out", "accum_out")
            if is_out and out_ap is None and i != "accum_out":
                out_ap = v
            sp = str(v.space)
            if sp == "DRAM":
                if is_dma:
                    nbytes = max(nbytes, _free_elems(v) * v.shape[0] * _esize(v.dtype))
                continue
            for t in self._auto(v):
                tk = _trk(t)
                if tk.manual:
                    continue
                if is_out or sp == "PSUM":
                    if tk not in ws:
                        ws.append(tk)
                elif tk not in rs:
                    rs.append(tk)
            if is_dma:
                nbytes = max(nbytes, _free_elems(v) * v.shape[0] * _esize(v.dtype))
        op = Op()
        op.idx = len(self.ops)
        op.eng, op.name, op.args, op.kw, op.is_dma = eng, name, a, k, is_dma
        n = _free_elems(out_ap) if out_ap is not None else 64
        if is_dma:
            op.cost = 60.0
            op.dma_t = nbytes / 120.0
        elif eng == "pe":
            passes = 1
            if name == "matmul" and k["lhsT"].dtype == F32:
                passes = 4
            op.cost = 70.0 + 0.5 * n * passes
            op.dma_t = 0.0
        elif eng == "act":
            op.cost = 250.0 + 0.85 * n
            op.dma_t = 0.0
        elif eng == "dve":
            if name == "scalar_tensor_tensor":
                op.cost = 150.0 + 2.0 * n
            elif name == "reciprocal":
                op.cost = 100.0 + 6.0 * n
            elif name == "tensor_tensor_scan":
                op.cost = 100.0 + 2.0 * n
            else:
                op.cost = 90.0 + 1.1 * n
            op.dma_t = 0.0
        else:
            op.cost = 200.0 + 1.9 * n
            op.dma_t = 0.0
        preds = set()
        for t in rs:
            preds.update(t.writers)
        for t in ws:
            preds.update(t.writers)
            preds.update(t.readers)
        preds.discard(op.idx)
        op.preds = preds
        op.succs = []
        for t in rs:
            t.readers.append(op.idx)
            self.touched.add(t)
        for t in ws:
            t.writers = [op.idx]
            t.readers = []
            self.touched.add(t)
        self.ops.append(op)

    def op(self, eng, fn, reads=(), writes=()):
        self._mk(eng, fn, reads, writes, False)

    def dma(self, fn, reads=(), writes=(), eng="sp"):
        self._mk(eng, fn, reads, writes, True)

    def flush(self):
        ops = self.ops
        if not ops:
            return
        n = len(ops)
        order = {e: [] for e in self.ENGS}
        if not self.reorder:
            for op in ops:
                order[op.eng].append(op)
        else:
            for op in ops:
                op.npred = len(op.preds)
                for p in op.preds:
                    ops[p].succs.append(op.idx)
            ready = {e: [] for e in self.ENGS}
            bl = [0.0] * n
            for op in reversed(ops):
                m = 0.0
                for s in op.succs:
                    if bl[s] > m:
                        m = bl[s]
                bl[op.idx] = m + (op.cost if not op.is_dma else 2000.0 + op.dma_t)
            if self.prio_mode == 0:
                key = list(range(n))
            else:
                order_ix = sorted(range(n), key=lambda i: (-bl[i], i))
                key = [0] * n
                for r_, i in enumerate(order_ix):
                    key[i] = r_
            inv = {}
            for i in range(n):
                inv[key[i]] = i
            for op in ops:
                if op.npred == 0:
                    heapq.heappush(ready[op.eng], key[op.idx])
            busy = {e: 0.0 for e in self.ENGS}
            crit = [-1] * n
            blk = [-1] * n
            stt = [0.0] * n
            rdy = {}
            stall = {}
            ev = []
            now = 0.0
            dma_free = 0.0
            done = 0
            while done < n:
                for e in self.ENGS:
                    if busy[e] <= now and ready[e]:
                        i = inv[heapq.heappop(ready[e])]
                        op = ops[i]
                        if self.verbose:
                            rt = rdy.get(i, 0.0)
                            if order[e] and rt < now - 1e-9:
                                blk[i] = order[e][-1].idx
                            else:
                                blk[i] = crit[i]
                            stt[i] = now
                        order[e].append(op)
                        if self.verbose and e == "pe":
                            st_ = now - max(busy[e], 0.0)
                            if st_ > 0 and crit[i] >= 0:
                                cp = ops[crit[i]]
                                kk = (cp.eng, cp.name, getattr(cp, "tag", ""))
                                stall[kk] = stall.get(kk, 0.0) + st_
                        busy[e] = now + op.cost
                        if op.is_dma:
                            st = max(now + op.cost, dma_free)
                            dma_free = st + op.dma_t
                            fin = dma_free + 2000.0
                        else:
                            fin = busy[e]
                        heapq.heappush(ev, (fin, 1, i))
                        heapq.heappush(ev, (busy[e], 0, -1))
                if not ev:
                    raise RuntimeError("scheduler deadlock")
                t, kind, i = heapq.heappop(ev)
                now = max(now, t)
                if kind == 1:
                    done += 1
                    for s in ops[i].succs:
                        so = ops[s]
                        so.npred -= 1
                        crit[s] = i
                        if so.npred == 0:
                            rdy[s] = now
                            heapq.heappush(ready[so.eng], key[s])
            self.est_ns += now
            if self.verbose:
                bs = {e: sum(o.cost for o in order[e]) / 1e3 for e in self.ENGS}
                last = max(range(n), key=lambda i: stt[i] + ops[i].cost)
                cp = {}
                i = last
                guard = 0
                while i >= 0 and guard < 10 * n:
                    guard += 1
                    o_ = ops[i]
                    kk = (o_.eng, o_.name, "dma" if o_.is_dma else "")
                    dur = (2000.0 + o_.dma_t) if o_.is_dma else o_.cost
                    cp[kk] = cp.get(kk, 0.0) + dur
                    i = blk[i]
                for kk, vv in sorted(cp.items(), key=lambda x: -x[1])[:10]:
                    print("      critpath %-40s %.0f us" % (str(kk), vv / 1e3))
                for kk, vv in sorted(stall.items(), key=lambda x: -x[1])[:0]:
                    print("      pe stall %-40s %.0f us" % (str(kk), vv / 1e3))
                print("  segment %-8s n=%6d est_us=%8.1f busy_us: %s" % (self.seg_name, n, now / 1e3, " ".join("%s=%.0f" % (e, bs[e]) for e in self.ENGS)), flush=True)
        for e in self.ENGS:
            for op in order[e]:
                if op.is_dma:
                    i = self.dma_next
                    self.dma_next = (i + 1) % self.NDMA
                    key = "d%d" % i
                    prev = self.dma_cnt[i]
                    self.dma_cnt[i] += 16
                    op.ev = (key, self.dma_cnt[i], prev)
                else:
                    self.cnt[e] += 1
                    op.ev = (e, self.cnt[e], 0)
        for e in self.ENGS:
            h = self.h[e]
            seen = self.seen[e]
            for op in order[e]:
                need = {}
                for p in op.preds:
                    k, v, _ = ops[p].ev
                    if k == e and (e == "pe" or not self.self_wait):
                        continue
                    if need.get(k, 0) < v:
                        need[k] = v
                if op.is_dma and op.ev[2]:
                    k, _, prev = op.ev
                    if need.get(k, 0) < prev:
                        need[k] = prev
                for k, v in need.items():
                    if seen.get(k, 0) >= v:
                        continue
                    seen[k] = v
                    h.wait_ge(self.semobj[k], v)
                ins = getattr(h, op.name)(*op.args, **op.kw)
                ins.then_inc(self.semobj[op.ev[0]], 16 if op.is_dma else 1)
                self.nins[e] += 1
        self.ops = []
        for t in self.touched:
            t.writers = []
            t.readers = []
        self.touched = set()
        self._full_wait()

    def _full_wait(self):
        cur = dict(self.cnt)
        for i in range(self.NDMA):
            cur["d%d" % i] = self.dma_cnt[i]
        for e in self.ENGS:
            for k, v in cur.items():
                if k == e or v == 0 or self.seen[e].get(k, 0) >= v:
                    continue
                self.seen[e][k] = v
                self.h[e].wait_ge(self.semobj[k], v)

    def barrier(self):
        self.flush()


ARENA_N = 85000


def build(T, dbg=None, self_wait=True, reorder=True, verbose=False):
    NC = T // 128
    nc = bass.Bass("TRN2", target_bir_lowering=False)
    es = ExitStack()

    def dram(name, shape, dt, kind="Internal"):
        return nc.dram_tensor(name, shape, dt, kind=kind).ap()

    x_in = dram("x", [T, D], F32, "ExternalInput")
    norm_g = dram("norm_g", [2, 4, D], F32, "ExternalInput")
    ab_w_in = dram("ab_w_in", [D, 6720], F32, "ExternalInput")
    ab_conv_w = dram("ab_conv_w", [5, 1536], F32, "ExternalInput")
    ab_conv_b = dram("ab_conv_b", [1536], F32, "ExternalInput")
    ab_dt_bias = dram("ab_dt_bias", [32], F32, "ExternalInput")
    ab_a_log = dram("ab_a_log", [32], F32, "ExternalInput")
    ab_d_skip = dram("ab_d_skip", [16], F32, "ExternalInput")
    ab_ssd_norm = dram("ab_ssd_norm", [1024], F32, "ExternalInput")
    ab_ig_bias = dram("ab_ig_bias", [16], F32, "ExternalInput")
    ab_fg_bias = dram("ab_fg_bias", [16], F32, "ExternalInput")
    ab_mlstm_norm = dram("ab_mlstm_norm", [1024], F32, "ExternalInput")
    ab_w_out = dram("ab_w_out", [2048, D], F32, "ExternalInput")
    c_w_in = dram("c_w_in", [D, 3104], F32, "ExternalInput")
    c_gate_w2 = dram("c_gate_w2", [2, 16, 512], F32, "ExternalInput")
    c_gate_b = dram("c_gate_b", [2, 512], F32, "ExternalInput")
    c_norm = dram("c_norm", [1024], F32, "ExternalInput")
    c_w_out = dram("c_w_out", [D, D], F32, "ExternalInput")
    ffn_w_up = dram("ffn_w_up", [2, D, 5632], F32, "ExternalInput")
    ffn_conv_w = dram("ffn_conv_w", [2, 3, 5632], F32, "ExternalInput")
    ffn_conv_b = dram("ffn_conv_b", [2, 5632], F32, "ExternalInput")
    ffn_w_down = dram("ffn_w_down", [2, 2816, D], F32, "ExternalInput")
    x_halo = dram("x_halo", [2, D], F32, "ExternalInput")
    sel_in = dram("sel", [2], F32, "ExternalInput")
    out = dram("out", [T, D], F32, "ExternalOutput")
    exS = dram("exS", [128, 2056], F32)
    exG = dram("exG", [256, 2056], F32)
    exS1 = dram("exS1", [128, 1024], F32)
    exG1 = dram("exG1", [256, 1024], F32)
    exR = [dram("exR%d" % i, [1, D], F32) for i in range(2)]
    exRG = [dram("exRG%d" % i, [2, D], F32) for i in range(2)]
    PAIRS = [[0, 1], [2, 3], [4, 5], [6, 7]]
    WB_L0P2 = dram("WB_L0P2", [128, 32 * 1024], BF16)
    WB_up = [dram("WB_up%d" % i, [128, 8 * 5632], BF16) for i in range(2)]
    WB_dn = [dram("WB_dn%d" % i, [128, 22 * 1024], BF16) for i in range(2)]
    WB_g = dram("WB_g", [128, 8 * 3104], BF16)
    WB_o1 = dram("WB_o1", [128, 8 * 1024], BF16)
    WK = Trk("weights")

    PT0 = dram("PT0", [T, 3072], BF16)
    XSB = dram("XSB", [T, 1280], BF16)
    BCT = dram("BCT", [NC, 128, 512], BF16)
    GT = dram("GT", [T, 64], F32)
    HS = dram("HS", [NC, 128, 1024], BF16)
    HM = dram("HM", [NC, 128, 1032], BF16)
    X1 = dram("X1", [T, D], F32)
    X2 = dram("X2", [T, D], F32)
    QKT = dram("QKT", [NC, 128, 1024], BF16)
    LTd = dram("LTd", [NC, 128, 1024], F32)
    VR = dram("VR", [T, 2048], BF16)
    SG = dram("SG", [NC, 128, 1024], BF16)
    X3 = dram("X3", [T, D], F32)
    dk = {}

    def DK(name, c):
        key = (name, c)
        if key not in dk:
            dk[key] = Trk("%s%d" % key)
        return dk[key]

    with es:
        S = Sched(nc, es, self_wait=self_wait, reorder=reorder)
        S.verbose = verbose
        PS = [Tile(es.enter_context(nc.psum_tensor("ps%d" % i, [128, 512], F32)), "ps%d" % i) for i in range(8)]
        for i in range(8):
            S.register("ps%d" % i, 0, 1 << 30, PS[i])

        def sb(name, shape, dt):
            t = Tile(es.enter_context(nc.sbuf_tensor(name, shape, dt)), name)
            S.register(name, 0, 1 << 30, t)
            return t

        arena_t = es.enter_context(nc.sbuf_tensor("arena", [128, ARENA_N], BF16))
        aoff = [0]

        def reset_arena():
            if verbose:
                print("  arena used", aoff[0] * 2, "of", ARENA_N * 2)
            S.barrier()
            S.clear_reg("arena")
            aoff[0] = 0

        dbs = []

        def ar2(name, shape, dt, nbuf=2):
            d = DB([ar("%s_%d" % (name, i), shape, dt) for i in range(nbuf)])
            dbs.append(d)
            return d

        def setpar(c):
            for d in dbs:
                d.i = c % len(d.tiles)

        def ar(name, shape, dt, manual=False):
            n = 1
            for v in shape[1:]:
                n *= v
            nb = n * 2 if dt == F32 else n
            if aoff[0] % 2:
                aoff[0] += 1
            o = aoff[0]
            aoff[0] += nb
            assert aoff[0] <= ARENA_N, (name, aoff[0])
            ap = arena_t[0:shape[0], o:o + nb]
            if dt == F32:
                ap = ap.bitcast(F32)
            if len(shape) == 3:
                ap = ap.rearrange("p (a b) -> p a b", a=shape[1])
            elif len(shape) == 4:
                ap = ap.rearrange("p (a b c) -> p a b c", a=shape[1], b=shape[2])
            t = Tile(ap, name, manual)
            S.register("arena", o * 2, (o + nb) * 2, t)
            return t

        def V(fn, r=(), w=()):
            S.op("dve", fn, r, w)

        def A(fn, r=(), w=()):
            S.op("act", fn, r, w)

        def G(fn, r=(), w=()):
            S.op("pool", fn, r, w)

        def P(fn, r=(), w=()):
            S.op("pe", fn, r, w)

        def MM(bank, o, lhsT, rhs, r, start=True, stop=True):
            P(lambda h: h.matmul(o, lhsT=lhsT, rhs=rhs, start=start, stop=stop), r, [bank])

        def TR(bank, o, in_, ident, r):
            P(lambda h: h.transpose(out=o, in_=in_, identity=ident), r, [bank])

        def LD(o, i, r, w):
            S.dma(lambda h: h.dma_start(out=o, in_=i), r, w)

        def LDS(o, i, r, w):
            S.dma(lambda h: h.dma_start(out=o, in_=i, allow_slow_non_contiguous=True), r, w)

        def v3(ap, a):
            return ap.rearrange("p (a b) -> p a b", a=a)

        def bc_in(ap, a, b):
            return ap.unsqueeze(2).to_broadcast([128, a, b])

        def bc_mid(ap, a, b):
            return ap.unsqueeze(1).to_broadcast([128, a, b])

        block = es.enter_context(nc.Block())
        cur_scope = [None]

        def SC(name):
            S.seg_name = name or ""
            return
            if cur_scope[0] is not None:
                cur_scope[0].__exit__(None, None, None)
                cur_scope[0] = None
            if name is not None:
                cm = nc.named_scope(name)
                cm.__enter__()
                cur_scope[0] = cm

        onesf = sb("onesf", [128, 128], F32)
        zerof = sb("zerof", [128, 128], F32)
        identf = sb("identf", [128, 128], F32)
        identb = sb("identb", [128, 128], BF16)
        Uf = sb("Uf", [128, 128], F32)
        Ub = sb("Ub", [128, 128], F32)
        maskb16 = sb("maskb16", [128, 2, 128], BF16)
        NEG = sb("NEG", [128, 2, 128], BF16)
        negf = sb("negf", [128, 2, 128], F32)
        onesb = sb("onesb", [128, 8], BF16)
        epsc = sb("epsc", [128, 1], F32)
        G(lambda h: h.memset(onesf[:], 1.0), w=[onesf])
        G(lambda h: h.memset(zerof[:], 0.0), w=[zerof])
        G(lambda h: h.memset(onesb[:], 1.0), w=[onesb])
        G(lambda h: h.memset(epsc[:], EPS), w=[epsc])
        G(lambda h: h.affine_select(out=identf[:], in_=onesf[:], pattern=[[1, 128]], compare_op=ALU.is_equal, fill=0.0, base=0, channel_multiplier=-1), [onesf], [identf])
        G(lambda h: h.tensor_copy(out=identb[:], in_=identf[:]), [identf], [identb])
        G(lambda h: h.affine_select(out=Uf[:], in_=onesf[:], pattern=[[1, 128]], compare_op=ALU.is_ge, fill=0.0, base=0, channel_multiplier=-1), [onesf], [Uf])
        G(lambda h: h.affine_select(out=Ub[:], in_=onesf[:], pattern=[[-1, 128]], compare_op=ALU.is_ge, fill=0.0, base=0, channel_multiplier=1), [onesf], [Ub])
        G(lambda h: h.tensor_copy(out=maskb16[:, 0, :], in_=Uf[:]), [Uf], [maskb16])
        G(lambda h: h.tensor_copy(out=maskb16[:, 1, :], in_=Ub[:]), [Ub], [maskb16])
        G(lambda h: h.affine_select(out=negf[:, 0, :], in_=zerof[:], pattern=[[1, 128]], compare_op=ALU.is_ge, fill=NEGBIG, base=0, channel_multiplier=-1), [zerof], [negf])
        G(lambda h: h.affine_select(out=negf[:, 1, :], in_=zerof[:], pattern=[[-1, 128]], compare_op=ALU.is_ge, fill=NEGBIG, base=0, channel_multiplier=1), [zerof], [negf])
        G(lambda h: h.tensor_copy(out=NEG[:], in_=negf[:]), [negf], [NEG])

        stg = [sb("stg%d" % i, [128, 1024], F32) for i in range(2)]
        stg_ctr = [0]
        gcol = sb("gcol", [128, 8, 8], F32)
        gres = sb("gres", [128, 1024], F32)
        xin = [sb("xin%d" % i, [128, 1024], F32) for i in range(2)]
        sqj = sb("sqj", [128, 1024], BF16)
        ss = sb("ss", [128, 16], F32)
        hb = sb("hb", [128, 1024], BF16)
        dumm = sb("dumm", [128, 8], F32)
        selt = sb("selt", [128, 2], F32)
        LD(selt[:], sel_in.partition_broadcast(128), [WK], [selt])

        def allgather(src_d, dst_d, rk, wk):
            for rep_ in range(2):
                G(lambda h: h.collective_compute("AllGather", ALU.bypass, replica_groups=PAIRS, ins=[src_d.opt()], outs=[dst_d.opt()]), [rk], [wk])

        def select2(out_ap, a0, a1, r, w):
            V(lambda h: h.tensor_scalar(out=out_ap, in0=a0, scalar1=selt[:, 0:1], scalar2=None, op0=ALU.mult), r, w)
            V(lambda h: h.scalar_tensor_tensor(out=out_ap, in0=a1, scalar=selt[:, 1:2], in1=out_ap, op0=ALU.mult, op1=ALU.add), r, w)

        def halo_cands(src_rows_ap, src_trk, nrows, dst_tile):
            xi = xin[0]
            V(lambda h: h.memset(xi[:], 0.0), [], [xi])
            LD(xi[0:nrows, :], src_rows_ap, [src_trk], [xi])
            norm_chunk(None, 0, None, xi, load=False)
            pb = PS[3][:].bitcast(BF16)
            for k in range(8):
                TR(PS[3], pb[:, k * 128:(k + 1) * 128], hb[:, k * 128:(k + 1) * 128], identb[:], [hb, identb])
            V(lambda h: h.tensor_copy(out=dst_tile[:], in_=v3(pb, 8)[:, :, 0:nrows]), [], [PS[3], dst_tile])
        WG = Trk("wguard")

        gvecs = [norm_g[0, 0], norm_g[0, 2], norm_g[1, 0], norm_g[1, 2], ab_ssd_norm, ab_mlstm_norm, c_norm]
        for i, gv in enumerate(gvecs):
            LDS(gcol[:, i, :], gv.rearrange("(k p) -> p k", p=128), [WK], [gcol])

        class WLoad:
            def __init__(self, wt):
                self.wt = wt
                self.tmps = []
                G(lambda h: h.memset(dumm[:, 0:1], 0.0), [], [wt, WG, dumm])

            def load(self, dst_fn, src, KC, ncols, gc, factor=1.0):
                for kc in range(KC):
                    for c0 in range(0, ncols, 1024):
                        w = min(1024, ncols - c0)
                        st = stg[stg_ctr[0] % 2]
                        stg_ctr[0] += 1
                        LD(st[:, 0:w], src[kc * 128:(kc + 1) * 128, c0:c0 + w], [WK], [st])
                        o = dst_fn(kc, c0, w)
                        sc1 = gc(kc) if gc is not None else 1.0
                        tk = Trk("wtmp")
                        self.tmps.append(tk)
                        if factor == 1.0 and stg_ctr[0] % 2 == 0:
                            A(lambda h, o=o, st=st, w=w, sc1=sc1: h.activation(out=o, in_=st[:, 0:w], func=AF.Copy, scale=sc1), [st, gcol, WG], [tk])
                        else:
                            V(lambda h, o=o, st=st, w=w, sc1=sc1: h.tensor_scalar(out=o, in0=st[:, 0:w], scalar1=sc1, scalar2=float(factor), op0=ALU.mult, op1=ALU.mult), [st, gcol, WG], [tk])

            def done(self):
                G(lambda h: h.memset(dumm[:, 1:2], 0.0), self.tmps, [self.wt, dumm])

        cvt = [sb("cvt%d" % i, [128, 1024], BF16) for i in range(2)]
        prep_ctr = [0]
        gcolq = sb("gcolq", [128, 8], F32)
        V(lambda h: h.tensor_scalar(out=gcolq[:], in0=gcol[:, 2, :], scalar1=128 ** -0.5, scalar2=None, op0=ALU.mult), [gcol], [gcolq])

        def prep(dst_d, dst_off_fn, src, KC, ncols, gc_fn):
            for kc in range(KC):
                for c0 in range(0, ncols, 1024):
                    w = min(1024, ncols - c0)
                    i = prep_ctr[0]
                    prep_ctr[0] += 1
                    st, cv = stg[i % 2], cvt[i % 2]
                    LD(st[:, 0:w], src[kc * 128:(kc + 1) * 128, c0:c0 + w], [WK], [st])
                    if gc_fn is not None:
                        gc = gc_fn(kc)
                        G(lambda h, st=st, cv=cv, w=w, gc=gc: h.tensor_tensor(out=cv[:, 0:w], in0=st[:, 0:w], in1=gc.to_broadcast([128, w]), op=ALU.mult), [st, gcol, gcolq], [cv])
                    else:
                        G(lambda h, st=st, cv=cv, w=w: h.tensor_copy(out=cv[:, 0:w], in_=st[:, 0:w]), [st], [cv])
                    o = dst_off_fn(kc, c0)
                    LD(dst_d[:, o:o + w], cv[:, 0:w], [cv], [Trk("wbst")])

        def prep_ffn_up(li):
            prep(WB_up[li], lambda kc, c0: kc * 5632 + c0, ffn_w_up[li], 8, 5632, lambda kc: gcol[:, 1 + 2 * li, kc:kc + 1])

        def prep_ffn_dn(li):
            prep(WB_dn[li], lambda kc, c0: kc * 1024 + c0, ffn_w_down[li], 22, 1024, None)

        def sigmoid_act(out_ap, in_ap, r, w, scale=1.0):
            A(lambda h: h.activation(out=out_ap, in_=in_ap, func=AF.Exp, scale=-float(scale)), r, w)
            A(lambda h: h.activation(out=out_ap, in_=out_ap, func=AF.Ln, bias=1.0), [], w)
            A(lambda h: h.activation(out=out_ap, in_=out_ap, func=AF.Exp, scale=-1.0), [], w)

        def rstd_from(src_ap, n, col, junk_ap, r, w_extra=()):
            V(lambda h: h.memset(ss[:, col:col + 1], 0.0), [], [ss])
            A(lambda h: h.activation(out=junk_ap, in_=src_ap, func=AF.Square, accum_out=ss[:, col:col + 1]), r, [sqj, ss] + list(w_extra))
            A(lambda h: h.activation(out=ss[:, col:col + 1], in_=ss[:, col:col + 1], func=AF.Ln, scale=1.0 / n, bias=epsc[:, 0:1]), [epsc], [ss])
            A(lambda h: h.activation(out=ss[:, col:col + 1], in_=ss[:, col:col + 1], func=AF.Exp, scale=-0.5), [], [ss])

        def norm_chunk(src_dram, c, src_trk, xi, load=True):
            if load:
                LD(xi[:], src_dram[c * 128:(c + 1) * 128, :], [src_trk], [xi])
            rstd_from(xi[:], 1024, 0, sqj[:], [xi])
            V(lambda h: h.tensor_scalar(out=hb[:], in0=xi[:], scalar1=ss[:, 0:1], scalar2=None, op0=ALU.mult), [xi, ss], [hb])

        def transpose8(src_tile, src_c0, dst_ap, dst_trk, eng="dve"):
            pb = PS[3][:].bitcast(BF16)
            for k in range(8):
                TR(PS[3], pb[:, k * 128:(k + 1) * 128], src_tile[:, src_c0 + k * 128:src_c0 + (k + 1) * 128], identb[:], [src_tile, identb])
            if eng == "dve":
                V(lambda h: h.tensor_copy(out=dst_ap, in_=v3(pb, 8)), [], [PS[3]] + dst_trk)
            else:
                A(lambda h: h.activation(out=dst_ap, in_=v3(pb, 8), func=AF.Copy), [], [PS[3]] + dst_trk)

        def resid_out(banks, xres, dst_dram, c, dst_name, xo):
            V(lambda h: h.memset(ss[:, 3:5], 0.0), [], [ss])
            for n2 in range(2):
                A(lambda h, n2=n2: h.activation(out=sqj[:, n2 * 512:(n2 + 1) * 512], in_=banks[n2][:, :], func=AF.Square, accum_out=ss[:, 3 + n2:4 + n2]), [], [banks[n2], sqj, ss])
            V(lambda h: h.tensor_tensor(out=ss[:, 3:4], in0=ss[:, 3:4], in1=ss[:, 4:5], op=ALU.add), [], [ss])
            A(lambda h: h.activation(out=ss[:, 3:4], in_=ss[:, 3:4], func=AF.Ln, scale=1.0 / 1024, bias=epsc[:, 0:1]), [epsc], [ss])
            A(lambda h: h.activation(out=ss[:, 3:4], in_=ss[:, 3:4], func=AF.Exp, scale=-0.5), [], [ss])
            for n2 in range(2):
                V(lambda h, n2=n2: h.scalar_tensor_tensor(out=xo[:, n2 * 512:(n2 + 1) * 512], in0=banks[n2][:, :], scalar=ss[:, 3:4], in1=gres[:, n2 * 512:(n2 + 1) * 512], op0=ALU.mult, op1=ALU.mult), [ss, gres], [banks[n2], xo])
            G(lambda h: h.tensor_tensor(out=xo[:], in0=xo[:], in1=xres[:], op=ALU.add), [xres], [xo])
            LD(dst_dram[c * 128:(c + 1) * 128, :], xo[:], [xo], [DK(dst_name, c)])

        def layer0():
            reset_arena()
            SC("L0P1_w")
            WT1 = ar("W1", [128, 8, 4672], BF16, manual=True)
            hTr = ar("hTr", [128, 8, 516], BF16, manual=True)
            hslot = [Trk("hslot%d" % i) for i in range(4)]
            hmirL, hmirR = Trk("hmirL"), Trk("hmirR")
            dtb = ar("dtb", [128, 32], F32)
            negA = ar("negA", [128, 32], F32)
            igfg = ar("igfg", [128, 32], F32)
            dsk = ar("dsk", [128, 16], F32)
            cw = ar("cw", [128, 12, 5], F32)
            cbias = ar("cbias", [128, 12], F32)
            Hs = ar("Hs", [128, 1024], F32)
            Cm = ar("Cm", [128, 1032], F32)
            n0p1 = len(dbs)
            hsb16 = ar2("hsb16", [128, 1024], BF16)
            cmb16 = ar2("cmb16", [128, 1032], BF16)
            pt = [ar("pt%d" % i, [128, 3072], BF16) for i in range(2)]
            gt = [ar("gt%d" % i, [128, 64], F32) for i in range(2)]
            acc = [ar("acc%d" % i, [128, 12, 128], F32) for i in range(2)]
            sg12s = [ar("sg12_0", [128, 12, 128], F32)] * 2
            xbcTs = [ar("xbcT%d" % i, [128, 12, 128], BF16) for i in range(2)]
            xsb = [ar("xsb%d" % i, [128, 1280], BF16) for i in range(2)]
            gw = ar2("gw", [128, 64], F32)
            G48 = ar2("G48", [128, 48], F32)
            cst = ar2("cst", [128, 96], F32)
            scw = ar2("scw", [128, 48], F32)
            xe = ar2("xe", [128, 1024], BF16)
            kw = ar2("kw", [128, 1024], BF16)
            dts = ar2("dts", [128, 32], F32)
            igs = ar2("igs", [128, 16], F32)
            halo0 = ar("halo0", [128, 8, 2], BF16)
            xring = [ar("xring%d" % i, [128, 1024], F32) for i in range(3)]
            if verbose:
                print("  L0P1 arena", aoff[0] * 2)
            p1_end = aoff[0]

            def load_params():
                LD(dtb[:], ab_dt_bias.partition_broadcast(128), [WK], [dtb])
                LD(negA[:], ab_a_log.partition_broadcast(128), [WK], [negA])
                LD(igfg[:, 0:16], ab_ig_bias.partition_broadcast(128), [WK], [igfg])
                LD(igfg[:, 16:32], ab_fg_bias.partition_broadcast(128), [WK], [igfg])
                LD(dsk[:], ab_d_skip.partition_broadcast(128), [WK], [dsk])
                for j in range(5):
                    LDS(cw[:, :, j], ab_conv_w[j].rearrange("(i p) -> p i", p=128), [WK], [cw])
                LDS(cbias[:], ab_conv_b.rearrange("(i p) -> p i", p=128), [WK], [cbias])
                A(lambda h: h.activation(out=negA[:], in_=negA[:], func=AF.Exp), [], [negA])
                V(lambda h: h.tensor_scalar(out=negA[:], in0=negA[:], scalar1=-1.0, scalar2=None, op0=ALU.mult), [], [negA])
            load_params()
            wl = WLoad(WT1)
            segs = [(2592, 1024, 0, 1.0), (3616, 1024, 1024, 128 ** -0.5), (4640, 1024, 2048, 1.0),
                    (2560, 32, 3072, 1.0), (6688, 32, 3104, 1.0), (1024, 1536, 3136, 1.0)]
            for (s0, n, d0, fac) in segs:
                wl.load(lambda kc, c0, w, d0=d0: WT1[:, kc, d0 + c0:d0 + c0 + w], ab_w_in[:, s0:s0 + n], 8, n, lambda kc: gcol[:, 0, kc:kc + 1], fac)
            wl.done()
            V(lambda h: h.memset(Hs[:], 0.0), [], [Hs])
            V(lambda h: h.memset(Cm[:], 0.0), [], [Cm])
            V(lambda h: h.memset(hTr[:], 0.0), [], hslot + [hmirL, hmirR])
            halo_cands(x_halo, WK, 2, halo0)

            def gates(gtile, dirs):
                for d in dirs:
                    c0 = d * 16
                    V(lambda h, c0=c0: h.tensor_tensor(out=gw[:, c0:c0 + 16], in0=gtile[:, c0:c0 + 16], in1=dtb[:, c0:c0 + 16], op=ALU.add), [gtile, dtb], [gw])
                    A(lambda h, c0=c0: h.activation(out=gw[:, c0:c0 + 16], in_=gw[:, c0:c0 + 16], func=AF.Exp), [], [gw])
                    A(lambda h, c0=c0: h.activation(out=dts[:, c0:c0 + 16], in_=gw[:, c0:c0 + 16], func=AF.Ln, bias=1.0), [gw], [dts])
                    V(lambda h, c0=c0: h.tensor_tensor(out=G48[:, c0:c0 + 16], in0=dts[:, c0:c0 + 16], in1=negA[:, c0:c0 + 16], op=ALU.mult), [dts, negA], [G48])
                    i0 = 32 + d * 8
                    f0 = 48 + d * 8
                    V(lambda h, i0=i0, d=d: h.tensor_tensor(out=igs[:, d * 8:d * 8 + 8], in0=gtile[:, i0:i0 + 8], in1=igfg[:, d * 8:d * 8 + 8], op=ALU.add), [gtile, igfg], [igs])
                    V(lambda h, f0=f0, d=d: h.tensor_tensor(out=gw[:, f0:f0 + 8], in0=gtile[:, f0:f0 + 8], in1=igfg[:, 16 + d * 8:24 + d * 8], op=ALU.add), [gtile, igfg], [gw])
                    A(lambda h, f0=f0: h.activation(out=gw[:, f0:f0 + 8], in_=gw[:, f0:f0 + 8], func=AF.Exp, scale=-1.0), [], [gw])
                    A(lambda h, f0=f0: h.activation(out=gw[:, f0:f0 + 8], in_=gw[:, f0:f0 + 8], func=AF.Ln, bias=1.0), [], [gw])
                    V(lambda h, f0=f0, d=d: h.tensor_scalar(out=G48[:, 32 + d * 8:40 + d * 8], in0=gw[:, f0:f0 + 8], scalar1=-1.0, scalar2=None, op0=ALU.mult), [gw], [G48])

            def cumsums(dirs, bank):
                for d in dirs:
                    U = Uf if d == 0 else Ub
                    MM(bank, bank[:, d * 16:d * 16 + 16], U[:], G48[:, d * 16:d * 16 + 16], [U, G48])
                    MM(bank, bank[:, 32 + d * 8:40 + d * 8], U[:], G48[:, 32 + d * 8:40 + d * 8], [U, G48])
                MM(bank, bank[:, 64:112], onesf[:], G48[:], [onesf, G48])
                V(lambda h: h.tensor_copy(out=cst[:, 0:48], in_=bank[:, 0:48]), [], [bank, cst])
                V(lambda h: h.tensor_copy(out=cst[:, 48:96], in_=bank[:, 64:112]), [], [bank, cst])

            def state_step(d, xs_tile, pt_tile, Hst, Cst, store_fn):
                o16, o8 = d * 16, 32 + d * 8
                k_ap = pt_tile[:, 1024:2048]
                V(lambda h: h.tensor_tensor(out=scw[:, 0:16], in0=cst[:, 48 + o16:64 + o16], in1=cst[:, o16:o16 + 16], op=ALU.subtract), [cst], [scw])
                A(lambda h: h.activation(out=scw[:, 0:16], in_=scw[:, 0:16], func=AF.Exp), [], [scw])
                V(lambda h: h.tensor_tensor(out=scw[:, 0:16], in0=scw[:, 0:16], in1=dts[:, o16:o16 + 16], op=ALU.mult), [dts], [scw])
                V(lambda h: h.tensor_tensor(out=v3(xe[:], 16), in0=v3(xs_tile[:, 0:1024], 16), in1=bc_in(scw[:, 0:16], 16, 64), op=ALU.mult), [xs_tile, scw], [xe])
                for g in range(2):
                    MM(PS[4 + g], PS[4 + g][:, :], xs_tile[:, 1024 + g * 128:1152 + g * 128], xe[:, g * 512:(g + 1) * 512], [xs_tile, xe])
                V(lambda h: h.tensor_tensor(out=scw[:, 16:24], in0=cst[:, 48 + o8:56 + o8], in1=cst[:, o8:o8 + 8], op=ALU.subtract), [cst], [scw])
                V(lambda h: h.tensor_tensor(out=scw[:, 16:24], in0=scw[:, 16:24], in1=igs[:, d * 8:d * 8 + 8], op=ALU.add), [igs], [scw])
                A(lambda h: h.activation(out=scw[:, 16:24], in_=scw[:, 16:24], func=AF.Exp), [], [scw])
                V(lambda h: h.tensor_tensor(out=v3(kw[:], 8), in0=v3(k_ap, 8), in1=bc_in(scw[:, 16:24], 8, 128), op=ALU.mult), [pt_tile, scw], [kw])
                for hh in range(8):
                    bk = PS[6 + hh // 4]
                    MM(bk, bk[:, (hh % 4) * 128:(hh % 4 + 1) * 128], kw[:, hh * 128:(hh + 1) * 128], pt_tile[:, 2048 + hh * 128:2048 + (hh + 1) * 128], [kw, pt_tile])
                for hh in range(8):
                    MM(PS[2], PS[2][:, 128 + hh:129 + hh], kw[:, hh * 128:(hh + 1) * 128], onesb[:, 0:1], [kw, onesb])
                store_fn()
                A(lambda h: h.activation(out=scw[:, 24:40], in_=cst[:, 48 + o16:64 + o16], func=AF.Exp), [cst], [scw])
                A(lambda h: h.activation(out=scw[:, 40:48], in_=cst[:, 48 + o8:56 + o8], func=AF.Exp), [cst], [scw])
                V(lambda h: h.tensor_tensor(out=v3(Hst[:], 16), in0=v3(Hst[:], 16), in1=bc_in(scw[:, 24:40], 16, 64), op=ALU.mult), [scw], [Hst])
                for g in range(2):
                    V(lambda h, g=g: h.tensor_tensor(out=Hst[:, g * 512:(g + 1) * 512], in0=Hst[:, g * 512:(g + 1) * 512], in1=PS[4 + g][:, :], op=ALU.add), [], [Hst, PS[4 + g]])
                V(lambda h: h.tensor_tensor(out=v3(Cst[:, 0:1024], 8), in0=v3(Cst[:, 0:1024], 8), in1=bc_in(scw[:, 40:48], 8, 128), op=ALU.mult), [scw], [Cst])
                for q in range(2):
                    V(lambda h, q=q: h.tensor_tensor(out=Cst[:, q * 512:(q + 1) * 512], in0=Cst[:, q * 512:(q + 1) * 512], in1=PS[6 + q][:, :], op=ALU.add), [], [Cst, PS[6 + q]])
                V(lambda h: h.tensor_tensor(out=Cst[:, 1024:1032], in0=Cst[:, 1024:1032], in1=scw[:, 40:48], op=ALU.mult), [scw], [Cst])
                V(lambda h: h.tensor_tensor(out=Cst[:, 1024:1032], in0=Cst[:, 1024:1032], in1=PS[2][:, 128:136], op=ALU.add), [], [Cst, PS[2]])

            xk = Trk("x_in")

            def norm_to_ring(c):
                s = c % 4
                norm_chunk(x_in, c, xk, xring[c % 3])
                transpose8(hb, 0, hTr[:, :, 2 + s * 128:2 + (s + 1) * 128], [hslot[s]], eng="act")
                if s == 3:
                    V(lambda h: h.tensor_copy(out=hTr[:, :, 0:2], in_=hTr[:, :, 2 + 3 * 128 + 126:2 + 4 * 128]), [hslot[3]], [hmirL])
                if s == 0:
                    V(lambda h: h.tensor_copy(out=hTr[:, :, 514:516], in_=hTr[:, :, 2:4]), [hslot[0]], [hmirR])
                if c == NC - 1:
                    if s == 3:
                        V(lambda h: h.tensor_copy(out=hTr[:, :, 514:516], in_=halo0[:]), [halo0], [hmirR])
                    else:
                        V(lambda h: h.tensor_copy(out=hTr[:, :, 2 + (s + 1) * 128:4 + (s + 1) * 128], in_=halo0[:]), [halo0], [hslot[(s + 1) % 4]])

            SC("L0P1")
            norm_to_ring(0)
            for c in range(NC):
                if c + 1 < NC:
                    norm_to_ring(c + 1)
                s = c % 4
                setpar(c)
                ptc, gtc, xsc = pt[c % 2], gt[c % 2], xsb[c % 2]
                sg12, xbcT = sg12s[c % 2], xbcTs[c % 2]
                rr = [hslot[(c - 1) % 4], hslot[s], hslot[(c + 1) % 4], hmirL, hmirR, WT1]
                for j in range(6):
                    bk = PS[j % 2]
                    for k in range(8):
                        MM(bk, bk[:, :], hTr[:, k, 2 + s * 128:2 + (s + 1) * 128], WT1[:, k, j * 512:(j + 1) * 512], rr, start=(k == 0), stop=(k == 7))
                    A(lambda h, bk=bk, j=j: h.activation(out=ptc[:, j * 512:(j + 1) * 512], in_=bk[:, :], func=AF.Copy), [], [bk, ptc])
                for k in range(8):
                    MM(PS[0], PS[0][:, 0:64], hTr[:, k, 2 + s * 128:2 + (s + 1) * 128], WT1[:, k, 3072:3136], rr, start=(k == 0), stop=(k == 7))
                V(lambda h: h.tensor_copy(out=gtc[:], in_=PS[0][:, 0:64]), [], [PS[0], gtc])
                LD(PT0[c * 128:(c + 1) * 128, :], ptc[:], [ptc], [DK("PT0", c)])
                LD(GT[c * 128:(c + 1) * 128, :], gtc[:], [gtc], [DK("GT", c)])
                for i in range(12):
                    bk = PS[(1, 2, 6, 7)[i % 4]]
                    ac = acc[c % 2][:, i, :]
                    ack = acc[c % 2]
                    for k in range(8):
                        MM(bk, bk[:, 0:132], WT1[:, k, 3136 + i * 128:3136 + (i + 1) * 128], hTr[:, k, s * 128:s * 128 + 132], rr, start=(k == 0), stop=(k == 7))
                    A(lambda h, bk=bk, i=i, ac=ac: h.activation(out=ac, in_=bk[:, 2:130], func=AF.Identity, scale=cw[:, i, 2:3], bias=cbias[:, i:i + 1]), [cw, cbias], [bk, ack])
                    for j in (0, 1, 3, 4):
                        V(lambda h, bk=bk, i=i, j=j, ac=ac: h.scalar_tensor_tensor(out=ac, in0=bk[:, j:j + 128], scalar=cw[:, i, j:j + 1], in1=ac, op0=ALU.mult, op1=ALU.add), [cw], [bk, ack])
                ack = acc[c % 2]
                sigmoid_act(sg12[:], ack[:], [ack], [sg12])
                G(lambda h, ack=ack: h.tensor_tensor(out=xbcT[:], in0=ack[:], in1=sg12[:], op=ALU.mult), [ack, sg12], [xbcT])
                LD(BCT[c], xbcT[:, 8:12, :].rearrange("p a b -> p (a b)"), [xbcT], [DK("BCT", c)])
                pb = PS[3][:].bitcast(BF16)
                for i in range(8):
                    TR(PS[3], pb[:, i * 128:(i + 1) * 128], xbcT[:, i, :], identb[:], [xbcT, identb])
                A(lambda h: h.activation(out=xsc[:, 0:1024], in_=pb, func=AF.Copy), [], [PS[3], xsc])
                for i in range(2):
                    TR(PS[3], pb[:, i * 128:(i + 1) * 128], xbcT[:, 8 + i, :], identb[:], [xbcT, identb])
                A(lambda h: h.activation(out=xsc[:, 1024:1280], in_=pb[:, 0:256], func=AF.Copy), [], [PS[3], xsc])
                LD(XSB[c * 128:(c + 1) * 128, :], xsc[:], [xsc], [DK("XSB", c)])
                gates(gtc, [0])
                cumsums([0], PS[2])

                def store1(c=c):
                    A(lambda h: h.activation(out=hsb16[:], in_=Hs[:], func=AF.Copy), [Hs], [hsb16])
                    G(lambda h: h.tensor_copy(out=cmb16[:], in_=Cm[:]), [Cm], [cmb16])
                    LD(HS[c], hsb16[:], [hsb16], [DK("HS", c)])
                    LD(HM[c], cmb16[:], [cmb16], [DK("HM", c)])
                state_step(0, xsc, ptc, Hs, Cm, store1)
            del dbs[n0p1:]
            prep(WB_L0P2, lambda kc, c0: kc * 2048 + c0, ab_w_in[:, 0:1024], 8, 1024, lambda kc: gcol[:, 0, kc:kc + 1])
            prep(WB_L0P2, lambda kc, c0: kc * 2048 + 1024 + c0, ab_w_in[:, 5664:6688], 8, 1024, lambda kc: gcol[:, 0, kc:kc + 1])
            prep(WB_L0P2, lambda kc, c0: 16 * 1024 + kc * 1024 + c0, ab_w_out, 16, 1024, lambda kc: gcol[:, 4 + kc // 8, (kc % 8):(kc % 8) + 1])
            LD(exS[:, 0:1024], Hs[:], [Hs], [DK("exS", 0)])
            LD(exS[:, 1024:2056], Cm[:], [Cm], [DK("exS", 0)])
            allgather(exS, exG, DK("exS", 0), DK("exG", 0))

            S.barrier()
            S.clear_reg("arena")
            SC("L0P2_w")
            aoff[0] = 0
            Sel = ar("Sel", [112, 48, 128], BF16)
            Hsb = ar("Hsb", [128, 1024], F32)
            Cmb = ar("Cmb", [128, 1032], F32)
            sv = aoff[0]
            gath = ar("gath", [128, 2, 2056], F32)
            LD(gath[:], exG.rearrange("(r p) n -> p r n", p=128), [DK("exG", 0)], [gath])
            select2(Hsb[:], gath[:, 0, 0:1024], gath[:, 1, 0:1024], [gath, selt], [Hsb])
            select2(Cmb[:], gath[:, 0, 1024:2056], gath[:, 1, 1024:2056], [gath, selt], [Cmb])
            ones3 = ar("ones3", [112, 48, 128], BF16)
            selA = ar("selA", [112, 48, 128], BF16)
            selB = ar("selB", [112, 48, 128], BF16)
            G(lambda h: h.memset(ones3[:], 1.0), [], [ones3])
            G(lambda h: h.affine_select(out=selA[:], in_=ones3[:], pattern=[[1, 48], [0, 128]], compare_op=ALU.is_equal, fill=0.0, base=0, channel_multiplier=-1), [ones3], [selA])
            G(lambda h: h.affine_select(out=selB[:], in_=ones3[:], pattern=[[1, 48], [0, 128]], compare_op=ALU.is_equal, fill=0.0, base=64, channel_multiplier=-1), [ones3], [selB])
            G(lambda h: h.tensor_tensor(out=Sel[:], in0=selA[:], in1=selB[:], op=ALU.add), [selA, selB], [Sel])
            S.barrier()
            for tt in (ones3, selA, selB, gath):
                S.unregister("arena", tt)
            aoff[0] = sv
            WT2 = ar("W2", [128, 32, 1024], BF16, manual=True)
            Wzo = WT2[:, 0:16, :].rearrange("p a b -> p (a b)").rearrange("p (k n) -> p k n", k=8)
            Wo = WT2[:, 16:32, :]
            dtb = ar("dtb", [128, 32], F32)
            negA = ar("negA", [128, 32], F32)
            igfg = ar("igfg", [128, 32], F32)
            dsk = ar("dsk", [128, 16], F32)
            cw = None
            LD(dtb[:], ab_dt_bias.partition_broadcast(128), [WK], [dtb])
            LD(negA[:], ab_a_log.partition_broadcast(128), [WK], [negA])
            LD(igfg[:, 0:16], ab_ig_bias.partition_broadcast(128), [WK], [igfg])
            LD(igfg[:, 16:32], ab_fg_bias.partition_broadcast(128), [WK], [igfg])
            LD(dsk[:], ab_d_skip.partition_broadcast(128), [WK], [dsk])
            A(lambda h: h.activation(out=negA[:], in_=negA[:], func=AF.Exp), [], [negA])
            V(lambda h: h.tensor_scalar(out=negA[:], in0=negA[:], scalar1=-1.0, scalar2=None, op0=ALU.mult), [], [negA])
            hsb16 = ar("hsb16", [128, 1024], BF16)
            cmb16 = ar("cmb16", [128, 1032], BF16)
            gw = ar("gw", [128, 64], F32)
            G48 = ar("G48", [128, 48], F32)
            cst = ar("cst", [128, 96], F32)
            scw = ar("scw", [128, 48], F32)
            xe = ar("xe", [128, 1024], BF16)
            kw = ar("kw", [128, 1024], BF16)
            dts = ar("dts", [128, 32], F32)
            igs = ar("igs", [128, 16], F32)
            pt = [ar("pt%d" % i, [128, 3072], BF16) for i in range(2)]
            xsb = [ar("xsb%d" % i, [128, 1280], BF16) for i in range(2)]
            bct = [ar("bct%d" % i, [128, 4, 128], BF16) for i in range(2)]
            gt = [ar("gt%d" % i, [128, 64], F32) for i in range(2)]
            hsl = [ar("hsl%d" % i, [128, 1024], BF16) for i in range(2)]
            hml = [ar("hml%d" % i, [128, 1032], BF16) for i in range(2)]
            hTc = ar("hTc", [128, 8, 128], BF16)
            zo = ar("zo", [128, 2048], BF16)
            LS = ar("LS", [128, 2, 112], F32)
            X2t = ar("X2t", [112, 2, 128], BF16)
            tmpb = ar("tmpb", [112, 2, 128], BF16)
            DT = [ar("DT%d" % i, [128, 512], BF16) for i in range(2)]
            WT = [ar("WT%d" % i, [128, 512], BF16) for i in range(2)]
            cbt = ar("cbt", [128, 2, 128], BF16)
            xdt = ar("xdt", [128, 2, 1024], BF16)
            sfac = ar("sfac", [128, 48], F32)
            T1 = ar("T1", [128, 1024], F32)
            T2 = ar("T2", [128, 1024], F32)
            yv = ar("yv", [128, 1024], F32)
            ycat = ar("ycat", [128, 2048], BF16)
            kqT = ar("kqT", [128, 16, 128], BF16)
            hdir = ar("hdir", [128, 512], F32)
            dn = ar("dn", [128, 16], F32)
            YT = ar("YT", [128, 16, 128], BF16)
            sz, hmv, xo = T2, yv, T1
            wzoT, woT = Trk("wzoT"), Trk("woT")
            WB2v = WB_L0P2.rearrange("p (a n) -> p a n", a=32)
            for q in range(2):
                LD(WT2[:, q * 8:(q + 1) * 8, :], WB2v[:, q * 8:(q + 1) * 8, :], [], [wzoT])
            for q in range(2, 4):
                LD(WT2[:, q * 8:(q + 1) * 8, :], WB2v[:, q * 8:(q + 1) * 8, :], [], [woT])
            LD(gres[:], norm_g[0, 1].partition_broadcast(128), [WK], [gres])
            NEG4 = ar("NEG4", [128, 4, 128], BF16)
            for j in range(4):
                G(lambda h, j=j: h.tensor_copy(out=NEG4[:, j, :], in_=NEG[:, j % 2, :]), [NEG], [NEG4])
            V(lambda h: h.memset(LS[:], 0.0), [], [LS])
            V(lambda h: h.memset(X2t[:], 0.0), [], [X2t])
            V(lambda h: h.memset(tmpb[:], 0.0), [], [tmpb])

            def p2_loads(c):
                i = c % 2
                LD(pt[i][:], PT0[c * 128:(c + 1) * 128, :], [DK("PT0", c)], [pt[i]])
                LD(xsb[i][:], XSB[c * 128:(c + 1) * 128, :], [DK("XSB", c)], [xsb[i]])
                LD(bct[i][:].rearrange("p a b -> p (a b)"), BCT[c], [DK("BCT", c)], [bct[i]])
                LD(gt[i][:], GT[c * 128:(c + 1) * 128, :], [DK("GT", c)], [gt[i]])
                LD(hsl[i][:], HS[c], [DK("HS", c)], [hsl[i]])
                LD(hml[i][:], HM[c], [DK("HM", c)], [hml[i]])
                LD(xin[i][:], x_in[c * 128:(c + 1) * 128, :], [xk], [xin[i]])

            def decay_block(b, hds):
                bk = PS[b % 2]
                MM(bk, bk[:, :], identb[:], NEG4[:], [identb, NEG4], start=True, stop=False)
                for j, (hd, d) in enumerate(hds):
                    o = bk[:, j * 128:(j + 1) * 128]
                    MM(bk, o, Sel[:, hd, :], X2t[:, 0, :], [Sel, X2t], start=False, stop=(j == 3))
                for j, (hd, d) in enumerate(hds):
                    A(lambda h, j=j, hd=hd: h.activation(out=DT[b % 2][:, j * 128:(j + 1) * 128], in_=bk[:, j * 128:(j + 1) * 128], func=AF.Exp, bias=LS[:, 1, hd:hd + 1]), [LS], [bk, DT[b % 2]])

            SC("L0P2")
            p2_loads(NC - 1)
            for c in range(NC - 1, -1, -1):
                if c - 1 >= 0:
                    p2_loads(c - 1)
                i2 = c % 2
                ptc, xsc, bcc, gtc, hsc, hmc, xic = pt[i2], xsb[i2], bct[i2], gt[i2], hsl[i2], hml[i2], xin[i2]
                norm_chunk(x_in, c, xk, xic, load=False)
                transpose8(hb, 0, hTc[:], [hTc], eng="act")
                for j in range(4):
                    bk = PS[4 + j % 2]
                    for k in range(8):
                        MM(bk, bk[:, :], hTc[:, k, :], Wzo[:, k, j * 512:(j + 1) * 512], [hTc, wzoT], start=(k == 0), stop=(k == 7))
                    A(lambda h, bk=bk, j=j: h.activation(out=zo[:, j * 512:(j + 1) * 512], in_=bk[:, :], func=AF.Copy), [], [bk, zo])
                gates(gtc, [0, 1])
                cumsums([0, 1], PS[2])
                V(lambda h: h.tensor_copy(out=LS[:, 0, 0:48], in_=cst[:, 0:48]), [cst], [LS])
                V(lambda h: h.tensor_scalar(out=LS[:, 1, 0:32], in0=cst[:, 0:32], scalar1=-1.0, scalar2=None, op0=ALU.mult), [cst], [LS])
                V(lambda h: h.tensor_tensor(out=LS[:, 1, 32:48], in0=igs[:, 0:16], in1=cst[:, 32:48], op=ALU.subtract), [cst, igs], [LS])
                V(lambda h: h.tensor_copy(out=LS[:, :, 64:112], in_=LS[:, :, 0:48]), [], [LS])
                TR(PS[2], PS[2][0:112, 0:128], LS[:, 0, :], identf[:], [LS, identf])
                V(lambda h: h.tensor_copy(out=X2t[0:48, 0, :], in_=PS[2][0:48, 0:128]), [], [PS[2], X2t])
                V(lambda h: h.tensor_copy(out=tmpb[64:112, 0, :], in_=PS[2][64:112, 0:128]), [], [PS[2], tmpb])
                V(lambda h: h.tensor_tensor(out=X2t[64:112, 0, :], in0=PS[2][64:112, 0:128], in1=tmpb[64:112, 0, :], op=ALU.subtract), [tmpb], [PS[2], X2t])
                A(lambda h: h.activation(out=sfac[:], in_=cst[:, 0:48], func=AF.Exp), [cst], [sfac])
                for g in range(2):
                    MM(PS[2], PS[2][:, 256 + g * 128:384 + g * 128], bcc[:, g, :], bcc[:, 2 + g, :], [bcc])
                V(lambda h: h.tensor_copy(out=cbt[:], in_=v3(PS[2][:, 256:512], 2)), [], [PS[2], cbt])
                for d in range(2):
                    V(lambda h, d=d: h.tensor_tensor(out=v3(xdt[:, d, :], 16), in0=v3(xsc[:, 0:1024], 16), in1=bc_in(dts[:, d * 16:d * 16 + 16], 16, 64), op=ALU.mult), [xsc, dts], [xdt])
                for b in range(8):
                    g, e0 = b // 4, (b % 4) * 2
                    hds = [(g * 8 + e0, 0), (16 + g * 8 + e0, 1), (g * 8 + e0 + 1, 0), (16 + g * 8 + e0 + 1, 1)]
                    decay_block(b, hds)
                    V(lambda h, b=b, g=g: h.tensor_tensor(out=v3(WT[b % 2][:], 4), in0=v3(DT[b % 2][:], 4), in1=bc_mid(cbt[:, g, :], 4, 128), op=ALU.mult), [DT[b % 2], cbt], [WT[b % 2]])
                    for j, (hd, d) in enumerate(hds):
                        hh = hd % 16
                        e = hh % 8
                        MM(PS[4 + g], PS[4 + g][:, e * 64:(e + 1) * 64], WT[b % 2][:, j * 128:(j + 1) * 128], xdt[:, d, hh * 64:(hh + 1) * 64], [WT[b % 2], xdt], start=(d == 0), stop=(d == 1))
                A(lambda h: h.activation(out=hsb16[:], in_=Hsb[:], func=AF.Copy), [Hsb], [hsb16])
                for g in range(2):
                    MM(PS[6 + g], PS[6 + g][:, :], bcc[:, 2 + g, :], hsc[:, g * 512:(g + 1) * 512], [bcc, hsc])
                for g in range(2):
                    V(lambda h, g=g: h.tensor_tensor(out=v3(T1[:, g * 512:(g + 1) * 512], 8), in0=v3(PS[6 + g][:, :], 8), in1=bc_in(sfac[:, g * 8:g * 8 + 8], 8, 64), op=ALU.mult), [sfac], [PS[6 + g], T1])
                for g in range(2):
                    MM(PS[6 + g], PS[6 + g][:, :], bcc[:, 2 + g, :], hsb16[:, g * 512:(g + 1) * 512], [bcc, hsb16])
                for g in range(2):
                    V(lambda h, g=g: h.tensor_tensor(out=v3(T2[:, g * 512:(g + 1) * 512], 8), in0=v3(PS[6 + g][:, :], 8), in1=bc_in(sfac[:, 16 + g * 8:24 + g * 8], 8, 64), op=ALU.mult), [sfac], [PS[6 + g], T2])
                G(lambda h: h.tensor_tensor(out=T1[:], in0=T1[:], in1=T2[:], op=ALU.add), [T2], [T1])
                G(lambda h: h.tensor_tensor(out=v3(T2[:], 16), in0=v3(xsc[:, 0:1024], 16), in1=bc_in(dsk[:], 16, 64), op=ALU.mult), [xsc, dsk], [T2])
                G(lambda h: h.tensor_tensor(out=T1[:], in0=T1[:], in1=T2[:], op=ALU.add), [T2], [T1])
                for g in range(2):
                    V(lambda h, g=g: h.tensor_tensor(out=yv[:, g * 512:(g + 1) * 512], in0=PS[4 + g][:, :], in1=T1[:, g * 512:(g + 1) * 512], op=ALU.add), [T1], [PS[4 + g], yv])
                sigmoid_act(sz[:], zo[:, 0:1024], [zo], [sz])
                G(lambda h: h.tensor_tensor(out=sz[:], in0=sz[:], in1=zo[:, 0:1024], op=ALU.mult), [zo], [sz])
                V(lambda h: h.tensor_tensor(out=yv[:], in0=yv[:], in1=sz[:], op=ALU.mult), [sz], [yv])
                for g in range(2):
                    rstd_from(yv[:, g * 512:(g + 1) * 512], 512, 1 + g, sqj[:, g * 512:(g + 1) * 512], [yv])
                for g in range(2):
                    V(lambda h, g=g: h.tensor_scalar(out=ycat[:, g * 512:(g + 1) * 512], in0=yv[:, g * 512:(g + 1) * 512], scalar1=ss[:, 1 + g:2 + g], scalar2=None, op0=ALU.mult), [yv, ss], [ycat])
                transpose8(ptc, 1024, kqT[:, 0:8, :], [kqT], eng="act")
                transpose8(ptc, 0, kqT[:, 8:16, :], [kqT], eng="act")
                A(lambda h: h.activation(out=cmb16[:], in_=Cmb[:], func=AF.Copy), [Cmb], [cmb16])
                for hf in range(2):
                    h0 = hf * 4
                    for j in range(4):
                        MM(PS[4], PS[4][:, j * 128:(j + 1) * 128], kqT[:, h0 + j, :], kqT[:, 8 + h0 + j, :], [kqT])
                    for bb in range(2):
                        b = 8 + hf * 2 + bb
                        ha = h0 + bb * 2
                        hds = [(32 + ha, 0), (40 + ha, 1), (32 + ha + 1, 0), (40 + ha + 1, 1)]
                        decay_block(b, hds)
                        V(lambda h, b=b, bb=bb: h.tensor_tensor(out=WT[b % 2][:].rearrange("p (a d b) -> p a d b", a=2, d=2),
                                                               in0=v3(PS[4][:, bb * 256:(bb + 1) * 256], 2).unsqueeze(2).to_broadcast([128, 2, 2, 128]),
                                                               in1=DT[b % 2][:].rearrange("p (a d b) -> p a d b", a=2, d=2), op=ALU.mult), [DT[b % 2]], [PS[4], WT[b % 2]])
                        for j, (hd, d) in enumerate(hds):
                            hh = (hd - 32) % 8
                            bkn = PS[5 + d]
                            MM(bkn, bkn[:, (hh % 4) * 128:(hh % 4 + 1) * 128], WT[b % 2][:, j * 128:(j + 1) * 128], ptc[:, 2048 + hh * 128:2048 + (hh + 1) * 128], [WT[b % 2], ptc])
                            MM(PS[2], PS[2][:, d * 8 + hh:d * 8 + hh + 1], WT[b % 2][:, j * 128:(j + 1) * 128], onesb[:, 0:1], [WT[b % 2], onesb])
                    for d in range(2):
                        st16 = hmc if d == 0 else cmb16
                        for j in range(4):
                            hh = h0 + j
                            MM(PS[7], PS[7][:, j * 128:(j + 1) * 128], kqT[:, 8 + hh, :], st16[:, hh * 128:(hh + 1) * 128], [kqT, st16])
                            MM(PS[2], PS[2][:, 16 + d * 8 + hh:17 + d * 8 + hh], kqT[:, 8 + hh, :], st16[:, 1024 + hh:1025 + hh], [kqT, st16])
                        sc = sfac[:, 32 + d * 8 + h0:32 + d * 8 + h0 + 4]
                        V(lambda h, sc=sc: h.tensor_tensor(out=v3(hdir[:], 4), in0=v3(PS[7][:, :], 4), in1=bc_in(sc, 4, 128), op=ALU.mult), [sfac], [PS[7], hdir])
                        V(lambda h, d=d: h.tensor_tensor(out=hdir[:], in0=hdir[:], in1=PS[5 + d][:, :], op=ALU.add), [], [hdir, PS[5 + d]])
                        V(lambda h, d=d, sc=sc, h0=h0: h.tensor_tensor(out=dn[:, 0:4], in0=PS[2][:, 16 + d * 8 + h0:16 + d * 8 + h0 + 4], in1=sc, op=ALU.mult), [sfac], [PS[2], dn])
                        V(lambda h, d=d, h0=h0: h.tensor_tensor(out=dn[:, 0:4], in0=dn[:, 0:4], in1=PS[2][:, d * 8 + h0:d * 8 + h0 + 4], op=ALU.add), [], [PS[2], dn])
                        V(lambda h: h.scalar_tensor_tensor(out=dn[:, 4:8], in0=dn[:, 0:4], scalar=-1.0, in1=dn[:, 0:4], op0=ALU.mult, op1=ALU.max), [], [dn])
                        V(lambda h: h.tensor_scalar(out=dn[:, 4:8], in0=dn[:, 4:8], scalar1=1.0, scalar2=None, op0=ALU.max), [], [dn])
                        V(lambda h: h.reciprocal(out=dn[:, 8:12], in_=dn[:, 4:8]), [], [dn])
                        if d == 0:
                            V(lambda h, h0=h0: h.tensor_tensor(out=v3(hmv[:, h0 * 128:(h0 + 4) * 128], 4), in0=v3(hdir[:], 4), in1=bc_in(dn[:, 8:12], 4, 128), op=ALU.mult), [hdir, dn], [hmv])
                        else:
                            V(lambda h: h.tensor_tensor(out=v3(hdir[:], 4), in0=v3(hdir[:], 4), in1=bc_in(dn[:, 8:12], 4, 128), op=ALU.mult), [dn], [hdir])
                            V(lambda h, h0=h0: h.tensor_tensor(out=hmv[:, h0 * 128:(h0 + 4) * 128], in0=hmv[:, h0 * 128:(h0 + 4) * 128], in1=hdir[:], op=ALU.add), [hdir], [hmv])
                sigmoid_act(sz[:], zo[:, 1024:2048], [zo], [sz])
                V(lambda h: h.tensor_tensor(out=hmv[:], in0=hmv[:], in1=sz[:], op=ALU.mult), [sz], [hmv])
                G(lambda h: h.tensor_tensor(out=sz[:], in0=hmv[:], in1=hmv[:], op=ALU.mult), [hmv], [sz])
                V(lambda h: h.tensor_reduce(out=ss[:, 8:16], in_=v3(sz[:], 8), axis=AX.X, op=ALU.add), [sz], [ss])
                A(lambda h: h.activation(out=ss[:, 8:16], in_=ss[:, 8:16], func=AF.Ln, scale=1.0 / 128, bias=epsc[:, 0:1]), [epsc], [ss])
                A(lambda h: h.activation(out=ss[:, 8:16], in_=ss[:, 8:16], func=AF.Exp, scale=-0.5), [], [ss])
                V(lambda h: h.tensor_tensor(out=v3(ycat[:, 1024:2048], 8), in0=v3(hmv[:], 8), in1=bc_in(ss[:, 8:16], 8, 128), op=ALU.mult), [hmv, ss], [ycat])
                transpose8(ycat, 0, YT[:, 0:8, :], [YT], eng="act")
                transpose8(ycat, 1024, YT[:, 8:16, :], [YT], eng="act")
                for n2 in range(2):
                    for kk in range(16):
                        MM(PS[6 + n2], PS[6 + n2][:, :], YT[:, kk, :], Wo[:, kk, n2 * 512:(n2 + 1) * 512], [YT, woT], start=(kk == 0), stop=(kk == 15))
                resid_out([PS[6], PS[7]], xic, X1, c, "X1", xo)
                if c == NC - 1:
                    LD(exR[0], X1[T - 1:T, :], [DK("X1", c)], [DK("exR0", 0)])
                    allgather(exR[0], exRG[0], DK("exR0", 0), DK("exRG0", 0))
                state_step(1, xsc, ptc, Hsb, Cmb, lambda: None)
            prep_ffn_up(0)
            prep_ffn_dn(0)

        NBT = 256

        def ffn(li, src, src_name, dst, dst_name):
            reset_arena()
            SC("FFN%d_w" % li)
            NB = T // NBT
            CPB = NBT // 128
            Wup = ar("Wup", [128, 8, 5632], BF16, manual=True)
            Wdn = ar("Wdn", [128, 22, 1024], BF16, manual=True)
            fw = ar("fw", [128, 44, 3], F32)
            fb = ar("fb", [128, 44], F32)
            hT2 = [ar("hT2_%d" % i, [128, 8, NBT + 2], BF16, manual=True) for i in range(3)]
            hmain = [Trk("hmain%d" % i) for i in range(3)]
            hleft = [Trk("hleft%d" % i) for i in range(3)]
            hright = [Trk("hright%d" % i) for i in range(3)]
            usb = [ar("usb%d" % i, [128, NBT + 2], F32) for i in range(3)]
            cgs = [ar("cg%d" % i, [128, NBT], F32) for i in range(2)]
            cvs = [ar("cvv%d" % i, [128, NBT], F32) for i in range(2)]
            ggs = [ar("gg%d" % i, [128, NBT], F32) for i in range(2)]
            gTs = [ar("gT%d" % i, [128, NBT], BF16) for i in range(3)]
            xos = [ar("xo0", [128, 1024], F32)] * 2
            neg20 = ar("neg20", [128, NBT], F32)
            G(lambda h: h.memset(neg20[:], -20.0), [], [neg20])
            xr = [ar("xr0", [128, 1024], F32)] * 2
            for j in range(3):
                LDS(fw[:, :, j], ffn_conv_w[li, j].rearrange("(i p) -> p i", p=128), [WK], [fw])
            LDS(fb[:], ffn_conv_b[li].rearrange("(i p) -> p i", p=128), [WK], [fb])
            LD(gres[:], norm_g[li, 3].partition_broadcast(128), [WK], [gres])
            wgT = [Trk("wffn0"), Trk("wffn1")]
            WBu = WB_up[li].rearrange("p (k n) -> p k n", k=8)
            WBd = WB_dn[li].rearrange("p (k n) -> p k n", k=22)
            for g in range(2):
                for a in (g * 1408, 2816 + g * 1408):
                    LD(Wup[:, :, a:a + 1408], WBu[:, :, a:a + 1408], [], [wgT[g]])
                LD(Wdn[:, g * 11:(g + 1) * 11, :], WBd[:, g * 11:(g + 1) * 11, :], [], [wgT[g]])
            for i in range(3):
                V(lambda h, i=i: h.memset(hT2[i][:], 0.0), [], [hmain[i], hleft[i], hright[i]])
            hcand = ar("hcand", [128, 8, 2], BF16)
            halo1 = ar("halo1", [128, 8, 1], BF16)
            halo_cands(exRG[li], DK("exRG%d" % li, 0), 2, hcand)
            select2(halo1[:], hcand[:, :, 0:1], hcand[:, :, 1:2], [hcand, selt], [halo1])

            def norm_block(b):
                p, pp = b % 3, (b - 1) % 3
                for cc in range(CPB):
                    c = b * CPB + cc
                    norm_chunk(src, c, DK(src_name, c), xin[c % 2])
                    transpose8(hb, 0, hT2[p][:, :, 1 + cc * 128:1 + (cc + 1) * 128], [hmain[p]])
                if b > 0:
                    V(lambda h: h.tensor_copy(out=hT2[pp][:, :, NBT + 1:NBT + 2], in_=hT2[p][:, :, 1:2]), [hmain[p]], [hright[pp]])
                    V(lambda h: h.tensor_copy(out=hT2[p][:, :, 0:1], in_=hT2[pp][:, :, NBT:NBT + 1]), [hmain[pp]], [hleft[p]])
                else:
                    V(lambda h: h.memset(hT2[p][:, :, 0:1], 0.0), [], [hleft[p]])
                if b + 1 == NB:
                    V(lambda h: h.tensor_copy(out=hT2[p][:, :, NBT + 1:NBT + 2], in_=halo1[:]), [halo1], [hright[p]])

            SC("FFN%d" % li)
            norm_block(0)
            if NB > 1:
                norm_block(1)
            tctr = [0]
            accb = [[PS[0], PS[1]], [PS[2], PS[7]]]
            for b in range(NB):
                p = b % 3
                if b + 2 < NB:
                    norm_block(b + 2)
                for i in range(22):
                    hr = [hmain[p], hleft[p], hright[p], wgT[i // 11]]
                    cg, cvv, gg, gTi = cgs[i % 2], cvs[i % 2], ggs[i % 2], gTs[i % 3]
                    for t in (i, 22 + i):
                        q = tctr[0] % 3
                        tctr[0] += 1
                        bk = PS[4 + q]
                        for k in range(8):
                            MM(bk, bk[:, 0:NBT + 2], Wup[:, k, t * 128:(t + 1) * 128], hT2[p][:, k, :], hr, start=(k == 0), stop=(k == 7))
                        us = usb[q]
                        A(lambda h, bk=bk, us=us: h.activation(out=us[:], in_=bk[:, 0:NBT + 2], func=AF.Copy), [], [bk, us])
                        cv = cg if t < 22 else cvv
                        A(lambda h, us=us, cv=cv, t=t: h.activation(out=cv[:], in_=us[:, 1:NBT + 1], func=AF.Identity, scale=fw[:, t, 1:2], bias=fb[:, t:t + 1]), [us, fw, fb], [cv])
                        V(lambda h, us=us, cv=cv, t=t: h.scalar_tensor_tensor(out=cv[:], in0=us[:, 0:NBT], scalar=fw[:, t, 0:1], in1=cv[:], op0=ALU.mult, op1=ALU.add), [us, fw], [cv])
                        V(lambda h, us=us, cv=cv, t=t: h.scalar_tensor_tensor(out=cv[:], in0=us[:, 2:NBT + 2], scalar=fw[:, t, 2:3], in1=cv[:], op0=ALU.mult, op1=ALU.add), [us, fw], [cv])
                    A(lambda h, cg=cg, gg=gg: h.activation(out=gg[:], in_=cg[:], func=AF.Square), [cg], [gg])
                    V(lambda h, gg=gg: h.tensor_scalar(out=gg[:], in0=gg[:], scalar1=0.044715, scalar2=1.0, op0=ALU.mult, op1=ALU.add), [], [gg])
                    G(lambda h, gg=gg, cg=cg: h.tensor_tensor(out=gg[:], in0=gg[:], in1=cg[:], op=ALU.mult), [cg], [gg])
                    V(lambda h, gg=gg: h.tensor_scalar(out=gg[:], in0=gg[:], scalar1=-20.0, scalar2=None, op0=ALU.max), [], [gg])
                    A(lambda h, gg=gg: h.activation(out=gg[:], in_=gg[:], func=AF.Exp, scale=-1.5957691216), [], [gg])
                    A(lambda h, gg=gg: h.activation(out=gg[:], in_=gg[:], func=AF.Ln, bias=1.0), [], [gg])
                    A(lambda h, gg=gg: h.activation(out=gg[:], in_=gg[:], func=AF.Exp, scale=-1.0), [], [gg])
                    G(lambda h, gg=gg, cg=cg: h.tensor_tensor(out=gg[:], in0=gg[:], in1=cg[:], op=ALU.mult), [cg], [gg])
                    G(lambda h, gg=gg, cvv=cvv, gTi=gTi: h.tensor_tensor(out=gTi[:], in0=gg[:], in1=cvv[:], op=ALU.mult), [gg, cvv], [gTi])
                    for m in range(CPB):
                        for n2 in range(2):
                            bk = accb[m][n2]
                            MM(bk, bk[:, :], gTi[:, m * 128:(m + 1) * 128], Wdn[:, i, n2 * 512:(n2 + 1) * 512], [gTi, wgT[i // 11]], start=(i == 0), stop=(i == 21))
                for m in range(CPB):
                    c = b * CPB + m
                    xrc = xr[c % 2]
                    LD(xrc[:], src[c * 128:(c + 1) * 128, :], [DK(src_name, c)], [xrc])
                    resid_out(accb[m], xrc, dst, c, dst_name, xos[c % 2])
            if li == 0:
                for (s0, n, d0, gq) in [(1024, 2048, 0, False), (0, 512, 2048, True), (512, 512, 2560, False), (3072, 32, 3072, False)]:
                    prep(WB_g, lambda kc, c0, d0=d0: kc * 3104 + d0 + c0, c_w_in[:, s0:s0 + n], 8, n,
                         (lambda kc: gcolq[:, kc:kc + 1]) if gq else (lambda kc: gcol[:, 2, kc:kc + 1]))
                prep(WB_o1, lambda kc, c0: kc * 1024 + c0, c_w_out, 8, 1024, lambda kc: gcol[:, 6, kc:kc + 1])

        def layer1():
            reset_arena()
            SC("L1P1_w")
            W1 = ar("Wg", [128, 8, 3104], BF16, manual=True)
            gw2 = ar("gw2", [16, 2, 512], BF16)
            gw2f = ar("gw2f", [16, 2, 512], F32)
            gbias = ar("gbias", [128, 2, 4], F32)
            ones1 = ar("ones1", [128, 128], F32)
            Sst = ar("Sst", [128, 1024], F32)
            n0 = len(dbs)
            hTc = ar2("hTc", [128, 8, 128], BF16)
            vr = ar2("vr", [128, 2048], BF16, 1)
            qkT = ar2("qkT", [128, 8, 128], BF16)
            lt = ar2("lt", [128, 8, 128], F32)
            glr = ar2("glr", [16, 2, 128], BF16)
            Lc = ar2("Lc", [128, 8, 128], F32)
            ex = ar2("ex", [128, 8, 128], F32)
            kend = ar2("kend", [128, 4, 128], BF16)
            kendT = ar2("kendT", [128, 512], BF16)
            S16 = ar2("S16", [128, 1024], BF16)
            tots = ar2("tots", [128, 16], F32)
            for d in range(2):
                LD(gw2f[:, d, :], c_gate_w2[d], [WK], [gw2f])
            V(lambda h: h.tensor_copy(out=gw2[:], in_=gw2f[:]), [gw2f], [gw2])
            for d in range(2):
                LDS(gbias[:, d, :], c_gate_b[d].rearrange("(j p) -> p j", p=128), [WK], [gbias])
            V(lambda h: h.tensor_scalar(out=gbias[:], in0=gbias[:], scalar1=-1.0, scalar2=None, op0=ALU.mult), [], [gbias])
            V(lambda h: h.memset(ones1[:], 1.0), [], [ones1])
            WBgv = WB_g.rearrange("p (k n) -> p k n", k=8)
            for q in range(2):
                LD(W1[:, q * 4:(q + 1) * 4, :], WBgv[:, q * 4:(q + 1) * 4, :], [], [W1])
            V(lambda h: h.memset(Sst[:], 0.0), [], [Sst])

            def scans(ltile, dirs, Lc, tots):
                for d in dirs:
                    for hh in range(4):
                        t = d * 4 + hh
                        V(lambda h, t=t: h.tensor_tensor_scan(out=Lc[:, t, :], data0=ones1[:], data1=ltile[:, t, :], initial=0.0, op0=ALU.mult, op1=ALU.add), [ones1, ltile], [Lc])
                    V(lambda h, d=d: h.tensor_copy(out=tots[:, d * 4:d * 4 + 4], in_=Lc[:, d * 4:d * 4 + 4, 127:128].rearrange("p a b -> p (a b)")), [Lc], [tots])
                    if d == 1:
                        V(lambda h: h.tensor_tensor(out=Lc[:, 4:8, :], in0=ltile[:, 4:8, :], in1=Lc[:, 4:8, :], op=ALU.subtract), [ltile], [Lc])
                        V(lambda h: h.tensor_tensor(out=Lc[:, 4:8, :], in0=Lc[:, 4:8, :], in1=bc_in(tots[:, 4:8], 4, 128), op=ALU.add), [tots], [Lc])
                V(lambda h: h.tensor_scalar(out=tots[:, 8:16], in0=tots[:, 0:8], scalar1=-1.0 / 16, scalar2=None, op0=ALU.mult), [], [tots])

            def gla_state(d, qk_tile, vr_tile, Stt, store_fn, Lc, tots, ex, kend, kendT):
                for hh in range(4):
                    t = d * 4 + hh
                    A(lambda h, t=t: h.activation(out=ex[:, t, :], in_=Lc[:, t, :], func=AF.Exp, scale=1.0 / 16, bias=tots[:, 8 + t:9 + t]), [Lc, tots], [ex])
                V(lambda h: h.tensor_tensor(out=kend[:], in0=qk_tile[:, 4:8, :], in1=ex[:, d * 4:d * 4 + 4, :], op=ALU.mult), [qk_tile, ex], [kend])
                pb = PS[3][:].bitcast(BF16)
                for hh in range(4):
                    TR(PS[3], pb[:, hh * 128:(hh + 1) * 128], kend[:, hh, :], identb[:], [kend, identb])
                V(lambda h: h.tensor_copy(out=kendT[:], in_=pb[:, 0:512]), [], [PS[3], kendT])
                for hh in range(4):
                    bk = PS[4 + hh // 2]
                    MM(bk, bk[:, (hh % 2) * 256:(hh % 2 + 1) * 256], kendT[:, hh * 128:(hh + 1) * 128], vr_tile[:, hh * 256:(hh + 1) * 256], [kendT, vr_tile])
                store_fn()
                A(lambda h: h.activation(out=tots[:, 0:4], in_=tots[:, 8 + d * 4:12 + d * 4], func=AF.Exp), [], [tots])
                for hh in range(4):
                    bk = PS[4 + hh // 2]
                    V(lambda h, hh=hh, bk=bk: h.scalar_tensor_tensor(out=Stt[:, hh * 256:(hh + 1) * 256], in0=Stt[:, hh * 256:(hh + 1) * 256], scalar=tots[:, hh:hh + 1], in1=bk[:, (hh % 2) * 256:(hh % 2 + 1) * 256], op0=ALU.mult, op1=ALU.add), [tots], [Stt, bk])

            x2k = lambda c: DK("X2", c)
            SC("L1P1")
            for c in range(NC):
                setpar(c)
                i2 = c % 2
                norm_chunk(X2, c, x2k(c), xin[i2])
                transpose8(hb, 0, hTc[:], [hTc])
                for j in range(4):
                    bk = PS[j % 2]
                    for k in range(8):
                        MM(bk, bk[:, :], hTc[:, k, :], W1[:, k, j * 512:(j + 1) * 512], [hTc, W1], start=(k == 0), stop=(k == 7))
                    if j % 2 == 0:
                        A(lambda h, bk=bk, j=j: h.activation(out=vr[:, j * 512:(j + 1) * 512], in_=bk[:, :], func=AF.Copy), [], [bk, vr])
                    else:
                        V(lambda h, bk=bk, j=j: h.tensor_copy(out=vr[:, j * 512:(j + 1) * 512], in_=bk[:, :]), [], [bk, vr])
                LD(VR[c * 128:(c + 1) * 128, :], vr[:], [vr], [DK("VR", c)])
                for i in range(8):
                    bk = PS[6 + i // 4]
                    for k in range(8):
                        MM(bk, bk[:, (i % 4) * 128:(i % 4 + 1) * 128], W1[:, k, 2048 + i * 128:2048 + (i + 1) * 128], hTc[:, k, :], [hTc, W1], start=(k == 0), stop=(k == 7))
                    if i % 4 == 3:
                        q4 = i // 4
                        A(lambda h, bk=bk, q4=q4: h.activation(out=qkT[:, q4 * 4:q4 * 4 + 4, :], in_=v3(bk[:, :], 4), func=AF.Copy), [], [bk, qkT])
                LD(QKT[c], qkT[:].rearrange("p a b -> p (a b)"), [qkT], [DK("QKT", c)])
                for d in range(2):
                    for k in range(8):
                        MM(PS[2], PS[2][0:16, d * 128:(d + 1) * 128], W1[:, k, 3072 + d * 16:3088 + d * 16], hTc[:, k, :], [hTc, W1], start=(k == 0), stop=(k == 7))
                V(lambda h: h.tensor_copy(out=glr[:], in_=v3(PS[2][0:16, 0:256], 2)), [], [PS[2], glr])
                for d in range(2):
                    bk = PS[d]
                    for j in range(4):
                        MM(bk, bk[:, j * 128:(j + 1) * 128], gw2[:, d, j * 128:(j + 1) * 128], glr[:, d, :], [gw2, glr])
                    for j in range(4):
                        t = d * 4 + j
                        A(lambda h, bk=bk, j=j, t=t, d=d: h.activation(out=lt[:, t, :], in_=bk[:, j * 128:(j + 1) * 128], func=AF.Exp, scale=-1.0, bias=gbias[:, d, j:j + 1]), [gbias], [bk, lt])
                A(lambda h: h.activation(out=lt[:], in_=lt[:], func=AF.Ln, bias=1.0), [], [lt])
                LD(LTd[c], lt[:].rearrange("p a b -> p (a b)"), [lt], [DK("LTd", c)])
                scans(lt, [0], Lc, tots)

                def store1(c=c):
                    A(lambda h: h.activation(out=S16[:], in_=Sst[:], func=AF.Copy), [Sst], [S16])
                    LD(SG[c], S16[:], [S16], [DK("SG", c)])
                gla_state(0, qkT, vr, Sst, store1, Lc, tots, ex, kend, kendT)
            prep_ffn_up(1)
            LD(exS1, Sst[:], [Sst], [DK("exS1", 0)])
            allgather(exS1, exG1, DK("exS1", 0), DK("exG1", 0))
            del dbs[n0:]

            S.barrier()
            S.clear_reg("arena")
            aoff[0] = 0
            SC("L1P2_w")
            Wo = ar("Wo1", [128, 8, 1024], BF16, manual=True)
            ones1 = ar("ones1", [128, 128], F32)
            Sb = ar("Sb", [128, 1024], F32)
            gath1 = ar("gath1", [128, 2, 1024], F32)
            n0 = len(dbs)
            vr = ar2("vr", [128, 2048], BF16, 2)
            qkT = ar2("qkT", [128, 8, 128], BF16, 2)
            lt = ar2("lt", [128, 8, 128], F32, 2)
            sfl = ar2("sfl", [128, 1024], BF16, 2)
            xres = ar2("xres", [128, 1024], F32, 2)
            Lc = ar2("Lc", [128, 8, 128], F32)
            ex = ar2("ex", [128, 8, 128], F32)
            kend = ar2("kend", [128, 4, 128], BF16)
            kendT = ar2("kendT", [128, 512], BF16)
            tots = ar2("tots", [128, 16], F32)
            Sb16 = ar2("Sb16", [128, 1024], BF16)
            qin = ar2("qin", [128, 8, 128], BF16)
            kout = ar2("kout", [128, 8, 128], BF16)
            ex2 = ar2("ex2", [128, 8, 128], F32)
            ex3 = ar2("ex3", [128, 8, 128], F32)
            attm = ar2("attm", [128, 8, 128], BF16)
            ov = ar2("ov", [128, 1024], F32)
            t4 = ar2("t4", [128, 1024], F32)
            t5 = ar2("t5", [128, 1024], F32)
            ycat = ar2("ycat1", [128, 1024], BF16)
            YT = ar2("YT1", [128, 8, 128], BF16)
            xo = ar2("xo1", [128, 1024], F32)
            V(lambda h: h.memset(ones1[:], 1.0), [], [ones1])
            LD(Wo[:], WB_o1.rearrange("p (k n) -> p k n", k=8), [], [Wo])
            LD(gres[:], norm_g[1, 1].partition_broadcast(128), [WK], [gres])
            LD(gath1[:], exG1.rearrange("(r p) n -> p r n", p=128), [DK("exG1", 0)], [gath1])
            select2(Sb[:], gath1[:, 0, :], gath1[:, 1, :], [gath1, selt], [Sb])

            def p2_loads(c):
                setpar(c)
                LD(vr[:], VR[c * 128:(c + 1) * 128, :], [DK("VR", c)], [vr])
                LD(qkT[:].rearrange("p a b -> p (a b)"), QKT[c], [DK("QKT", c)], [qkT])
                LD(lt[:].rearrange("p a b -> p (a b)"), LTd[c], [DK("LTd", c)], [lt])
                LD(sfl[:], SG[c], [DK("SG", c)], [sfl])
                LD(xres[:], X2[c * 128:(c + 1) * 128, :], [x2k(c)], [xres])

            SC("L1P2")
            p2_loads(NC - 1)
            for c in range(NC - 1, -1, -1):
                if c - 1 >= 0:
                    p2_loads(c - 1)
                setpar(c)
                scans(lt, [0, 1], Lc, tots)
                A(lambda h: h.activation(out=ex2[:], in_=Lc[:], func=AF.Exp, scale=-1.0 / 16), [Lc], [ex2])
                for d in range(2):
                    V(lambda h, d=d: h.tensor_tensor(out=qin[:, d * 4:d * 4 + 4, :], in0=qkT[:, 0:4, :], in1=ex2[:, d * 4:d * 4 + 4, :], op=ALU.mult), [qkT, ex2], [qin])
                A(lambda h: h.activation(out=ex3[:], in_=Lc[:], func=AF.Exp, scale=1.0 / 16), [Lc], [ex3])
                for d in range(2):
                    G(lambda h, d=d: h.tensor_tensor(out=kout[:, d * 4:d * 4 + 4, :], in0=qkT[:, 4:8, :], in1=ex3[:, d * 4:d * 4 + 4, :], op=ALU.mult), [qkT, ex3], [kout])
                for d in range(2):
                    bk = PS[d]
                    for hh in range(4):
                        MM(bk, bk[:, hh * 128:(hh + 1) * 128], kout[:, d * 4 + hh, :], qin[:, d * 4 + hh, :], [kout, qin])
                    V(lambda h, d=d, bk=bk: h.tensor_tensor(out=attm[:, d * 4:d * 4 + 4, :], in0=v3(bk[:, :], 4), in1=bc_mid(maskb16[:, d, :], 4, 128), op=ALU.mult), [maskb16], [bk, attm])
                A(lambda h: h.activation(out=Sb16[:], in_=Sb[:], func=AF.Copy), [Sb], [Sb16])
                for hh in range(4):
                    bk = PS[6 + hh // 2]
                    o = bk[:, (hh % 2) * 256:(hh % 2 + 1) * 256]
                    vh = vr[:, hh * 256:(hh + 1) * 256]
                    MM(bk, o, attm[:, hh, :], vh, [attm, vr], start=True, stop=False)
                    MM(bk, o, qin[:, hh, :], sfl[:, hh * 256:(hh + 1) * 256], [qin, sfl], start=False, stop=False)
                    MM(bk, o, attm[:, 4 + hh, :], vh, [attm, vr], start=False, stop=False)
                    MM(bk, o, qin[:, 4 + hh, :], Sb16[:, hh * 256:(hh + 1) * 256], [qin, Sb16], start=False, stop=True)
                for q in range(2):
                    A(lambda h, q=q: h.activation(out=ov[:, q * 512:(q + 1) * 512], in_=PS[6 + q][:, :], func=AF.Copy), [], [PS[6 + q], ov])
                G(lambda h: h.tensor_tensor(out=t4[:], in0=ov[:], in1=ov[:], op=ALU.mult), [ov], [t4])
                V(lambda h: h.tensor_reduce(out=ss[:, 8:12], in_=v3(t4[:], 4), axis=AX.X, op=ALU.add), [t4], [ss])
                A(lambda h: h.activation(out=ss[:, 8:12], in_=ss[:, 8:12], func=AF.Ln, scale=1.0 / 256, bias=epsc[:, 0:1]), [epsc], [ss])
                A(lambda h: h.activation(out=ss[:, 8:12], in_=ss[:, 8:12], func=AF.Exp, scale=-0.5), [], [ss])
                sigmoid_act(t5[:], vr[:, 1024:2048], [vr], [t5])
                G(lambda h: h.tensor_tensor(out=t5[:], in0=t5[:], in1=vr[:, 1024:2048], op=ALU.mult), [vr], [t5])
                G(lambda h: h.tensor_tensor(out=v3(ov[:], 4), in0=v3(ov[:], 4), in1=bc_in(ss[:, 8:12], 4, 256), op=ALU.mult), [ss], [ov])
                V(lambda h: h.tensor_tensor(out=ycat[:], in0=ov[:], in1=t5[:], op=ALU.mult), [ov, t5], [ycat])
                transpose8(ycat, 0, YT[:], [YT])
                for n2 in range(2):
                    for kk in range(8):
                        MM(PS[4 + n2], PS[4 + n2][:, :], YT[:, kk, :], Wo[:, kk, n2 * 512:(n2 + 1) * 512], [YT, Wo], start=(kk == 0), stop=(kk == 7))
                resid_out([PS[4], PS[5]], xres, X3, c, "X3", xo)
                if c == NC - 1:
                    LD(exR[1], X3[T - 1:T, :], [DK("X3", c)], [DK("exR1", 0)])
                    allgather(exR[1], exRG[1], DK("exR1", 0), DK("exRG1", 0))
                gla_state(1, qkT, vr, Sb, lambda: None, Lc, tots, ex, kend, kendT)
            prep_ffn_dn(1)
            del dbs[n0:]

        final = None
        layer0()
        if dbg == "x1":
            final = (X1, "X1")
        else:
            ffn(0, X1, "X1", X2, "X2")
            if dbg == "x2":
                final = (X2, "X2")
            else:
                layer1()
                if dbg == "x3":
                    final = (X3, "X3")
                else:
                    ffn(1, X3, "X3", out, "out")
        SC(None)
        if final is not None:
            for c in range(NC):
                t = xin[c % 2]
                LD(t[:], final[0][c * 128:(c + 1) * 128, :], [DK(final[1], c)], [t])
                LD(out[c * 128:(c + 1) * 128, :], t[:], [t], [DK("out", c)])
        S.flush()
        print("instructions:", S.nins, "est_us=%.0f" % (S.est_ns / 1e3), flush=True)
    return nc


_CACHE = {}


def _core_inputs(inputs, b, half):
    f = lambda k: np.asarray(inputs[k], dtype=np.float32)
    x = f("x")[b]
    S_ = x.shape[0]
    T = S_ // 2
    rv = half == 1
    if not rv:
        xl = x[:T]
        xh = x[T:T + 2]
    else:
        xl = x[T:][::-1]
        xh = x[T - 2:T][::-1]
    w_in = f("ab_w_in")[0]
    cw = f("ab_conv_w")[0]
    dtb = f("ab_dt_bias")[0]
    alog = f("ab_a_log")[0]
    igb = f("ab_ig_bias")[0]
    fgb = f("ab_fg_bias")[0]
    c_w_in = f("c_w_in")[0]
    gw2 = f("c_gate_w2")[0]
    gb = f("c_gate_b")[0]
    fcw = f("ffn_conv_w")
    if rv:
        perm = np.arange(w_in.shape[1])
        perm[2560:2576], perm[2576:2592] = np.arange(2576, 2592), np.arange(2560, 2576)
        perm[6688:6696], perm[6696:6704] = np.arange(6696, 6704), np.arange(6688, 6696)
        perm[6704:6712], perm[6712:6720] = np.arange(6712, 6720), np.arange(6704, 6712)
        w_in = w_in[:, perm]
        cw = cw[::-1]
        dtb, alog, igb, fgb = dtb[::-1], alog[::-1], igb[::-1], fgb[::-1]
        p2 = np.arange(c_w_in.shape[1])
        p2[3072:3088], p2[3088:3104] = np.arange(3088, 3104), np.arange(3072, 3088)
        c_w_in = c_w_in[:, p2]
        gw2 = gw2[::-1]
        gb = gb[::-1]
        fcw = fcw[:, ::-1]
    c = np.ascontiguousarray
    return {
        "x": c(xl), "x_halo": c(xh), "sel": np.array([0.0, 1.0] if half == 0 else [1.0, 0.0], np.float32),
        "norm_g": c(f("norm_g")),
        "ab_w_in": c(w_in), "ab_conv_w": c(cw), "ab_conv_b": c(f("ab_conv_b")[0]),
        "ab_dt_bias": c(dtb).reshape(32), "ab_a_log": c(alog).reshape(32),
        "ab_d_skip": c(f("ab_d_skip")[0]), "ab_ssd_norm": c(f("ab_ssd_norm")[0]),
        "ab_ig_bias": c(igb).reshape(16), "ab_fg_bias": c(fgb).reshape(16),
        "ab_mlstm_norm": c(f("ab_mlstm_norm")[0]), "ab_w_out": c(f("ab_w_out")[0]),
        "c_w_in": c(c_w_in), "c_gate_w2": c(gw2), "c_gate_b": c(gb),
        "c_norm": c(f("c_norm")[0]), "c_w_out": c(f("c_w_out")[0]),
        "ffn_w_up": c(f("ffn_w_up")), "ffn_conv_w": c(fcw), "ffn_conv_b": c(f("ffn_conv_b")), "ffn_w_down": c(f("ffn_w_down")),
    }


def kernel(**inputs):
    x = np.asarray(inputs["x"])
    B, S_, _ = x.shape
    T = S_ // 2
    if T not in _CACHE:
        _CACHE[T] = build(T)
    nc = _CACHE[T]
    in_maps = [_core_inputs(inputs, b, half) for b in range(B) for half in range(2)]
    res = run_bass_kernel_spmd(nc, in_maps, core_ids=list(range(2 * B)))
    outp = np.empty((B, S_, D), np.float32)
    for b in range(B):
        outp[b, :T] = np.asarray(res.results[2 * b]["out"], dtype=np.float32)
        outp[b, T:] = np.asarray(res.results[2 * b + 1]["out"], dtype=np.float32)[::-1]
    return outp
```

```python
import heapq
import numpy as np
from contextlib import ExitStack
import concourse.bass as bass
import concourse.mybir as mybir
from concourse.bass_utils import run_bass_kernel_spmd

F32 = mybir.dt.float32
BF16 = mybir.dt.bfloat16
AF = mybir.ActivationFunctionType
ALU = mybir.AluOpType
AX = mybir.AxisListType

D = 1024
EPS = 1e-6
NEGBIG = -30000.0


class Trk:
    __slots__ = ("name", "writers", "readers", "manual")

    def __init__(self, name, manual=False):
        self.name = name
        self.writers = []
        self.readers = []
        self.manual = manual


class Tile:
    def __init__(self, t, name, manual=False):
        self.t = t
        self.k = Trk(name, manual)

    def __getitem__(self, idx):
        return self.t[idx]


class DB:
    def __init__(self, tiles):
        self.tiles = tiles
        self.i = 0

    def __getitem__(self, idx):
        return self.tiles[self.i].t[idx]

    @property
    def k(self):
        return self.tiles[self.i].k


def _trk(x):
    return x.k if isinstance(x, (Tile, DB)) else x


class _Rec:
    def __getattr__(self, name):
        return lambda *a, **k: (name, a, k)


_REC = _Rec()


def _is_ap(v):
    return hasattr(v, "ap") and hasattr(v, "tensor") and hasattr(v, "offset")


def _esize(dt):
    return 4 if dt == F32 else 2


def _free_elems(ap):
    n = 1
    for v in ap.shape[1:]:
        n *= v
    return n


class Op:
    __slots__ = ("idx", "eng", "name", "args", "kw", "cost", "is_dma", "dma_t", "preds", "succs", "npred", "finish", "ev")


class Sched:
    ENGS = ("pe", "act", "dve", "pool", "sp")
    NDMA = 24

    def __init__(self, nc, es, self_wait=True, reorder=True):
        self.nc = nc
        self.self_wait = self_wait
        self.reorder = reorder
        self.semobj = {e: es.enter_context(nc.semaphore("s_" + e)) for e in self.ENGS}
        self.cnt = {e: 0 for e in self.ENGS}
        self.seen = {e: {} for e in self.ENGS}
        for i in range(self.NDMA):
            self.semobj["d%d" % i] = es.enter_context(nc.semaphore("s_d%d" % i))
        self.dma_cnt = [0] * self.NDMA
        self.dma_next = 0
        self.h = {"pe": nc.tensor, "act": nc.scalar, "dve": nc.vector, "pool": nc.gpsimd, "sp": nc.sync}
        self.nins = {e: 0 for e in self.ENGS}
        self.ops = []
        self.touched = set()
        self.reg = {}
        self.est_ns = 0.0
        self.verbose = False
        self.prio_mode = 1
        self.seg_name = ""

    def register(self, tname, lo, hi, tile):
        self.reg.setdefault(tname, []).append((lo, hi, tile))

    def clear_reg(self, tname):
        self.reg[tname] = []

    def unregister(self, tname, tile):
        self.reg[tname] = [x for x in self.reg.get(tname, []) if x[2] is not tile]

    def _auto(self, ap):
        lst = self.reg.get(ap.name)
        if not lst:
            return ()
        es = _esize(ap.dtype)
        lo = int(ap.offset) * es
        ext = 0
        for st, n in ap.ap[1:]:
            ext += (n - 1) * abs(st)
        hi = lo + (ext + 1) * es
        return [t for (a, b, t) in lst if a < hi and lo < b]

    def _mk(self, eng, fn, reads, writes, is_dma):
        name, a, k = fn(_REC)
        rs = [_trk(t) for t in reads]
        ws = [_trk(t) for t in writes]
        out_ap = None
        nbytes = 0
        for i, v in list(enumerate(a)) + list(k.items()):
            if not _is_ap(v):
                continue
            is_out = (i == 0 and isinstance(i, int)) or i in ("out", "accum_out")
            if is_out and out_ap is None and i != "accum_out":
                out_ap = v
            sp = str(v.space)
            if sp == "DRAM":
                if is_dma:
                    nbytes = max(nbytes, _free_elems(v) * v.shape[0] * _esize(v.dtype))
                continue
            for t in self._auto(v):
                tk = _trk(t)
                if tk.manual:
                    continue
                if is_out or sp == "PSUM":
                    if tk not in ws:
                        ws.append(tk)
                elif tk not in rs:
                    rs.append(tk)
            if is_dma:
                nbytes = max(nbytes, _free_elems(v) * v.shape[0] * _esize(v.dtype))
        op = Op()
        op.idx = len(self.ops)
        op.eng, op.name, op.args, op.kw, op.is_dma = eng, name, a, k, is_dma
        n = _free_elems(out_ap) if out_ap is not None else 64
        if is_dma:
            op.cost = 60.0
            op.dma_t = nbytes / 120.0
        elif eng == "pe":
            passes = 1
            if name == "matmul" and k["lhsT"].dtype == F32:
                passes = 4
            op.cost = 70.0 + 0.5 * n * passes
            op.dma_t = 0.0
        elif eng == "act":
            op.cost = 250.0 + 0.85 * n
            op.dma_t = 0.0
        elif eng == "dve":
            if name == "scalar_tensor_tensor":
                op.cost = 150.0 + 2.0 * n
            elif name == "reciprocal":
                op.cost = 100.0 + 6.0 * n
            elif name == "tensor_tensor_scan":
                op.cost = 100.0 + 2.0 * n
            else:
                op.cost = 90.0 + 1.1 * n
            op.dma_t = 0.0
        else:
            op.cost = 200.0 + 1.9 * n
            op.dma_t = 0.0
        preds = set()
        for t in rs:
            preds.update(t.writers)
        for t in ws:
            preds.update(t.writers)
            preds.update(t.readers)
        preds.discard(op.idx)
        op.preds = preds
        op.succs = []
        for t in rs:
            t.readers.append(op.idx)
            self.touched.add(t)
        for t in ws:
            t.writers = [op.idx]
            t.readers = []
            self.touched.add(t)
        self.ops.append(op)

    def op(self, eng, fn, reads=(), writes=()):
        self._mk(eng, fn, reads, writes, False)

    def dma(self, fn, reads=(), writes=(), eng="sp"):
        self._mk(eng, fn, reads, writes, True)

    def flush(self):
        ops = self.ops
        if not ops:
            return
        n = len(ops)
        order = {e: [] for e in self.ENGS}
        if not self.reorder:
            for op in ops:
                order[op.eng].append(op)
        else:
            for op in ops:
                op.npred = len(op.preds)
                for p in op.preds:
                    ops[p].succs.append(op.idx)
            ready = {e: [] for e in self.ENGS}
            bl = [0.0] * n
            for op in reversed(ops):
                m = 0.0
                for s in op.succs:
                    if bl[s] > m:
                        m = bl[s]
                bl[op.idx] = m + (op.cost if not op.is_dma else 2000.0 + op.dma_t)
            if self.prio_mode == 0:
                key = list(range(n))
            else:
                order_ix = sorted(range(n), key=lambda i: (-bl[i], i))
                key = [0] * n
                for r_, i in enumerate(order_ix):
                    key[i] = r_
            inv = {}
            for i in range(n):
                inv[key[i]] = i
            for op in ops:
                if op.npred == 0:
                    heapq.heappush(ready[op.eng], key[op.idx])
            busy = {e: 0.0 for e in self.ENGS}
            crit = [-1] * n
            blk = [-1] * n
            stt = [0.0] * n
            rdy = {}
            stall = {}
            ev = []
            now = 0.0
            dma_free = 0.0
            done = 0
            while done < n:
                for e in self.ENGS:
                    if busy[e] <= now and ready[e]:
                        i = inv[heapq.heappop(ready[e])]
                        op = ops[i]
                        if self.verbose:
                            rt = rdy.get(i, 0.0)
                            if order[e] and rt < now - 1e-9:
                                blk[i] = order[e][-1].idx
                            else:
                                blk[i] = crit[i]
                            stt[i] = now
                        order[e].append(op)
                        if self.verbose and e == "pe":
                            st_ = now - max(busy[e], 0.0)
                            if st_ > 0 and crit[i] >= 0:
                                cp = ops[crit[i]]
                                kk = (cp.eng, cp.name, getattr(cp, "tag", ""))
                                stall[kk] = stall.get(kk, 0.0) + st_
                        busy[e] = now + op.cost
                        if op.is_dma:
                            st = max(now + op.cost, dma_free)
                            dma_free = st + op.dma_t
                            fin = dma_free + 2000.0
                        else:
                            fin = busy[e]
                        heapq.heappush(ev, (fin, 1, i))
                        heapq.heappush(ev, (busy[e], 0, -1))
                if not ev:
                    raise RuntimeError("scheduler deadlock")
                t, kind, i = heapq.heappop(ev)
                now = max(now, t)
                if kind == 1:
                    done += 1
                    for s in ops[i].succs:
                        so = ops[s]
                        so.npred -= 1
                        crit[s] = i
                        if so.npred == 0:
                            rdy[s] = now
                            heapq.heappush(ready[so.eng], key[s])
            self.est_ns += now
            if self.verbose:
                bs = {e: sum(o.cost for o in order[e]) / 1e3 for e in self.ENGS}
                last = max(range(n), key=lambda i: stt[i] + ops[i].cost)
                cp = {}
                i = last
                guard = 0
                while i >= 0 and guard < 10 * n:
                    guard += 1
                    o_ = ops[i]
                    kk = (o_.eng, o_.name, "dma" if o_.is_dma else "")
                    dur = (2000.0 + o_.dma_t) if o_.is_dma else o_.cost
                    cp[kk] = cp.get(kk, 0.0) + dur
                    i = blk[i]
                for kk, vv in sorted(cp.items(), key=lambda x: -x[1])[:10]:
                    print("      critpath %-40s %.0f us" % (str(kk), vv / 1e3))
                for kk, vv in sorted(stall.items(), key=lambda x: -x[1])[:0]:
                    print("      pe stall %-40s %.0f us" % (str(kk), vv / 1e3))
                print("  segment %-8s n=%6d est_us=%8.1f busy_us: %s" % (self.seg_name, n, now / 1e3, " ".join("%s=%.0f" % (e, bs[e]) for e in self.ENGS)), flush=True)
        for e in self.ENGS:
            for op in order[e]:
                if op.is_dma:
                    i = self.dma_next
                    self.dma_next = (i + 1) % self.NDMA
                    key = "d%d" % i
                    prev = self.dma_cnt[i]
                    self.dma_cnt[i] += 16
                    op.ev = (key, self.dma_cnt[i], prev)
                else:
                    self.cnt[e] += 1
                    op.ev = (e, self.cnt[e], 0)
        for e in self.ENGS:
            h = self.h[e]
            seen = self.seen[e]
            for op in order[e]:
                need = {}
                for p in op.preds:
                    k, v, _ = ops[p].ev
                    if k == e and (e == "pe" or not self.self_wait):
                        continue
                    if need.get(k, 0) < v:
                        need[k] = v
                if op.is_dma and op.ev[2]:
                    k, _, prev = op.ev
                    if need.get(k, 0) < prev:
                        need[k] = prev
                for k, v in need.items():
                    if seen.get(k, 0) >= v:
                        continue
                    seen[k] = v
                    h.wait_ge(self.semobj[k], v)
                ins = getattr(h, op.name)(*op.args, **op.kw)
                ins.then_inc(self.semobj[op.ev[0]], 16 if op.is_dma else 1)
                self.nins[e] += 1
        self.ops = []
        for t in self.touched:
            t.writers = []
            t.readers = []
        self.touched = set()
        self._full_wait()

    def _full_wait(self):
        cur = dict(self.cnt)
        for i in range(self.NDMA):
            cur["d%d" % i] = self.dma_cnt[i]
        for e in self.ENGS:
            for k, v in cur.items():
                if k == e or v == 0 or self.seen[e].get(k, 0) >= v:
                    continue
                self.seen[e][k] = v
                self.h[e].wait_ge(self.semobj[k], v)

    def barrier(self):
        self.flush()


ARENA_N = 85000


def build(T, dbg=None, self_wait=True, reorder=True, verbose=False):
    NC = T // 128
    nc = bass.Bass("TRN2", target_bir_lowering=False)
    es = ExitStack()

    def dram(name, shape, dt, kind="Internal"):
        return nc.dram_tensor(name, shape, dt, kind=kind).ap()

    x_in = dram("x", [T, D], F32, "ExternalInput")
    norm_g = dram("norm_g", [2, 4, D], F32, "ExternalInput")
    ab_w_in = dram("ab_w_in", [D, 6720], F32, "ExternalInput")
    ab_conv_w = dram("ab_conv_w", [5, 1536], F32, "ExternalInput")
    ab_conv_b = dram("ab_conv_b", [1536], F32, "ExternalInput")
    ab_dt_bias = dram("ab_dt_bias", [32], F32, "ExternalInput")
    ab_a_log = dram("ab_a_log", [32], F32, "ExternalInput")
    ab_d_skip = dram("ab_d_skip", [16], F32, "ExternalInput")
    ab_ssd_norm = dram("ab_ssd_norm", [1024], F32, "ExternalInput")
    ab_ig_bias = dram("ab_ig_bias", [16], F32, "ExternalInput")
    ab_fg_bias = dram("ab_fg_bias", [16], F32, "ExternalInput")
    ab_mlstm_norm = dram("ab_mlstm_norm", [1024], F32, "ExternalInput")
    ab_w_out = dram("ab_w_out", [2048, D], F32, "ExternalInput")
    c_w_in = dram("c_w_in", [D, 3104], F32, "ExternalInput")
    c_gate_w2 = dram("c_gate_w2", [2, 16, 512], F32, "ExternalInput")
    c_gate_b = dram("c_gate_b", [2, 512], F32, "ExternalInput")
    c_norm = dram("c_norm", [1024], F32, "ExternalInput")
    c_w_out = dram("c_w_out", [D, D], F32, "ExternalInput")
    ffn_w_up = dram("ffn_w_up", [2, D, 5632], F32, "ExternalInput")
    ffn_conv_w = dram("ffn_conv_w", [2, 3, 5632], F32, "ExternalInput")
    ffn_conv_b = dram("ffn_conv_b", [2, 5632], F32, "ExternalInput")
    ffn_w_down = dram("ffn_w_down", [2, 2816, D], F32, "ExternalInput")
    x_halo = dram("x_halo", [2, D], F32, "ExternalInput")
    sel_in = dram("sel", [2], F32, "ExternalInput")
    out = dram("out", [T, D], F32, "ExternalOutput")
    exS = dram("exS", [128, 2056], F32)
    exG = dram("exG", [256, 2056], F32)
    exS1 = dram("exS1", [128, 1024], F32)
    exG1 = dram("exG1", [256, 1024], F32)
    exR = [dram("exR%d" % i, [1, D], F32) for i in range(2)]
    exRG = [dram("exRG%d" % i, [2, D], F32) for i in range(2)]
    PAIRS = [[0, 1], [2, 3], [4, 5], [6, 7]]
    WB_L0P2 = dram("WB_L0P2", [128, 32 * 1024], BF16)
    WB_up = [dram("WB_up%d" % i, [128, 8 * 5632], BF16) for i in range(2)]
    WB_dn = [dram("WB_dn%d" % i, [128, 22 * 1024], BF16) for i in range(2)]
    WB_g = dram("WB_g", [128, 8 * 3104], BF16)
    WB_o1 = dram("WB_o1", [128, 8 * 1024], BF16)
    WK = Trk("weights")

    PT0 = dram("PT0", [T, 3072], BF16)
    XSB = dram("XSB", [T, 1280], BF16)
    BCT = dram("BCT", [NC, 128, 512], BF16)
    GT = dram("GT", [T, 64], F32)
    HS = dram("HS", [NC, 128, 1024], BF16)
    HM = dram("HM", [NC, 128, 1032], BF16)
    X1 = dram("X1", [T, D], F32)
    X2 = dram("X2", [T, D], F32)
    QKT = dram("QKT", [NC, 128, 1024], BF16)
    LTd = dram("LTd", [NC, 128, 1024], F32)
    VR = dram("VR", [T, 2048], BF16)
    SG = dram("SG", [NC, 128, 1024], BF16)
    X3 = dram("X3", [T, D], F32)
    dk = {}

    def DK(name, c):
        key = (name, c)
        if key not in dk:
            dk[key] = Trk("%s%d" % key)
        return dk[key]

    with es:
        S = Sched(nc, es, self_wait=self_wait, reorder=reorder)
        S.verbose = verbose
        PS = [Tile(es.enter_context(nc.psum_tensor("ps%d" % i, [128, 512], F32)), "ps%d" % i) for i in range(8)]
        for i in range(8):
            S.register("ps%d" % i, 0, 1 << 30, PS[i])

        def sb(name, shape, dt):
            t = Tile(es.enter_context(nc.sbuf_tensor(name, shape, dt)), name)
            S.register(name, 0, 1 << 30, t)
            return t

        arena_t = es.enter_context(nc.sbuf_tensor("arena", [128, ARENA_N], BF16))
        aoff = [0]

        def reset_arena():
            if verbose:
                print("  arena used", aoff[0] * 2, "of", ARENA_N * 2)
            S.barrier()
            S.clear_reg("arena")
            aoff[0] = 0

        dbs = []

        def ar2(name, shape, dt, nbuf=2):
            d = DB([ar("%s_%d" % (name, i), shape, dt) for i in range(nbuf)])
            dbs.append(d)
            return d

        def setpar(c):
            for d in dbs:
                d.i = c % len(d.tiles)

        def ar(name, shape, dt, manual=False):
            n = 1
            for v in shape[1:]:
                n *= v
            nb = n * 2 if dt == F32 else n
            if aoff[0] % 2:
                aoff[0] += 1
            o = aoff[0]
            aoff[0] += nb
            assert aoff[0] <= ARENA_N, (name, aoff[0])
            ap = arena_t[0:shape[0], o:o + nb]
            if dt == F32:
                ap = ap.bitcast(F32)
            if len(shape) == 3:
                ap = ap.rearrange("p (a b) -> p a b", a=shape[1])
            elif len(shape) == 4:
                ap = ap.rearrange("p (a b c) -> p a b c", a=shape[1], b=shape[2])
            t = Tile(ap, name, manual)
            S.register("arena", o * 2, (o + nb) * 2, t)
            return t

        def V(fn, r=(), w=()):
            S.op("dve", fn, r, w)

        def A(fn, r=(), w=()):
            S.op("act", fn, r, w)

        def G(fn, r=(), w=()):
            S.op("pool", fn, r, w)

        def P(fn, r=(), w=()):
            S.op("pe", fn, r, w)

        def MM(bank, o, lhsT, rhs, r, start=True, stop=True):
            P(lambda h: h.matmul(o, lhsT=lhsT, rhs=rhs, start=start, stop=stop), r, [bank])

        def TR(bank, o, in_, ident, r):
            P(lambda h: h.transpose(out=o, in_=in_, identity=ident), r, [bank])

        def LD(o, i, r, w):
            S.dma(lambda h: h.dma_start(out=o, in_=i), r, w)

        def LDS(o, i, r, w):
            S.dma(lambda h: h.dma_start(out=o, in_=i, allow_slow_non_contiguous=True), r, w)

        def v3(ap, a):
            return ap.rearrange("p (a b) -> p a b", a=a)

        def bc_in(ap, a, b):
            return ap.unsqueeze(2).to_broadcast([128, a, b])

        def bc_mid(ap, a, b):
            return ap.unsqueeze(1).to_broadcast([128, a, b])

        block = es.enter_context(nc.Block())
        cur_scope = [None]

        def SC(name):
            S.seg_name = name or ""
            return
            if cur_scope[0] is not None:
                cur_scope[0].__exit__(None, None, None)
                cur_scope[0] = None
            if name is not None:
                cm = nc.named_scope(name)
                cm.__enter__()
                cur_scope[0] = cm

        onesf = sb("onesf", [128, 128], F32)
        zerof = sb("zerof", [128, 128], F32)
        identf = sb("identf", [128, 128], F32)
        identb = sb("identb", [128, 128], BF16)
        Uf = sb("Uf", [128, 128], F32)
        Ub = sb("Ub", [128, 128], F32)
        maskb16 = sb("maskb16", [128, 2, 128], BF16)
        NEG = sb("NEG", [128, 2, 128], BF16)
        negf = sb("negf", [128, 2, 128], F32)
        onesb = sb("onesb", [128, 8], BF16)
        epsc = sb("epsc", [128, 1], F32)
        G(lambda h: h.memset(onesf[:], 1.0), w=[onesf])
        G(lambda h: h.memset(zerof[:], 0.0), w=[zerof])
        G(lambda h: h.memset(onesb[:], 1.0), w=[onesb])
        G(lambda h: h.memset(epsc[:], EPS), w=[epsc])
        G(lambda h: h.affine_select(out=identf[:], in_=onesf[:], pattern=[[1, 128]], compare_op=ALU.is_equal, fill=0.0, base=0, channel_multiplier=-1), [onesf], [identf])
        G(lambda h: h.tensor_copy(out=identb[:], in_=identf[:]), [identf], [identb])
        G(lambda h: h.affine_select(out=Uf[:], in_=onesf[:], pattern=[[1, 128]], compare_op=ALU.is_ge, fill=0.0, base=0, channel_multiplier=-1), [onesf], [Uf])
        G(lambda h: h.affine_select(out=Ub[:], in_=onesf[:], pattern=[[-1, 128]], compare_op=ALU.is_ge, fill=0.0, base=0, channel_multiplier=1), [onesf], [Ub])
        G(lambda h: h.tensor_copy(out=maskb16[:, 0, :], in_=Uf[:]), [Uf], [maskb16])
        G(lambda h: h.tensor_copy(out=maskb16[:, 1, :], in_=Ub[:]), [Ub], [maskb16])
        G(lambda h: h.affine_select(out=negf[:, 0, :], in_=zerof[:], pattern=[[1, 128]], compare_op=ALU.is_ge, fill=NEGBIG, base=0, channel_multiplier=-1), [zerof], [negf])
        G(lambda h: h.affine_select(out=negf[:, 1, :], in_=zerof[:], pattern=[[-1, 128]], compare_op=ALU.is_ge, fill=NEGBIG, base=0, channel_multiplier=1), [zerof], [negf])
        G(lambda h: h.tensor_copy(out=NEG[:], in_=negf[:]), [negf], [NEG])

        stg = [sb("stg%d" % i, [128, 1024], F32) for i in range(2)]
        stg_ctr = [0]
        gcol = sb("gcol", [128, 8, 8], F32)
        gres = sb("gres", [128, 1024], F32)
        xin = [sb("xin%d" % i, [128, 1024], F32) for i in range(2)]
        sqj = sb("sqj", [128, 1024], BF16)
        ss = sb("ss", [128, 16], F32)
        hb = sb("hb", [128, 1024], BF16)
        dumm = sb("dumm", [128, 8], F32)
        selt = sb("selt", [128, 2], F32)
        LD(selt[:], sel_in.partition_broadcast(128), [WK], [selt])

        def allgather(src_d, dst_d, rk, wk):
            for rep_ in range(2):
                G(lambda h: h.collective_compute("AllGather", ALU.bypass, replica_groups=PAIRS, ins=[src_d.opt()], outs=[dst_d.opt()]), [rk], [wk])

        def select2(out_ap, a0, a1, r, w):
            V(lambda h: h.tensor_scalar(out=out_ap, in0=a0, scalar1=selt[:, 0:1], scalar2=None, op0=ALU.mult), r, w)
            V(lambda h: h.scalar_tensor_tensor(out=out_ap, in0=a1, scalar=selt[:, 1:2], in1=out_ap, op0=ALU.mult, op1=ALU.add), r, w)

        def halo_cands(src_rows_ap, src_trk, nrows, dst_tile):
            xi = xin[0]
            V(lambda h: h.memset(xi[:], 0.0), [], [xi])
            LD(xi[0:nrows, :], src_rows_ap, [src_trk], [xi])
            norm_chunk(None, 0, None, xi, load=False)
            pb = PS[3][:].bitcast(BF16)
            for k in range(8):
                TR(PS[3], pb[:, k * 128:(k + 1) * 128], hb[:, k * 128:(k + 1) * 128], identb[:], [hb, identb])
            V(lambda h: h.tensor_copy(out=dst_tile[:], in_=v3(pb, 8)[:, :, 0:nrows]), [], [PS[3], dst_tile])
        WG = Trk("wguard")

        gvecs = [norm_g[0, 0], norm_g[0, 2], norm_g[1, 0], norm_g[1, 2], ab_ssd_norm, ab_mlstm_norm, c_norm]
        for i, gv in enumerate(gvecs):
            LDS(gcol[:, i, :], gv.rearrange("(k p) -> p k", p=128), [WK], [gcol])

        class WLoad:
            def __init__(self, wt):
                self.wt = wt
                self.tmps = []
                G(lambda h: h.memset(dumm[:, 0:1], 0.0), [], [wt, WG, dumm])

            def load(self, dst_fn, src, KC, ncols, gc, factor=1.0):
                for kc in range(KC):
                    for c0 in range(0, ncols, 1024):
                        w = min(1024, ncols - c0)
                        st = stg[stg_ctr[0] % 2]
                        stg_ctr[0] += 1
                        LD(st[:, 0:w], src[kc * 128:(kc + 1) * 128, c0:c0 + w], [WK], [st])
                        o = dst_fn(kc, c0, w)
                        sc1 = gc(kc) if gc is not None else 1.0
                        tk = Trk("wtmp")
                        self.tmps.append(tk)
                        if factor == 1.0 and stg_ctr[0] % 2 == 0:
                            A(lambda h, o=o, st=st, w=w, sc1=sc1: h.activation(out=o, in_=st[:, 0:w], func=AF.Copy, scale=sc1), [st, gcol, WG], [tk])
                        else:
                            V(lambda h, o=o, st=st, w=w, sc1=sc1: h.tensor_scalar(out=o, in0=st[:, 0:w], scalar1=sc1, scalar2=float(factor), op0=ALU.mult, op1=ALU.mult), [st, gcol, WG], [tk])

            def done(self):
                G(lambda h: h.memset(dumm[:, 1:2], 0.0), self.tmps, [self.wt, dumm])

        cvt = [sb("cvt%d" % i, [128, 1024], BF16) for i in range(2)]
        prep_ctr = [0]
        prep_on_pool = [False]
        gcolq = sb("gcolq", [128, 8], F32)
        V(lambda h: h.tensor_scalar(out=gcolq[:], in0=gcol[:, 2, :], scalar1=128 ** -0.5, scalar2=None, op0=ALU.mult), [gcol], [gcolq])

        def prep(dst_d, dst_off_fn, src, KC, ncols, gc_fn):
            for kc in range(KC):
                for c0 in range(0, ncols, 1024):
                    w = min(1024, ncols - c0)
                    i = prep_ctr[0]
                    prep_ctr[0] += 1
                    st, cv = stg[i % 2], cvt[i % 2]
                    LD(st[:, 0:w], src[kc * 128:(kc + 1) * 128, c0:c0 + w], [WK], [st])
                    if not prep_on_pool[0]:
                        sc1 = gc_fn(kc) if gc_fn is not None else 1.0
                        A(lambda h, st=st, cv=cv, w=w, sc1=sc1: h.activation(out=cv[:, 0:w], in_=st[:, 0:w], func=AF.Copy, scale=sc1), [st, gcol, gcolq], [cv])
                    elif gc_fn is not None:
                        gc = gc_fn(kc)
                        G(lambda h, st=st, cv=cv, w=w, gc=gc: h.tensor_tensor(out=cv[:, 0:w], in0=st[:, 0:w], in1=gc.to_broadcast([128, w]), op=ALU.mult), [st, gcol, gcolq], [cv])
                    else:
                        G(lambda h, st=st, cv=cv, w=w: h.tensor_copy(out=cv[:, 0:w], in_=st[:, 0:w]), [st], [cv])
                    o = dst_off_fn(kc, c0)
                    LD(dst_d[:, o:o + w], cv[:, 0:w], [cv], [Trk("wbst")])

        def prep_ffn_up(li):
            prep(WB_up[li], lambda kc, c0: kc * 5632 + c0, ffn_w_up[li], 8, 5632, lambda kc: gcol[:, 1 + 2 * li, kc:kc + 1])

        def prep_ffn_dn(li):
            prep(WB_dn[li], lambda kc, c0: kc * 1024 + c0, ffn_w_down[li], 22, 1024, None)

        def sigmoid_act(out_ap, in_ap, r, w, scale=1.0):
            A(lambda h: h.activation(out=out_ap, in_=in_ap, func=AF.Exp, scale=-float(scale)), r, w)
            A(lambda h: h.activation(out=out_ap, in_=out_ap, func=AF.Ln, bias=1.0), [], w)
            A(lambda h: h.activation(out=out_ap, in_=out_ap, func=AF.Exp, scale=-1.0), [], w)

        def rstd_from(src_ap, n, col, junk_ap, r, w_extra=()):
            V(lambda h: h.memset(ss[:, col:col + 1], 0.0), [], [ss])
            A(lambda h: h.activation(out=junk_ap, in_=src_ap, func=AF.Square, accum_out=ss[:, col:col + 1]), r, [sqj, ss] + list(w_extra))
            A(lambda h: h.activation(out=ss[:, col:col + 1], in_=ss[:, col:col + 1], func=AF.Ln, scale=1.0 / n, bias=epsc[:, 0:1]), [epsc], [ss])
            A(lambda h: h.activation(out=ss[:, col:col + 1], in_=ss[:, col:col + 1], func=AF.Exp, scale=-0.5), [], [ss])

        def norm_chunk(src_dram, c, src_trk, xi, load=True):
            if load:
                LD(xi[:], src_dram[c * 128:(c + 1) * 128, :], [src_trk], [xi])
            rstd_from(xi[:], 1024, 0, sqj[:], [xi])
            V(lambda h: h.tensor_scalar(out=hb[:], in0=xi[:], scalar1=ss[:, 0:1], scalar2=None, op0=ALU.mult), [xi, ss], [hb])

        def transpose8(src_tile, src_c0, dst_ap, dst_trk, eng="dve"):
            pb = PS[3][:].bitcast(BF16)
            for k in range(8):
                TR(PS[3], pb[:, k * 128:(k + 1) * 128], src_tile[:, src_c0 + k * 128:src_c0 + (k + 1) * 128], identb[:], [src_tile, identb])
            if eng == "dve":
                V(lambda h: h.tensor_copy(out=dst_ap, in_=v3(pb, 8)), [], [PS[3]] + dst_trk)
            else:
                A(lambda h: h.activation(out=dst_ap, in_=v3(pb, 8), func=AF.Copy), [], [PS[3]] + dst_trk)

        def resid_out(banks, xres, dst_dram, c, dst_name, xo):
            V(lambda h: h.memset(ss[:, 3:5], 0.0), [], [ss])
            for n2 in range(2):
                A(lambda h, n2=n2: h.activation(out=sqj[:, n2 * 512:(n2 + 1) * 512], in_=banks[n2][:, :], func=AF.Square, accum_out=ss[:, 3 + n2:4 + n2]), [], [banks[n2], sqj, ss])
            V(lambda h: h.tensor_tensor(out=ss[:, 3:4], in0=ss[:, 3:4], in1=ss[:, 4:5], op=ALU.add), [], [ss])
            A(lambda h: h.activation(out=ss[:, 3:4], in_=ss[:, 3:4], func=AF.Ln, scale=1.0 / 1024, bias=epsc[:, 0:1]), [epsc], [ss])
            A(lambda h: h.activation(out=ss[:, 3:4], in_=ss[:, 3:4], func=AF.Exp, scale=-0.5), [], [ss])
            for n2 in range(2):
                V(lambda h, n2=n2: h.scalar_tensor_tensor(out=xo[:, n2 * 512:(n2 + 1) * 512], in0=banks[n2][:, :], scalar=ss[:, 3:4], in1=gres[:, n2 * 512:(n2 + 1) * 512], op0=ALU.mult, op1=ALU.mult), [ss, gres], [banks[n2], xo])
            G(lambda h: h.tensor_tensor(out=xo[:], in0=xo[:], in1=xres[:], op=ALU.add), [xres], [xo])
            LD(dst_dram[c * 128:(c + 1) * 128, :], xo[:], [xo], [DK(dst_name, c)])

        def layer0():
            reset_arena()
            SC("L0P1_w")
            WT1 = ar("W1", [128, 8, 4672], BF16, manual=True)
            hTr = ar("hTr", [128, 8, 516], BF16, manual=True)
            hslot = [Trk("hslot%d" % i) for i in range(4)]
            hmirL, hmirR = Trk("hmirL"), Trk("hmirR")
            dtb = ar("dtb", [128, 32], F32)
            negA = ar("negA", [128, 32], F32)
            igfg = ar("igfg", [128, 32], F32)
            dsk = ar("dsk", [128, 16], F32)
            cw = ar("cw", [128, 12, 5], F32)
            cbias = ar("cbias", [128, 12], F32)
            Hs = ar("Hs", [128, 1024], F32)
            Cm = ar("Cm", [128, 1032], F32)
            n0p1 = len(dbs)
            hsb16 = ar2("hsb16", [128, 1024], BF16)
            cmb16 = ar2("cmb16", [128, 1032], BF16)
            pt = [ar("pt%d" % i, [128, 3072], BF16) for i in range(2)]
            gt = [ar("gt%d" % i, [128, 64], F32) for i in range(2)]
            acc = [ar("acc%d" % i, [128, 12, 128], F32) for i in range(2)]
            sg12s = [ar("sg12_0", [128, 12, 128], F32)] * 2
            xbcTs = [ar("xbcT%d" % i, [128, 12, 128], BF16) for i in range(2)]
            xsb = [ar("xsb%d" % i, [128, 1280], BF16) for i in range(2)]
            gw = ar2("gw", [128, 64], F32)
            G48 = ar2("G48", [128, 48], F32)
            cst = ar2("cst", [128, 96], F32)
            scw = ar2("scw", [128, 48], F32)
            xe = ar2("xe", [128, 1024], BF16)
            kw = ar2("kw", [128, 1024], BF16)
            dts = ar2("dts", [128, 32], F32)
            igs = ar2("igs", [128, 16], F32)
            halo0 = ar("halo0", [128, 8, 2], BF16)
            xring = [ar("xring%d" % i, [128, 1024], F32) for i in range(3)]
            if verbose:
                print("  L0P1 arena", aoff[0] * 2)
            p1_end = aoff[0]

            def load_params():
                LD(dtb[:], ab_dt_bias.partition_broadcast(128), [WK], [dtb])
                LD(negA[:], ab_a_log.partition_broadcast(128), [WK], [negA])
                LD(igfg[:, 0:16], ab_ig_bias.partition_broadcast(128), [WK], [igfg])
                LD(igfg[:, 16:32], ab_fg_bias.partition_broadcast(128), [WK], [igfg])
                LD(dsk[:], ab_d_skip.partition_broadcast(128), [WK], [dsk])
                for j in range(5):
                    LDS(cw[:, :, j], ab_conv_w[j].rearrange("(i p) -> p i", p=128), [WK], [cw])
                LDS(cbias[:], ab_conv_b.rearrange("(i p) -> p i", p=128), [WK], [cbias])
                A(lambda h: h.activation(out=negA[:], in_=negA[:], func=AF.Exp), [], [negA])
                V(lambda h: h.tensor_scalar(out=negA[:], in0=negA[:], scalar1=-1.0, scalar2=None, op0=ALU.mult), [], [negA])
            load_params()
            wl = WLoad(WT1)
            segs = [(2592, 1024, 0, 1.0), (3616, 1024, 1024, 128 ** -0.5), (4640, 1024, 2048, 1.0),
                    (2560, 32, 3072, 1.0), (6688, 32, 3104, 1.0), (1024, 1536, 3136, 1.0)]
            for (s0, n, d0, fac) in segs:
                wl.load(lambda kc, c0, w, d0=d0: WT1[:, kc, d0 + c0:d0 + c0 + w], ab_w_in[:, s0:s0 + n], 8, n, lambda kc: gcol[:, 0, kc:kc + 1], fac)
            wl.done()
            V(lambda h: h.memset(Hs[:], 0.0), [], [Hs])
            V(lambda h: h.memset(Cm[:], 0.0), [], [Cm])
            V(lambda h: h.memset(hTr[:], 0.0), [], hslot + [hmirL, hmirR])
            halo_cands(x_halo, WK, 2, halo0)

            def gates(gtile, dirs):
                for d in dirs:
                    c0 = d * 16
                    V(lambda h, c0=c0: h.tensor_tensor(out=gw[:, c0:c0 + 16], in0=gtile[:, c0:c0 + 16], in1=dtb[:, c0:c0 + 16], op=ALU.add), [gtile, dtb], [gw])
                    A(lambda h, c0=c0: h.activation(out=gw[:, c0:c0 + 16], in_=gw[:, c0:c0 + 16], func=AF.Exp), [], [gw])
                    A(lambda h, c0=c0: h.activation(out=dts[:, c0:c0 + 16], in_=gw[:, c0:c0 + 16], func=AF.Ln, bias=1.0), [gw], [dts])
                    V(lambda h, c0=c0: h.tensor_tensor(out=G48[:, c0:c0 + 16], in0=dts[:, c0:c0 + 16], in1=negA[:, c0:c0 + 16], op=ALU.mult), [dts, negA], [G48])
                    i0 = 32 + d * 8
                    f0 = 48 + d * 8
                    V(lambda h, i0=i0, d=d: h.tensor_tensor(out=igs[:, d * 8:d * 8 + 8], in0=gtile[:, i0:i0 + 8], in1=igfg[:, d * 8:d * 8 + 8], op=ALU.add), [gtile, igfg], [igs])
                    V(lambda h, f0=f0, d=d: h.tensor_tensor(out=gw[:, f0:f0 + 8], in0=gtile[:, f0:f0 + 8], in1=igfg[:, 16 + d * 8:24 + d * 8], op=ALU.add), [gtile, igfg], [gw])
                    A(lambda h, f0=f0: h.activation(out=gw[:, f0:f0 + 8], in_=gw[:, f0:f0 + 8], func=AF.Exp, scale=-1.0), [], [gw])
                    A(lambda h, f0=f0: h.activation(out=gw[:, f0:f0 + 8], in_=gw[:, f0:f0 + 8], func=AF.Ln, bias=1.0), [], [gw])
                    V(lambda h, f0=f0, d=d: h.tensor_scalar(out=G48[:, 32 + d * 8:40 + d * 8], in0=gw[:, f0:f0 + 8], scalar1=-1.0, scalar2=None, op0=ALU.mult), [gw], [G48])

            def cumsums(dirs, bank):
                for d in dirs:
                    U = Uf if d == 0 else Ub
                    MM(bank, bank[:, d * 16:d * 16 + 16], U[:], G48[:, d * 16:d * 16 + 16], [U, G48])
                    MM(bank, bank[:, 32 + d * 8:40 + d * 8], U[:], G48[:, 32 + d * 8:40 + d * 8], [U, G48])
                MM(bank, bank[:, 64:112], onesf[:], G48[:], [onesf, G48])
                V(lambda h: h.tensor_copy(out=cst[:, 0:48], in_=bank[:, 0:48]), [], [bank, cst])
                V(lambda h: h.tensor_copy(out=cst[:, 48:96], in_=bank[:, 64:112]), [], [bank, cst])

            def state_step(d, xs_tile, pt_tile, Hst, Cst, store_fn):
                o16, o8 = d * 16, 32 + d * 8
                k_ap = pt_tile[:, 1024:2048]
                V(lambda h: h.tensor_tensor(out=scw[:, 0:16], in0=cst[:, 48 + o16:64 + o16], in1=cst[:, o16:o16 + 16], op=ALU.subtract), [cst], [scw])
                A(lambda h: h.activation(out=scw[:, 0:16], in_=scw[:, 0:16], func=AF.Exp), [], [scw])
                V(lambda h: h.tensor_tensor(out=scw[:, 0:16], in0=scw[:, 0:16], in1=dts[:, o16:o16 + 16], op=ALU.mult), [dts], [scw])
                V(lambda h: h.tensor_tensor(out=v3(xe[:], 16), in0=v3(xs_tile[:, 0:1024], 16), in1=bc_in(scw[:, 0:16], 16, 64), op=ALU.mult), [xs_tile, scw], [xe])
                for g in range(2):
                    MM(PS[4 + g], PS[4 + g][:, :], xs_tile[:, 1024 + g * 128:1152 + g * 128], xe[:, g * 512:(g + 1) * 512], [xs_tile, xe])
                V(lambda h: h.tensor_tensor(out=scw[:, 16:24], in0=cst[:, 48 + o8:56 + o8], in1=cst[:, o8:o8 + 8], op=ALU.subtract), [cst], [scw])
                V(lambda h: h.tensor_tensor(out=scw[:, 16:24], in0=scw[:, 16:24], in1=igs[:, d * 8:d * 8 + 8], op=ALU.add), [igs], [scw])
                A(lambda h: h.activation(out=scw[:, 16:24], in_=scw[:, 16:24], func=AF.Exp), [], [scw])
                V(lambda h: h.tensor_tensor(out=v3(kw[:], 8), in0=v3(k_ap, 8), in1=bc_in(scw[:, 16:24], 8, 128), op=ALU.mult), [pt_tile, scw], [kw])
                for hh in range(8):
                    bk = PS[6 + hh // 4]
                    MM(bk, bk[:, (hh % 4) * 128:(hh % 4 + 1) * 128], kw[:, hh * 128:(hh + 1) * 128], pt_tile[:, 2048 + hh * 128:2048 + (hh + 1) * 128], [kw, pt_tile])
                for hh in range(8):
                    MM(PS[2], PS[2][:, 128 + hh:129 + hh], kw[:, hh * 128:(hh + 1) * 128], onesb[:, 0:1], [kw, onesb])
                store_fn()
                A(lambda h: h.activation(out=scw[:, 24:40], in_=cst[:, 48 + o16:64 + o16], func=AF.Exp), [cst], [scw])
                A(lambda h: h.activation(out=scw[:, 40:48], in_=cst[:, 48 + o8:56 + o8], func=AF.Exp), [cst], [scw])
                V(lambda h: h.tensor_tensor(out=v3(Hst[:], 16), in0=v3(Hst[:], 16), in1=bc_in(scw[:, 24:40], 16, 64), op=ALU.mult), [scw], [Hst])
                for g in range(2):
                    V(lambda h, g=g: h.tensor_tensor(out=Hst[:, g * 512:(g + 1) * 512], in0=Hst[:, g * 512:(g + 1) * 512], in1=PS[4 + g][:, :], op=ALU.add), [], [Hst, PS[4 + g]])
                V(lambda h: h.tensor_tensor(out=v3(Cst[:, 0:1024], 8), in0=v3(Cst[:, 0:1024], 8), in1=bc_in(scw[:, 40:48], 8, 128), op=ALU.mult), [scw], [Cst])
                for q in range(2):
                    V(lambda h, q=q: h.tensor_tensor(out=Cst[:, q * 512:(q + 1) * 512], in0=Cst[:, q * 512:(q + 1) * 512], in1=PS[6 + q][:, :], op=ALU.add), [], [Cst, PS[6 + q]])
                V(lambda h: h.tensor_tensor(out=Cst[:, 1024:1032], in0=Cst[:, 1024:1032], in1=scw[:, 40:48], op=ALU.mult), [scw], [Cst])
                V(lambda h: h.tensor_tensor(out=Cst[:, 1024:1032], in0=Cst[:, 1024:1032], in1=PS[2][:, 128:136], op=ALU.add), [], [Cst, PS[2]])

            xk = Trk("x_in")

            def norm_to_ring(c):
                s = c % 4
                norm_chunk(x_in, c, xk, xring[c % 3])
                transpose8(hb, 0, hTr[:, :, 2 + s * 128:2 + (s + 1) * 128], [hslot[s]], eng="act")
                if s == 3:
                    V(lambda h: h.tensor_copy(out=hTr[:, :, 0:2], in_=hTr[:, :, 2 + 3 * 128 + 126:2 + 4 * 128]), [hslot[3]], [hmirL])
                if s == 0:
                    V(lambda h: h.tensor_copy(out=hTr[:, :, 514:516], in_=hTr[:, :, 2:4]), [hslot[0]], [hmirR])
                if c == NC - 1:
                    if s == 3:
                        V(lambda h: h.tensor_copy(out=hTr[:, :, 514:516], in_=halo0[:]), [halo0], [hmirR])
                    else:
                        V(lambda h: h.tensor_copy(out=hTr[:, :, 2 + (s + 1) * 128:4 + (s + 1) * 128], in_=halo0[:]), [halo0], [hslot[(s + 1) % 4]])

            SC("L0P1")
            norm_to_ring(0)
            for c in range(NC):
                if c + 1 < NC:
                    norm_to_ring(c + 1)
                s = c % 4
                setpar(c)
                ptc, gtc, xsc = pt[c % 2], gt[c % 2], xsb[c % 2]
                sg12, xbcT = sg12s[c % 2], xbcTs[c % 2]
                rr = [hslot[(c - 1) % 4], hslot[s], hslot[(c + 1) % 4], hmirL, hmirR, WT1]
                for j in range(6):
                    bk = PS[j % 2]
                    for k in range(8):
                        MM(bk, bk[:, :], hTr[:, k, 2 + s * 128:2 + (s + 1) * 128], WT1[:, k, j * 512:(j + 1) * 512], rr, start=(k == 0), stop=(k == 7))
                    A(lambda h, bk=bk, j=j: h.activation(out=ptc[:, j * 512:(j + 1) * 512], in_=bk[:, :], func=AF.Copy), [], [bk, ptc])
                for k in range(8):
                    MM(PS[0], PS[0][:, 0:64], hTr[:, k, 2 + s * 128:2 + (s + 1) * 128], WT1[:, k, 3072:3136], rr, start=(k == 0), stop=(k == 7))
                V(lambda h: h.tensor_copy(out=gtc[:], in_=PS[0][:, 0:64]), [], [PS[0], gtc])
                LD(PT0[c * 128:(c + 1) * 128, :], ptc[:], [ptc], [DK("PT0", c)])
                LD(GT[c * 128:(c + 1) * 128, :], gtc[:], [gtc], [DK("GT", c)])
                for i in range(12):
                    bk = PS[(1, 2, 6, 7)[i % 4]]
                    ac = acc[c % 2][:, i, :]
                    ack = acc[c % 2]
                    for k in range(8):
                        MM(bk, bk[:, 0:132], WT1[:, k, 3136 + i * 128:3136 + (i + 1) * 128], hTr[:, k, s * 128:s * 128 + 132], rr, start=(k == 0), stop=(k == 7))
                    A(lambda h, bk=bk, i=i, ac=ac: h.activation(out=ac, in_=bk[:, 2:130], func=AF.Identity, scale=cw[:, i, 2:3], bias=cbias[:, i:i + 1]), [cw, cbias], [bk, ack])
                    for j in (0, 1, 3, 4):
                        V(lambda h, bk=bk, i=i, j=j, ac=ac: h.scalar_tensor_tensor(out=ac, in0=bk[:, j:j + 128], scalar=cw[:, i, j:j + 1], in1=ac, op0=ALU.mult, op1=ALU.add), [cw], [bk, ack])
                ack = acc[c % 2]
                sigmoid_act(sg12[:], ack[:], [ack], [sg12])
                G(lambda h, ack=ack: h.tensor_tensor(out=xbcT[:], in0=ack[:], in1=sg12[:], op=ALU.mult), [ack, sg12], [xbcT])
                LD(BCT[c], xbcT[:, 8:12, :].rearrange("p a b -> p (a b)"), [xbcT], [DK("BCT", c)])
                pb = PS[3][:].bitcast(BF16)
                for i in range(8):
                    TR(PS[3], pb[:, i * 128:(i + 1) * 128], xbcT[:, i, :], identb[:], [xbcT, identb])
                A(lambda h: h.activation(out=xsc[:, 0:1024], in_=pb, func=AF.Copy), [], [PS[3], xsc])
                for i in range(2):
                    TR(PS[3], pb[:, i * 128:(i + 1) * 128], xbcT[:, 8 + i, :], identb[:], [xbcT, identb])
                A(lambda h: h.activation(out=xsc[:, 1024:1280], in_=pb[:, 0:256], func=AF.Copy), [], [PS[3], xsc])
                LD(XSB[c * 128:(c + 1) * 128, :], xsc[:], [xsc], [DK("XSB", c)])
                gates(gtc, [0])
                cumsums([0], PS[2])

                def store1(c=c):
                    A(lambda h: h.activation(out=hsb16[:], in_=Hs[:], func=AF.Copy), [Hs], [hsb16])
                    G(lambda h: h.tensor_copy(out=cmb16[:], in_=Cm[:]), [Cm], [cmb16])
                    LD(HS[c], hsb16[:], [hsb16], [DK("HS", c)])
                    LD(HM[c], cmb16[:], [cmb16], [DK("HM", c)])
                state_step(0, xsc, ptc, Hs, Cm, store1)
            del dbs[n0p1:]
            prep(WB_L0P2, lambda kc, c0: kc * 2048 + c0, ab_w_in[:, 0:1024], 8, 1024, lambda kc: gcol[:, 0, kc:kc + 1])
            prep(WB_L0P2, lambda kc, c0: kc * 2048 + 1024 + c0, ab_w_in[:, 5664:6688], 8, 1024, lambda kc: gcol[:, 0, kc:kc + 1])
            prep(WB_L0P2, lambda kc, c0: 16 * 1024 + kc * 1024 + c0, ab_w_out, 16, 1024, lambda kc: gcol[:, 4 + kc // 8, (kc % 8):(kc % 8) + 1])
            LD(exS[:, 0:1024], Hs[:], [Hs], [DK("exS", 0)])
            LD(exS[:, 1024:2056], Cm[:], [Cm], [DK("exS", 0)])
            allgather(exS, exG, DK("exS", 0), DK("exG", 0))

            S.barrier()
            S.clear_reg("arena")
            SC("L0P2_w")
            aoff[0] = 0
            Sel = ar("Sel", [112, 48, 128], BF16)
            Hsb = ar("Hsb", [128, 1024], F32)
            Cmb = ar("Cmb", [128, 1032], F32)
            sv = aoff[0]
            gath = ar("gath", [128, 2, 2056], F32)
            LD(gath[:], exG.rearrange("(r p) n -> p r n", p=128), [DK("exG", 0)], [gath])
            select2(Hsb[:], gath[:, 0, 0:1024], gath[:, 1, 0:1024], [gath, selt], [Hsb])
            select2(Cmb[:], gath[:, 0, 1024:2056], gath[:, 1, 1024:2056], [gath, selt], [Cmb])
            ones3 = ar("ones3", [112, 48, 128], BF16)
            selA = ar("selA", [112, 48, 128], BF16)
            selB = ar("selB", [112, 48, 128], BF16)
            G(lambda h: h.memset(ones3[:], 1.0), [], [ones3])
            G(lambda h: h.affine_select(out=selA[:], in_=ones3[:], pattern=[[1, 48], [0, 128]], compare_op=ALU.is_equal, fill=0.0, base=0, channel_multiplier=-1), [ones3], [selA])
            G(lambda h: h.affine_select(out=selB[:], in_=ones3[:], pattern=[[1, 48], [0, 128]], compare_op=ALU.is_equal, fill=0.0, base=64, channel_multiplier=-1), [ones3], [selB])
            G(lambda h: h.tensor_tensor(out=Sel[:], in0=selA[:], in1=selB[:], op=ALU.add), [selA, selB], [Sel])
            S.barrier()
            for tt in (ones3, selA, selB, gath):
                S.unregister("arena", tt)
            aoff[0] = sv
            WT2 = ar("W2", [128, 32, 1024], BF16, manual=True)
            Wzo = WT2[:, 0:16, :].rearrange("p a b -> p (a b)").rearrange("p (k n) -> p k n", k=8)
            Wo = WT2[:, 16:32, :]
            dtb = ar("dtb", [128, 32], F32)
            negA = ar("negA", [128, 32], F32)
            igfg = ar("igfg", [128, 32], F32)
            dsk = ar("dsk", [128, 16], F32)
            cw = None
            LD(dtb[:], ab_dt_bias.partition_broadcast(128), [WK], [dtb])
            LD(negA[:], ab_a_log.partition_broadcast(128), [WK], [negA])
            LD(igfg[:, 0:16], ab_ig_bias.partition_broadcast(128), [WK], [igfg])
            LD(igfg[:, 16:32], ab_fg_bias.partition_broadcast(128), [WK], [igfg])
            LD(dsk[:], ab_d_skip.partition_broadcast(128), [WK], [dsk])
            A(lambda h: h.activation(out=negA[:], in_=negA[:], func=AF.Exp), [], [negA])
            V(lambda h: h.tensor_scalar(out=negA[:], in0=negA[:], scalar1=-1.0, scalar2=None, op0=ALU.mult), [], [negA])
            hsb16 = ar("hsb16", [128, 1024], BF16)
            cmb16 = ar("cmb16", [128, 1032], BF16)
            gw = ar("gw", [128, 64], F32)
            G48 = ar("G48", [128, 48], F32)
            cst = ar("cst", [128, 96], F32)
            scw = ar("scw", [128, 48], F32)
            xe = ar("xe", [128, 1024], BF16)
            kw = ar("kw", [128, 1024], BF16)
            dts = ar("dts", [128, 32], F32)
            igs = ar("igs", [128, 16], F32)
            pt = [ar("pt%d" % i, [128, 3072], BF16) for i in range(2)]
            xsb = [ar("xsb%d" % i, [128, 1280], BF16) for i in range(2)]
            bct = [ar("bct%d" % i, [128, 4, 128], BF16) for i in range(2)]
            gt = [ar("gt%d" % i, [128, 64], F32) for i in range(2)]
            hsl = [ar("hsl%d" % i, [128, 1024], BF16) for i in range(2)]
            hml = [ar("hml%d" % i, [128, 1032], BF16) for i in range(2)]
            hTc = ar("hTc", [128, 8, 128], BF16)
            zo = ar("zo", [128, 2048], BF16)
            LS = ar("LS", [128, 2, 112], F32)
            X2t = ar("X2t", [112, 2, 128], BF16)
            tmpb = ar("tmpb", [112, 2, 128], BF16)
            DT = [ar("DT%d" % i, [128, 512], BF16) for i in range(2)]
            WT = [ar("WT%d" % i, [128, 512], BF16) for i in range(2)]
            cbt = ar("cbt", [128, 2, 128], BF16)
            xdt = ar("xdt", [128, 2, 1024], BF16)
            sfac = ar("sfac", [128, 48], F32)
            T1 = ar("T1", [128, 1024], F32)
            T2 = ar("T2", [128, 1024], F32)
            yv = ar("yv", [128, 1024], F32)
            ycat = ar("ycat", [128, 2048], BF16)
            kqT = ar("kqT", [128, 16, 128], BF16)
            hdir = ar("hdir", [128, 512], F32)
            dn = ar("dn", [128, 16], F32)
            YT = ar("YT", [128, 16, 128], BF16)
            sz, hmv, xo = T2, yv, T1
            wzoT, woT = Trk("wzoT"), Trk("woT")
            WB2v = WB_L0P2.rearrange("p (a n) -> p a n", a=32)
            for q in range(2):
                LD(WT2[:, q * 8:(q + 1) * 8, :], WB2v[:, q * 8:(q + 1) * 8, :], [], [wzoT])
            for q in range(2, 4):
                LD(WT2[:, q * 8:(q + 1) * 8, :], WB2v[:, q * 8:(q + 1) * 8, :], [], [woT])
            LD(gres[:], norm_g[0, 1].partition_broadcast(128), [WK], [gres])
            NEG4 = ar("NEG4", [128, 4, 128], BF16)
            for j in range(4):
                G(lambda h, j=j: h.tensor_copy(out=NEG4[:, j, :], in_=NEG[:, j % 2, :]), [NEG], [NEG4])
            V(lambda h: h.memset(LS[:], 0.0), [], [LS])
            V(lambda h: h.memset(X2t[:], 0.0), [], [X2t])
            V(lambda h: h.memset(tmpb[:], 0.0), [], [tmpb])

            def p2_loads(c):
                i = c % 2
                LD(pt[i][:], PT0[c * 128:(c + 1) * 128, :], [DK("PT0", c)], [pt[i]])
                LD(xsb[i][:], XSB[c * 128:(c + 1) * 128, :], [DK("XSB", c)], [xsb[i]])
                LD(bct[i][:].rearrange("p a b -> p (a b)"), BCT[c], [DK("BCT", c)], [bct[i]])
                LD(gt[i][:], GT[c * 128:(c + 1) * 128, :], [DK("GT", c)], [gt[i]])
                LD(hsl[i][:], HS[c], [DK("HS", c)], [hsl[i]])
                LD(hml[i][:], HM[c], [DK("HM", c)], [hml[i]])
                LD(xin[i][:], x_in[c * 128:(c + 1) * 128, :], [xk], [xin[i]])

            def decay_block(b, hds):
                bk = PS[b % 2]
                MM(bk, bk[:, :], identb[:], NEG4[:], [identb, NEG4], start=True, stop=False)
                for j, (hd, d) in enumerate(hds):
                    o = bk[:, j * 128:(j + 1) * 128]
                    MM(bk, o, Sel[:, hd, :], X2t[:, 0, :], [Sel, X2t], start=False, stop=(j == 3))
                for j, (hd, d) in enumerate(hds):
                    A(lambda h, j=j, hd=hd: h.activation(out=DT[b % 2][:, j * 128:(j + 1) * 128], in_=bk[:, j * 128:(j + 1) * 128], func=AF.Exp, bias=LS[:, 1, hd:hd + 1]), [LS], [bk, DT[b % 2]])

            SC("L0P2")
            p2_loads(NC - 1)
            for c in range(NC - 1, -1, -1):
                if c - 1 >= 0:
                    p2_loads(c - 1)
                i2 = c % 2
                ptc, xsc, bcc, gtc, hsc, hmc, xic = pt[i2], xsb[i2], bct[i2], gt[i2], hsl[i2], hml[i2], xin[i2]
                norm_chunk(x_in, c, xk, xic, load=False)
                transpose8(hb, 0, hTc[:], [hTc], eng="act")
                for j in range(4):
                    bk = PS[4 + j % 2]
                    for k in range(8):
                        MM(bk, bk[:, :], hTc[:, k, :], Wzo[:, k, j * 512:(j + 1) * 512], [hTc, wzoT], start=(k == 0), stop=(k == 7))
                    A(lambda h, bk=bk, j=j: h.activation(out=zo[:, j * 512:(j + 1) * 512], in_=bk[:, :], func=AF.Copy), [], [bk, zo])
                gates(gtc, [0, 1])
                cumsums([0, 1], PS[2])
                V(lambda h: h.tensor_copy(out=LS[:, 0, 0:48], in_=cst[:, 0:48]), [cst], [LS])
                V(lambda h: h.tensor_scalar(out=LS[:, 1, 0:32], in0=cst[:, 0:32], scalar1=-1.0, scalar2=None, op0=ALU.mult), [cst], [LS])
                V(lambda h: h.tensor_tensor(out=LS[:, 1, 32:48], in0=igs[:, 0:16], in1=cst[:, 32:48], op=ALU.subtract), [cst, igs], [LS])
                V(lambda h: h.tensor_copy(out=LS[:, :, 64:112], in_=LS[:, :, 0:48]), [], [LS])
                TR(PS[2], PS[2][0:112, 0:128], LS[:, 0, :], identf[:], [LS, identf])
                V(lambda h: h.tensor_copy(out=X2t[0:48, 0, :], in_=PS[2][0:48, 0:128]), [], [PS[2], X2t])
                V(lambda h: h.tensor_copy(out=tmpb[64:112, 0, :], in_=PS[2][64:112, 0:128]), [], [PS[2], tmpb])
                V(lambda h: h.tensor_tensor(out=X2t[64:112, 0, :], in0=PS[2][64:112, 0:128], in1=tmpb[64:112, 0, :], op=ALU.subtract), [tmpb], [PS[2], X2t])
                A(lambda h: h.activation(out=sfac[:], in_=cst[:, 0:48], func=AF.Exp), [cst], [sfac])
                for g in range(2):
                    MM(PS[2], PS[2][:, 256 + g * 128:384 + g * 128], bcc[:, g, :], bcc[:, 2 + g, :], [bcc])
                V(lambda h: h.tensor_copy(out=cbt[:], in_=v3(PS[2][:, 256:512], 2)), [], [PS[2], cbt])
                for d in range(2):
                    V(lambda h, d=d: h.tensor_tensor(out=v3(xdt[:, d, :], 16), in0=v3(xsc[:, 0:1024], 16), in1=bc_in(dts[:, d * 16:d * 16 + 16], 16, 64), op=ALU.mult), [xsc, dts], [xdt])
                for b in range(8):
                    g, e0 = b // 4, (b % 4) * 2
                    hds = [(g * 8 + e0, 0), (16 + g * 8 + e0, 1), (g * 8 + e0 + 1, 0), (16 + g * 8 + e0 + 1, 1)]
                    decay_block(b, hds)
                    V(lambda h, b=b, g=g: h.tensor_tensor(out=v3(WT[b % 2][:], 4), in0=v3(DT[b % 2][:], 4), in1=bc_mid(cbt[:, g, :], 4, 128), op=ALU.mult), [DT[b % 2], cbt], [WT[b % 2]])
                    for j, (hd, d) in enumerate(hds):
                        hh = hd % 16
                        e = hh % 8
                        MM(PS[4 + g], PS[4 + g][:, e * 64:(e + 1) * 64], WT[b % 2][:, j * 128:(j + 1) * 128], xdt[:, d, hh * 64:(hh + 1) * 64], [WT[b % 2], xdt], start=(d == 0), stop=(d == 1))
                A(lambda h: h.activation(out=hsb16[:], in_=Hsb[:], func=AF.Copy), [Hsb], [hsb16])
                for g in range(2):
                    MM(PS[6 + g], PS[6 + g][:, :], bcc[:, 2 + g, :], hsc[:, g * 512:(g + 1) * 512], [bcc, hsc])
                for g in range(2):
                    V(lambda h, g=g: h.tensor_tensor(out=v3(T1[:, g * 512:(g + 1) * 512], 8), in0=v3(PS[6 + g][:, :], 8), in1=bc_in(sfac[:, g * 8:g * 8 + 8], 8, 64), op=ALU.mult), [sfac], [PS[6 + g], T1])
                for g in range(2):
                    MM(PS[6 + g], PS[6 + g][:, :], bcc[:, 2 + g, :], hsb16[:, g * 512:(g + 1) * 512], [bcc, hsb16])
                for g in range(2):
                    V(lambda h, g=g: h.tensor_tensor(out=v3(T2[:, g * 512:(g + 1) * 512], 8), in0=v3(PS[6 + g][:, :], 8), in1=bc_in(sfac[:, 16 + g * 8:24 + g * 8], 8, 64), op=ALU.mult), [sfac], [PS[6 + g], T2])
                G(lambda h: h.tensor_tensor(out=T1[:], in0=T1[:], in1=T2[:], op=ALU.add), [T2], [T1])
                G(lambda h: h.tensor_tensor(out=v3(T2[:], 16), in0=v3(xsc[:, 0:1024], 16), in1=bc_in(dsk[:], 16, 64), op=ALU.mult), [xsc, dsk], [T2])
                G(lambda h: h.tensor_tensor(out=T1[:], in0=T1[:], in1=T2[:], op=ALU.add), [T2], [T1])
                for g in range(2):
                    V(lambda h, g=g: h.tensor_tensor(out=yv[:, g * 512:(g + 1) * 512], in0=PS[4 + g][:, :], in1=T1[:, g * 512:(g + 1) * 512], op=ALU.add), [T1], [PS[4 + g], yv])
                sigmoid_act(sz[:], zo[:, 0:1024], [zo], [sz])
                G(lambda h: h.tensor_tensor(out=sz[:], in0=sz[:], in1=zo[:, 0:1024], op=ALU.mult), [zo], [sz])
                V(lambda h: h.tensor_tensor(out=yv[:], in0=yv[:], in1=sz[:], op=ALU.mult), [sz], [yv])
                for g in range(2):
                    rstd_from(yv[:, g * 512:(g + 1) * 512], 512, 1 + g, sqj[:, g * 512:(g + 1) * 512], [yv])
                for g in range(2):
                    V(lambda h, g=g: h.tensor_scalar(out=ycat[:, g * 512:(g + 1) * 512], in0=yv[:, g * 512:(g + 1) * 512], scalar1=ss[:, 1 + g:2 + g], scalar2=None, op0=ALU.mult), [yv, ss], [ycat])
                transpose8(ptc, 1024, kqT[:, 0:8, :], [kqT], eng="act")
                transpose8(ptc, 0, kqT[:, 8:16, :], [kqT], eng="act")
                A(lambda h: h.activation(out=cmb16[:], in_=Cmb[:], func=AF.Copy), [Cmb], [cmb16])
                for hf in range(2):
                    h0 = hf * 4
                    for j in range(4):
                        MM(PS[4], PS[4][:, j * 128:(j + 1) * 128], kqT[:, h0 + j, :], kqT[:, 8 + h0 + j, :], [kqT])
                    for bb in range(2):
                        b = 8 + hf * 2 + bb
                        ha = h0 + bb * 2
                        hds = [(32 + ha, 0), (40 + ha, 1), (32 + ha + 1, 0), (40 + ha + 1, 1)]
                        decay_block(b, hds)
                        V(lambda h, b=b, bb=bb: h.tensor_tensor(out=WT[b % 2][:].rearrange("p (a d b) -> p a d b", a=2, d=2),
                                                               in0=v3(PS[4][:, bb * 256:(bb + 1) * 256], 2).unsqueeze(2).to_broadcast([128, 2, 2, 128]),
                                                               in1=DT[b % 2][:].rearrange("p (a d b) -> p a d b", a=2, d=2), op=ALU.mult), [DT[b % 2]], [PS[4], WT[b % 2]])
                        for j, (hd, d) in enumerate(hds):
                            hh = (hd - 32) % 8
                            bkn = PS[5 + d]
                            MM(bkn, bkn[:, (hh % 4) * 128:(hh % 4 + 1) * 128], WT[b % 2][:, j * 128:(j + 1) * 128], ptc[:, 2048 + hh * 128:2048 + (hh + 1) * 128], [WT[b % 2], ptc])
                            MM(PS[2], PS[2][:, d * 8 + hh:d * 8 + hh + 1], WT[b % 2][:, j * 128:(j + 1) * 128], onesb[:, 0:1], [WT[b % 2], onesb])
                    for d in range(2):
                        st16 = hmc if d == 0 else cmb16
                        for j in range(4):
                            hh = h0 + j
                            MM(PS[7], PS[7][:, j * 128:(j + 1) * 128], kqT[:, 8 + hh, :], st16[:, hh * 128:(hh + 1) * 128], [kqT, st16])
                            MM(PS[2], PS[2][:, 16 + d * 8 + hh:17 + d * 8 + hh], kqT[:, 8 + hh, :], st16[:, 1024 + hh:1025 + hh], [kqT, st16])
                        sc = sfac[:, 32 + d * 8 + h0:32 + d * 8 + h0 + 4]
                        V(lambda h, sc=sc: h.tensor_tensor(out=v3(hdir[:], 4), in0=v3(PS[7][:, :], 4), in1=bc_in(sc, 4, 128), op=ALU.mult), [sfac], [PS[7], hdir])
                        V(lambda h, d=d: h.tensor_tensor(out=hdir[:], in0=hdir[:], in1=PS[5 + d][:, :], op=ALU.add), [], [hdir, PS[5 + d]])
                        V(lambda h, d=d, sc=sc, h0=h0: h.tensor_tensor(out=dn[:, 0:4], in0=PS[2][:, 16 + d * 8 + h0:16 + d * 8 + h0 + 4], in1=sc, op=ALU.mult), [sfac], [PS[2], dn])
                        V(lambda h, d=d, h0=h0: h.tensor_tensor(out=dn[:, 0:4], in0=dn[:, 0:4], in1=PS[2][:, d * 8 + h0:d * 8 + h0 + 4], op=ALU.add), [], [PS[2], dn])
                        V(lambda h: h.scalar_tensor_tensor(out=dn[:, 4:8], in0=dn[:, 0:4], scalar=-1.0, in1=dn[:, 0:4], op0=ALU.mult, op1=ALU.max), [], [dn])
                        V(lambda h: h.tensor_scalar(out=dn[:, 4:8], in0=dn[:, 4:8], scalar1=1.0, scalar2=None, op0=ALU.max), [], [dn])
                        V(lambda h: h.reciprocal(out=dn[:, 8:12], in_=dn[:, 4:8]), [], [dn])
                        if d == 0:
                            V(lambda h, h0=h0: h.tensor_tensor(out=v3(hmv[:, h0 * 128:(h0 + 4) * 128], 4), in0=v3(hdir[:], 4), in1=bc_in(dn[:, 8:12], 4, 128), op=ALU.mult), [hdir, dn], [hmv])
                        else:
                            V(lambda h: h.tensor_tensor(out=v3(hdir[:], 4), in0=v3(hdir[:], 4), in1=bc_in(dn[:, 8:12], 4, 128), op=ALU.mult), [dn], [hdir])
                            V(lambda h, h0=h0: h.tensor_tensor(out=hmv[:, h0 * 128:(h0 + 4) * 128], in0=hmv[:, h0 * 128:(h0 + 4) * 128], in1=hdir[:], op=ALU.add), [hdir], [hmv])
                sigmoid_act(sz[:], zo[:, 1024:2048], [zo], [sz])
                V(lambda h: h.tensor_tensor(out=hmv[:], in0=hmv[:], in1=sz[:], op=ALU.mult), [sz], [hmv])
                G(lambda h: h.tensor_tensor(out=sz[:], in0=hmv[:], in1=hmv[:], op=ALU.mult), [hmv], [sz])
                V(lambda h: h.tensor_reduce(out=ss[:, 8:16], in_=v3(sz[:], 8), axis=AX.X, op=ALU.add), [sz], [ss])
                A(lambda h: h.activation(out=ss[:, 8:16], in_=ss[:, 8:16], func=AF.Ln, scale=1.0 / 128, bias=epsc[:, 0:1]), [epsc], [ss])
                A(lambda h: h.activation(out=ss[:, 8:16], in_=ss[:, 8:16], func=AF.Exp, scale=-0.5), [], [ss])
                V(lambda h: h.tensor_tensor(out=v3(ycat[:, 1024:2048], 8), in0=v3(hmv[:], 8), in1=bc_in(ss[:, 8:16], 8, 128), op=ALU.mult), [hmv, ss], [ycat])
                transpose8(ycat, 0, YT[:, 0:8, :], [YT], eng="act")
                transpose8(ycat, 1024, YT[:, 8:16, :], [YT], eng="act")
                for n2 in range(2):
                    for kk in range(16):
                        MM(PS[6 + n2], PS[6 + n2][:, :], YT[:, kk, :], Wo[:, kk, n2 * 512:(n2 + 1) * 512], [YT, woT], start=(kk == 0), stop=(kk == 15))
                resid_out([PS[6], PS[7]], xic, X1, c, "X1", xo)
                if c == NC - 1:
                    LD(exR[0], X1[T - 1:T, :], [DK("X1", c)], [DK("exR0", 0)])
                    allgather(exR[0], exRG[0], DK("exR0", 0), DK("exRG0", 0))
                state_step(1, xsc, ptc, Hsb, Cmb, lambda: None)
            prep_ffn_up(0)
            prep_ffn_dn(0)

        NBT = 256

        def ffn(li, src, src_name, dst, dst_name):
            reset_arena()
            SC("FFN%d_w" % li)
            NB = T // NBT
            CPB = NBT // 128
            Wup = ar("Wup", [128, 8, 5632], BF16, manual=True)
            Wdn = ar("Wdn", [128, 22, 1024], BF16, manual=True)
            fw = ar("fw", [128, 44, 3], F32)
            fb = ar("fb", [128, 44], F32)
            hT2 = [ar("hT2_%d" % i, [128, 8, NBT + 2], BF16, manual=True) for i in range(3)]
            hmain = [Trk("hmain%d" % i) for i in range(3)]
            hleft = [Trk("hleft%d" % i) for i in range(3)]
            hright = [Trk("hright%d" % i) for i in range(3)]
            usb = [ar("usb%d" % i, [128, NBT + 2], F32) for i in range(3)]
            cgs = [ar("cg%d" % i, [128, NBT], F32) for i in range(2)]
            cvs = [ar("cvv%d" % i, [128, NBT], F32) for i in range(2)]
            ggs = [ar("gg%d" % i, [128, NBT], F32) for i in range(2)]
            gTs = [ar("gT%d" % i, [128, NBT], BF16) for i in range(3)]
            xos = [ar("xo0", [128, 1024], F32)] * 2
            neg20 = ar("neg20", [128, NBT], F32)
            G(lambda h: h.memset(neg20[:], -20.0), [], [neg20])
            xr = [ar("xr0", [128, 1024], F32)] * 2
            for j in range(3):
                LDS(fw[:, :, j], ffn_conv_w[li, j].rearrange("(i p) -> p i", p=128), [WK], [fw])
            LDS(fb[:], ffn_conv_b[li].rearrange("(i p) -> p i", p=128), [WK], [fb])
            LD(gres[:], norm_g[li, 3].partition_broadcast(128), [WK], [gres])
            wgT = [Trk("wffn0"), Trk("wffn1")]
            WBu = WB_up[li].rearrange("p (k n) -> p k n", k=8)
            WBd = WB_dn[li].rearrange("p (k n) -> p k n", k=22)
            for g in range(2):
                for a in (g * 1408, 2816 + g * 1408):
                    LD(Wup[:, :, a:a + 1408], WBu[:, :, a:a + 1408], [], [wgT[g]])
                LD(Wdn[:, g * 11:(g + 1) * 11, :], WBd[:, g * 11:(g + 1) * 11, :], [], [wgT[g]])
            for i in range(3):
                V(lambda h, i=i: h.memset(hT2[i][:], 0.0), [], [hmain[i], hleft[i], hright[i]])
            hcand = ar("hcand", [128, 8, 2], BF16)
            halo1 = ar("halo1", [128, 8, 1], BF16)
            halo_cands(exRG[li], DK("exRG%d" % li, 0), 2, hcand)
            select2(halo1[:], hcand[:, :, 0:1], hcand[:, :, 1:2], [hcand, selt], [halo1])

            def norm_block(b):
                p, pp = b % 3, (b - 1) % 3
                for cc in range(CPB):
                    c = b * CPB + cc
                    norm_chunk(src, c, DK(src_name, c), xin[c % 2])
                    transpose8(hb, 0, hT2[p][:, :, 1 + cc * 128:1 + (cc + 1) * 128], [hmain[p]])
                if b > 0:
                    V(lambda h: h.tensor_copy(out=hT2[pp][:, :, NBT + 1:NBT + 2], in_=hT2[p][:, :, 1:2]), [hmain[p]], [hright[pp]])
                    V(lambda h: h.tensor_copy(out=hT2[p][:, :, 0:1], in_=hT2[pp][:, :, NBT:NBT + 1]), [hmain[pp]], [hleft[p]])
                else:
                    V(lambda h: h.memset(hT2[p][:, :, 0:1], 0.0), [], [hleft[p]])
                if b + 1 == NB:
                    V(lambda h: h.tensor_copy(out=hT2[p][:, :, NBT + 1:NBT + 2], in_=halo1[:]), [halo1], [hright[p]])

            SC("FFN%d" % li)
            norm_block(0)
            if NB > 1:
                norm_block(1)
            tctr = [0]
            accb = [[PS[0], PS[1]], [PS[2], PS[7]]]
            for b in range(NB):
                p = b % 3
                if b + 2 < NB:
                    norm_block(b + 2)
                for i in range(22):
                    hr = [hmain[p], hleft[p], hright[p], wgT[i // 11]]
                    cg, cvv, gg, gTi = cgs[i % 2], cvs[i % 2], ggs[i % 2], gTs[i % 3]
                    for t in (i, 22 + i):
                        q = tctr[0] % 3
                        tctr[0] += 1
                        bk = PS[4 + q]
                        for k in range(8):
                            MM(bk, bk[:, 0:NBT + 2], Wup[:, k, t * 128:(t + 1) * 128], hT2[p][:, k, :], hr, start=(k == 0), stop=(k == 7))
                        us = usb[q]
                        A(lambda h, bk=bk, us=us: h.activation(out=us[:], in_=bk[:, 0:NBT + 2], func=AF.Copy), [], [bk, us])
                        cv = cg if t < 22 else cvv
                        A(lambda h, us=us, cv=cv, t=t: h.activation(out=cv[:], in_=us[:, 1:NBT + 1], func=AF.Identity, scale=fw[:, t, 1:2], bias=fb[:, t:t + 1]), [us, fw, fb], [cv])
                        V(lambda h, us=us, cv=cv, t=t: h.scalar_tensor_tensor(out=cv[:], in0=us[:, 0:NBT], scalar=fw[:, t, 0:1], in1=cv[:], op0=ALU.mult, op1=ALU.add), [us, fw], [cv])
                        V(lambda h, us=us, cv=cv, t=t: h.scalar_tensor_tensor(out=cv[:], in0=us[:, 2:NBT + 2], scalar=fw[:, t, 2:3], in1=cv[:], op0=ALU.mult, op1=ALU.add), [us, fw], [cv])
                    A(lambda h, cg=cg, gg=gg: h.activation(out=gg[:], in_=cg[:], func=AF.Square), [cg], [gg])
                    V(lambda h, gg=gg: h.tensor_scalar(out=gg[:], in0=gg[:], scalar1=0.044715, scalar2=1.0, op0=ALU.mult, op1=ALU.add), [], [gg])
                    G(lambda h, gg=gg, cg=cg: h.tensor_tensor(out=gg[:], in0=gg[:], in1=cg[:], op=ALU.mult), [cg], [gg])
                    V(lambda h, gg=gg: h.tensor_scalar(out=gg[:], in0=gg[:], scalar1=-20.0, scalar2=None, op0=ALU.max), [], [gg])
                    A(lambda h, gg=gg: h.activation(out=gg[:], in_=gg[:], func=AF.Exp, scale=-1.5957691216), [], [gg])
                    A(lambda h, gg=gg: h.activation(out=gg[:], in_=gg[:], func=AF.Ln, bias=1.0), [], [gg])
                    A(lambda h, gg=gg: h.activation(out=gg[:], in_=gg[:], func=AF.Exp, scale=-1.0), [], [gg])
                    G(lambda h, gg=gg, cg=cg: h.tensor_tensor(out=gg[:], in0=gg[:], in1=cg[:], op=ALU.mult), [cg], [gg])
                    G(lambda h, gg=gg, cvv=cvv, gTi=gTi: h.tensor_tensor(out=gTi[:], in0=gg[:], in1=cvv[:], op=ALU.mult), [gg, cvv], [gTi])
                    for m in range(CPB):
                        for n2 in range(2):
                            bk = accb[m][n2]
                            MM(bk, bk[:, :], gTi[:, m * 128:(m + 1) * 128], Wdn[:, i, n2 * 512:(n2 + 1) * 512], [gTi, wgT[i // 11]], start=(i == 0), stop=(i == 21))
                for m in range(CPB):
                    c = b * CPB + m
                    xrc = xr[c % 2]
                    LD(xrc[:], src[c * 128:(c + 1) * 128, :], [DK(src_name, c)], [xrc])
                    resid_out(accb[m], xrc, dst, c, dst_name, xos[c % 2])
            if li == 0:
                prep_on_pool[0] = True
                for (s0, n, d0, gq) in [(1024, 2048, 0, False), (0, 512, 2048, True), (512, 512, 2560, False), (3072, 32, 3072, False)]:
                    prep(WB_g, lambda kc, c0, d0=d0: kc * 3104 + d0 + c0, c_w_in[:, s0:s0 + n], 8, n,
                         (lambda kc: gcolq[:, kc:kc + 1]) if gq else (lambda kc: gcol[:, 2, kc:kc + 1]))
                prep(WB_o1, lambda kc, c0: kc * 1024 + c0, c_w_out, 8, 1024, lambda kc: gcol[:, 6, kc:kc + 1])
                prep_on_pool[0] = False

        def layer1():
            reset_arena()
            SC("L1P1_w")
            W1 = ar("Wg", [128, 8, 3104], BF16, manual=True)
            gw2 = ar("gw2", [16, 2, 512], BF16)
            gw2f = ar("gw2f", [16, 2, 512], F32)
            gbias = ar("gbias", [128, 2, 4], F32)
            ones1 = ar("ones1", [128, 128], F32)
            Sst = ar("Sst", [128, 1024], F32)
            n0 = len(dbs)
            hTc = ar2("hTc", [128, 8, 128], BF16)
            vr = ar2("vr", [128, 2048], BF16, 1)
            qkT = ar2("qkT", [128, 8, 128], BF16)
            lt = ar2("lt", [128, 8, 128], F32)
            glr = ar2("glr", [16, 2, 128], BF16)
            Lc = ar2("Lc", [128, 8, 128], F32)
            ex = ar2("ex", [128, 8, 128], F32)
            kend = ar2("kend", [128, 4, 128], BF16)
            kendT = ar2("kendT", [128, 512], BF16)
            S16 = ar2("S16", [128, 1024], BF16)
            tots = ar2("tots", [128, 16], F32)
            for d in range(2):
                LD(gw2f[:, d, :], c_gate_w2[d], [WK], [gw2f])
            V(lambda h: h.tensor_copy(out=gw2[:], in_=gw2f[:]), [gw2f], [gw2])
            for d in range(2):
                LDS(gbias[:, d, :], c_gate_b[d].rearrange("(j p) -> p j", p=128), [WK], [gbias])
            V(lambda h: h.tensor_scalar(out=gbias[:], in0=gbias[:], scalar1=-1.0, scalar2=None, op0=ALU.mult), [], [gbias])
            V(lambda h: h.memset(ones1[:], 1.0), [], [ones1])
            WBgv = WB_g.rearrange("p (k n) -> p k n", k=8)
            for q in range(2):
                LD(W1[:, q * 4:(q + 1) * 4, :], WBgv[:, q * 4:(q + 1) * 4, :], [], [W1])
            V(lambda h: h.memset(Sst[:], 0.0), [], [Sst])

            def scans(ltile, dirs, Lc, tots):
                for d in dirs:
                    for hh in range(4):
                        t = d * 4 + hh
                        V(lambda h, t=t: h.tensor_tensor_scan(out=Lc[:, t, :], data0=ones1[:], data1=ltile[:, t, :], initial=0.0, op0=ALU.mult, op1=ALU.add), [ones1, ltile], [Lc])
                    V(lambda h, d=d: h.tensor_copy(out=tots[:, d * 4:d * 4 + 4], in_=Lc[:, d * 4:d * 4 + 4, 127:128].rearrange("p a b -> p (a b)")), [Lc], [tots])
                    if d == 1:
                        V(lambda h: h.tensor_tensor(out=Lc[:, 4:8, :], in0=ltile[:, 4:8, :], in1=Lc[:, 4:8, :], op=ALU.subtract), [ltile], [Lc])
                        V(lambda h: h.tensor_tensor(out=Lc[:, 4:8, :], in0=Lc[:, 4:8, :], in1=bc_in(tots[:, 4:8], 4, 128), op=ALU.add), [tots], [Lc])
                V(lambda h: h.tensor_scalar(out=tots[:, 8:16], in0=tots[:, 0:8], scalar1=-1.0 / 16, scalar2=None, op0=ALU.mult), [], [tots])

            def gla_state(d, qk_tile, vr_tile, Stt, store_fn, Lc, tots, ex, kend, kendT):
                for hh in range(4):
                    t = d * 4 + hh
                    A(lambda h, t=t: h.activation(out=ex[:, t, :], in_=Lc[:, t, :], func=AF.Exp, scale=1.0 / 16, bias=tots[:, 8 + t:9 + t]), [Lc, tots], [ex])
                V(lambda h: h.tensor_tensor(out=kend[:], in0=qk_tile[:, 4:8, :], in1=ex[:, d * 4:d * 4 + 4, :], op=ALU.mult), [qk_tile, ex], [kend])
                pb = PS[3][:].bitcast(BF16)
                for hh in range(4):
                    TR(PS[3], pb[:, hh * 128:(hh + 1) * 128], kend[:, hh, :], identb[:], [kend, identb])
                V(lambda h: h.tensor_copy(out=kendT[:], in_=pb[:, 0:512]), [], [PS[3], kendT])
                for hh in range(4):
                    bk = PS[4 + hh // 2]
                    MM(bk, bk[:, (hh % 2) * 256:(hh % 2 + 1) * 256], kendT[:, hh * 128:(hh + 1) * 128], vr_tile[:, hh * 256:(hh + 1) * 256], [kendT, vr_tile])
                store_fn()
                A(lambda h: h.activation(out=tots[:, 0:4], in_=tots[:, 8 + d * 4:12 + d * 4], func=AF.Exp), [], [tots])
                for hh in range(4):
                    bk = PS[4 + hh // 2]
                    V(lambda h, hh=hh, bk=bk: h.scalar_tensor_tensor(out=Stt[:, hh * 256:(hh + 1) * 256], in0=Stt[:, hh * 256:(hh + 1) * 256], scalar=tots[:, hh:hh + 1], in1=bk[:, (hh % 2) * 256:(hh % 2 + 1) * 256], op0=ALU.mult, op1=ALU.add), [tots], [Stt, bk])

            x2k = lambda c: DK("X2", c)
            SC("L1P1")
            for c in range(NC):
                setpar(c)
                i2 = c % 2
                norm_chunk(X2, c, x2k(c), xin[i2])
                transpose8(hb, 0, hTc[:], [hTc])
                for j in range(4):
                    bk = PS[j % 2]
                    for k in range(8):
                        MM(bk, bk[:, :], hTc[:, k, :], W1[:, k, j * 512:(j + 1) * 512], [hTc, W1], start=(k == 0), stop=(k == 7))
                    if j % 2 == 0:
                        A(lambda h, bk=bk, j=j: h.activation(out=vr[:, j * 512:(j + 1) * 512], in_=bk[:, :], func=AF.Copy), [], [bk, vr])
                    else:
                        V(lambda h, bk=bk, j=j: h.tensor_copy(out=vr[:, j * 512:(j + 1) * 512], in_=bk[:, :]), [], [bk, vr])
                LD(VR[c * 128:(c + 1) * 128, :], vr[:], [vr], [DK("VR", c)])
                for i in range(8):
                    bk = PS[6 + i // 4]
                    for k in range(8):
                        MM(bk, bk[:, (i % 4) * 128:(i % 4 + 1) * 128], W1[:, k, 2048 + i * 128:2048 + (i + 1) * 128], hTc[:, k, :], [hTc, W1], start=(k == 0), stop=(k == 7))
                    if i % 4 == 3:
                        q4 = i // 4
                        A(lambda h, bk=bk, q4=q4: h.activation(out=qkT[:, q4 * 4:q4 * 4 + 4, :], in_=v3(bk[:, :], 4), func=AF.Copy), [], [bk, qkT])
                LD(QKT[c], qkT[:].rearrange("p a b -> p (a b)"), [qkT], [DK("QKT", c)])
                for d in range(2):
                    for k in range(8):
                        MM(PS[2], PS[2][0:16, d * 128:(d + 1) * 128], W1[:, k, 3072 + d * 16:3088 + d * 16], hTc[:, k, :], [hTc, W1], start=(k == 0), stop=(k == 7))
                V(lambda h: h.tensor_copy(out=glr[:], in_=v3(PS[2][0:16, 0:256], 2)), [], [PS[2], glr])
                for d in range(2):
                    bk = PS[d]
                    for j in range(4):
                        MM(bk, bk[:, j * 128:(j + 1) * 128], gw2[:, d, j * 128:(j + 1) * 128], glr[:, d, :], [gw2, glr])
                    for j in range(4):
                        t = d * 4 + j
                        A(lambda h, bk=bk, j=j, t=t, d=d: h.activation(out=lt[:, t, :], in_=bk[:, j * 128:(j + 1) * 128], func=AF.Exp, scale=-1.0, bias=gbias[:, d, j:j + 1]), [gbias], [bk, lt])
                A(lambda h: h.activation(out=lt[:], in_=lt[:], func=AF.Ln, bias=1.0), [], [lt])
                LD(LTd[c], lt[:].rearrange("p a b -> p (a b)"), [lt], [DK("LTd", c)])
                scans(lt, [0], Lc, tots)

                def store1(c=c):
                    A(lambda h: h.activation(out=S16[:], in_=Sst[:], func=AF.Copy), [Sst], [S16])
                    LD(SG[c], S16[:], [S16], [DK("SG", c)])
                gla_state(0, qkT, vr, Sst, store1, Lc, tots, ex, kend, kendT)
            prep_ffn_up(1)
            LD(exS1, Sst[:], [Sst], [DK("exS1", 0)])
            allgather(exS1, exG1, DK("exS1", 0), DK("exG1", 0))
            del dbs[n0:]

            S.barrier()
            S.clear_reg("arena")
            aoff[0] = 0
            SC("L1P2_w")
            Wo = ar("Wo1", [128, 8, 1024], BF16, manual=True)
            ones1 = ar("ones1", [128, 128], F32)
            Sb = ar("Sb", [128, 1024], F32)
            gath1 = ar("gath1", [128, 2, 1024], F32)
            n0 = len(dbs)
            vr = ar2("vr", [128, 2048], BF16, 2)
            qkT = ar2("qkT", [128, 8, 128], BF16, 2)
            lt = ar2("lt", [128, 8, 128], F32, 2)
            sfl = ar2("sfl", [128, 1024], BF16, 2)
            xres = ar2("xres", [128, 1024], F32, 2)
            Lc = ar2("Lc", [128, 8, 128], F32)
            ex = ar2("ex", [128, 8, 128], F32)
            kend = ar2("kend", [128, 4, 128], BF16)
            kendT = ar2("kendT", [128, 512], BF16)
            tots = ar2("tots", [128, 16], F32)
            Sb16 = ar2("Sb16", [128, 1024], BF16)
            qin = ar2("qin", [128, 8, 128], BF16)
            kout = ar2("kout", [128, 8, 128], BF16)
            ex2 = ar2("ex2", [128, 8, 128], F32)
            ex3 = ar2("ex3", [128, 8, 128], F32)
            attm = ar2("attm", [128, 8, 128], BF16)
            ov = ar2("ov", [128, 1024], F32)
            t4 = ar2("t4", [128, 1024], F32)
            t5 = ar2("t5", [128, 1024], F32)
            ycat = ar2("ycat1", [128, 1024], BF16)
            YT = ar2("YT1", [128, 8, 128], BF16)
            xo = ar2("xo1", [128, 1024], F32)
            V(lambda h: h.memset(ones1[:], 1.0), [], [ones1])
            LD(Wo[:], WB_o1.rearrange("p (k n) -> p k n", k=8), [], [Wo])
            LD(gres[:], norm_g[1, 1].partition_broadcast(128), [WK], [gres])
            LD(gath1[:], exG1.rearrange("(r p) n -> p r n", p=128), [DK("exG1", 0)], [gath1])
            select2(Sb[:], gath1[:, 0, :], gath1[:, 1, :], [gath1, selt], [Sb])

            def p2_loads(c):
                setpar(c)
                LD(vr[:], VR[c * 128:(c + 1) * 128, :], [DK("VR", c)], [vr])
                LD(qkT[:].rearrange("p a b -> p (a b)"), QKT[c], [DK("QKT", c)], [qkT])
                LD(lt[:].rearrange("p a b -> p (a b)"), LTd[c], [DK("LTd", c)], [lt])
                LD(sfl[:], SG[c], [DK("SG", c)], [sfl])
                LD(xres[:], X2[c * 128:(c + 1) * 128, :], [x2k(c)], [xres])

            SC("L1P2")
            p2_loads(NC - 1)
            for c in range(NC - 1, -1, -1):
                if c - 1 >= 0:
                    p2_loads(c - 1)
                setpar(c)
                scans(lt, [0, 1], Lc, tots)
                A(lambda h: h.activation(out=ex2[:], in_=Lc[:], func=AF.Exp, scale=-1.0 / 16), [Lc], [ex2])
                for d in range(2):
                    V(lambda h, d=d: h.tensor_tensor(out=qin[:, d * 4:d * 4 + 4, :], in0=qkT[:, 0:4, :], in1=ex2[:, d * 4:d * 4 + 4, :], op=ALU.mult), [qkT, ex2], [qin])
                A(lambda h: h.activation(out=ex3[:], in_=Lc[:], func=AF.Exp, scale=1.0 / 16), [Lc], [ex3])
                for d in range(2):
                    G(lambda h, d=d: h.tensor_tensor(out=kout[:, d * 4:d * 4 + 4, :], in0=qkT[:, 4:8, :], in1=ex3[:, d * 4:d * 4 + 4, :], op=ALU.mult), [qkT, ex3], [kout])
                for d in range(2):
                    bk = PS[d]
                    for hh in range(4):
                        MM(bk, bk[:, hh * 128:(hh + 1) * 128], kout[:, d * 4 + hh, :], qin[:, d * 4 + hh, :], [kout, qin])
                    V(lambda h, d=d, bk=bk: h.tensor_tensor(out=attm[:, d * 4:d * 4 + 4, :], in0=v3(bk[:, :], 4), in1=bc_mid(maskb16[:, d, :], 4, 128), op=ALU.mult), [maskb16], [bk, attm])
                A(lambda h: h.activation(out=Sb16[:], in_=Sb[:], func=AF.Copy), [Sb], [Sb16])
                for hh in range(4):
                    bk = PS[6 + hh // 2]
                    o = bk[:, (hh % 2) * 256:(hh % 2 + 1) * 256]
                    vh = vr[:, hh * 256:(hh + 1) * 256]
                    MM(bk, o, attm[:, hh, :], vh, [attm, vr], start=True, stop=False)
                    MM(bk, o, qin[:, hh, :], sfl[:, hh * 256:(hh + 1) * 256], [qin, sfl], start=False, stop=False)
                    MM(bk, o, attm[:, 4 + hh, :], vh, [attm, vr], start=False, stop=False)
                    MM(bk, o, qin[:, 4 + hh, :], Sb16[:, hh * 256:(hh + 1) * 256], [qin, Sb16], start=False, stop=True)
                for q in range(2):
                    A(lambda h, q=q: h.activation(out=ov[:, q * 512:(q + 1) * 512], in_=PS[6 + q][:, :], func=AF.Copy), [], [PS[6 + q], ov])
                G(lambda h: h.tensor_tensor(out=t4[:], in0=ov[:], in1=ov[:], op=ALU.mult), [ov], [t4])
                V(lambda h: h.tensor_reduce(out=ss[:, 8:12], in_=v3(t4[:], 4), axis=AX.X, op=ALU.add), [t4], [ss])
                A(lambda h: h.activation(out=ss[:, 8:12], in_=ss[:, 8:12], func=AF.Ln, scale=1.0 / 256, bias=epsc[:, 0:1]), [epsc], [ss])
                A(lambda h: h.activation(out=ss[:, 8:12], in_=ss[:, 8:12], func=AF.Exp, scale=-0.5), [], [ss])
                sigmoid_act(t5[:], vr[:, 1024:2048], [vr], [t5])
                G(lambda h: h.tensor_tensor(out=t5[:], in0=t5[:], in1=vr[:, 1024:2048], op=ALU.mult), [vr], [t5])
                G(lambda h: h.tensor_tensor(out=v3(ov[:], 4), in0=v3(ov[:], 4), in1=bc_in(ss[:, 8:12], 4, 256), op=ALU.mult), [ss], [ov])
                V(lambda h: h.tensor_tensor(out=ycat[:], in0=ov[:], in1=t5[:], op=ALU.mult), [ov, t5], [ycat])
                transpose8(ycat, 0, YT[:], [YT])
                for n2 in range(2):
                    for kk in range(8):
                        MM(PS[4 + n2], PS[4 + n2][:, :], YT[:, kk, :], Wo[:, kk, n2 * 512:(n2 + 1) * 512], [YT, Wo], start=(kk == 0), stop=(kk == 7))
                resid_out([PS[4], PS[5]], xres, X3, c, "X3", xo)
                if c == NC - 1:
                    LD(exR[1], X3[T - 1:T, :], [DK("X3", c)], [DK("exR1", 0)])
                    allgather(exR[1], exRG[1], DK("exR1", 0), DK("exRG1", 0))
                gla_state(1, qkT, vr, Sb, lambda: None, Lc, tots, ex, kend, kendT)
            prep_ffn_dn(1)
            del dbs[n0:]

        final = None
        layer0()
        if dbg == "x1":
            final = (X1, "X1")
        else:
            ffn(0, X1, "X1", X2, "X2")
            if dbg == "x2":
                final = (X2, "X2")
            else:
                layer1()
                if dbg == "x3":
                    final = (X3, "X3")
                else:
                    ffn(1, X3, "X3", out, "out")
        SC(None)
        if final is not None:
            for c in range(NC):
                t = xin[c % 2]
                LD(t[:], final[0][c * 128:(c + 1) * 128, :], [DK(final[1], c)], [t])
                LD(out[c * 128:(c + 1) * 128, :], t[:], [t], [DK("out", c)])
        S.flush()
        print("instructions:", S.nins, "est_us=%.0f" % (S.est_ns / 1e3), flush=True)
    return nc


_CACHE = {}


def _core_inputs(inputs, b, half):
    f = lambda k: np.asarray(inputs[k], dtype=np.float32)
    x = f("x")[b]
    S_ = x.shape[0]
    T = S_ // 2
    rv = half == 1
    if not rv:
        xl = x[:T]
        xh = x[T:T + 2]
    else:
        xl = x[T:][::-1]
        xh = x[T - 2:T][::-1]
    w_in = f("ab_w_in")[0]
    cw = f("ab_conv_w")[0]
    dtb = f("ab_dt_bias")[0]
    alog = f("ab_a_log")[0]
    igb = f("ab_ig_bias")[0]
    fgb = f("ab_fg_bias")[0]
    c_w_in = f("c_w_in")[0]
    gw2 = f("c_gate_w2")[0]
    gb = f("c_gate_b")[0]
    fcw = f("ffn_conv_w")
    if rv:
        perm = np.arange(w_in.shape[1])
        perm[2560:2576], perm[2576:2592] = np.arange(2576, 2592), np.arange(2560, 2576)
        perm[6688:6696], perm[6696:6704] = np.arange(6696, 6704), np.arange(6688, 6696)
        perm[6704:6712], perm[6712:6720] = np.arange(6712, 6720), np.arange(6704, 6712)
        w_in = w_in[:, perm]
        cw = cw[::-1]
        dtb, alog, igb, fgb = dtb[::-1], alog[::-1], igb[::-1], fgb[::-1]
        p2 = np.arange(c_w_in.shape[1])
        p2[3072:3088], p2[3088:3104] = np.arange(3088, 3104), np.arange(3072, 3088)
        c_w_in = c_w_in[:, p2]
        gw2 = gw2[::-1]
        gb = gb[::-1]
        fcw = fcw[:, ::-1]
    c = np.ascontiguousarray
    return {
        "x": c(xl), "x_halo": c(xh), "sel": np.array([0.0, 1.0] if half == 0 else [1.0, 0.0], np.float32),
        "norm_g": c(f("norm_g")),
        "ab_w_in": c(w_in), "ab_conv_w": c(cw), "ab_conv_b": c(f("ab_conv_b")[0]),
        "ab_dt_bias": c(dtb).reshape(32), "ab_a_log": c(alog).reshape(32),
        "ab_d_skip": c(f("ab_d_skip")[0]), "ab_ssd_norm": c(f("ab_ssd_norm")[0]),
        "ab_ig_bias": c(igb).reshape(16), "ab_fg_bias": c(fgb).reshape(16),
        "ab_mlstm_norm": c(f("ab_mlstm_norm")[0]), "ab_w_out": c(f("ab_w_out")[0]),
        "c_w_in": c(c_w_in), "c_gate_w2": c(gw2), "c_gate_b": c(gb),
        "c_norm": c(f("c_norm")[0]), "c_w_out": c(f("c_w_out")[0]),
        "ffn_w_up": c(f("ffn_w_up")), "ffn_conv_w": c(fcw), "ffn_conv_b": c(f("ffn_conv_b")), "ffn_w_down": c(f("ffn_w_down")),
    }


def kernel(**inputs):
    x = np.asarray(inputs["x"])
    B, S_, _ = x.shape
    T = S_ // 2
    if T not in _CACHE:
        _CACHE[T] = build(T)
    nc = _CACHE[T]
    in_maps = [_core_inputs(inputs, b, half) for b in range(B) for half in range(2)]
    res = run_bass_kernel_spmd(nc, in_maps, core_ids=list(range(2 * B)))
    outp = np.empty((B, S_, D), np.float32)
    for b in range(B):
        outp[b, :T] = np.asarray(res.results[2 * b]["out"], dtype=np.float32)
        outp[b, T:] = np.asarray(res.results[2 * b + 1]["out"], dtype=np.float32)[::-1]
    return outp
```
